# Optimizing a Trainium2 kernel written in Bass

```python
import math
import jax, jax.numpy as jnp
from jax import lax
import numpy as np

D_MODEL = 1024
BATCH = 32
SEQ = 2048
DEPTH = 1

PLE_DIM = 256
ATT_HEADS = 8
HEAD_DIM = 64
ATT_W = ATT_HEADS * HEAD_DIM
MOBA_BLOCK = 256
MOBA_TOPK = 3
Q_CHUNK = 128
CONV_CH = 512
CONV_WIDTH = 31
NUM_BUCKETS = 32
MAX_DISTANCE = 128
FFN_HIDDEN = -(-8 * D_MODEL // (3 * 256)) * 256
N_IN = 3 * ATT_W + 2 * CONV_CH + 2 * D_MODEL
IN_SPLITS = [ATT_W, 2 * ATT_W, 3 * ATT_W, 3 * ATT_W + 2 * CONV_CH]
DEEPNORM_ALPHA = (2.0 * DEPTH) ** 0.25
DEEPNORM_BETA = (8.0 * DEPTH) ** -0.25
LN_EPS = 1e-5
NEG_INF = -1e30

kernel_name = "hybrid_moba_conformer_deepnorm_block"


def _layer_norm(x, g, b):
    xf = x.astype(jnp.float32)
    mu = jnp.mean(xf, axis=-1, keepdims=True)
    var = jnp.mean(jnp.square(xf - mu), axis=-1, keepdims=True)
    y = (xf - mu) * lax.rsqrt(var + LN_EPS)
    return (y * g.astype(jnp.float32) + b.astype(jnp.float32)).astype(x.dtype)


def _t5_bucket(rel):
    n = jnp.maximum(rel, 0)
    max_exact = NUM_BUCKETS // 2
    nf = jnp.maximum(n, 1).astype(jnp.float32)
    large = max_exact + (jnp.log(nf / max_exact) / math.log(MAX_DISTANCE / max_exact)
                         * (NUM_BUCKETS - max_exact)).astype(jnp.int32)
    large = jnp.minimum(large, NUM_BUCKETS - 1)
    return jnp.where(n < max_exact, n, large)


def _moba_attention(q, k, v, bias_table):
    b, s, h, dh = q.shape
    nb = -(-s // MOBA_BLOCK)
    s_pad = nb * MOBA_BLOCK
    topk = min(MOBA_TOPK, nb)
    n_chunks = s // Q_CHUNK
    scale = HEAD_DIM ** -0.5
    q = q.transpose(0, 2, 1, 3)
    pad = ((0, 0), (0, 0), (0, s_pad - s), (0, 0))
    k = jnp.pad(k.transpose(0, 2, 1, 3), pad)
    v = jnp.pad(v.transpose(0, 2, 1, 3), pad)
    k_mean = jnp.mean(k.reshape(b, h, nb, MOBA_BLOCK, dh).astype(jnp.float32), axis=3)
    gate = jnp.einsum('bhsd,bhnd->bhsn', q.astype(jnp.float32), k_mean)
    q_blk = jnp.arange(s) // MOBA_BLOCK
    past = jnp.arange(nb)[None, :] < q_blk[:, None]
    gate = jnp.where(past, gate, NEG_INF)
    _, sel = lax.top_k(gate, topk)
    sel_valid = sel < q_blk[:, None]
    tbl = bias_table.T.astype(jnp.float32)
    head_ix = jnp.arange(h)

    def per_batch(args):
        q_b, k_b, v_b, sel_b, valid_b = args
        kb = k_b.reshape(h, nb, MOBA_BLOCK, dh)
        vb = v_b.reshape(h, nb, MOBA_BLOCK, dh)
        q_c = q_b.reshape(h, n_chunks, Q_CHUNK, dh).transpose(1, 0, 2, 3)
        sel_c = sel_b.reshape(h, n_chunks, Q_CHUNK, topk).transpose(1, 0, 2, 3)
        valid_c = valid_b.reshape(h, n_chunks, Q_CHUNK, topk).transpose(1, 0, 2, 3)

        def per_chunk(cargs):
            qc, selc, validc, c = cargs
            q_pos = c * Q_CHUNK + jnp.arange(Q_CHUNK)
            own = (c * Q_CHUNK) // MOBA_BLOCK
            k_own = lax.dynamic_index_in_dim(kb, own, axis=1, keepdims=False)
            v_own = lax.dynamic_index_in_dim(vb, own, axis=1, keepdims=False)
            rel_own = q_pos[:, None] - (own * MOBA_BLOCK + jnp.arange(MOBA_BLOCK))[None, :]
            logit_own = (jnp.einsum('hqd,hkd->hqk', qc, k_own).astype(jnp.float32) * scale
                         + tbl[:, _t5_bucket(rel_own)])
            logit_own = jnp.where(rel_own >= 0, logit_own, NEG_INF)
            k_sel = kb[head_ix[:, None, None], selc]
            v_sel = vb[head_ix[:, None, None], selc]
            k_pos_sel = selc[..., None] * MOBA_BLOCK + jnp.arange(MOBA_BLOCK)
            rel_sel = q_pos[None, :, None, None] - k_pos_sel
            bias_sel = tbl[head_ix[:, None, None, None], _t5_bucket(rel_sel)]
            logit_sel = (jnp.einsum('hqd,hqrkd->hqrk', qc, k_sel).astype(jnp.float32) * scale
                         + bias_sel)
            logit_sel = jnp.where(validc[..., None], logit_sel, NEG_INF)
            logits = jnp.concatenate(
                [logit_sel.reshape(h, Q_CHUNK, topk * MOBA_BLOCK), logit_own], axis=-1)
            probs = jax.nn.softmax(logits, axis=-1).astype(v_b.dtype)
            p_sel = probs[..., :topk * MOBA_BLOCK].reshape(h, Q_CHUNK, topk, MOBA_BLOCK)
            p_own = probs[..., topk * MOBA_BLOCK:]
            return (jnp.einsum('hqrk,hqrkd->hqd', p_sel, v_sel)
                    + jnp.einsum('hqk,hkd->hqd', p_own, v_own))

        out_c = lax.map(per_chunk, (q_c, sel_c, valid_c, jnp.arange(n_chunks)))
        return out_c.transpose(1, 0, 2, 3).reshape(h, s, dh)

    out = lax.map(per_batch, (q, k, v, sel, sel_valid))
    return out.transpose(0, 2, 1, 3).reshape(b, s, h * dh)


def _conformer_conv(u_in, conv_w, conv_b, ln_g, ln_b, w_out):
    a, g = jnp.split(u_in, 2, axis=-1)
    u = a * jax.nn.sigmoid(g)
    y = lax.conv_general_dilated(u, conv_w[:, None, :], window_strides=(1,),
                                 padding=[(CONV_WIDTH - 1, 0)],
                                 dimension_numbers=('NWC', 'WIO', 'NWC'),
                                 feature_group_count=CONV_CH) + conv_b
    y = jax.nn.silu(_layer_norm(y, ln_g, ln_b))
    return y @ w_out


def setup_inputs(seed: int = 0) -> dict:
    key = jax.random.key(seed)
    ks = jax.random.split(key, 24)
    f32 = jnp.float32
    nrm = lambda k, shape, s: jax.random.normal(k, shape, f32) * s
    return {
        "x": nrm(ks[0], (BATCH, SEQ, D_MODEL), 1.0),
        "p": nrm(ks[1], (DEPTH, BATCH, SEQ, PLE_DIM), 1.0),
        "w_in": nrm(ks[2], (DEPTH, D_MODEL, N_IN), D_MODEL ** -0.5),
        "b_gate": nrm(ks[3], (DEPTH, 2 * D_MODEL), 0.02),
        "bias_table": nrm(ks[4], (NUM_BUCKETS, ATT_HEADS), 0.1),
        "w_att_out": nrm(ks[5], (DEPTH, ATT_W, D_MODEL), ATT_W ** -0.5 * DEEPNORM_BETA),
        "conv_w": nrm(ks[6], (DEPTH, CONV_WIDTH, CONV_CH), CONV_WIDTH ** -0.5),
        "conv_b": nrm(ks[7], (DEPTH, CONV_CH), 0.02),
        "conv_ln_g": 1.0 + nrm(ks[8], (DEPTH, CONV_CH), 0.02),
        "conv_ln_b": nrm(ks[9], (DEPTH, CONV_CH), 0.02),
        "w_conv_out": nrm(ks[10], (DEPTH, CONV_CH, D_MODEL), CONV_CH ** -0.5 * DEEPNORM_BETA),
        "w_mix_out": nrm(ks[11], (DEPTH, D_MODEL, D_MODEL), D_MODEL ** -0.5 * DEEPNORM_BETA),
        "ln_mix_g": 1.0 + nrm(ks[12], (DEPTH, D_MODEL), 0.02),
        "ln_mix_b": nrm(ks[13], (DEPTH, D_MODEL), 0.02),
        "w_ffn_gate": nrm(ks[14], (DEPTH, D_MODEL, FFN_HIDDEN), D_MODEL ** -0.5),
        "w_ffn_up": nrm(ks[15], (DEPTH, D_MODEL, FFN_HIDDEN), D_MODEL ** -0.5),
        "w_ffn_down": nrm(ks[16], (DEPTH, FFN_HIDDEN, D_MODEL), FFN_HIDDEN ** -0.5 * DEEPNORM_BETA),
        "w_ple": nrm(ks[17], (DEPTH, PLE_DIM, D_MODEL), PLE_DIM ** -0.5 * DEEPNORM_BETA),
        "w_ple_gate": nrm(ks[18], (DEPTH, D_MODEL, D_MODEL), D_MODEL ** -0.5),
        "b_ple_gate": nrm(ks[19], (DEPTH, D_MODEL), 0.02),
        "ln_ffn_g": 1.0 + nrm(ks[20], (DEPTH, D_MODEL), 0.02),
        "ln_ffn_b": nrm(ks[21], (DEPTH, D_MODEL), 0.02),
    }


def reference(x, p, w_in, b_gate, bias_table, w_att_out, conv_w, conv_b, conv_ln_g, conv_ln_b,
              w_conv_out, w_mix_out, ln_mix_g, ln_mix_b, w_ffn_gate, w_ffn_up, w_ffn_down,
              w_ple, w_ple_gate, b_ple_gate, ln_ffn_g, ln_ffn_b):
    b, s, _ = x.shape
    for i in range(DEPTH):
        proj = x @ w_in[i]
        q, k, v, u_in, g_logits = jnp.split(proj, IN_SPLITS, axis=-1)
        q = q.reshape(b, s, ATT_HEADS, HEAD_DIM)
        k = k.reshape(b, s, ATT_HEADS, HEAD_DIM)
        v = v.reshape(b, s, ATT_HEADS, HEAD_DIM)
        y_att = _moba_attention(q, k, v, bias_table) @ w_att_out[i]
        y_conv = _conformer_conv(u_in, conv_w[i], conv_b[i], conv_ln_g[i], conv_ln_b[i],
                                 w_conv_out[i])
        g_att, g_conv = jnp.split(jax.nn.sigmoid(g_logits + b_gate[i]), 2, axis=-1)
        mixed = (g_att * y_att + g_conv * y_conv) @ w_mix_out[i]
        x = _layer_norm(DEEPNORM_ALPHA * x + mixed, ln_mix_g[i], ln_mix_b[i])
        hid = jax.nn.silu(x @ w_ffn_gate[i]) * (x @ w_ffn_up[i])
        ffn = hid @ w_ffn_down[i]
        ple = jax.nn.sigmoid(x @ w_ple_gate[i] + b_ple_gate[i]) * (p[i] @ w_ple[i])
        x = _layer_norm(DEEPNORM_ALPHA * x + ffn + ple, ln_ffn_g[i], ln_ffn_b[i])
    return x
```

```python
import contextlib
import numpy as np
import concourse.bass as bass
import concourse.mybir as mybir
from concourse.bass_utils import run_bass_kernel_spmd

F32 = mybir.dt.float32
BF = mybir.dt.bfloat16
AF = mybir.ActivationFunctionType
ALU = mybir.AluOpType
AX = mybir.AxisListType

NCORES = 8
D = 1024
S = 2048
NH = 8
DH = 64
NIN = 4608
FFN = 2816
NJ = FFN // 128
PLE = 256
CW = 31
ALPHA = 2.0 ** 0.25
EPS = 1e-5
BIG = 30000.0
NCV = 161
DBG = False

ENGS = ("pe", "act", "dve", "pool", "sp")


class Tok:
    __slots__ = ("eng", "idx", "sem", "val", "is_dma", "stream", "key")

    def __init__(self, eng, idx):
        self.eng = eng
        self.idx = idx
        self.sem = None
        self.val = None
        self.is_dma = False
        self.stream = None
        self.key = eng


class Buf:
    def __init__(self, ap, name):
        self.ap = ap
        self.name = name
        self.w = {}
        self.r = {}


class Prog:
    SEM_ROT = 4000

    def __init__(self):
        self.ops = {e: [] for e in ENGS}
        self.streams = {}
        self.need = set()
        self.last = {}

    def op(self, eng, fn, deps=(), stream=None):
        t = Tok(eng, len(self.ops[eng]))
        deps = [d for d in deps if d is not None]
        if stream is not None:
            t.is_dma = True
            t.stream = stream
            t.key = "s:" + stream
            n = self.streams.get(stream, 0) + 1
            self.streams[stream] = n
            t.val = 16 * n
        self.ops[eng].append((t, fn, deps))
        for d in deps:
            if not d.is_dma:
                self.need.add((d.eng, d.idx))
        self.last[t.key] = t
        return t

    def E(self, eng, fn, reads=(), writes=(), stream=None, extra=()):
        deps = {}

        def add(t):
            k = t.key
            o = deps.get(k)
            if o is None or t.idx > o.idx or (t.is_dma and t.val > o.val):
                deps[k] = t

        for t in extra:
            if t is not None:
                add(t)
        for b in reads:
            for t in b.w.values():
                add(t)
        for b in writes:
            for t in b.w.values():
                add(t)
            for t in b.r.values():
                add(t)
        if eng == "pe":
            deps.pop("pe", None)
        tok = self.op(eng, fn, list(deps.values()), stream=stream)
        for b in reads:
            b.r[tok.key] = tok
        for b in writes:
            b.w[tok.key] = tok
        return tok

    def barrier(self, bufs):
        snap = dict(self.last)
        for b in bufs:
            for k, t in snap.items():
                b.r[k] = t
            b.w = {}

    def emit(self, nc, final_wait=()):
        with contextlib.ExitStack() as es:
            nsem = 0
            for e in ENGS:
                cnt = 0
                cur = None
                for t, fn, deps in self.ops[e]:
                    if t.is_dma:
                        continue
                    if (e, t.idx) in self.need:
                        if cur is None or cnt >= self.SEM_ROT:
                            cur = es.enter_context(nc.semaphore(f"c_{e}_{nsem}"))
                            nsem += 1
                            cnt = 0
                        cnt += 1
                        t.sem = cur
                        t.val = cnt
            dsem = {}
            for s in self.streams:
                dsem[s] = es.enter_context(nc.semaphore(f"d_{s}"))
            for e in ENGS:
                for t, fn, deps in self.ops[e]:
                    if t.is_dma:
                        t.sem = dsem[t.stream]
            block = es.enter_context(nc.Block())

            def make(e):
                def body(eng):
                    waited = {}
                    for t, fn, deps in self.ops[e]:
                        for d in deps:
                            k = id(d.sem)
                            if waited.get(k, 0) >= d.val:
                                continue
                            eng.wait_ge(d.sem, d.val)
                            waited[k] = d.val
                        ins = fn(eng)
                        if t.is_dma:
                            ins.then_inc(t.sem, 16)
                        elif t.sem is not None:
                            ins.then_inc(t.sem, 1)
                    if e == "sp":
                        for d in final_wait:
                            eng.wait_ge(d.sem, d.val)

                return body

            block.tensor(make("pe"))
            block.scalar(make("act"))
            block.vector(make("dve"))
            block.gpsimd(make("pool"))
            block.sync(make("sp"))


def build_nc(NB):
    nc = bass.Bass("TRN2", target_bir_lowering=False)
    dr = {}

    def din(name, shape):
        dr[name] = nc.dram_tensor(name, list(shape), F32, kind="ExternalInput").ap()
        return dr[name]

    x_d = din("x", [NB, S, D])
    p_d = din("p", [NB, S, PLE])
    w_in = din("w_in", [D, NIN]).rearrange("(kc p) n -> p kc n", p=128)
    w_ao = din("w_att_out", [512, D]).rearrange("(kc p) n -> p kc n", p=128)
    w_co = din("w_conv_out", [512, D]).rearrange("(kc p) n -> p kc n", p=128)
    w_mix = din("w_mix_out", [D, D]).rearrange("(kc p) n -> p kc n", p=128)
    w_fg = din("w_ffn_gate", [D, FFN]).rearrange("(kc p) n -> p kc n", p=128)
    w_fu = din("w_ffn_up", [D, FFN]).rearrange("(kc p) n -> p kc n", p=128)
    w_fd = din("w_ffn_down", [FFN, D]).rearrange("(kc p) n -> p kc n", p=128)
    w_pl = din("w_ple", [PLE, D]).rearrange("(kc p) n -> p kc n", p=128)
    w_pg = din("w_ple_gate", [D, D]).rearrange("(kc p) n -> p kc n", p=128)
    cvec_d = din("cvec", [128, NCV])
    lnv_d = din("lnv", [128, 4, D])
    bple_d = din("bple", [1, D])
    toep_d = din("toep", [128, NH, 256])
    oneh_d = din("onehot", [8, S])
    out_d = nc.dram_tensor("out", [NB, S, D], F32, kind="ExternalOutput").ap()

    P = Prog()
    E = P.E
    fin = []
    if DBG:
        dbg_att = nc.dram_tensor("dbg_att", [128, 4, S], F32, kind="ExternalOutput").ap()
        dbg_c = nc.dram_tensor("dbg_c", [128, 4, S], F32, kind="ExternalOutput").ap()

    with contextlib.ExitStack() as es:
        def sb(name, shape, dt):
            return es.enter_context(nc.sbuf_tensor("sb_" + name, list(shape), dt))

        def psum(name, shape, dt):
            return es.enter_context(nc.psum_tensor("ps_" + name, list(shape), dt))

        ident = Buf(sb("ident", [128, 128], BF)[:], "ident")
        ones_bf = Buf(sb("ones_bf", [128, 128], BF)[:], "ones")
        Ebf = Buf(sb("Ebf", [128, NH, 256], BF)[:], "Ebf")
        cvec = Buf(sb("cvec", [128, NCV], F32)[:], "cvec")
        negc = Buf(sb("negc", [128, NH], F32)[:], "negc")
        lnv = Buf(sb("lnv", [128, 4, D], F32)[:], "lnv")
        bple = Buf(sb("bple", [1, D], BF)[:], "bple")
        attT_t = sb("attT", [128, 4, S], BF)
        cT_t = sb("cT", [128, 4, S], BF)
        attT = [Buf(attT_t[:, :, g * 512:(g + 1) * 512], f"attT{g}") for g in range(4)]
        cT = [Buf(cT_t[:, c, :], f"cT{c}") for c in range(4)]
        NRING = 4
        ring = [Buf(sb(f"ring{i}", [128, 4096], BF)[:], f"ring{i}") for i in range(NRING)]
        ring_i = [0]

        def ring_next():
            b = ring[ring_i[0] % NRING]
            ring_i[0] += 1
            return b

        def slab(b, kc, n):
            return b.ap[:, 0:kc * n].rearrange("p (k n) -> p k n", k=kc)

        def wload(b, kc, n, src, k0=0, kn=None, n0=0):
            kn = kc - k0 if kn is None else kn
            nn = src.shape[2]
            dst = slab(b, kc, n)[:, k0:k0 + kn, n0:n0 + nn]
            return E("pool", lambda e: e.dma_start(out=dst, in_=src), writes=[b], stream=b.name)

        NU = 56320
        U = sb("U", [128, NU], BF)
        off = [0]

        def carve(nel, dt):
            nb = nel * (4 if dt == F32 else 2)
            nb = (nb + 63) // 64 * 64
            a = off[0] // 2
            off[0] += nb
            assert off[0] <= NU * 2, ("union overflow", off[0])
            v = U[:, a:a + nb // 2]
            if dt == F32:
                v = v.bitcast(F32)[:, 0:nel]
            else:
                v = v[:, 0:nel]
            return v

        off[0] = 0
        xT_t = carve(8 * S, BF).rearrange("p (k t) -> p k t", k=8)
        xT = [Buf(xT_t[:, :, g * 512:(g + 1) * 512], f"xT{g}") for g in range(4)]
        uT_t = carve(4 * (S + 30), BF).rearrange("p (c t) -> p c t", c=4)
        uT = [Buf(uT_t[:, c, :], f"uT{c}") for c in range(4)]
        qT = Buf(carve(2 * S, BF).rearrange("p (h t) -> p h t", h=2), "qT")
        kT = Buf(carve(2 * S, BF).rearrange("p (h t) -> p h t", h=2), "kT")
        vA = Buf(carve(16 * 2 * 65, BF).rearrange("p (j h d) -> p j h d", j=16, h=2), "vA")
        xb = [Buf(carve(2 * D, BF).rearrange("p (j d) -> p j d", j=2), f"xb{i}") for i in range(2)]
        acc = [Buf(carve(S, F32), "acc0")]
        sgt = [Buf(carve(512, F32), f"sgt{i}") for i in range(2)]
        ysq = Buf(carve(4 * 512, BF).rearrange("p (c t) -> p c t", c=4), "ysq")
        mean_t = Buf(carve(512, F32), "mean")
        msq_t = Buf(carve(512, F32), "msq")
        rstd_t = Buf(carve(512, F32), "rstd")
        t1_t = [Buf(carve(512, F32), f"t1_{i}") for i in range(2)]
        PT = [Buf(carve(512, BF), f"PT{i}") for i in range(4)]
        att_tm = [Buf(carve(4 * 128, BF).rearrange("p (s d) -> p s d", s=4), f"atm{i}") for i in range(2)]
        ksum = Buf(carve(8, F32), "ksum")
        kmean = Buf(carve(8, BF), "kmean")
        gs = Buf(carve(16, F32).rearrange("p (h n) -> p h n", h=2), "gs")
        g2 = Buf(carve(16, F32).rearrange("p (h n) -> p h n", h=2), "g2")
        eq = Buf(carve(16, F32).rearrange("p (h n) -> p h n", h=2), "eq")
        mx = Buf(carve(2, F32), "mx")
        mpad = Buf(carve(2 * 72, BF).rearrange("p (h n) -> p h n", h=2), "mpad")
        rinv = Buf(carve(4, F32), "rinv")
        AB_END = off[0]
        AB_BUFS = (xT + uT + [qT, kT, vA] + xb + acc + sgt + [ysq, mean_t, msq_t, rstd_t] + t1_t + PT
                   + att_tm + [ksum, kmean, gs, g2, eq, mx, mpad, rinv])

        off[0] = 0
        xres = [Buf(carve(D, F32), f"xres{i}") for i in range(2)]
        z_t = carve(4 * D, F32).rearrange("p (t d) -> p t d", t=4)
        z = [Buf(z_t[:, t, :], f"z{t}") for t in range(4)]
        mT = Buf(carve(8 * 512, BF).rearrange("p (k t) -> p k t", k=8), "mT")
        csg = [Buf(carve(512, F32), f"csg{i}") for i in range(2)]
        cpc = [Buf(carve(512, F32), f"cpc{i}") for i in range(2)]
        x1b = [Buf(carve(D, BF), f"x1b{i}") for i in range(2)]
        x1T = Buf(carve(8 * 512, BF).rearrange("p (k t) -> p k t", k=8), "x1T")
        hT_raw = carve(NJ * 512, BF)
        hT = Buf(hT_raw.rearrange("p (j t) -> p j t", j=NJ), "hT")
        hTf = hT_raw[:, 0:8192].bitcast(F32).rearrange("p (m t) -> p m t", m=8)
        xTg = Buf(carve(8 * 512, BF).rearrange("p (k t) -> p k t", k=8), "xTg")
        xbg = Buf(carve(4 * D, BF).rearrange("p (j d) -> p j d", j=4), "xbg")
        pbg = Buf(carve(4 * PLE, BF).rearrange("p (j d) -> p j d", j=4), "pbg")
        pT = Buf(carve(2 * 512, BF).rearrange("p (k t) -> p k t", k=2), "pT")
        lst = Buf(carve(12, F32).rearrange("p (a b) -> p a b", a=2), "lst")
        lmv = Buf(carve(2, F32), "lmv")
        lsd = Buf(carve(1, F32), "lsd")
        lrs = Buf(carve(1, F32), "lrs")
        lnm = Buf(carve(1, F32), "lnm")
        C_END = off[0]
        C_BUFS = (xres + z + [mT] + csg + cpc + x1b + [x1T, hT, xTg, xbg, pbg, pT, lst, lmv, lsd, lrs, lnm])

        PA = [Buf(psum(f"pa{i}", [128, 512], F32)[:], f"pa{i}") for i in range(3)]
        PBt = [psum(f"pb{i}", [128, 1024], F32) for i in range(2)]
        PBh = [Buf(PBt[i][:, h * 512:(h + 1) * 512], f"pb{i}{h}") for i in range(2) for h in range(2)]
        PM_t = psum("pm", [128, 1024], BF)
        PM = Buf(PM_t[:], "pm")
        pa_i = [0]

        def pa_next():
            b = PA[pa_i[0] % 3]
            pa_i[0] += 1
            return b

        E("sp", lambda e: e.dma_start(out=cvec.ap, in_=cvec_d), writes=[cvec], stream="c_cvec")
        E("sp", lambda e: e.dma_start(out=lnv.ap, in_=lnv_d), writes=[lnv], stream="c_lnv")
        E("pool", lambda e: e.dma_start(out=bple.ap, in_=bple_d), writes=[bple], stream="c_bple")
        idf = Buf(U[:, 0:256].bitcast(F32), "idf")
        tf = Buf(U[:, 4096:4096 + NH * 256 * 2].bitcast(F32).rearrange("p (h c) -> p h c", h=NH), "tf")
        E("dve", lambda e: e.memset(idf.ap, 0.0), writes=[idf])
        E("pool", lambda e: e.affine_select(out=idf.ap, in_=idf.ap, pattern=[[-1, 128]], compare_op=ALU.not_equal,
                                            fill=1.0, base=0, channel_multiplier=1), reads=[idf], writes=[idf])
        E("dve", lambda e: e.tensor_copy(out=ident.ap, in_=idf.ap), reads=[idf], writes=[ident])
        E("dve", lambda e: e.memset(ones_bf.ap, 1.0), writes=[ones_bf])
        E("sp", lambda e: e.dma_start(out=tf.ap, in_=toep_d), writes=[tf], stream="c_toep")
        E("dve", lambda e: e.tensor_scalar(out=negc.ap, in0=cvec.ap[:, 152:160], scalar1=-1.0, scalar2=None, op0=ALU.mult),
          reads=[cvec], writes=[negc])
        for h in range(NH):
            E("act", lambda e, h=h: e.activation(out=tf.ap[:, h, :], in_=tf.ap[:, h, :], func=AF.Exp,
                                                 bias=negc.ap[:, h:h + 1], scale=1.0), reads=[tf, negc], writes=[tf])
        E("pool", lambda e: e.affine_select(out=tf.ap, in_=tf.ap, pattern=[[0, NH], [1, 256]], compare_op=ALU.is_ge,
                                            fill=0.0, base=0, channel_multiplier=-1), reads=[tf], writes=[tf])
        E("dve", lambda e: e.tensor_copy(out=Ebf.ap, in_=tf.ap), reads=[tf], writes=[Ebf])

        cv = cvec.ap

        def transposes_to(dst_ap, dst_buf, src_ap_fn, src_buf, n, evac_eng):
            for i in range(n):
                E("pe", lambda e, i=i: e.transpose(out=PM_t[:, i * 128:(i + 1) * 128], in_=src_ap_fn(i), identity=ident.ap),
                  reads=[src_buf, ident], writes=[PM])
            src = PM_t[:, 0:n * 128]
            if len(dst_ap.shape) == 3:
                src = src.rearrange("p (k t) -> p k t", k=n)
            if evac_eng == "act":
                return E("act", lambda e: e.activation(out=dst_ap, in_=src, func=AF.Copy), reads=[PM], writes=[dst_buf])
            return E("dve", lambda e: e.tensor_copy(out=dst_ap, in_=src), reads=[PM], writes=[dst_buf])

        def layer_norm(zb, gi, out_ap_fn=None):
            for hh in range(2):
                E("dve", lambda e, hh=hh: e.bn_stats(out=lst.ap[:, hh, :], in_=zb.ap[:, hh * 512:(hh + 1) * 512]),
                  reads=[zb], writes=[lst])
            E("dve", lambda e: e.bn_aggr(out=lmv.ap, in_=lst.ap), reads=[lst], writes=[lmv])
            E("act", lambda e: e.activation(out=lsd.ap, in_=lmv.ap[:, 1:2], func=AF.Sqrt, bias=cv[:, 160:161], scale=1.0),
              reads=[lmv, cvec], writes=[lsd])
            E("dve", lambda e: e.reciprocal(out=lrs.ap, in_=lsd.ap), reads=[lsd], writes=[lrs])
            E("dve", lambda e: e.scalar_tensor_tensor(out=lnm.ap, in0=lmv.ap[:, 0:1], scalar=-1.0, in1=lrs.ap,
                                                      op0=ALU.mult, op1=ALU.mult), reads=[lmv, lrs], writes=[lnm])
            E("act", lambda e: e.activation(out=zb.ap, in_=zb.ap, func=AF.Identity, bias=lnm.ap, scale=lrs.ap),
              reads=[zb, lrs, lnm], writes=[zb])
            E("dve", lambda e: e.tensor_tensor(out=zb.ap, in0=zb.ap, in1=lnv.ap[:, gi, :], op=ALU.mult),
              reads=[zb, lnv], writes=[zb])
            return E("dve", lambda e: e.tensor_tensor(out=zb.ap, in0=zb.ap, in1=lnv.ap[:, gi + 1, :], op=ALU.add),
                     reads=[zb, lnv], writes=[zb])

        def mm(out_ap, out_buf, lhsT, rhs, rd, start, stop):
            return E("pe", lambda e: e.matmul(out_ap, lhsT=lhsT, rhs=rhs, start=start, stop=stop),
                     reads=rd, writes=[out_buf])

        for b in range(NB):
            P.barrier(AB_BUFS)
            for c in range(4):
                E("dve", lambda e, c=c: e.memset(uT_t[:, c, 0:30], 0.0), writes=[uT[c]])
            E("dve", lambda e: e.memset(kT.ap[0:64, 1, :], 0.0), writes=[kT])
            E("dve", lambda e: e.memset(qT.ap[0:64, 1, :], 0.0), writes=[qT])
            E("dve", lambda e: e.memset(qT.ap[64:72, 0, :], 0.0), writes=[qT])
            E("dve", lambda e: e.memset(vA.ap[:, :, :, 64:65], 1.0), writes=[vA])
            E("pool", lambda e: e.dma_start(out=kT.ap[64:72, 0, :], in_=oneh_d), writes=[kT], stream="oh0")
            E("pool", lambda e: e.dma_start(out=kT.ap[0:8, 1, :], in_=oneh_d), writes=[kT], stream="oh1")

            for i in range(8):
                xbb = xb[i % 2]
                src = x_d[b, i * 256:(i + 1) * 256, :].rearrange("(j p) d -> p j d", p=128)
                E("pool", lambda e, xbb=xbb, src=src: e.dma_start(out=xbb.ap, in_=src), writes=[xbb], stream=xbb.name)
                g = i // 2
                for j in range(2):
                    tt = (i % 2) * 2 + j
                    dst = xT[g].ap[:, :, tt * 128:(tt + 1) * 128]
                    transposes_to(dst, xT[g], lambda k, xbb=xbb, j=j: xbb.ap[:, j, k * 128:(k + 1) * 128], xbb, 8,
                                  "act" if j == 0 else "dve")

            ra = ring_next()
            wload(ra, 8, 512, w_in[:, :, 1536:2048])
            rg = ring_next()
            wload(rg, 8, 512, w_in[:, :, 2048:2560])
            sa = slab(ra, 8, 512)
            sg_ = slab(rg, 8, 512)
            gi = 0
            for g in range(4):
                for c in range(4):
                    pa_ = pa_next()
                    pg_ = pa_next()
                    for kc in range(8):
                        mm(pa_.ap, pa_, sa[:, kc, c * 128:(c + 1) * 128], xT[g].ap[:, kc, :], [ra, xT[g]], kc == 0, kc == 7)
                    for kc in range(8):
                        mm(pg_.ap, pg_, sg_[:, kc, c * 128:(c + 1) * 128], xT[g].ap[:, kc, :], [rg, xT[g]], kc == 0, kc == 7)
                    st = sgt[gi % 2]
                    gi += 1
                    E("act", lambda e, st=st, pg_=pg_: e.activation(out=st.ap, in_=pg_.ap, func=AF.Sigmoid),
                      reads=[pg_], writes=[st])
                    E("dve", lambda e, st=st, pa_=pa_, c=c, g=g: e.tensor_tensor(
                        out=uT_t[:, c, 30 + g * 512:30 + (g + 1) * 512], in0=pa_.ap, in1=st.ap, op=ALU.mult),
                      reads=[pa_, st], writes=[uT[c]])

            def conv_chunk(c):
                a = acc[0]
                th = []
                th.append(lambda: E("dve", lambda e: e.tensor_scalar(out=a.ap, in0=uT_t[:, c, 0:S], scalar1=cv[:, 28 + c * CW:29 + c * CW],
                                                   scalar2=None, op0=ALU.mult), reads=[uT[c], cvec], writes=[a]))
                for j in range(1, CW):
                    th.append(lambda j=j: E("dve", lambda e: e.scalar_tensor_tensor(
                        out=a.ap, in0=uT_t[:, c, j:j + S], scalar=cv[:, 28 + c * CW + j:29 + c * CW + j], in1=a.ap,
                        op0=ALU.mult, op1=ALU.add), reads=[uT[c], cvec, a], writes=[a]))
                th.append(lambda: E("act", lambda e: e.activation(out=cT[c].ap, in_=a.ap, func=AF.Identity, bias=cv[:, 16 + c:17 + c], scale=1.0),
                  reads=[a, cvec], writes=[cT[c]]))
                return th

            def conv_ln(g):
                gr = slice(g * 512, (g + 1) * 512)
                for c in range(4):
                    E("act", lambda e, c=c: e.activation(out=ysq.ap[:, c, :], in_=cT_t[:, c, gr], func=AF.Square),
                      reads=[cT[c]], writes=[ysq])
                s1 = pa_next()
                s2 = pa_next()
                for c in range(4):
                    mm(s1.ap, s1, ones_bf.ap, cT_t[:, c, gr], [ones_bf, cT[c]], c == 0, c == 3)
                for c in range(4):
                    mm(s2.ap, s2, ones_bf.ap, ysq.ap[:, c, :], [ones_bf, ysq], c == 0, c == 3)
                E("dve", lambda e: e.tensor_scalar(out=mean_t.ap, in0=s1.ap, scalar1=1.0 / 512, scalar2=None, op0=ALU.mult),
                  reads=[s1], writes=[mean_t])
                E("dve", lambda e: e.tensor_tensor(out=msq_t.ap, in0=mean_t.ap, in1=mean_t.ap, op=ALU.mult),
                  reads=[mean_t], writes=[msq_t])
                E("dve", lambda e: e.scalar_tensor_tensor(out=msq_t.ap, in0=s2.ap, scalar=1.0 / 512, in1=msq_t.ap,
                                                          op0=ALU.mult, op1=ALU.subtract), reads=[s2, msq_t], writes=[msq_t])
                E("act", lambda e: e.activation(out=rstd_t.ap, in_=msq_t.ap, func=AF.Sqrt, bias=cv[:, 160:161], scale=1.0),
                  reads=[msq_t, cvec], writes=[rstd_t])
                E("dve", lambda e: e.reciprocal(out=rstd_t.ap, in_=rstd_t.ap), reads=[rstd_t], writes=[rstd_t])
                for c in range(4):
                    t1 = t1_t[c % 2]
                    E("dve", lambda e, c=c, t1=t1: e.tensor_tensor(out=t1.ap, in0=cT_t[:, c, gr], in1=mean_t.ap, op=ALU.subtract),
                      reads=[cT[c], mean_t], writes=[t1])
                    E("dve", lambda e, t1=t1: e.tensor_tensor(out=t1.ap, in0=t1.ap, in1=rstd_t.ap, op=ALU.mult),
                      reads=[t1, rstd_t], writes=[t1])
                    E("act", lambda e, c=c, t1=t1: e.activation(out=cT_t[:, c, gr], in_=t1.ap, func=AF.Silu,
                                                               bias=cv[:, 24 + c:25 + c], scale=cv[:, 20 + c:21 + c]),
                      reads=[t1, cvec], writes=[cT[c]])

            for hp in range(4):
                rq = ring_next()
                wload(rq, 8, 384, w_in[:, :, hp * 128:(hp + 1) * 128], n0=0)
                wload(rq, 8, 384, w_in[:, :, 512 + hp * 128:512 + (hp + 1) * 128], n0=128)
                wload(rq, 8, 384, w_in[:, :, 1024 + hp * 128:1024 + (hp + 1) * 128], n0=256)
                sq = slab(rq, 8, 384)
                for g in range(4):
                    gr = slice(g * 512, (g + 1) * 512)
                    pq = pa_next()
                    for kc in range(8):
                        mm(pq.ap, pq, sq[:, kc, 0:128], xT[g].ap[:, kc, :], [rq, xT[g]], kc == 0, kc == 7)
                    E("act", lambda e, pq=pq, gr=gr: e.activation(out=qT.ap[0:64, 0, gr], in_=pq.ap[0:64, :], func=AF.Copy),
                      reads=[pq], writes=[qT])
                    E("dve", lambda e, pq=pq, gr=gr: e.tensor_copy(out=qT.ap[64:128, 1, gr], in_=pq.ap[64:128, :]),
                      reads=[pq], writes=[qT])
                    pk = pa_next()
                    for kc in range(8):
                        mm(pk.ap, pk, sq[:, kc, 128:256], xT[g].ap[:, kc, :], [rq, xT[g]], kc == 0, kc == 7)
                    E("act", lambda e, pk=pk, gr=gr: e.activation(out=kT.ap[0:64, 0, gr], in_=pk.ap[0:64, :], func=AF.Copy),
                      reads=[pk], writes=[kT])
                    E("dve", lambda e, pk=pk, gr=gr: e.tensor_copy(out=kT.ap[64:128, 1, gr], in_=pk.ap[64:128, :]),
                      reads=[pk], writes=[kT])
                    E("dve", lambda e, pk=pk, g=g: e.tensor_reduce(out=ksum.ap[:, 2 * g:2 * g + 2],
                                                                  in_=pk.ap.rearrange("p (a b) -> p a b", a=2),
                                                                  axis=AX.X, op=ALU.add), reads=[pk], writes=[ksum])
                    pv = pa_next()
                    for j in range(4):
                        for kc in range(8):
                            mm(pv.ap[:, j * 128:(j + 1) * 128], pv, xT[g].ap[:, kc, j * 128:(j + 1) * 128], sq[:, kc, 256:384],
                               [rq, xT[g]], kc == 0, kc == 7)
                    E("act", lambda e, pv=pv, g=g: e.activation(
                        out=vA.ap[:, 4 * g:4 * g + 4, :, 0:64],
                        in_=pv.ap.rearrange("p (j h d) -> p j h d", j=4, h=2), func=AF.Copy), reads=[pv], writes=[vA])
                E("dve", lambda e: e.tensor_scalar(out=kmean.ap, in0=ksum.ap, scalar1=1.0 / 256, scalar2=None, op0=ALU.mult),
                  reads=[ksum], writes=[kmean])
                for qt in range(8, 16):
                    bq = qt // 2
                    qs = slice(qt * 128, (qt + 1) * 128)
                    pg_ = pa_next()
                    mm(pg_.ap[:, 0:8], pg_, qT.ap[0:64, 0, qs], kmean.ap[0:64, :], [qT, kmean], True, True)
                    mm(pg_.ap[:, 8:16], pg_, qT.ap[64:128, 1, qs], kmean.ap[64:128, :], [qT, kmean], True, True)
                    pgv = pg_.ap[:, 0:16].rearrange("p (h n) -> p h n", h=2)
                    E("dve", lambda e, pgv=pgv, bq=bq: e.tensor_copy(out=gs.ap[:, :, 0:bq], in_=pgv[:, :, 0:bq]),
                      reads=[pg_], writes=[gs])
                    cur = gs
                    for it in range(3):
                        E("dve", lambda e, cur=cur, bq=bq: e.tensor_reduce(out=mx.ap, in_=cur.ap[:, :, 0:bq], axis=AX.X, op=ALU.max),
                          reads=[cur], writes=[mx])
                        if it == 2:
                            break
                        E("dve", lambda e, cur=cur, bq=bq: e.tensor_tensor(
                            out=eq.ap[:, :, 0:bq], in0=cur.ap[:, :, 0:bq], in1=mx.ap.unsqueeze(2).to_broadcast([128, 2, bq]),
                            op=ALU.is_equal), reads=[cur, mx], writes=[eq])
                        E("dve", lambda e, cur=cur, bq=bq: e.scalar_tensor_tensor(
                            out=g2.ap[:, :, 0:bq], in0=eq.ap[:, :, 0:bq], scalar=-1e30, in1=cur.ap[:, :, 0:bq],
                            op0=ALU.mult, op1=ALU.add), reads=[eq, cur], writes=[g2])
                        cur = g2
                    E("dve", lambda e, bq=bq: e.tensor_tensor(
                        out=eq.ap[:, :, 0:bq], in0=gs.ap[:, :, 0:bq], in1=mx.ap.unsqueeze(2).to_broadcast([128, 2, bq]),
                        op=ALU.is_ge), reads=[gs, mx], writes=[eq])
                    E("dve", lambda e: e.memset(mpad.ap, 0.0), writes=[mpad])
                    E("dve", lambda e, bq=bq: e.tensor_scalar(out=mpad.ap[:, 0, 64:64 + bq], in0=eq.ap[:, 0, 0:bq], scalar1=-1.0,
                                                              scalar2=BIG, op0=ALU.add, op1=ALU.mult), reads=[eq], writes=[mpad])
                    E("dve", lambda e, bq=bq: e.tensor_scalar(out=mpad.ap[:, 1, 0:bq], in0=eq.ap[:, 1, 0:bq], scalar1=-1.0,
                                                              scalar2=BIG, op0=ALU.add, op1=ALU.mult), reads=[eq], writes=[mpad])
                    pm_ = pa_next()
                    mm(pm_.ap[0:72, 0:128], pm_, mpad.ap[:, 0, 0:72], ident.ap, [mpad, ident], True, True)
                    mm(pm_.ap[0:8, 128:256], pm_, mpad.ap[:, 1, 0:8], ident.ap, [mpad, ident], True, True)
                    E("act", lambda e, pm_=pm_, qs=qs: e.activation(out=qT.ap[64:72, 0, qs], in_=pm_.ap[64:72, 0:128], func=AF.Copy),
                      reads=[pm_], writes=[qT])
                    E("act", lambda e, pm_=pm_, qs=qs: e.activation(out=qT.ap[0:8, 1, qs], in_=pm_.ap[0:8, 128:256], func=AF.Copy),
                      reads=[pm_], writes=[qT])
                cth = conv_chunk(hp)
                it_n = 0
                pti = 0
                for G in range(4):
                    atm = att_tm[G % 2]
                    for h in range(2):
                        rows = slice(0, 72) if h == 0 else slice(0, 128)
                        for j in range(4 * G + 4):
                            d = j - 4 * G
                            c0 = max(0, d) * 128
                            N = 512 - c0
                            ps_ = pa_next()
                            mm(ps_.ap[:, 0:N], ps_, kT.ap[rows, h, j * 128:(j + 1) * 128],
                               qT.ap[rows, h, G * 512 + c0:(G + 1) * 512], [kT, qT], True, True)
                            pt = PT[pti % 4]
                            pti += 1
                            E("act", lambda e, pt=pt, ps_=ps_, N=N: e.activation(out=pt.ap[:, 0:N], in_=ps_.ap[:, 0:N],
                                                                               func=AF.Exp, scale=DH ** -0.5),
                              reads=[ps_], writes=[pt])
                            hh = 2 * hp + h
                            if d >= 0:
                                w_ = min(256, N)
                                E("dve", lambda e, pt=pt, w_=w_, hh=hh: e.tensor_tensor(
                                    out=pt.ap[:, 0:w_], in0=pt.ap[:, 0:w_], in1=Ebf.ap[:, hh, 0:w_], op=ALU.mult),
                                  reads=[pt, Ebf], writes=[pt])
                            elif d == -1:
                                E("dve", lambda e, pt=pt, hh=hh: e.tensor_tensor(
                                    out=pt.ap[:, 0:128], in0=pt.ap[:, 0:128], in1=Ebf.ap[:, hh, 128:256], op=ALU.mult),
                                  reads=[pt, Ebf], writes=[pt])
                            for s_ in range(max(0, d), 4):
                                lc = s_ * 128 - c0
                                mm(PBh[s_].ap[:, 0:65], PBh[s_], pt.ap[:, lc:lc + 128], vA.ap[:, j, h, :], [pt, vA], j == 0, j == 4 * G + s_)
                            it_n += 1
                            if it_n % 2 == 0 and cth:
                                cth.pop(0)()
                        for s_ in range(4):
                            O = PBh[s_]
                            E("dve", lambda e, O=O, s_=s_: e.reciprocal(out=rinv.ap[:, s_:s_ + 1], in_=O.ap[:, 64:65]), reads=[O], writes=[rinv])
                            E("dve", lambda e, O=O, s_=s_, h=h, atm=atm: e.tensor_scalar(
                                out=atm.ap[:, s_, h * 64:(h + 1) * 64], in0=O.ap[:, 0:64], scalar1=rinv.ap[:, s_:s_ + 1], scalar2=None,
                                op0=ALU.mult), reads=[O, rinv], writes=[atm])
                    transposes_to(attT_t[:, hp, G * 512:(G + 1) * 512], attT[G], lambda s_, atm=atm: atm.ap[:, s_, :], atm, 4, "act")
                while cth:
                    cth.pop(0)()
            for g in range(4):
                conv_ln(g)

            if DBG and b == 0:
                fin.append(E("pool", lambda e: e.dma_start(out=dbg_att, in_=attT_t[:]), reads=attT, stream="dbg1"))
                fin.append(E("pool", lambda e: e.dma_start(out=dbg_c, in_=cT_t[:]), reads=cT, stream="dbg2"))
            P.barrier(C_BUFS)
            for g in range(4):
                gr = slice(g * 512, (g + 1) * 512)
                srcx = x_d[b, g * 512:(g + 1) * 512, :].rearrange("(j p) d -> p j d", p=128)
                E("pool", lambda e, srcx=srcx: e.dma_start(out=xbg.ap, in_=srcx), writes=[xbg], stream="xbg")
                srcp = p_d[b, g * 512:(g + 1) * 512, :].rearrange("(j p) d -> p j d", p=128)
                E("pool", lambda e, srcp=srcp: e.dma_start(out=pbg.ap, in_=srcp), writes=[pbg], stream="pbg")
                for j in range(4):
                    transposes_to(xTg.ap[:, :, j * 128:(j + 1) * 128], xTg, lambda k, j=j: xbg.ap[:, j, k * 128:(k + 1) * 128],
                                  xbg, 8, "act" if j % 2 == 0 else "dve")
                for j in range(4):
                    for k in range(2):
                        E("pe", lambda e, j=j, k=k: e.transpose(out=PM_t[:, (k * 4 + j) * 128:(k * 4 + j + 1) * 128],
                                                                in_=pbg.ap[:, j, k * 128:(k + 1) * 128], identity=ident.ap),
                          reads=[pbg, ident], writes=[PM])
                E("dve", lambda e: e.tensor_copy(out=pT.ap, in_=PM_t[:, 0:1024].rearrange("p (k t) -> p k t", k=2)),
                  reads=[PM], writes=[pT])
                rao = ring_next()
                wload(rao, 4, 1024, w_ao)
                sao = slab(rao, 4, 1024)
                ci = 0
                for part in range(2):
                    if part == 1:
                        rco = ring_next()
                        wload(rco, 4, 1024, w_co)
                        sco = slab(rco, 4, 1024)
                    for mh in range(2):
                        rgl = ring_next()
                        col0 = 2560 + part * 1024 + mh * 512
                        wload(rgl, 8, 512, w_in[:, :, col0:col0 + 512])
                        sgl = slab(rgl, 8, 512)
                        for mm_ in range(4):
                            m = mh * 4 + mm_
                            py = pa_next()
                            pl = pa_next()
                            for kc in range(4):
                                if part == 0:
                                    mm(py.ap, py, sao[:, kc, m * 128:(m + 1) * 128], attT[g].ap[:, kc, :], [rao, attT[g]], kc == 0, kc == 3)
                                else:
                                    mm(py.ap, py, sco[:, kc, m * 128:(m + 1) * 128], cT_t[:, kc, gr], [rco, cT[kc]], kc == 0, kc == 3)
                            for kc in range(8):
                                mm(pl.ap, pl, sgl[:, kc, mm_ * 128:(mm_ + 1) * 128], xTg.ap[:, kc, :], [rgl, xTg], kc == 0, kc == 7)
                            sgb = csg[ci % 2]
                            ci += 1
                            bcol = part * 8 + m
                            E("act", lambda e, sgb=sgb, pl=pl, bcol=bcol: e.activation(
                                out=sgb.ap, in_=pl.ap, func=AF.Sigmoid, bias=cv[:, bcol:bcol + 1], scale=1.0),
                              reads=[pl, cvec], writes=[sgb])
                            if part == 0:
                                E("dve", lambda e, py=py, sgb=sgb, m=m: e.tensor_tensor(
                                    out=hTf[:, m, :], in0=py.ap, in1=sgb.ap, op=ALU.mult), reads=[py, sgb], writes=[hT])
                            else:
                                pc = cpc[m % 2]
                                E("dve", lambda e, py=py, sgb=sgb, pc=pc: e.tensor_tensor(out=pc.ap, in0=py.ap, in1=sgb.ap, op=ALU.mult),
                                  reads=[py, sgb], writes=[pc])
                                E("dve", lambda e, pc=pc, m=m: e.tensor_tensor(out=mT.ap[:, m, :], in0=pc.ap, in1=hTf[:, m, :], op=ALU.add),
                                  reads=[pc, hT], writes=[mT])
                rm = [ring_next(), ring_next()]
                for hh in range(2):
                    wload(rm[hh], 8, 512, w_mix[:, :, hh * 512:(hh + 1) * 512])
                for t in range(4):
                    xr = xres[t % 2]
                    srcr = x_d[b, g * 512 + t * 128:g * 512 + (t + 1) * 128, :]
                    E("sp", lambda e, xr=xr, srcr=srcr: e.dma_start(out=xr.ap, in_=srcr), writes=[xr], stream=xr.name)
                    pbi = t % 2
                    for hh in range(2):
                        ob = PBh[pbi * 2 + hh]
                        sm = slab(rm[hh], 8, 512)
                        for kc in range(8):
                            mm(ob.ap, ob, mT.ap[:, kc, t * 128:(t + 1) * 128], sm[:, kc, :], [mT, rm[hh]], kc == 0, kc == 7)
                        E("dve", lambda e, xr=xr, ob=ob, t=t, hh=hh: e.scalar_tensor_tensor(
                            out=z[t].ap[:, hh * 512:(hh + 1) * 512], in0=xr.ap[:, hh * 512:(hh + 1) * 512], scalar=ALPHA,
                            in1=ob.ap, op0=ALU.mult, op1=ALU.add), reads=[xr, ob], writes=[z[t]])
                    layer_norm(z[t], 0)
                    xb1 = x1b[t % 2]
                    E("act", lambda e, xb1=xb1, t=t: e.activation(out=xb1.ap, in_=z[t].ap, func=AF.Copy), reads=[z[t]], writes=[xb1])
                    transposes_to(x1T.ap[:, :, t * 128:(t + 1) * 128], x1T, lambda k, xb1=xb1: xb1.ap[:, k * 128:(k + 1) * 128],
                                  xb1, 8, "dve")
                rfg = None
                rfu = None
                for j in range(NJ):
                    if j % 4 == 0:
                        ncol = min(512, FFN - j * 128)
                        rfg = ring_next()
                        wload(rfg, 8, 512, w_fg[:, :, j * 128:j * 128 + ncol])
                        rfu = ring_next()
                        wload(rfu, 8, 512, w_fu[:, :, j * 128:j * 128 + ncol])
                    sfg = slab(rfg, 8, 512)
                    sfu = slab(rfu, 8, 512)
                    jj = j % 4
                    pg_ = pa_next()
                    pu_ = pa_next()
                    for kc in range(8):
                        mm(pg_.ap, pg_, sfg[:, kc, jj * 128:(jj + 1) * 128], x1T.ap[:, kc, :], [rfg, x1T], kc == 0, kc == 7)
                    for kc in range(8):
                        mm(pu_.ap, pu_, sfu[:, kc, jj * 128:(jj + 1) * 128], x1T.ap[:, kc, :], [rfu, x1T], kc == 0, kc == 7)
                    sgb = csg[j % 2]
                    E("act", lambda e, sgb=sgb, pg_=pg_: e.activation(out=sgb.ap, in_=pg_.ap, func=AF.Silu), reads=[pg_], writes=[sgb])
                    E("dve", lambda e, sgb=sgb, pu_=pu_, j=j: e.tensor_tensor(out=hT.ap[:, j, :], in0=sgb.ap, in1=pu_.ap, op=ALU.mult),
                      reads=[sgb, pu_], writes=[hT])
                for hh in range(2):
                    for sl in range(3):
                        k0 = sl * 8
                        kn = min(8, NJ - k0)
                        rd_ = ring_next()
                        wload(rd_, 8, 512, w_fd[:, k0:k0 + kn, hh * 512:(hh + 1) * 512], kn=kn)
                        sd = slab(rd_, 8, 512)
                        for t in range(4):
                            ob = PBh[t]
                            for kk in range(kn):
                                j = k0 + kk
                                mm(ob.ap, ob, hT.ap[:, j, t * 128:(t + 1) * 128], sd[:, kk, :], [hT, rd_], j == 0, j == NJ - 1)
                    for t in range(4):
                        ob = PBh[t]
                        E("dve", lambda e, ob=ob, t=t, hh=hh: e.scalar_tensor_tensor(
                            out=z[t].ap[:, hh * 512:(hh + 1) * 512], in0=z[t].ap[:, hh * 512:(hh + 1) * 512], scalar=ALPHA,
                            in1=ob.ap, op0=ALU.mult, op1=ALU.add), reads=[ob, z[t]], writes=[z[t]])
                rpl = ring_next()
                wload(rpl, 2, 1024, w_pl)
                spl = slab(rpl, 2, 1024)
                for hh in range(2):
                    rpg = ring_next()
                    wload(rpg, 8, 512, w_pg[:, :, hh * 512:(hh + 1) * 512])
                    spg = slab(rpg, 8, 512)
                    for t in range(4):
                        ppl = pa_next()
                        ppg = pa_next()
                        for kc in range(2):
                            mm(ppl.ap, ppl, pT.ap[:, kc, t * 128:(t + 1) * 128], spl[:, kc, hh * 512:(hh + 1) * 512], [pT, rpl], kc == 0, kc == 1)
                        for kc in range(8):
                            mm(ppg.ap, ppg, x1T.ap[:, kc, t * 128:(t + 1) * 128], spg[:, kc, :], [x1T, rpg], kc == 0, False)
                        mm(ppg.ap, ppg, ones_bf.ap[0:1, :], bple.ap[0:1, hh * 512:(hh + 1) * 512], [ones_bf, bple], False, True)
                        sgb = csg[t % 2]
                        E("act", lambda e, sgb=sgb, ppg=ppg: e.activation(out=sgb.ap, in_=ppg.ap, func=AF.Sigmoid), reads=[ppg], writes=[sgb])
                        pc = cpc[t % 2]
                        E("dve", lambda e, pc=pc, sgb=sgb, ppl=ppl: e.tensor_tensor(out=pc.ap, in0=sgb.ap, in1=ppl.ap, op=ALU.mult),
                          reads=[sgb, ppl], writes=[pc])
                        E("dve", lambda e, pc=pc, t=t, hh=hh: e.tensor_tensor(
                            out=z[t].ap[:, hh * 512:(hh + 1) * 512], in0=z[t].ap[:, hh * 512:(hh + 1) * 512], in1=pc.ap, op=ALU.add),
                          reads=[pc, z[t]], writes=[z[t]])
                for t in range(4):
                    layer_norm(z[t], 2)
                    dsto = out_d[b, g * 512 + t * 128:g * 512 + (t + 1) * 128, :]
                    tok = E("sp", lambda e, t=t, dsto=dsto: e.dma_start(out=dsto, in_=z[t].ap), reads=[z[t]], stream=f"out{t}")
                    fin.append(tok)
        P.emit(nc, final_wait=fin[-8:] + fin[:2])
    return nc


def _t5_bucket_np(rel):
    n = np.maximum(rel, 0)
    max_exact = 16
    nf = np.maximum(n, 1).astype(np.float32)
    large = max_exact + (np.log(nf / np.float32(max_exact)) / np.float32(np.log(128 / max_exact))
                         * np.float32(32 - max_exact)).astype(np.int32)
    large = np.minimum(large, 31)
    return np.where(n < max_exact, n, large)


_NC_CACHE = {}
_LAST = []


def kernel(x, p, w_in, b_gate, bias_table, w_att_out, conv_w, conv_b, conv_ln_g, conv_ln_b,
           w_conv_out, w_mix_out, ln_mix_g, ln_mix_b, w_ffn_gate, w_ffn_up, w_ffn_down,
           w_ple, w_ple_gate, b_ple_gate, ln_ffn_g, ln_ffn_b):
    f = lambda a: np.ascontiguousarray(np.asarray(a, dtype=np.float32))
    x = f(x)
    p = f(p)
    B = x.shape[0]
    NB = B // NCORES
    cvec = np.zeros((128, NCV), np.float32)
    cvec[:, 0:16] = f(b_gate)[0].reshape(16, 128).T
    cvec[:, 16:20] = f(conv_b)[0].reshape(4, 128).T
    cvec[:, 20:24] = f(conv_ln_g)[0].reshape(4, 128).T
    cvec[:, 24:28] = f(conv_ln_b)[0].reshape(4, 128).T
    cw = f(conv_w)[0]
    cvec[:, 28:152] = cw.T.reshape(4, 128, CW).transpose(1, 0, 2).reshape(128, 4 * CW)
    bt = f(bias_table)
    cvec[:, 152:160] = bt[31][None, :]
    cvec[:, 160] = EPS
    lnv = np.stack([np.broadcast_to(f(v)[0][None, :], (128, D)) for v in (ln_mix_g, ln_mix_b, ln_ffn_g, ln_ffn_b)], axis=1)
    lnv = np.ascontiguousarray(lnv)
    kk = np.arange(128)[:, None]
    cc = np.arange(256)[None, :]
    bidx = _t5_bucket_np(cc - kk)
    toep = np.ascontiguousarray(bt[bidx].transpose(0, 2, 1))
    onehot = (np.arange(S)[None, :] // 256 == np.arange(8)[:, None]).astype(np.float32)
    shared = {
        "w_in": f(w_in)[0], "w_att_out": f(w_att_out)[0], "w_conv_out": f(w_conv_out)[0], "w_mix_out": f(w_mix_out)[0],
        "w_ffn_gate": f(w_ffn_gate)[0], "w_ffn_up": f(w_ffn_up)[0], "w_ffn_down": f(w_ffn_down)[0],
        "w_ple": f(w_ple)[0], "w_ple_gate": f(w_ple_gate)[0], "cvec": cvec, "lnv": lnv,
        "bple": f(b_ple_gate).reshape(1, D), "toep": toep, "onehot": onehot,
    }
    if NB not in _NC_CACHE:
        _NC_CACHE[NB] = build_nc(NB)
    nc = _NC_CACHE[NB]
    in_maps = []
    for c in range(NCORES):
        m = dict(shared)
        m["x"] = x[c * NB:(c + 1) * NB]
        m["p"] = p[0, c * NB:(c + 1) * NB]
        in_maps.append(m)
    res = run_bass_kernel_spmd(nc, in_maps, core_ids=list(range(NCORES)))
    if DBG:
        _LAST.append(res.results[0])
    return np.concatenate([r["out"] for r in res.results], axis=0).astype(np.float32)
```

```python
import contextlib
import numpy as np
import concourse.bass as bass
import concourse.mybir as mybir
from concourse.bass_utils import run_bass_kernel_spmd

F32 = mybir.dt.float32
BF = mybir.dt.bfloat16
AF = mybir.ActivationFunctionType
ALU = mybir.AluOpType
AX = mybir.AxisListType

NCORES = 8
D = 1024
S = 2048
NH = 8
DH = 64
NIN = 4608
FFN = 2816
NJ = FFN // 128
PLE = 256
CW = 31
ALPHA = 2.0 ** 0.25
EPS = 1e-5
BIG = 30000.0
NCV = 161
DBG = False
PE_BIAS = False

ENGS = ("pe", "act", "dve", "pool", "sp")


class Tok:
    __slots__ = ("eng", "idx", "sem", "val", "is_dma", "stream", "key")

    def __init__(self, eng, idx):
        self.eng = eng
        self.idx = idx
        self.sem = None
        self.val = None
        self.is_dma = False
        self.stream = None
        self.key = eng


class Buf:
    def __init__(self, ap, name):
        self.ap = ap
        self.name = name
        self.w = {}
        self.r = {}


class Prog:
    SEM_ROT = 4000

    def __init__(self):
        self.ops = {e: [] for e in ENGS}
        self.streams = {}
        self.need = set()
        self.last = {}

    def op(self, eng, fn, deps=(), stream=None):
        t = Tok(eng, len(self.ops[eng]))
        deps = [d for d in deps if d is not None]
        if stream is not None:
            t.is_dma = True
            t.stream = stream
            t.key = "s:" + stream
            n = self.streams.get(stream, 0) + 1
            self.streams[stream] = n
            t.val = 16 * n
        self.ops[eng].append((t, fn, deps))
        for d in deps:
            if not d.is_dma:
                self.need.add((d.eng, d.idx))
        self.last[t.key] = t
        return t

    def E(self, eng, fn, reads=(), writes=(), stream=None, extra=()):
        deps = {}

        def add(t):
            k = t.key
            o = deps.get(k)
            if o is None or t.idx > o.idx or (t.is_dma and t.val > o.val):
                deps[k] = t

        for t in extra:
            if t is not None:
                add(t)
        for b in reads:
            for t in b.w.values():
                add(t)
        for b in writes:
            for t in b.w.values():
                add(t)
            for t in b.r.values():
                add(t)
        if eng == "pe":
            deps.pop("pe", None)
        tok = self.op(eng, fn, list(deps.values()), stream=stream)
        for b in reads:
            b.r[tok.key] = tok
        for b in writes:
            b.w[tok.key] = tok
        return tok

    def barrier(self, bufs):
        snap = dict(self.last)
        for b in bufs:
            for k, t in snap.items():
                b.r[k] = t
            b.w = {}

    def emit(self, nc, final_wait=()):
        with contextlib.ExitStack() as es:
            nsem = 0
            for e in ENGS:
                cnt = 0
                cur = None
                for t, fn, deps in self.ops[e]:
                    if t.is_dma:
                        continue
                    if (e, t.idx) in self.need:
                        if cur is None or cnt >= self.SEM_ROT:
                            cur = es.enter_context(nc.semaphore(f"c_{e}_{nsem}"))
                            nsem += 1
                            cnt = 0
                        cnt += 1
                        t.sem = cur
                        t.val = cnt
            dsem = {}
            for s in self.streams:
                dsem[s] = es.enter_context(nc.semaphore(f"d_{s}"))
            for e in ENGS:
                for t, fn, deps in self.ops[e]:
                    if t.is_dma:
                        t.sem = dsem[t.stream]
            block = es.enter_context(nc.Block())

            def make(e):
                def body(eng):
                    waited = {}
                    for t, fn, deps in self.ops[e]:
                        for d in deps:
                            k = id(d.sem)
                            if waited.get(k, 0) >= d.val:
                                continue
                            eng.wait_ge(d.sem, d.val)
                            waited[k] = d.val
                        ins = fn(eng)
                        if t.is_dma:
                            ins.then_inc(t.sem, 16)
                        elif t.sem is not None:
                            ins.then_inc(t.sem, 1)
                    if e == "sp":
                        for d in final_wait:
                            eng.wait_ge(d.sem, d.val)

                return body

            block.tensor(make("pe"))
            block.scalar(make("act"))
            block.vector(make("dve"))
            block.gpsimd(make("pool"))
            block.sync(make("sp"))


def build_nc(NB):
    nc = bass.Bass("TRN2", target_bir_lowering=False)
    dr = {}

    def din(name, shape):
        dr[name] = nc.dram_tensor(name, list(shape), F32, kind="ExternalInput").ap()
        return dr[name]

    x_d = din("x", [NB, S, D])
    p_d = din("p", [NB, S, PLE])
    w_in = din("w_in", [D, NIN]).rearrange("(kc p) n -> p kc n", p=128)
    w_ao = din("w_att_out", [512, D]).rearrange("(kc p) n -> p kc n", p=128)
    w_co = din("w_conv_out", [512, D]).rearrange("(kc p) n -> p kc n", p=128)
    w_mix = din("w_mix_out", [D, D]).rearrange("(kc p) n -> p kc n", p=128)
    w_fg = din("w_ffn_gate", [D, FFN]).rearrange("(kc p) n -> p kc n", p=128)
    w_fu = din("w_ffn_up", [D, FFN]).rearrange("(kc p) n -> p kc n", p=128)
    w_fd = din("w_ffn_down", [FFN, D]).rearrange("(kc p) n -> p kc n", p=128)
    w_pl = din("w_ple", [PLE, D]).rearrange("(kc p) n -> p kc n", p=128)
    w_pg = din("w_ple_gate", [D, D]).rearrange("(kc p) n -> p kc n", p=128)
    cvec_d = din("cvec", [128, NCV])
    lnv_d = din("lnv", [128, 4, D])
    bple_d = din("bple", [1, D])
    toep_d = din("toep", [128, NH, 256])
    oneh_d = din("onehot", [8, S])
    out_d = nc.dram_tensor("out", [NB, S, D], F32, kind="ExternalOutput").ap()

    P = Prog()
    E = P.E
    fin = []
    if DBG:
        dbg_att = nc.dram_tensor("dbg_att", [128, 4, S], F32, kind="ExternalOutput").ap()
        dbg_c = nc.dram_tensor("dbg_c", [128, 4, S], F32, kind="ExternalOutput").ap()

    with contextlib.ExitStack() as es:
        def sb(name, shape, dt):
            return es.enter_context(nc.sbuf_tensor("sb_" + name, list(shape), dt))

        def psum(name, shape, dt):
            return es.enter_context(nc.psum_tensor("ps_" + name, list(shape), dt))

        ident = Buf(sb("ident", [128, 128], BF)[:], "ident")
        ones_bf = Buf(sb("ones_bf", [128, 128], BF)[:], "ones")
        Ebf = Buf(sb("Ebf", [128, NH, 256], BF)[:], "Ebf")
        cvec = Buf(sb("cvec", [128, NCV], F32)[:], "cvec")
        negc = Buf(sb("negc", [128, NH], F32)[:], "negc")
        lnv = Buf(sb("lnv", [128, 4, D], F32)[:], "lnv")
        bple = Buf(sb("bple", [1, D], BF)[:], "bple")
        attT_t = sb("attT", [128, 4, S], BF)
        cT_t = sb("cT", [128, 4, S], BF)
        attT = [Buf(attT_t[:, :, g * 512:(g + 1) * 512], f"attT{g}") for g in range(4)]
        cT = [Buf(cT_t[:, c, :], f"cT{c}") for c in range(4)]
        NRING = 4
        ring = [Buf(sb(f"ring{i}", [128, 4096], BF)[:], f"ring{i}") for i in range(NRING)]
        ring_i = [0]

        def ring_next():
            b = ring[ring_i[0] % NRING]
            ring_i[0] += 1
            return b

        def slab(b, kc, n):
            return b.ap[:, 0:kc * n].rearrange("p (k n) -> p k n", k=kc)

        def wload(b, kc, n, src, k0=0, kn=None, n0=0):
            kn = kc - k0 if kn is None else kn
            nn = src.shape[2]
            dst = slab(b, kc, n)[:, k0:k0 + kn, n0:n0 + nn]
            return E("pool", lambda e: e.dma_start(out=dst, in_=src), writes=[b], stream=b.name)

        NU = 57600
        U = sb("U", [128, NU], BF)
        off = [0]

        def carve(nel, dt):
            nb = nel * (4 if dt == F32 else 2)
            nb = (nb + 63) // 64 * 64
            a = off[0] // 2
            off[0] += nb
            assert off[0] <= NU * 2, ("union overflow", off[0])
            v = U[:, a:a + nb // 2]
            if dt == F32:
                v = v.bitcast(F32)[:, 0:nel]
            else:
                v = v[:, 0:nel]
            return v

        off[0] = 0
        xT_t = carve(8 * S, BF).rearrange("p (k t) -> p k t", k=8)
        xT = [Buf(xT_t[:, :, g * 512:(g + 1) * 512], f"xT{g}") for g in range(4)]
        uT_t = carve(4 * (S + 30), BF).rearrange("p (c t) -> p c t", c=4)
        uT = [Buf(uT_t[:, c, :], f"uT{c}") for c in range(4)]
        qT = Buf(carve(2 * S, BF).rearrange("p (h t) -> p h t", h=2), "qT")
        kT = Buf(carve(2 * S, BF).rearrange("p (h t) -> p h t", h=2), "kT")
        vA = Buf(carve(16 * 2 * 65, BF).rearrange("p (j h d) -> p j h d", j=16, h=2), "vA")
        xb = [Buf(carve(2 * D, BF).rearrange("p (j d) -> p j d", j=2), f"xb{i}") for i in range(2)]
        diag = Buf(carve(CW * 128, BF).rearrange("p (j m) -> p j m", j=CW), "diag")
        sgt = [Buf(carve(512, F32), f"sgt{i}") for i in range(2)]
        ysq = Buf(carve(4 * 512, BF).rearrange("p (c t) -> p c t", c=4), "ysq")
        mean_t = Buf(carve(512, F32), "mean")
        msq_t = Buf(carve(512, F32), "msq")
        rstd_t = Buf(carve(512, F32), "rstd")
        t1_t = [Buf(carve(512, F32), f"t1_{i}") for i in range(2)]
        PT = [Buf(carve(512, BF), f"PT{i}") for i in range(4)]
        att_tm = [Buf(carve(4 * 128, BF).rearrange("p (s d) -> p s d", s=4), f"atm{i}") for i in range(2)]
        ksum = Buf(carve(8, F32), "ksum")
        kmean = Buf(carve(8, BF), "kmean")
        gs = Buf(carve(128, F32).rearrange("p (a n) -> p a n", a=16), "gs")
        g2 = Buf(carve(128, F32).rearrange("p (a n) -> p a n", a=16), "g2")
        eq = Buf(carve(128, F32).rearrange("p (a n) -> p a n", a=16), "eq")
        mx = Buf(carve(16, F32), "mx")
        mpad = Buf(carve(8 * 2 * 72, BF).rearrange("p (q h n) -> p q h n", q=8, h=2), "mpad")
        rinv = Buf(carve(4, F32), "rinv")
        AB_END = off[0]
        AB_BUFS = (xT + uT + [qT, kT, vA] + xb + [diag] + sgt + [ysq, mean_t, msq_t, rstd_t] + t1_t + PT
                   + att_tm + [ksum, kmean, gs, g2, eq, mx, mpad, rinv])

        off[0] = 0
        xres = [Buf(carve(D, F32), f"xres{i}") for i in range(2)]
        z_t = carve(4 * D, F32).rearrange("p (t d) -> p t d", t=4)
        z = [Buf(z_t[:, t, :], f"z{t}") for t in range(4)]
        mT = Buf(carve(8 * 512, BF).rearrange("p (k t) -> p k t", k=8), "mT")
        csg = [Buf(carve(512, F32), f"csg{i}") for i in range(2)]
        cpc = [Buf(carve(512, F32), f"cpc{i}") for i in range(2)]
        x1b = [Buf(carve(D, BF), f"x1b{i}") for i in range(2)]
        x1T = Buf(carve(8 * 512, BF).rearrange("p (k t) -> p k t", k=8), "x1T")
        hT_raw = carve(NJ * 512, BF)
        hT = Buf(hT_raw.rearrange("p (j t) -> p j t", j=NJ), "hT")
        hTf = hT_raw[:, 0:8192].bitcast(F32).rearrange("p (m t) -> p m t", m=8)
        xTg = Buf(carve(8 * 512, BF).rearrange("p (k t) -> p k t", k=8), "xTg")
        xbg = Buf(carve(4 * D, BF).rearrange("p (j d) -> p j d", j=4), "xbg")
        pbg = Buf(carve(4 * PLE, BF).rearrange("p (j d) -> p j d", j=4), "pbg")
        pT = Buf(carve(2 * 512, BF).rearrange("p (k t) -> p k t", k=2), "pT")
        lst_ = [Buf(carve(12, F32).rearrange("p (a b) -> p a b", a=2), f"lst{i}") for i in range(4)]
        lmv_ = [Buf(carve(2, F32), f"lmv{i}") for i in range(4)]
        lsd_ = [Buf(carve(1, F32), f"lsd{i}") for i in range(4)]
        lrs_ = [Buf(carve(1, F32), f"lrs{i}") for i in range(4)]
        lnm_ = [Buf(carve(1, F32), f"lnm{i}") for i in range(4)]
        C_END = off[0]
        C_BUFS = (xres + z + [mT] + csg + cpc + x1b + [x1T, hT, xTg, xbg, pbg, pT] + lst_ + lmv_ + lsd_ + lrs_ + lnm_)

        PA = [Buf(psum(f"pa{i}", [128, 512], F32)[:], f"pa{i}") for i in range(3)]
        PBt = [psum(f"pb{i}", [128, 1024], F32) for i in range(2)]
        PBh = [Buf(PBt[i][:, h * 512:(h + 1) * 512], f"pb{i}{h}") for i in range(2) for h in range(2)]
        PM_t = psum("pm", [128, 1024], BF)
        PM = Buf(PM_t[:], "pm")
        pa_i = [0]

        def pa_next():
            b = PA[pa_i[0] % 3]
            pa_i[0] += 1
            return b

        E("sp", lambda e: e.dma_start(out=cvec.ap, in_=cvec_d), writes=[cvec], stream="c_cvec")
        E("sp", lambda e: e.dma_start(out=lnv.ap, in_=lnv_d), writes=[lnv], stream="c_lnv")
        E("pool", lambda e: e.dma_start(out=bple.ap, in_=bple_d), writes=[bple], stream="c_bple")
        idf = Buf(U[:, 0:256].bitcast(F32), "idf")
        tf = Buf(U[:, 4096:4096 + NH * 256 * 2].bitcast(F32).rearrange("p (h c) -> p h c", h=NH), "tf")
        E("dve", lambda e: e.memset(idf.ap, 0.0), writes=[idf])
        E("pool", lambda e: e.affine_select(out=idf.ap, in_=idf.ap, pattern=[[-1, 128]], compare_op=ALU.not_equal,
                                            fill=1.0, base=0, channel_multiplier=1), reads=[idf], writes=[idf])
        E("dve", lambda e: e.tensor_copy(out=ident.ap, in_=idf.ap), reads=[idf], writes=[ident])
        E("dve", lambda e: e.memset(ones_bf.ap, 1.0), writes=[ones_bf])
        E("sp", lambda e: e.dma_start(out=tf.ap, in_=toep_d), writes=[tf], stream="c_toep")
        E("dve", lambda e: e.tensor_scalar(out=negc.ap, in0=cvec.ap[:, 152:160], scalar1=-1.0, scalar2=None, op0=ALU.mult),
          reads=[cvec], writes=[negc])
        for h in range(NH):
            if PE_BIAS:
                E("dve", lambda e, h=h: e.tensor_scalar(out=tf.ap[:, h, :], in0=tf.ap[:, h, :], scalar1=negc.ap[:, h:h + 1],
                                                        scalar2=1.0 / (DH ** -0.5), op0=ALU.add, op1=ALU.mult),
                  reads=[tf, negc], writes=[tf])
            else:
                E("act", lambda e, h=h: e.activation(out=tf.ap[:, h, :], in_=tf.ap[:, h, :], func=AF.Exp,
                                                     bias=negc.ap[:, h:h + 1], scale=1.0), reads=[tf, negc], writes=[tf])
        E("pool", lambda e: e.affine_select(out=tf.ap, in_=tf.ap, pattern=[[0, NH], [1, 256]], compare_op=ALU.is_ge,
                                            fill=(-BIG / (DH ** -0.5) if PE_BIAS else 0.0), base=0, channel_multiplier=-1), reads=[tf], writes=[tf])
        E("dve", lambda e: e.tensor_copy(out=Ebf.ap, in_=tf.ap), reads=[tf], writes=[Ebf])

        cv = cvec.ap
        pen = Buf(sb("pen", [128, 16, 8], F32)[:], "pen")
        val = Buf(sb("val", [128, 16, 8], F32)[:], "val")
        E("dve", lambda e: e.memset(pen.ap, 0.0), writes=[pen])
        E("dve", lambda e: e.memset(val.ap, 1.0), writes=[val])
        for qi in range(6):
            bq_ = 4 + qi // 2
            E("dve", lambda e, qi=qi, bq_=bq_: e.memset(pen.ap[:, 2 * qi:2 * qi + 2, bq_:8], -1e30), writes=[pen])
            E("dve", lambda e, qi=qi, bq_=bq_: e.memset(val.ap[:, 2 * qi:2 * qi + 2, bq_:8], 0.0), writes=[val])
        for qi in range(6, 8):
            E("dve", lambda e, qi=qi: e.memset(pen.ap[:, 2 * qi:2 * qi + 2, 7:8], -1e30), writes=[pen])
            E("dve", lambda e, qi=qi: e.memset(val.ap[:, 2 * qi:2 * qi + 2, 7:8], 0.0), writes=[val])

        def transposes_to(dst_ap, dst_buf, src_ap_fn, src_buf, n, evac_eng):
            for i in range(n):
                E("pe", lambda e, i=i: e.transpose(out=PM_t[:, i * 128:(i + 1) * 128], in_=src_ap_fn(i), identity=ident.ap),
                  reads=[src_buf, ident], writes=[PM])
            src = PM_t[:, 0:n * 128]
            if len(dst_ap.shape) == 3:
                src = src.rearrange("p (k t) -> p k t", k=n)
            if evac_eng == "act":
                return E("act", lambda e: e.activation(out=dst_ap, in_=src, func=AF.Copy), reads=[PM], writes=[dst_buf])
            return E("dve", lambda e: e.tensor_copy(out=dst_ap, in_=src), reads=[PM], writes=[dst_buf])

        def layer_norm(zb, gi, ti=0):
            lst, lmv, lsd, lrs, lnm = lst_[ti], lmv_[ti], lsd_[ti], lrs_[ti], lnm_[ti]
            for hh in range(2):
                E("dve", lambda e, hh=hh: e.bn_stats(out=lst.ap[:, hh, :], in_=zb.ap[:, hh * 512:(hh + 1) * 512]),
                  reads=[zb], writes=[lst])
            E("dve", lambda e: e.bn_aggr(out=lmv.ap, in_=lst.ap), reads=[lst], writes=[lmv])
            E("act", lambda e: e.activation(out=lsd.ap, in_=lmv.ap[:, 1:2], func=AF.Sqrt, bias=cv[:, 160:161], scale=1.0),
              reads=[lmv, cvec], writes=[lsd])
            E("dve", lambda e: e.reciprocal(out=lrs.ap, in_=lsd.ap), reads=[lsd], writes=[lrs])
            E("dve", lambda e: e.scalar_tensor_tensor(out=lnm.ap, in0=lmv.ap[:, 0:1], scalar=-1.0, in1=lrs.ap,
                                                      op0=ALU.mult, op1=ALU.mult), reads=[lmv, lrs], writes=[lnm])
            E("act", lambda e: e.activation(out=zb.ap, in_=zb.ap, func=AF.Identity, bias=lnm.ap, scale=lrs.ap),
              reads=[zb, lrs, lnm], writes=[zb])
            E("dve", lambda e: e.tensor_tensor(out=zb.ap, in0=zb.ap, in1=lnv.ap[:, gi, :], op=ALU.mult),
              reads=[zb, lnv], writes=[zb])
            return E("dve", lambda e: e.tensor_tensor(out=zb.ap, in0=zb.ap, in1=lnv.ap[:, gi + 1, :], op=ALU.add),
                     reads=[zb, lnv], writes=[zb])

        def mm(out_ap, out_buf, lhsT, rhs, rd, start, stop):
            return E("pe", lambda e: e.matmul(out_ap, lhsT=lhsT, rhs=rhs, start=start, stop=stop),
                     reads=rd, writes=[out_buf])

        for b in range(NB):
            P.barrier(AB_BUFS)
            for c in range(4):
                E("dve", lambda e, c=c: e.memset(uT_t[:, c, 0:30], 0.0), writes=[uT[c]])
            E("dve", lambda e: e.memset(kT.ap[0:64, 1, :], 0.0), writes=[kT])
            E("dve", lambda e: e.memset(qT.ap[0:64, 1, :], 0.0), writes=[qT])
            E("dve", lambda e: e.memset(qT.ap[64:72, 0, :], 0.0), writes=[qT])
            E("dve", lambda e: e.memset(vA.ap[:, :, :, 64:65], 1.0), writes=[vA])
            E("dve", lambda e: e.memset(mpad.ap, 0.0), writes=[mpad])
            E("pool", lambda e: e.dma_start(out=kT.ap[64:72, 0, :], in_=oneh_d), writes=[kT], stream="oh0")
            E("pool", lambda e: e.dma_start(out=kT.ap[0:8, 1, :], in_=oneh_d), writes=[kT], stream="oh1")

            for i in range(8):
                xbb = xb[i % 2]
                src = x_d[b, i * 256:(i + 1) * 256, :].rearrange("(j p) d -> p j d", p=128)
                E("pool", lambda e, xbb=xbb, src=src: e.dma_start(out=xbb.ap, in_=src), writes=[xbb], stream=xbb.name)
                g = i // 2
                for j in range(2):
                    tt = (i % 2) * 2 + j
                    dst = xT[g].ap[:, :, tt * 128:(tt + 1) * 128]
                    transposes_to(dst, xT[g], lambda k, xbb=xbb, j=j: xbb.ap[:, j, k * 128:(k + 1) * 128], xbb, 8, "act")

            ra = ring_next()
            wload(ra, 8, 512, w_in[:, :, 1536:2048])
            rg = ring_next()
            wload(rg, 8, 512, w_in[:, :, 2048:2560])
            sa = slab(ra, 8, 512)
            sg_ = slab(rg, 8, 512)
            gi = 0
            for g in range(4):
                for c in range(4):
                    pa_ = pa_next()
                    pg_ = pa_next()
                    for kc in range(8):
                        mm(pa_.ap, pa_, sa[:, kc, c * 128:(c + 1) * 128], xT[g].ap[:, kc, :], [ra, xT[g]], kc == 0, kc == 7)
                    for kc in range(8):
                        mm(pg_.ap, pg_, sg_[:, kc, c * 128:(c + 1) * 128], xT[g].ap[:, kc, :], [rg, xT[g]], kc == 0, kc == 7)
                    st = sgt[gi % 2]
                    gi += 1
                    E("act", lambda e, st=st, pg_=pg_: e.activation(out=st.ap, in_=pg_.ap, func=AF.Sigmoid),
                      reads=[pg_], writes=[st])
                    E("dve", lambda e, st=st, pa_=pa_, c=c, g=g: e.tensor_tensor(
                        out=uT_t[:, c, 30 + g * 512:30 + (g + 1) * 512], in0=pa_.ap, in1=st.ap, op=ALU.mult),
                      reads=[pa_, st], writes=[uT[c]])

            def conv_diag(c):
                for j in range(CW):
                    E("dve", lambda e, j=j: e.tensor_scalar(out=diag.ap[:, j, :], in0=ident.ap, scalar1=cv[:, 28 + c * CW + j:29 + c * CW + j],
                                                            scalar2=None, op0=ALU.mult), reads=[ident, cvec], writes=[diag])

            def conv_chunk(c):
                for g in range(4):
                    pc_ = pa_next()
                    for j in range(CW):
                        mm(pc_.ap, pc_, diag.ap[:, j, :], uT_t[:, c, g * 512 + j:g * 512 + j + 512], [diag, uT[c]], j == 0, j == CW - 1)
                    E("act", lambda e, pc_=pc_, g=g: e.activation(out=cT_t[:, c, g * 512:(g + 1) * 512], in_=pc_.ap, func=AF.Identity,
                                                                 bias=cv[:, 16 + c:17 + c], scale=1.0), reads=[pc_, cvec], writes=[cT[c]])

            def conv_ln(g):
                gr = slice(g * 512, (g + 1) * 512)
                for c in range(4):
                    E("act", lambda e, c=c: e.activation(out=ysq.ap[:, c, :], in_=cT_t[:, c, gr], func=AF.Square),
                      reads=[cT[c]], writes=[ysq])
                s1 = pa_next()
                s2 = pa_next()
                for c in range(4):
                    mm(s1.ap, s1, ones_bf.ap, cT_t[:, c, gr], [ones_bf, cT[c]], c == 0, c == 3)
                for c in range(4):
                    mm(s2.ap, s2, ones_bf.ap, ysq.ap[:, c, :], [ones_bf, ysq], c == 0, c == 3)
                E("dve", lambda e: e.tensor_scalar(out=mean_t.ap, in0=s1.ap, scalar1=1.0 / 512, scalar2=None, op0=ALU.mult),
                  reads=[s1], writes=[mean_t])
                E("dve", lambda e: e.tensor_tensor(out=msq_t.ap, in0=mean_t.ap, in1=mean_t.ap, op=ALU.mult),
                  reads=[mean_t], writes=[msq_t])
                E("dve", lambda e: e.scalar_tensor_tensor(out=msq_t.ap, in0=s2.ap, scalar=1.0 / 512, in1=msq_t.ap,
                                                          op0=ALU.mult, op1=ALU.subtract), reads=[s2, msq_t], writes=[msq_t])
                E("act", lambda e: e.activation(out=rstd_t.ap, in_=msq_t.ap, func=AF.Sqrt, bias=cv[:, 160:161], scale=1.0),
                  reads=[msq_t, cvec], writes=[rstd_t])
                E("dve", lambda e: e.reciprocal(out=rstd_t.ap, in_=rstd_t.ap), reads=[rstd_t], writes=[rstd_t])
                for c in range(4):
                    t1 = t1_t[c % 2]
                    E("dve", lambda e, c=c, t1=t1: e.tensor_tensor(out=t1.ap, in0=cT_t[:, c, gr], in1=mean_t.ap, op=ALU.subtract),
                      reads=[cT[c], mean_t], writes=[t1])
                    E("dve", lambda e, t1=t1: e.tensor_tensor(out=t1.ap, in0=t1.ap, in1=rstd_t.ap, op=ALU.mult),
                      reads=[t1, rstd_t], writes=[t1])
                    E("act", lambda e, c=c, t1=t1: e.activation(out=cT_t[:, c, gr], in_=t1.ap, func=AF.Silu,
                                                               bias=cv[:, 24 + c:25 + c], scale=cv[:, 20 + c:21 + c]),
                      reads=[t1, cvec], writes=[cT[c]])

            for hp in range(4):
                rq = ring_next()
                wload(rq, 8, 384, w_in[:, :, hp * 128:(hp + 1) * 128], n0=0)
                wload(rq, 8, 384, w_in[:, :, 512 + hp * 128:512 + (hp + 1) * 128], n0=128)
                wload(rq, 8, 384, w_in[:, :, 1024 + hp * 128:1024 + (hp + 1) * 128], n0=256)
                sq = slab(rq, 8, 384)
                for g in range(4):
                    gr = slice(g * 512, (g + 1) * 512)
                    pq = pa_next()
                    for kc in range(8):
                        mm(pq.ap, pq, sq[:, kc, 0:128], xT[g].ap[:, kc, :], [rq, xT[g]], kc == 0, kc == 7)
                    E("act", lambda e, pq=pq, gr=gr: e.activation(out=qT.ap[0:64, 0, gr], in_=pq.ap[0:64, :], func=AF.Copy),
                      reads=[pq], writes=[qT])
                    E("dve", lambda e, pq=pq, gr=gr: e.tensor_copy(out=qT.ap[64:128, 1, gr], in_=pq.ap[64:128, :]),
                      reads=[pq], writes=[qT])
                    pk = pa_next()
                    for kc in range(8):
                        mm(pk.ap, pk, sq[:, kc, 128:256], xT[g].ap[:, kc, :], [rq, xT[g]], kc == 0, kc == 7)
                    E("act", lambda e, pk=pk, gr=gr: e.activation(out=kT.ap[0:64, 0, gr], in_=pk.ap[0:64, :], func=AF.Copy),
                      reads=[pk], writes=[kT])
                    E("dve", lambda e, pk=pk, gr=gr: e.tensor_copy(out=kT.ap[64:128, 1, gr], in_=pk.ap[64:128, :]),
                      reads=[pk], writes=[kT])
                    E("dve", lambda e, pk=pk, g=g: e.tensor_reduce(out=ksum.ap[:, 2 * g:2 * g + 2],
                                                                  in_=pk.ap.rearrange("p (a b) -> p a b", a=2),
                                                                  axis=AX.X, op=ALU.add), reads=[pk], writes=[ksum])
                    pv = pa_next()
                    for j in range(4):
                        for kc in range(8):
                            mm(pv.ap[:, j * 128:(j + 1) * 128], pv, xT[g].ap[:, kc, j * 128:(j + 1) * 128], sq[:, kc, 256:384],
                               [rq, xT[g]], kc == 0, kc == 7)
                    E("act", lambda e, pv=pv, g=g: e.activation(
                        out=vA.ap[:, 4 * g:4 * g + 4, :, 0:64],
                        in_=pv.ap.rearrange("p (j h d) -> p j h d", j=4, h=2), func=AF.Copy), reads=[pv], writes=[vA])
                E("dve", lambda e: e.tensor_scalar(out=kmean.ap, in0=ksum.ap, scalar1=1.0 / 256, scalar2=None, op0=ALU.mult),
                  reads=[ksum], writes=[kmean])
                conv_diag(hp)
                pg_ = pa_next()
                for qi in range(8):
                    qs = slice((8 + qi) * 128, (9 + qi) * 128)
                    mm(pg_.ap[:, qi * 16:qi * 16 + 8], pg_, qT.ap[0:64, 0, qs], kmean.ap[0:64, :], [qT, kmean], True, True)
                    mm(pg_.ap[:, qi * 16 + 8:qi * 16 + 16], pg_, qT.ap[64:128, 1, qs], kmean.ap[64:128, :], [qT, kmean], True, True)
                pgv = pg_.ap[:, 0:128].rearrange("p (a n) -> p a n", a=16)
                E("dve", lambda e, pgv=pgv: e.tensor_tensor(out=gs.ap, in0=pgv, in1=pen.ap, op=ALU.add), reads=[pg_, pen], writes=[gs])
                conv_chunk(hp)
                cur = gs
                for it in range(3):
                    E("dve", lambda e, cur=cur: e.tensor_reduce(out=mx.ap, in_=cur.ap, axis=AX.X, op=ALU.max), reads=[cur], writes=[mx])
                    if it == 2:
                        break
                    E("dve", lambda e, cur=cur: e.tensor_tensor(out=eq.ap, in0=cur.ap, in1=mx.ap.unsqueeze(2).to_broadcast([128, 16, 8]),
                                                               op=ALU.is_equal), reads=[cur, mx], writes=[eq])
                    E("dve", lambda e, cur=cur: e.scalar_tensor_tensor(out=g2.ap, in0=eq.ap, scalar=-1e30, in1=cur.ap,
                                                                      op0=ALU.mult, op1=ALU.add), reads=[eq, cur], writes=[g2])
                    cur = g2
                E("dve", lambda e: e.tensor_tensor(out=eq.ap, in0=gs.ap, in1=mx.ap.unsqueeze(2).to_broadcast([128, 16, 8]), op=ALU.is_ge),
                  reads=[gs, mx], writes=[eq])
                E("dve", lambda e: e.tensor_scalar(out=g2.ap, in0=eq.ap, scalar1=-1.0, scalar2=BIG, op0=ALU.add, op1=ALU.mult),
                  reads=[eq], writes=[g2])
                g2v = g2.ap.rearrange("p (q h) n -> p q h n", h=2)
                valv = val.ap.rearrange("p (q h) n -> p q h n", h=2)
                E("dve", lambda e: e.tensor_tensor(out=mpad.ap[:, :, 0, 64:72], in0=g2v[:, :, 0, :], in1=valv[:, :, 0, :], op=ALU.mult),
                  reads=[g2, val], writes=[mpad])
                E("dve", lambda e: e.tensor_tensor(out=mpad.ap[:, :, 1, 0:8], in0=g2v[:, :, 1, :], in1=valv[:, :, 1, :], op=ALU.mult),
                  reads=[g2, val], writes=[mpad])
                for qi in range(8):
                    mm(PBt[0][0:72, qi * 128:(qi + 1) * 128], PBh[qi // 4], mpad.ap[:, qi, 0, 0:72], ident.ap, [mpad, ident], True, True)
                for qi in range(8):
                    mm(PBt[1][0:8, qi * 128:(qi + 1) * 128], PBh[2 + qi // 4], mpad.ap[:, qi, 1, 0:8], ident.ap, [mpad, ident], True, True)
                E("act", lambda e: e.activation(out=qT.ap[64:72, 0, 1024:2048], in_=PBt[0][64:72, :], func=AF.Copy),
                  reads=[PBh[0], PBh[1]], writes=[qT])
                E("act", lambda e: e.activation(out=qT.ap[0:8, 1, 1024:2048], in_=PBt[1][0:8, :], func=AF.Copy),
                  reads=[PBh[2], PBh[3]], writes=[qT])
                cth = []
                it_n = 0
                pti = 0
                for G in range(4):
                    atm = att_tm[G % 2]
                    for h in range(2):
                        rows = slice(0, 72) if h == 0 else slice(0, 128)
                        for j in range(4 * G + 4):
                            d = j - 4 * G
                            c0 = max(0, d) * 128
                            N = 512 - c0
                            ps_ = pa_next()
                            hh = 2 * hp + h
                            near = (d >= -1) and PE_BIAS
                            mm(ps_.ap[:, 0:N], ps_, kT.ap[rows, h, j * 128:(j + 1) * 128],
                               qT.ap[rows, h, G * 512 + c0:(G + 1) * 512], [kT, qT], True, not near)
                            if PE_BIAS:
                                if d >= 0:
                                    w_ = min(256, N)
                                    mm(ps_.ap[:, 0:w_], ps_, ident.ap, Ebf.ap[:, hh, 0:w_], [ident, Ebf], False, True)
                                elif d == -1:
                                    mm(ps_.ap[:, 0:128], ps_, ident.ap, Ebf.ap[:, hh, 128:256], [ident, Ebf], False, True)
                            pt = PT[pti % 4]
                            pti += 1
                            E("act", lambda e, pt=pt, ps_=ps_, N=N: e.activation(out=pt.ap[:, 0:N], in_=ps_.ap[:, 0:N],
                                                                               func=AF.Exp, scale=DH ** -0.5),
                              reads=[ps_], writes=[pt])
                            if not PE_BIAS:
                                if d >= 0:
                                    w_ = min(256, N)
                                    E("dve", lambda e, pt=pt, w_=w_, hh=hh: e.tensor_tensor(
                                        out=pt.ap[:, 0:w_], in0=pt.ap[:, 0:w_], in1=Ebf.ap[:, hh, 0:w_], op=ALU.mult),
                                      reads=[pt, Ebf], writes=[pt])
                                elif d == -1:
                                    E("dve", lambda e, pt=pt, hh=hh: e.tensor_tensor(
                                        out=pt.ap[:, 0:128], in0=pt.ap[:, 0:128], in1=Ebf.ap[:, hh, 128:256], op=ALU.mult),
                                      reads=[pt, Ebf], writes=[pt])
                            for s_ in range(max(0, d), 4):
                                lc = s_ * 128 - c0
                                mm(PBh[s_].ap[:, 0:65], PBh[s_], pt.ap[:, lc:lc + 128], vA.ap[:, j, h, :], [pt, vA], j == 0, j == 4 * G + s_)
                            it_n += 1
                            if it_n % 2 == 0 and cth:
                                cth.pop(0)()
                        for s_ in range(4):
                            O = PBh[s_]
                            E("dve", lambda e, O=O, s_=s_: e.reciprocal(out=rinv.ap[:, s_:s_ + 1], in_=O.ap[:, 64:65]), reads=[O], writes=[rinv])
                            E("dve", lambda e, O=O, s_=s_, h=h, atm=atm: e.tensor_scalar(
                                out=atm.ap[:, s_, h * 64:(h + 1) * 64], in0=O.ap[:, 0:64], scalar1=rinv.ap[:, s_:s_ + 1], scalar2=None,
                                op0=ALU.mult), reads=[O, rinv], writes=[atm])
                    transposes_to(attT_t[:, hp, G * 512:(G + 1) * 512], attT[G], lambda s_, atm=atm: atm.ap[:, s_, :], atm, 4, "act")
                while cth:
                    cth.pop(0)()
            for g in range(4):
                conv_ln(g)

            if DBG and b == 0:
                fin.append(E("pool", lambda e: e.dma_start(out=dbg_att, in_=attT_t[:]), reads=attT, stream="dbg1"))
                fin.append(E("pool", lambda e: e.dma_start(out=dbg_c, in_=cT_t[:]), reads=cT, stream="dbg2"))
            P.barrier(C_BUFS)
            for g in range(4):
                gr = slice(g * 512, (g + 1) * 512)
                srcx = x_d[b, g * 512:(g + 1) * 512, :].rearrange("(j p) d -> p j d", p=128)
                E("pool", lambda e, srcx=srcx: e.dma_start(out=xbg.ap, in_=srcx), writes=[xbg], stream="xbg")
                srcp = p_d[b, g * 512:(g + 1) * 512, :].rearrange("(j p) d -> p j d", p=128)
                E("pool", lambda e, srcp=srcp: e.dma_start(out=pbg.ap, in_=srcp), writes=[pbg], stream="pbg")
                for j in range(4):
                    transposes_to(xTg.ap[:, :, j * 128:(j + 1) * 128], xTg, lambda k, j=j: xbg.ap[:, j, k * 128:(k + 1) * 128],
                                  xbg, 8, "act" if j % 2 == 0 else "dve")
                for j in range(4):
                    for k in range(2):
                        E("pe", lambda e, j=j, k=k: e.transpose(out=PM_t[:, (k * 4 + j) * 128:(k * 4 + j + 1) * 128],
                                                                in_=pbg.ap[:, j, k * 128:(k + 1) * 128], identity=ident.ap),
                          reads=[pbg, ident], writes=[PM])
                E("dve", lambda e: e.tensor_copy(out=pT.ap, in_=PM_t[:, 0:1024].rearrange("p (k t) -> p k t", k=2)),
                  reads=[PM], writes=[pT])
                rao = ring_next()
                wload(rao, 4, 1024, w_ao)
                sao = slab(rao, 4, 1024)
                ci = 0
                for part in range(2):
                    if part == 1:
                        rco = ring_next()
                        wload(rco, 4, 1024, w_co)
                        sco = slab(rco, 4, 1024)
                    for mh in range(2):
                        rgl = ring_next()
                        col0 = 2560 + part * 1024 + mh * 512
                        wload(rgl, 8, 512, w_in[:, :, col0:col0 + 512])
                        sgl = slab(rgl, 8, 512)
                        for mm_ in range(4):
                            m = mh * 4 + mm_
                            py = pa_next()
                            pl = pa_next()
                            for kc in range(4):
                                if part == 0:
                                    mm(py.ap, py, sao[:, kc, m * 128:(m + 1) * 128], attT[g].ap[:, kc, :], [rao, attT[g]], kc == 0, kc == 3)
                                else:
                                    mm(py.ap, py, sco[:, kc, m * 128:(m + 1) * 128], cT_t[:, kc, gr], [rco, cT[kc]], kc == 0, kc == 3)
                            for kc in range(8):
                                mm(pl.ap, pl, sgl[:, kc, mm_ * 128:(mm_ + 1) * 128], xTg.ap[:, kc, :], [rgl, xTg], kc == 0, kc == 7)
                            sgb = csg[ci % 2]
                            ci += 1
                            bcol = part * 8 + m
                            E("act", lambda e, sgb=sgb, pl=pl, bcol=bcol: e.activation(
                                out=sgb.ap, in_=pl.ap, func=AF.Sigmoid, bias=cv[:, bcol:bcol + 1], scale=1.0),
                              reads=[pl, cvec], writes=[sgb])
                            if part == 0:
                                E("dve", lambda e, py=py, sgb=sgb, m=m: e.tensor_tensor(
                                    out=hTf[:, m, :], in0=py.ap, in1=sgb.ap, op=ALU.mult), reads=[py, sgb], writes=[hT])
                            else:
                                pc = cpc[m % 2]
                                E("dve", lambda e, py=py, sgb=sgb, pc=pc: e.tensor_tensor(out=pc.ap, in0=py.ap, in1=sgb.ap, op=ALU.mult),
                                  reads=[py, sgb], writes=[pc])
                                E("dve", lambda e, pc=pc, m=m: e.tensor_tensor(out=mT.ap[:, m, :], in0=pc.ap, in1=hTf[:, m, :], op=ALU.add),
                                  reads=[pc, hT], writes=[mT])
                rm = [ring_next(), ring_next()]
                for hh in range(2):
                    wload(rm[hh], 8, 512, w_mix[:, :, hh * 512:(hh + 1) * 512])
                for t in range(4):
                    xr = xres[t % 2]
                    srcr = x_d[b, g * 512 + t * 128:g * 512 + (t + 1) * 128, :]
                    E("sp", lambda e, xr=xr, srcr=srcr: e.dma_start(out=xr.ap, in_=srcr), writes=[xr], stream=xr.name)
                    pbi = t % 2
                    for hh in range(2):
                        ob = PBh[pbi * 2 + hh]
                        sm = slab(rm[hh], 8, 512)
                        for kc in range(8):
                            mm(ob.ap, ob, mT.ap[:, kc, t * 128:(t + 1) * 128], sm[:, kc, :], [mT, rm[hh]], kc == 0, kc == 7)
                        E("dve", lambda e, xr=xr, ob=ob, t=t, hh=hh: e.scalar_tensor_tensor(
                            out=z[t].ap[:, hh * 512:(hh + 1) * 512], in0=xr.ap[:, hh * 512:(hh + 1) * 512], scalar=ALPHA,
                            in1=ob.ap, op0=ALU.mult, op1=ALU.add), reads=[xr, ob], writes=[z[t]])
                    layer_norm(z[t], 0, t)
                    xb1 = x1b[t % 2]
                    E("act", lambda e, xb1=xb1, t=t: e.activation(out=xb1.ap, in_=z[t].ap, func=AF.Copy), reads=[z[t]], writes=[xb1])
                    transposes_to(x1T.ap[:, :, t * 128:(t + 1) * 128], x1T, lambda k, xb1=xb1: xb1.ap[:, k * 128:(k + 1) * 128],
                                  xb1, 8, "dve")
                rfg = None
                rfu = None
                for j in range(NJ):
                    if j % 4 == 0:
                        ncol = min(512, FFN - j * 128)
                        rfg = ring_next()
                        wload(rfg, 8, 512, w_fg[:, :, j * 128:j * 128 + ncol])
                        rfu = ring_next()
                        wload(rfu, 8, 512, w_fu[:, :, j * 128:j * 128 + ncol])
                    sfg = slab(rfg, 8, 512)
                    sfu = slab(rfu, 8, 512)
                    jj = j % 4
                    pg_ = pa_next()
                    pu_ = pa_next()
                    for kc in range(8):
                        mm(pg_.ap, pg_, sfg[:, kc, jj * 128:(jj + 1) * 128], x1T.ap[:, kc, :], [rfg, x1T], kc == 0, kc == 7)
                    for kc in range(8):
                        mm(pu_.ap, pu_, sfu[:, kc, jj * 128:(jj + 1) * 128], x1T.ap[:, kc, :], [rfu, x1T], kc == 0, kc == 7)
                    sgb = csg[j % 2]
                    E("act", lambda e, sgb=sgb, pg_=pg_: e.activation(out=sgb.ap, in_=pg_.ap, func=AF.Silu), reads=[pg_], writes=[sgb])
                    E("dve", lambda e, sgb=sgb, pu_=pu_, j=j: e.tensor_tensor(out=hT.ap[:, j, :], in0=sgb.ap, in1=pu_.ap, op=ALU.mult),
                      reads=[sgb, pu_], writes=[hT])
                for hh in range(2):
                    for sl in range(3):
                        k0 = sl * 8
                        kn = min(8, NJ - k0)
                        rd_ = ring_next()
                        wload(rd_, 8, 512, w_fd[:, k0:k0 + kn, hh * 512:(hh + 1) * 512], kn=kn)
                        sd = slab(rd_, 8, 512)
                        for t in range(4):
                            ob = PBh[t]
                            for kk in range(kn):
                                j = k0 + kk
                                mm(ob.ap, ob, hT.ap[:, j, t * 128:(t + 1) * 128], sd[:, kk, :], [hT, rd_], j == 0, j == NJ - 1)
                    for t in range(4):
                        ob = PBh[t]
                        E("dve", lambda e, ob=ob, t=t, hh=hh: e.scalar_tensor_tensor(
                            out=z[t].ap[:, hh * 512:(hh + 1) * 512], in0=z[t].ap[:, hh * 512:(hh + 1) * 512], scalar=ALPHA,
                            in1=ob.ap, op0=ALU.mult, op1=ALU.add), reads=[ob, z[t]], writes=[z[t]])
                rpl = ring_next()
                wload(rpl, 2, 1024, w_pl)
                spl = slab(rpl, 2, 1024)
                for hh in range(2):
                    rpg = ring_next()
                    wload(rpg, 8, 512, w_pg[:, :, hh * 512:(hh + 1) * 512])
                    spg = slab(rpg, 8, 512)
                    for t in range(4):
                        ppl = pa_next()
                        ppg = pa_next()
                        for kc in range(2):
                            mm(ppl.ap, ppl, pT.ap[:, kc, t * 128:(t + 1) * 128], spl[:, kc, hh * 512:(hh + 1) * 512], [pT, rpl], kc == 0, kc == 1)
                        for kc in range(8):
                            mm(ppg.ap, ppg, x1T.ap[:, kc, t * 128:(t + 1) * 128], spg[:, kc, :], [x1T, rpg], kc == 0, False)
                        mm(ppg.ap, ppg, ones_bf.ap[0:1, :], bple.ap[0:1, hh * 512:(hh + 1) * 512], [ones_bf, bple], False, True)
                        sgb = csg[t % 2]
                        E("act", lambda e, sgb=sgb, ppg=ppg: e.activation(out=sgb.ap, in_=ppg.ap, func=AF.Sigmoid), reads=[ppg], writes=[sgb])
                        pc = cpc[t % 2]
                        E("dve", lambda e, pc=pc, sgb=sgb, ppl=ppl: e.tensor_tensor(out=pc.ap, in0=sgb.ap, in1=ppl.ap, op=ALU.mult),
                          reads=[sgb, ppl], writes=[pc])
                        E("dve", lambda e, pc=pc, t=t, hh=hh: e.tensor_tensor(
                            out=z[t].ap[:, hh * 512:(hh + 1) * 512], in0=z[t].ap[:, hh * 512:(hh + 1) * 512], in1=pc.ap, op=ALU.add),
                          reads=[pc, z[t]], writes=[z[t]])
                for t in range(4):
                    layer_norm(z[t], 2, t)
                    dsto = out_d[b, g * 512 + t * 128:g * 512 + (t + 1) * 128, :]
                    tok = E("sp", lambda e, t=t, dsto=dsto: e.dma_start(out=dsto, in_=z[t].ap), reads=[z[t]], stream=f"out{t}")
                    fin.append(tok)
        P.emit(nc, final_wait=fin[-8:] + fin[:2])
    return nc


def _t5_bucket_np(rel):
    n = np.maximum(rel, 0)
    max_exact = 16
    nf = np.maximum(n, 1).astype(np.float32)
    large = max_exact + (np.log(nf / np.float32(max_exact)) / np.float32(np.log(128 / max_exact))
                         * np.float32(32 - max_exact)).astype(np.int32)
    large = np.minimum(large, 31)
    return np.where(n < max_exact, n, large)


_NC_CACHE = {}
_LAST = []


def kernel(x, p, w_in, b_gate, bias_table, w_att_out, conv_w, conv_b, conv_ln_g, conv_ln_b,
           w_conv_out, w_mix_out, ln_mix_g, ln_mix_b, w_ffn_gate, w_ffn_up, w_ffn_down,
           w_ple, w_ple_gate, b_ple_gate, ln_ffn_g, ln_ffn_b):
    f = lambda a: np.ascontiguousarray(np.asarray(a, dtype=np.float32))
    x = f(x)
    p = f(p)
    B = x.shape[0]
    NB = B // NCORES
    cvec = np.zeros((128, NCV), np.float32)
    cvec[:, 0:16] = f(b_gate)[0].reshape(16, 128).T
    cvec[:, 16:20] = f(conv_b)[0].reshape(4, 128).T
    cvec[:, 20:24] = f(conv_ln_g)[0].reshape(4, 128).T
    cvec[:, 24:28] = f(conv_ln_b)[0].reshape(4, 128).T
    cw = f(conv_w)[0]
    cvec[:, 28:152] = cw.T.reshape(4, 128, CW).transpose(1, 0, 2).reshape(128, 4 * CW)
    bt = f(bias_table)
    cvec[:, 152:160] = bt[31][None, :]
    cvec[:, 160] = EPS
    lnv = np.stack([np.broadcast_to(f(v)[0][None, :], (128, D)) for v in (ln_mix_g, ln_mix_b, ln_ffn_g, ln_ffn_b)], axis=1)
    lnv = np.ascontiguousarray(lnv)
    kk = np.arange(128)[:, None]
    cc = np.arange(256)[None, :]
    bidx = _t5_bucket_np(cc - kk)
    toep = np.ascontiguousarray(bt[bidx].transpose(0, 2, 1))
    onehot = (np.arange(S)[None, :] // 256 == np.arange(8)[:, None]).astype(np.float32)
    shared = {
        "w_in": f(w_in)[0], "w_att_out": f(w_att_out)[0], "w_conv_out": f(w_conv_out)[0], "w_mix_out": f(w_mix_out)[0],
        "w_ffn_gate": f(w_ffn_gate)[0], "w_ffn_up": f(w_ffn_up)[0], "w_ffn_down": f(w_ffn_down)[0],
        "w_ple": f(w_ple)[0], "w_ple_gate": f(w_ple_gate)[0], "cvec": cvec, "lnv": lnv,
        "bple": f(b_ple_gate).reshape(1, D), "toep": toep, "onehot": onehot,
    }
    if NB not in _NC_CACHE:
        _NC_CACHE[NB] = build_nc(NB)
    nc = _NC_CACHE[NB]
    in_maps = []
    for c in range(NCORES):
        m = dict(shared)
        m["x"] = x[c * NB:(c + 1) * NB]
        m["p"] = p[0, c * NB:(c + 1) * NB]
        in_maps.append(m)
    res = run_bass_kernel_spmd(nc, in_maps, core_ids=list(range(NCORES)))
    if DBG:
        _LAST.append(res.results[0])
    return np.concatenate([r["out"] for r in res.results], axis=0).astype(np.float32)
```

```python
import contextlib
import numpy as np
import concourse.bass as bass
import concourse.mybir as mybir
from concourse.bass_utils import run_bass_kernel_spmd

F32 = mybir.dt.float32
BF = mybir.dt.bfloat16
AF = mybir.ActivationFunctionType
ALU = mybir.AluOpType
AX = mybir.AxisListType

NCORES = 8
D = 1024
S = 2048
NH = 8
DH = 64
NIN = 4608
FFN = 2816
NJ = FFN // 128
PLE = 256
CW = 31
ALPHA = 2.0 ** 0.25
EPS = 1e-5
BIG = 30000.0
NCV = 161
DBG = False
PE_BIAS = False

ENGS = ("pe", "act", "dve", "pool", "sp")


class Tok:
    __slots__ = ("eng", "idx", "sem", "val", "is_dma", "stream", "key")

    def __init__(self, eng, idx):
        self.eng = eng
        self.idx = idx
        self.sem = None
        self.val = None
        self.is_dma = False
        self.stream = None
        self.key = eng


class Buf:
    def __init__(self, ap, name):
        self.ap = ap
        self.name = name
        self.w = {}
        self.r = {}


class Prog:
    SEM_ROT = 4000

    def __init__(self):
        self.ops = {e: [] for e in ENGS}
        self.streams = {}
        self.need = set()
        self.last = {}

    def op(self, eng, fn, deps=(), stream=None):
        t = Tok(eng, len(self.ops[eng]))
        deps = [d for d in deps if d is not None]
        if stream is not None:
            t.is_dma = True
            t.stream = stream
            t.key = "s:" + stream
            n = self.streams.get(stream, 0) + 1
            self.streams[stream] = n
            t.val = 16 * n
        self.ops[eng].append((t, fn, deps))
        for d in deps:
            if not d.is_dma:
                self.need.add((d.eng, d.idx))
        self.last[t.key] = t
        return t

    def E(self, eng, fn, reads=(), writes=(), stream=None, extra=()):
        deps = {}

        def add(t):
            k = t.key
            o = deps.get(k)
            if o is None or t.idx > o.idx or (t.is_dma and t.val > o.val):
                deps[k] = t

        for t in extra:
            if t is not None:
                add(t)
        for b in reads:
            for t in b.w.values():
                add(t)
        for b in writes:
            for t in b.w.values():
                add(t)
            for t in b.r.values():
                add(t)
        if eng == "pe":
            deps.pop("pe", None)
        tok = self.op(eng, fn, list(deps.values()), stream=stream)
        for b in reads:
            b.r[tok.key] = tok
        for b in writes:
            b.w[tok.key] = tok
        return tok

    def barrier(self, bufs):
        snap = dict(self.last)
        for b in bufs:
            for k, t in snap.items():
                b.r[k] = t
            b.w = {}

    def emit(self, nc, final_wait=()):
        with contextlib.ExitStack() as es:
            nsem = 0
            for e in ENGS:
                cnt = 0
                cur = None
                for t, fn, deps in self.ops[e]:
                    if t.is_dma:
                        continue
                    if (e, t.idx) in self.need:
                        if cur is None or cnt >= self.SEM_ROT:
                            cur = es.enter_context(nc.semaphore(f"c_{e}_{nsem}"))
                            nsem += 1
                            cnt = 0
                        cnt += 1
                        t.sem = cur
                        t.val = cnt
            dsem = {}
            for s in self.streams:
                dsem[s] = es.enter_context(nc.semaphore(f"d_{s}"))
            for e in ENGS:
                for t, fn, deps in self.ops[e]:
                    if t.is_dma:
                        t.sem = dsem[t.stream]
            block = es.enter_context(nc.Block())

            def make(e):
                def body(eng):
                    waited = {}
                    for t, fn, deps in self.ops[e]:
                        for d in deps:
                            k = id(d.sem)
                            if waited.get(k, 0) >= d.val:
                                continue
                            eng.wait_ge(d.sem, d.val)
                            waited[k] = d.val
                        ins = fn(eng)
                        if t.is_dma:
                            ins.then_inc(t.sem, 16)
                        elif t.sem is not None:
                            ins.then_inc(t.sem, 1)
                    if e == "sp":
                        for d in final_wait:
                            eng.wait_ge(d.sem, d.val)

                return body

            block.tensor(make("pe"))
            block.scalar(make("act"))
            block.vector(make("dve"))
            block.gpsimd(make("pool"))
            block.sync(make("sp"))


def build_nc(NB):
    nc = bass.Bass("TRN2", target_bir_lowering=False)
    dr = {}

    def din(name, shape):
        dr[name] = nc.dram_tensor(name, list(shape), F32, kind="ExternalInput").ap()
        return dr[name]

    x_d = din("x", [NB, S, D])
    p_d = din("p", [NB, S, PLE])
    w_in = din("w_in", [D, NIN]).rearrange("(kc p) n -> p kc n", p=128)
    w_ao = din("w_att_out", [512, D]).rearrange("(kc p) n -> p kc n", p=128)
    w_co = din("w_conv_out", [512, D]).rearrange("(kc p) n -> p kc n", p=128)
    w_mix = din("w_mix_out", [D, D]).rearrange("(kc p) n -> p kc n", p=128)
    w_fg = din("w_ffn_gate", [D, FFN]).rearrange("(kc p) n -> p kc n", p=128)
    w_fu = din("w_ffn_up", [D, FFN]).rearrange("(kc p) n -> p kc n", p=128)
    w_fd = din("w_ffn_down", [FFN, D]).rearrange("(kc p) n -> p kc n", p=128)
    w_pl = din("w_ple", [PLE, D]).rearrange("(kc p) n -> p kc n", p=128)
    w_pg = din("w_ple_gate", [D, D]).rearrange("(kc p) n -> p kc n", p=128)
    cvec_d = din("cvec", [128, NCV])
    lnv_d = din("lnv", [128, 4, D])
    bple_d = din("bple", [1, D])
    toep_d = din("toep", [128, NH, 256])
    oneh_d = din("onehot", [8, S])
    out_d = nc.dram_tensor("out", [NB, S, D], F32, kind="ExternalOutput").ap()

    P = Prog()
    E = P.E
    fin = []
    if DBG:
        dbg_att = nc.dram_tensor("dbg_att", [128, 4, S], F32, kind="ExternalOutput").ap()
        dbg_c = nc.dram_tensor("dbg_c", [128, 4, S], F32, kind="ExternalOutput").ap()

    with contextlib.ExitStack() as es:
        def sb(name, shape, dt):
            return es.enter_context(nc.sbuf_tensor("sb_" + name, list(shape), dt))

        def psum(name, shape, dt):
            return es.enter_context(nc.psum_tensor("ps_" + name, list(shape), dt))

        ident = Buf(sb("ident", [128, 128], BF)[:], "ident")
        ones_bf = Buf(sb("ones_bf", [128, 128], BF)[:], "ones")
        Ebf = Buf(sb("Ebf", [128, NH, 256], BF)[:], "Ebf")
        cvec = Buf(sb("cvec", [128, NCV], F32)[:], "cvec")
        negc = Buf(sb("negc", [128, NH], F32)[:], "negc")
        lnv = Buf(sb("lnv", [128, 4, D], F32)[:], "lnv")
        bple = Buf(sb("bple", [1, D], BF)[:], "bple")
        attT_t = sb("attT", [128, 4, S], BF)
        cT_t = sb("cT", [128, 4, S], BF)
        attT = [Buf(attT_t[:, :, g * 512:(g + 1) * 512], f"attT{g}") for g in range(4)]
        cT = [Buf(cT_t[:, c, :], f"cT{c}") for c in range(4)]
        NRING = 4
        ring = [Buf(sb(f"ring{i}", [128, 4096], BF)[:], f"ring{i}") for i in range(NRING)]
        ring_i = [0]

        def ring_next():
            b = ring[ring_i[0] % NRING]
            ring_i[0] += 1
            return b

        def slab(b, kc, n):
            return b.ap[:, 0:kc * n].rearrange("p (k n) -> p k n", k=kc)

        def wload(b, kc, n, src, k0=0, kn=None, n0=0):
            kn = kc - k0 if kn is None else kn
            nn = src.shape[2]
            dst = slab(b, kc, n)[:, k0:k0 + kn, n0:n0 + nn]
            return E("pool", lambda e: e.dma_start(out=dst, in_=src), writes=[b], stream=b.name)

        SD = []
        SD.append((4, 1024, w_ao, 4))
        SD.append((8, 512, w_in[:, :, 2560:3072], 8))
        SD.append((8, 512, w_in[:, :, 3072:3584], 8))
        SD.append((4, 1024, w_co, 4))
        SD.append((8, 512, w_in[:, :, 3584:4096], 8))
        SD.append((8, 512, w_in[:, :, 4096:4608], 8))
        SD.append((8, 512, w_mix[:, :, 0:512], 8))
        SD.append((8, 512, w_mix[:, :, 512:1024], 8))
        for jq in range(6):
            ncol = min(512, FFN - jq * 512)
            SD.append((8, 512, w_fg[:, :, jq * 512:jq * 512 + ncol], 8))
            SD.append((8, 512, w_fu[:, :, jq * 512:jq * 512 + ncol], 8))
        for hh in range(2):
            for sl in range(3):
                k0 = sl * 8
                kn = min(8, NJ - k0)
                SD.append((8, 512, w_fd[:, k0:k0 + kn, hh * 512:(hh + 1) * 512], kn))
        SD.append((2, 1024, w_pl, 2))
        SD.append((8, 512, w_pg[:, :, 0:512], 8))
        SD.append((8, 512, w_pg[:, :, 512:1024], 8))
        NSL = len(SD)
        wscr = nc.dram_tensor("wscr", [NSL, 128, 4096], BF, kind="Internal").ap()
        scrb = [Buf(None, f"scr{i}") for i in range(NSL)]

        def convert_slab(i):
            kc, n, src, kn = SD[i]
            rb = ring_next()
            wload(rb, kc, n, src, kn=kn)
            E("sp", lambda e: e.dma_start(out=wscr[i], in_=rb.ap), reads=[rb], writes=[scrb[i]], stream=f"scr{i}")

        def wl(i):
            rb = ring_next()
            E("sp", lambda e: e.dma_start(out=rb.ap, in_=wscr[i]), reads=[scrb[i]], writes=[rb], stream=rb.name)
            return rb

        NU = 57600
        U = sb("U", [128, NU], BF)
        off = [0]

        def carve(nel, dt):
            nb = nel * (4 if dt == F32 else 2)
            nb = (nb + 63) // 64 * 64
            a = off[0] // 2
            off[0] += nb
            assert off[0] <= NU * 2, ("union overflow", off[0])
            v = U[:, a:a + nb // 2]
            if dt == F32:
                v = v.bitcast(F32)[:, 0:nel]
            else:
                v = v[:, 0:nel]
            return v

        off[0] = 0
        xT_t = carve(8 * S, BF).rearrange("p (k t) -> p k t", k=8)
        xT = [Buf(xT_t[:, :, g * 512:(g + 1) * 512], f"xT{g}") for g in range(4)]
        uT_t = carve(4 * (S + 30), BF).rearrange("p (c t) -> p c t", c=4)
        uT = [Buf(uT_t[:, c, :], f"uT{c}") for c in range(4)]
        qT = Buf(carve(2 * S, BF).rearrange("p (h t) -> p h t", h=2), "qT")
        kT = Buf(carve(2 * S, BF).rearrange("p (h t) -> p h t", h=2), "kT")
        vA = Buf(carve(16 * 2 * 65, BF).rearrange("p (j h d) -> p j h d", j=16, h=2), "vA")
        xb = [Buf(carve(2 * D, BF).rearrange("p (j d) -> p j d", j=2), f"xb{i}") for i in range(2)]
        diag = Buf(carve(CW * 128, BF).rearrange("p (j m) -> p j m", j=CW), "diag")
        sgt = [Buf(carve(512, F32), f"sgt{i}") for i in range(2)]
        ysq = Buf(carve(4 * 512, BF).rearrange("p (c t) -> p c t", c=4), "ysq")
        mean_t = Buf(carve(512, F32), "mean")
        msq_t = Buf(carve(512, F32), "msq")
        rstd_t = Buf(carve(512, F32), "rstd")
        t1_t = [Buf(carve(512, F32), f"t1_{i}") for i in range(2)]
        PT = [Buf(carve(512, BF), f"PT{i}") for i in range(4)]
        att_tm = [Buf(carve(4 * 128, BF).rearrange("p (s d) -> p s d", s=4), f"atm{i}") for i in range(2)]
        ksum = Buf(carve(8, F32), "ksum")
        kmean = Buf(carve(8, BF), "kmean")
        gs = Buf(carve(128, F32).rearrange("p (a n) -> p a n", a=16), "gs")
        g2 = Buf(carve(128, F32).rearrange("p (a n) -> p a n", a=16), "g2")
        eq = Buf(carve(128, F32).rearrange("p (a n) -> p a n", a=16), "eq")
        mx = Buf(carve(16, F32), "mx")
        mpad = Buf(carve(8 * 2 * 72, BF).rearrange("p (q h n) -> p q h n", q=8, h=2), "mpad")
        rinv = Buf(carve(4, F32), "rinv")
        AB_END = off[0]
        AB_BUFS = (xT + uT + [qT, kT, vA] + xb + [diag] + sgt + [ysq, mean_t, msq_t, rstd_t] + t1_t + PT
                   + att_tm + [ksum, kmean, gs, g2, eq, mx, mpad, rinv])

        off[0] = 0
        xres = [Buf(carve(D, F32), f"xres{i}") for i in range(2)]
        z_t = carve(4 * D, F32).rearrange("p (t d) -> p t d", t=4)
        z = [Buf(z_t[:, t, :], f"z{t}") for t in range(4)]
        mT = Buf(carve(8 * 512, BF).rearrange("p (k t) -> p k t", k=8), "mT")
        csg = [Buf(carve(512, F32), f"csg{i}") for i in range(2)]
        cpc = [Buf(carve(512, F32), f"cpc{i}") for i in range(2)]
        x1b = [Buf(carve(D, BF), f"x1b{i}") for i in range(2)]
        x1T = Buf(carve(8 * 512, BF).rearrange("p (k t) -> p k t", k=8), "x1T")
        hT_raw = carve(NJ * 512, BF)
        hT = Buf(hT_raw.rearrange("p (j t) -> p j t", j=NJ), "hT")
        hTf = hT_raw[:, 0:8192].bitcast(F32).rearrange("p (m t) -> p m t", m=8)
        xTg = Buf(carve(8 * 512, BF).rearrange("p (k t) -> p k t", k=8), "xTg")
        xbg = Buf(carve(4 * D, BF).rearrange("p (j d) -> p j d", j=4), "xbg")
        pbg = Buf(carve(4 * PLE, BF).rearrange("p (j d) -> p j d", j=4), "pbg")
        pT = Buf(carve(2 * 512, BF).rearrange("p (k t) -> p k t", k=2), "pT")
        lst_ = [Buf(carve(12, F32).rearrange("p (a b) -> p a b", a=2), f"lst{i}") for i in range(4)]
        lmv_ = [Buf(carve(2, F32), f"lmv{i}") for i in range(4)]
        lsd_ = [Buf(carve(1, F32), f"lsd{i}") for i in range(4)]
        lrs_ = [Buf(carve(1, F32), f"lrs{i}") for i in range(4)]
        lnm_ = [Buf(carve(1, F32), f"lnm{i}") for i in range(4)]
        C_END = off[0]
        C_BUFS = (xres + z + [mT] + csg + cpc + x1b + [x1T, hT, xTg, xbg, pbg, pT] + lst_ + lmv_ + lsd_ + lrs_ + lnm_)

        PA = [Buf(psum(f"pa{i}", [128, 512], F32)[:], f"pa{i}") for i in range(3)]
        PBt = [psum(f"pb{i}", [128, 1024], F32) for i in range(2)]
        PBh = [Buf(PBt[i][:, h * 512:(h + 1) * 512], f"pb{i}{h}") for i in range(2) for h in range(2)]
        PM_t = psum("pm", [128, 1024], BF)
        PM = Buf(PM_t[:], "pm")
        pa_i = [0]

        WIDE = PA + PBh
        pw_i = [0]

        def pa_next(wide=False):
            if wide:
                b = WIDE[pw_i[0] % 7]
                pw_i[0] += 1
                return b
            b = PA[pa_i[0] % 3]
            pa_i[0] += 1
            return b

        E("sp", lambda e: e.dma_start(out=cvec.ap, in_=cvec_d), writes=[cvec], stream="c_cvec")
        E("sp", lambda e: e.dma_start(out=lnv.ap, in_=lnv_d), writes=[lnv], stream="c_lnv")
        E("pool", lambda e: e.dma_start(out=bple.ap, in_=bple_d), writes=[bple], stream="c_bple")
        idf = Buf(U[:, 0:256].bitcast(F32), "idf")
        tf = Buf(U[:, 4096:4096 + NH * 256 * 2].bitcast(F32).rearrange("p (h c) -> p h c", h=NH), "tf")
        E("dve", lambda e: e.memset(idf.ap, 0.0), writes=[idf])
        E("pool", lambda e: e.affine_select(out=idf.ap, in_=idf.ap, pattern=[[-1, 128]], compare_op=ALU.not_equal,
                                            fill=1.0, base=0, channel_multiplier=1), reads=[idf], writes=[idf])
        E("dve", lambda e: e.tensor_copy(out=ident.ap, in_=idf.ap), reads=[idf], writes=[ident])
        E("dve", lambda e: e.memset(ones_bf.ap, 1.0), writes=[ones_bf])
        E("sp", lambda e: e.dma_start(out=tf.ap, in_=toep_d), writes=[tf], stream="c_toep")
        E("dve", lambda e: e.tensor_scalar(out=negc.ap, in0=cvec.ap[:, 152:160], scalar1=-1.0, scalar2=None, op0=ALU.mult),
          reads=[cvec], writes=[negc])
        for h in range(NH):
            if PE_BIAS:
                E("dve", lambda e, h=h: e.tensor_scalar(out=tf.ap[:, h, :], in0=tf.ap[:, h, :], scalar1=negc.ap[:, h:h + 1],
                                                        scalar2=1.0 / (DH ** -0.5), op0=ALU.add, op1=ALU.mult),
                  reads=[tf, negc], writes=[tf])
            else:
                E("act", lambda e, h=h: e.activation(out=tf.ap[:, h, :], in_=tf.ap[:, h, :], func=AF.Exp,
                                                     bias=negc.ap[:, h:h + 1], scale=1.0), reads=[tf, negc], writes=[tf])
        E("pool", lambda e: e.affine_select(out=tf.ap, in_=tf.ap, pattern=[[0, NH], [1, 256]], compare_op=ALU.is_ge,
                                            fill=(-BIG / (DH ** -0.5) if PE_BIAS else 0.0), base=0, channel_multiplier=-1), reads=[tf], writes=[tf])
        E("dve", lambda e: e.tensor_copy(out=Ebf.ap, in_=tf.ap), reads=[tf], writes=[Ebf])

        cv = cvec.ap
        pen = Buf(sb("pen", [128, 16, 8], F32)[:], "pen")
        val = Buf(sb("val", [128, 16, 8], F32)[:], "val")
        E("dve", lambda e: e.memset(pen.ap, 0.0), writes=[pen])
        E("dve", lambda e: e.memset(val.ap, 1.0), writes=[val])
        for qi in range(6):
            bq_ = 4 + qi // 2
            E("dve", lambda e, qi=qi, bq_=bq_: e.memset(pen.ap[:, 2 * qi:2 * qi + 2, bq_:8], -1e30), writes=[pen])
            E("dve", lambda e, qi=qi, bq_=bq_: e.memset(val.ap[:, 2 * qi:2 * qi + 2, bq_:8], 0.0), writes=[val])
        for qi in range(6, 8):
            E("dve", lambda e, qi=qi: e.memset(pen.ap[:, 2 * qi:2 * qi + 2, 7:8], -1e30), writes=[pen])
            E("dve", lambda e, qi=qi: e.memset(val.ap[:, 2 * qi:2 * qi + 2, 7:8], 0.0), writes=[val])

        def transposes_to(dst_ap, dst_buf, src_ap_fn, src_buf, n, evac_eng):
            for i in range(n):
                E("pe", lambda e, i=i: e.transpose(out=PM_t[:, i * 128:(i + 1) * 128], in_=src_ap_fn(i), identity=ident.ap),
                  reads=[src_buf, ident], writes=[PM])
            src = PM_t[:, 0:n * 128]
            if len(dst_ap.shape) == 3:
                src = src.rearrange("p (k t) -> p k t", k=n)
            if evac_eng == "act":
                return E("act", lambda e: e.activation(out=dst_ap, in_=src, func=AF.Copy), reads=[PM], writes=[dst_buf])
            return E("dve", lambda e: e.tensor_copy(out=dst_ap, in_=src), reads=[PM], writes=[dst_buf])

        def layer_norm(zb, gi, ti=0):
            lst, lmv, lsd, lrs, lnm = lst_[ti], lmv_[ti], lsd_[ti], lrs_[ti], lnm_[ti]
            for hh in range(2):
                E("dve", lambda e, hh=hh: e.bn_stats(out=lst.ap[:, hh, :], in_=zb.ap[:, hh * 512:(hh + 1) * 512]),
                  reads=[zb], writes=[lst])
            E("dve", lambda e: e.bn_aggr(out=lmv.ap, in_=lst.ap), reads=[lst], writes=[lmv])
            E("act", lambda e: e.activation(out=lsd.ap, in_=lmv.ap[:, 1:2], func=AF.Sqrt, bias=cv[:, 160:161], scale=1.0),
              reads=[lmv, cvec], writes=[lsd])
            E("dve", lambda e: e.reciprocal(out=lrs.ap, in_=lsd.ap), reads=[lsd], writes=[lrs])
            E("dve", lambda e: e.scalar_tensor_tensor(out=lnm.ap, in0=lmv.ap[:, 0:1], scalar=-1.0, in1=lrs.ap,
                                                      op0=ALU.mult, op1=ALU.mult), reads=[lmv, lrs], writes=[lnm])
            E("act", lambda e: e.activation(out=zb.ap, in_=zb.ap, func=AF.Identity, bias=lnm.ap, scale=lrs.ap),
              reads=[zb, lrs, lnm], writes=[zb])
            E("dve", lambda e: e.tensor_tensor(out=zb.ap, in0=zb.ap, in1=lnv.ap[:, gi, :], op=ALU.mult),
              reads=[zb, lnv], writes=[zb])
            return E("dve", lambda e: e.tensor_tensor(out=zb.ap, in0=zb.ap, in1=lnv.ap[:, gi + 1, :], op=ALU.add),
                     reads=[zb, lnv], writes=[zb])

        def mm(out_ap, out_buf, lhsT, rhs, rd, start, stop):
            return E("pe", lambda e: e.matmul(out_ap, lhsT=lhsT, rhs=rhs, start=start, stop=stop),
                     reads=rd, writes=[out_buf])

        for b in range(NB):
            P.barrier(AB_BUFS)
            for c in range(4):
                E("dve", lambda e, c=c: e.memset(uT_t[:, c, 0:30], 0.0), writes=[uT[c]])
            E("dve", lambda e: e.memset(kT.ap[0:64, 1, :], 0.0), writes=[kT])
            E("dve", lambda e: e.memset(qT.ap[0:64, 1, :], 0.0), writes=[qT])
            E("dve", lambda e: e.memset(qT.ap[64:72, 0, :], 0.0), writes=[qT])
            E("dve", lambda e: e.memset(vA.ap[:, :, :, 64:65], 1.0), writes=[vA])
            E("dve", lambda e: e.memset(mpad.ap, 0.0), writes=[mpad])
            E("pool", lambda e: e.dma_start(out=kT.ap[64:72, 0, :], in_=oneh_d), writes=[kT], stream="oh0")
            E("pool", lambda e: e.dma_start(out=kT.ap[0:8, 1, :], in_=oneh_d), writes=[kT], stream="oh1")

            for i in range(8):
                xbb = xb[i % 2]
                src = x_d[b, i * 256:(i + 1) * 256, :].rearrange("(j p) d -> p j d", p=128)
                E("pool", lambda e, xbb=xbb, src=src: e.dma_start(out=xbb.ap, in_=src), writes=[xbb], stream=xbb.name)
                g = i // 2
                for j in range(2):
                    tt = (i % 2) * 2 + j
                    dst = xT[g].ap[:, :, tt * 128:(tt + 1) * 128]
                    transposes_to(dst, xT[g], lambda k, xbb=xbb, j=j: xbb.ap[:, j, k * 128:(k + 1) * 128], xbb, 8, "act")

            ra = ring_next()
            wload(ra, 8, 512, w_in[:, :, 1536:2048])
            rg = ring_next()
            wload(rg, 8, 512, w_in[:, :, 2048:2560])
            sa = slab(ra, 8, 512)
            sg_ = slab(rg, 8, 512)
            gi = 0
            for g in range(4):
                for c in range(4):
                    pa_ = pa_next(True)
                    pg_ = pa_next(True)
                    for kc in range(8):
                        mm(pa_.ap, pa_, sa[:, kc, c * 128:(c + 1) * 128], xT[g].ap[:, kc, :], [ra, xT[g]], kc == 0, kc == 7)
                    for kc in range(8):
                        mm(pg_.ap, pg_, sg_[:, kc, c * 128:(c + 1) * 128], xT[g].ap[:, kc, :], [rg, xT[g]], kc == 0, kc == 7)
                    st = sgt[gi % 2]
                    gi += 1
                    E("act", lambda e, st=st, pg_=pg_: e.activation(out=st.ap, in_=pg_.ap, func=AF.Sigmoid),
                      reads=[pg_], writes=[st])
                    E("dve", lambda e, st=st, pa_=pa_, c=c, g=g: e.tensor_tensor(
                        out=uT_t[:, c, 30 + g * 512:30 + (g + 1) * 512], in0=pa_.ap, in1=st.ap, op=ALU.mult),
                      reads=[pa_, st], writes=[uT[c]])

            def conv_diag(c):
                for j in range(CW):
                    E("dve", lambda e, j=j: e.tensor_scalar(out=diag.ap[:, j, :], in0=ident.ap, scalar1=cv[:, 28 + c * CW + j:29 + c * CW + j],
                                                            scalar2=None, op0=ALU.mult), reads=[ident, cvec], writes=[diag])

            def conv_chunk(c):
                for g in range(4):
                    pc_ = pa_next()
                    for j in range(CW):
                        mm(pc_.ap, pc_, diag.ap[:, j, :], uT_t[:, c, g * 512 + j:g * 512 + j + 512], [diag, uT[c]], j == 0, j == CW - 1)
                    E("act", lambda e, pc_=pc_, g=g: e.activation(out=cT_t[:, c, g * 512:(g + 1) * 512], in_=pc_.ap, func=AF.Identity,
                                                                 bias=cv[:, 16 + c:17 + c], scale=1.0), reads=[pc_, cvec], writes=[cT[c]])

            def conv_ln(g):
                gr = slice(g * 512, (g + 1) * 512)
                for c in range(4):
                    E("act", lambda e, c=c: e.activation(out=ysq.ap[:, c, :], in_=cT_t[:, c, gr], func=AF.Square),
                      reads=[cT[c]], writes=[ysq])
                s1 = pa_next()
                s2 = pa_next()
                for c in range(4):
                    mm(s1.ap, s1, ones_bf.ap, cT_t[:, c, gr], [ones_bf, cT[c]], c == 0, c == 3)
                for c in range(4):
                    mm(s2.ap, s2, ones_bf.ap, ysq.ap[:, c, :], [ones_bf, ysq], c == 0, c == 3)
                E("dve", lambda e: e.tensor_scalar(out=mean_t.ap, in0=s1.ap, scalar1=1.0 / 512, scalar2=None, op0=ALU.mult),
                  reads=[s1], writes=[mean_t])
                E("dve", lambda e: e.tensor_tensor(out=msq_t.ap, in0=mean_t.ap, in1=mean_t.ap, op=ALU.mult),
                  reads=[mean_t], writes=[msq_t])
                E("dve", lambda e: e.scalar_tensor_tensor(out=msq_t.ap, in0=s2.ap, scalar=1.0 / 512, in1=msq_t.ap,
                                                          op0=ALU.mult, op1=ALU.subtract), reads=[s2, msq_t], writes=[msq_t])
                E("act", lambda e: e.activation(out=rstd_t.ap, in_=msq_t.ap, func=AF.Sqrt, bias=cv[:, 160:161], scale=1.0),
                  reads=[msq_t, cvec], writes=[rstd_t])
                E("dve", lambda e: e.reciprocal(out=rstd_t.ap, in_=rstd_t.ap), reads=[rstd_t], writes=[rstd_t])
                for c in range(4):
                    t1 = t1_t[c % 2]
                    E("dve", lambda e, c=c, t1=t1: e.tensor_tensor(out=t1.ap, in0=cT_t[:, c, gr], in1=mean_t.ap, op=ALU.subtract),
                      reads=[cT[c], mean_t], writes=[t1])
                    E("dve", lambda e, t1=t1: e.tensor_tensor(out=t1.ap, in0=t1.ap, in1=rstd_t.ap, op=ALU.mult),
                      reads=[t1, rstd_t], writes=[t1])
                    E("act", lambda e, c=c, t1=t1: e.activation(out=cT_t[:, c, gr], in_=t1.ap, func=AF.Silu,
                                                               bias=cv[:, 24 + c:25 + c], scale=cv[:, 20 + c:21 + c]),
                      reads=[t1, cvec], writes=[cT[c]])

            for hp in range(4):
                rq = ring_next()
                wload(rq, 8, 384, w_in[:, :, hp * 128:(hp + 1) * 128], n0=0)
                wload(rq, 8, 384, w_in[:, :, 512 + hp * 128:512 + (hp + 1) * 128], n0=128)
                wload(rq, 8, 384, w_in[:, :, 1024 + hp * 128:1024 + (hp + 1) * 128], n0=256)
                sq = slab(rq, 8, 384)
                for g in range(4):
                    gr = slice(g * 512, (g + 1) * 512)
                    pq = pa_next()
                    for kc in range(8):
                        mm(pq.ap, pq, sq[:, kc, 0:128], xT[g].ap[:, kc, :], [rq, xT[g]], kc == 0, kc == 7)
                    E("act", lambda e, pq=pq, gr=gr: e.activation(out=qT.ap[0:64, 0, gr], in_=pq.ap[0:64, :], func=AF.Copy),
                      reads=[pq], writes=[qT])
                    E("dve", lambda e, pq=pq, gr=gr: e.tensor_copy(out=qT.ap[64:128, 1, gr], in_=pq.ap[64:128, :]),
                      reads=[pq], writes=[qT])
                    pk = pa_next()
                    for kc in range(8):
                        mm(pk.ap, pk, sq[:, kc, 128:256], xT[g].ap[:, kc, :], [rq, xT[g]], kc == 0, kc == 7)
                    E("act", lambda e, pk=pk, gr=gr: e.activation(out=kT.ap[0:64, 0, gr], in_=pk.ap[0:64, :], func=AF.Copy),
                      reads=[pk], writes=[kT])
                    E("dve", lambda e, pk=pk, gr=gr: e.tensor_copy(out=kT.ap[64:128, 1, gr], in_=pk.ap[64:128, :]),
                      reads=[pk], writes=[kT])
                    E("dve", lambda e, pk=pk, g=g: e.tensor_reduce(out=ksum.ap[:, 2 * g:2 * g + 2],
                                                                  in_=pk.ap.rearrange("p (a b) -> p a b", a=2),
                                                                  axis=AX.X, op=ALU.add), reads=[pk], writes=[ksum])
                    pv = pa_next()
                    for j in range(4):
                        for kc in range(8):
                            mm(pv.ap[:, j * 128:(j + 1) * 128], pv, xT[g].ap[:, kc, j * 128:(j + 1) * 128], sq[:, kc, 256:384],
                               [rq, xT[g]], kc == 0, kc == 7)
                    E("act", lambda e, pv=pv, g=g: e.activation(
                        out=vA.ap[:, 4 * g:4 * g + 4, :, 0:64],
                        in_=pv.ap.rearrange("p (j h d) -> p j h d", j=4, h=2), func=AF.Copy), reads=[pv], writes=[vA])
                E("dve", lambda e: e.tensor_scalar(out=kmean.ap, in0=ksum.ap, scalar1=1.0 / 256, scalar2=None, op0=ALU.mult),
                  reads=[ksum], writes=[kmean])
                if b == 0:
                    for i in range(NSL):
                        if i * 4 // NSL == hp:
                            convert_slab(i)
                conv_diag(hp)
                pg_ = pa_next()
                for qi in range(8):
                    qs = slice((8 + qi) * 128, (9 + qi) * 128)
                    mm(pg_.ap[:, qi * 16:qi * 16 + 8], pg_, qT.ap[0:64, 0, qs], kmean.ap[0:64, :], [qT, kmean], True, True)
                    mm(pg_.ap[:, qi * 16 + 8:qi * 16 + 16], pg_, qT.ap[64:128, 1, qs], kmean.ap[64:128, :], [qT, kmean], True, True)
                pgv = pg_.ap[:, 0:128].rearrange("p (a n) -> p a n", a=16)
                E("dve", lambda e, pgv=pgv: e.tensor_tensor(out=gs.ap, in0=pgv, in1=pen.ap, op=ALU.add), reads=[pg_, pen], writes=[gs])
                conv_chunk(hp)
                cur = gs
                for it in range(3):
                    E("dve", lambda e, cur=cur: e.tensor_reduce(out=mx.ap, in_=cur.ap, axis=AX.X, op=ALU.max), reads=[cur], writes=[mx])
                    if it == 2:
                        break
                    E("dve", lambda e, cur=cur: e.tensor_tensor(out=eq.ap, in0=cur.ap, in1=mx.ap.unsqueeze(2).to_broadcast([128, 16, 8]),
                                                               op=ALU.is_equal), reads=[cur, mx], writes=[eq])
                    E("dve", lambda e, cur=cur: e.scalar_tensor_tensor(out=g2.ap, in0=eq.ap, scalar=-1e30, in1=cur.ap,
                                                                      op0=ALU.mult, op1=ALU.add), reads=[eq, cur], writes=[g2])
                    cur = g2
                E("dve", lambda e: e.tensor_tensor(out=eq.ap, in0=gs.ap, in1=mx.ap.unsqueeze(2).to_broadcast([128, 16, 8]), op=ALU.is_ge),
                  reads=[gs, mx], writes=[eq])
                E("dve", lambda e: e.tensor_scalar(out=g2.ap, in0=eq.ap, scalar1=-1.0, scalar2=BIG, op0=ALU.add, op1=ALU.mult),
                  reads=[eq], writes=[g2])
                g2v = g2.ap.rearrange("p (q h) n -> p q h n", h=2)
                valv = val.ap.rearrange("p (q h) n -> p q h n", h=2)
                E("dve", lambda e: e.tensor_tensor(out=mpad.ap[:, :, 0, 64:72], in0=g2v[:, :, 0, :], in1=valv[:, :, 0, :], op=ALU.mult),
                  reads=[g2, val], writes=[mpad])
                E("dve", lambda e: e.tensor_tensor(out=mpad.ap[:, :, 1, 0:8], in0=g2v[:, :, 1, :], in1=valv[:, :, 1, :], op=ALU.mult),
                  reads=[g2, val], writes=[mpad])
                for qi in range(8):
                    mm(PBt[0][0:72, qi * 128:(qi + 1) * 128], PBh[qi // 4], mpad.ap[:, qi, 0, 0:72], ident.ap, [mpad, ident], True, True)
                for qi in range(8):
                    mm(PBt[1][0:8, qi * 128:(qi + 1) * 128], PBh[2 + qi // 4], mpad.ap[:, qi, 1, 0:8], ident.ap, [mpad, ident], True, True)
                E("act", lambda e: e.activation(out=qT.ap[64:72, 0, 1024:2048], in_=PBt[0][64:72, :], func=AF.Copy),
                  reads=[PBh[0], PBh[1]], writes=[qT])
                E("act", lambda e: e.activation(out=qT.ap[0:8, 1, 1024:2048], in_=PBt[1][0:8, :], func=AF.Copy),
                  reads=[PBh[2], PBh[3]], writes=[qT])
                cth = []
                it_n = 0
                its = [(G, h, j) for G in range(4) for h in range(2) for j in range(4 * G + 4)]
                LA = 2
                pts = {}

                def emit_scores(k):
                    G, h, j = its[k]
                    rows = slice(0, 72) if h == 0 else slice(0, 128)
                    d = j - 4 * G
                    c0 = max(0, d) * 128
                    N = 512 - c0
                    ps_ = pa_next()
                    hh = 2 * hp + h
                    mm(ps_.ap[:, 0:N], ps_, kT.ap[rows, h, j * 128:(j + 1) * 128],
                       qT.ap[rows, h, G * 512 + c0:(G + 1) * 512], [kT, qT], True, True)
                    pt = PT[k % 4]
                    pts[k] = pt
                    E("act", lambda e: e.activation(out=pt.ap[:, 0:N], in_=ps_.ap[:, 0:N], func=AF.Exp, scale=DH ** -0.5),
                      reads=[ps_], writes=[pt])
                    if d >= 0:
                        w_ = min(256, N)
                        E("dve", lambda e: e.tensor_tensor(out=pt.ap[:, 0:w_], in0=pt.ap[:, 0:w_], in1=Ebf.ap[:, hh, 0:w_], op=ALU.mult),
                          reads=[pt, Ebf], writes=[pt])
                    elif d == -1:
                        E("dve", lambda e: e.tensor_tensor(out=pt.ap[:, 0:128], in0=pt.ap[:, 0:128], in1=Ebf.ap[:, hh, 128:256], op=ALU.mult),
                          reads=[pt, Ebf], writes=[pt])

                def emit_pv(k):
                    G, h, j = its[k]
                    d = j - 4 * G
                    c0 = max(0, d) * 128
                    pt = pts.pop(k)
                    atm = att_tm[G % 2]
                    for s_ in range(max(0, d), 4):
                        lc = s_ * 128 - c0
                        mm(PBh[s_].ap[:, 0:65], PBh[s_], pt.ap[:, lc:lc + 128], vA.ap[:, j, h, :], [pt, vA], j == 0, j == 4 * G + s_)
                    if j == 4 * G + 3:
                        for s_ in range(4):
                            O = PBh[s_]
                            E("dve", lambda e, O=O, s_=s_: e.reciprocal(out=rinv.ap[:, s_:s_ + 1], in_=O.ap[:, 64:65]), reads=[O], writes=[rinv])
                            E("dve", lambda e, O=O, s_=s_: e.tensor_scalar(
                                out=atm.ap[:, s_, h * 64:(h + 1) * 64], in0=O.ap[:, 0:64], scalar1=rinv.ap[:, s_:s_ + 1], scalar2=None,
                                op0=ALU.mult), reads=[O, rinv], writes=[atm])
                        if h == 1:
                            transposes_to(attT_t[:, hp, G * 512:(G + 1) * 512], attT[G], lambda s_: atm.ap[:, s_, :], atm, 4, "act")

                for k in range(len(its) + LA):
                    if k < len(its):
                        emit_scores(k)
                    if k - LA >= 0:
                        emit_pv(k - LA)
            for g in range(4):
                conv_ln(g)

            if DBG and b == 0:
                fin.append(E("pool", lambda e: e.dma_start(out=dbg_att, in_=attT_t[:]), reads=attT, stream="dbg1"))
                fin.append(E("pool", lambda e: e.dma_start(out=dbg_c, in_=cT_t[:]), reads=cT, stream="dbg2"))
            P.barrier(C_BUFS)
            for g in range(4):
                gr = slice(g * 512, (g + 1) * 512)
                srcx = x_d[b, g * 512:(g + 1) * 512, :].rearrange("(j p) d -> p j d", p=128)
                E("pool", lambda e, srcx=srcx: e.dma_start(out=xbg.ap, in_=srcx), writes=[xbg], stream="xbg")
                srcp = p_d[b, g * 512:(g + 1) * 512, :].rearrange("(j p) d -> p j d", p=128)
                E("pool", lambda e, srcp=srcp: e.dma_start(out=pbg.ap, in_=srcp), writes=[pbg], stream="pbg")
                for j in range(4):
                    transposes_to(xTg.ap[:, :, j * 128:(j + 1) * 128], xTg, lambda k, j=j: xbg.ap[:, j, k * 128:(k + 1) * 128],
                                  xbg, 8, "act" if j % 2 == 0 else "dve")
                for j in range(4):
                    for k in range(2):
                        E("pe", lambda e, j=j, k=k: e.transpose(out=PM_t[:, (k * 4 + j) * 128:(k * 4 + j + 1) * 128],
                                                                in_=pbg.ap[:, j, k * 128:(k + 1) * 128], identity=ident.ap),
                          reads=[pbg, ident], writes=[PM])
                E("dve", lambda e: e.tensor_copy(out=pT.ap, in_=PM_t[:, 0:1024].rearrange("p (k t) -> p k t", k=2)),
                  reads=[PM], writes=[pT])
                rao = wl(0)
                sao = slab(rao, 4, 1024)
                ci = 0
                for part in range(2):
                    if part == 1:
                        rco = wl(3)
                        sco = slab(rco, 4, 1024)
                    for mh in range(2):
                        rgl = wl([1, 2, 4, 5][part * 2 + mh])
                        sgl = slab(rgl, 8, 512)
                        for mm_ in range(4):
                            m = mh * 4 + mm_
                            py = pa_next(True)
                            pl = pa_next(True)
                            for kc in range(4):
                                if part == 0:
                                    mm(py.ap, py, sao[:, kc, m * 128:(m + 1) * 128], attT[g].ap[:, kc, :], [rao, attT[g]], kc == 0, kc == 3)
                                else:
                                    mm(py.ap, py, sco[:, kc, m * 128:(m + 1) * 128], cT_t[:, kc, gr], [rco, cT[kc]], kc == 0, kc == 3)
                            for kc in range(8):
                                mm(pl.ap, pl, sgl[:, kc, mm_ * 128:(mm_ + 1) * 128], xTg.ap[:, kc, :], [rgl, xTg], kc == 0, kc == 7)
                            sgb = csg[ci % 2]
                            ci += 1
                            bcol = part * 8 + m
                            E("act", lambda e, sgb=sgb, pl=pl, bcol=bcol: e.activation(
                                out=sgb.ap, in_=pl.ap, func=AF.Sigmoid, bias=cv[:, bcol:bcol + 1], scale=1.0),
                              reads=[pl, cvec], writes=[sgb])
                            if part == 0:
                                E("dve", lambda e, py=py, sgb=sgb, m=m: e.tensor_tensor(
                                    out=hTf[:, m, :], in0=py.ap, in1=sgb.ap, op=ALU.mult), reads=[py, sgb], writes=[hT])
                            else:
                                pc = cpc[m % 2]
                                E("dve", lambda e, py=py, sgb=sgb, pc=pc: e.tensor_tensor(out=pc.ap, in0=py.ap, in1=sgb.ap, op=ALU.mult),
                                  reads=[py, sgb], writes=[pc])
                                E("dve", lambda e, pc=pc, m=m: e.tensor_tensor(out=mT.ap[:, m, :], in0=pc.ap, in1=hTf[:, m, :], op=ALU.add),
                                  reads=[pc, hT], writes=[mT])
                rm = [wl(6), wl(7)]
                for t in range(4):
                    xr = xres[t % 2]
                    srcr = x_d[b, g * 512 + t * 128:g * 512 + (t + 1) * 128, :]
                    E("sp", lambda e, xr=xr, srcr=srcr: e.dma_start(out=xr.ap, in_=srcr), writes=[xr], stream=xr.name)
                    pbi = t % 2
                    for hh in range(2):
                        ob = PBh[pbi * 2 + hh]
                        sm = slab(rm[hh], 8, 512)
                        for kc in range(8):
                            mm(ob.ap, ob, mT.ap[:, kc, t * 128:(t + 1) * 128], sm[:, kc, :], [mT, rm[hh]], kc == 0, kc == 7)
                        E("dve", lambda e, xr=xr, ob=ob, t=t, hh=hh: e.scalar_tensor_tensor(
                            out=z[t].ap[:, hh * 512:(hh + 1) * 512], in0=xr.ap[:, hh * 512:(hh + 1) * 512], scalar=ALPHA,
                            in1=ob.ap, op0=ALU.mult, op1=ALU.add), reads=[xr, ob], writes=[z[t]])
                for t in range(4):
                    layer_norm(z[t], 0, t)
                    xb1 = x1b[t % 2]
                    E("act", lambda e, xb1=xb1, t=t: e.activation(out=xb1.ap, in_=z[t].ap, func=AF.Copy), reads=[z[t]], writes=[xb1])
                    transposes_to(x1T.ap[:, :, t * 128:(t + 1) * 128], x1T, lambda k, xb1=xb1: xb1.ap[:, k * 128:(k + 1) * 128],
                                  xb1, 8, "dve")
                rfg = None
                rfu = None
                for j in range(NJ):
                    if j % 4 == 0:
                        rfg = wl(8 + 2 * (j // 4))
                        rfu = wl(9 + 2 * (j // 4))
                    sfg = slab(rfg, 8, 512)
                    sfu = slab(rfu, 8, 512)
                    jj = j % 4
                    pg_ = pa_next(True)
                    pu_ = pa_next(True)
                    for kc in range(8):
                        mm(pg_.ap, pg_, sfg[:, kc, jj * 128:(jj + 1) * 128], x1T.ap[:, kc, :], [rfg, x1T], kc == 0, kc == 7)
                    for kc in range(8):
                        mm(pu_.ap, pu_, sfu[:, kc, jj * 128:(jj + 1) * 128], x1T.ap[:, kc, :], [rfu, x1T], kc == 0, kc == 7)
                    sgb = csg[j % 2]
                    E("act", lambda e, sgb=sgb, pg_=pg_: e.activation(out=sgb.ap, in_=pg_.ap, func=AF.Silu), reads=[pg_], writes=[sgb])
                    E("dve", lambda e, sgb=sgb, pu_=pu_, j=j: e.tensor_tensor(out=hT.ap[:, j, :], in0=sgb.ap, in1=pu_.ap, op=ALU.mult),
                      reads=[sgb, pu_], writes=[hT])
                for hh in range(2):
                    for sl in range(3):
                        k0 = sl * 8
                        kn = min(8, NJ - k0)
                        rd_ = wl(20 + hh * 3 + sl)
                        sd = slab(rd_, 8, 512)
                        for t in range(4):
                            ob = PBh[t]
                            for kk in range(kn):
                                j = k0 + kk
                                mm(ob.ap, ob, hT.ap[:, j, t * 128:(t + 1) * 128], sd[:, kk, :], [hT, rd_], j == 0, j == NJ - 1)
                    for t in range(4):
                        ob = PBh[t]
                        E("dve", lambda e, ob=ob, t=t, hh=hh: e.scalar_tensor_tensor(
                            out=z[t].ap[:, hh * 512:(hh + 1) * 512], in0=z[t].ap[:, hh * 512:(hh + 1) * 512], scalar=ALPHA,
                            in1=ob.ap, op0=ALU.mult, op1=ALU.add), reads=[ob, z[t]], writes=[z[t]])
                rpl = wl(26)
                spl = slab(rpl, 2, 1024)
                for hh in range(2):
                    rpg = wl(27 + hh)
                    spg = slab(rpg, 8, 512)
                    for t in range(4):
                        ppl = pa_next()
                        ppg = pa_next()
                        for kc in range(2):
                            mm(ppl.ap, ppl, pT.ap[:, kc, t * 128:(t + 1) * 128], spl[:, kc, hh * 512:(hh + 1) * 512], [pT, rpl], kc == 0, kc == 1)
                        for kc in range(8):
                            mm(ppg.ap, ppg, x1T.ap[:, kc, t * 128:(t + 1) * 128], spg[:, kc, :], [x1T, rpg], kc == 0, False)
                        mm(ppg.ap, ppg, ones_bf.ap[0:1, :], bple.ap[0:1, hh * 512:(hh + 1) * 512], [ones_bf, bple], False, True)
                        sgb = csg[t % 2]
                        E("act", lambda e, sgb=sgb, ppg=ppg: e.activation(out=sgb.ap, in_=ppg.ap, func=AF.Sigmoid), reads=[ppg], writes=[sgb])
                        pc = cpc[t % 2]
                        E("dve", lambda e, pc=pc, sgb=sgb, ppl=ppl: e.tensor_tensor(out=pc.ap, in0=sgb.ap, in1=ppl.ap, op=ALU.mult),
                          reads=[sgb, ppl], writes=[pc])
                        E("dve", lambda e, pc=pc, t=t, hh=hh: e.tensor_tensor(
                            out=z[t].ap[:, hh * 512:(hh + 1) * 512], in0=z[t].ap[:, hh * 512:(hh + 1) * 512], in1=pc.ap, op=ALU.add),
                          reads=[pc, z[t]], writes=[z[t]])
                for t in range(4):
                    layer_norm(z[t], 2, t)
                    dsto = out_d[b, g * 512 + t * 128:g * 512 + (t + 1) * 128, :]
                    tok = E("sp", lambda e, t=t, dsto=dsto: e.dma_start(out=dsto, in_=z[t].ap), reads=[z[t]], stream=f"out{t}")
                    fin.append(tok)
        P.emit(nc, final_wait=fin[-8:] + fin[:2])
    return nc


def _t5_bucket_np(rel):
    n = np.maximum(rel, 0)
    max_exact = 16
    nf = np.maximum(n, 1).astype(np.float32)
    large = max_exact + (np.log(nf / np.float32(max_exact)) / np.float32(np.log(128 / max_exact))
                         * np.float32(32 - max_exact)).astype(np.int32)
    large = np.minimum(large, 31)
    return np.where(n < max_exact, n, large)


_NC_CACHE = {}
_LAST = []


def kernel(x, p, w_in, b_gate, bias_table, w_att_out, conv_w, conv_b, conv_ln_g, conv_ln_b,
           w_conv_out, w_mix_out, ln_mix_g, ln_mix_b, w_ffn_gate, w_ffn_up, w_ffn_down,
           w_ple, w_ple_gate, b_ple_gate, ln_ffn_g, ln_ffn_b):
    f = lambda a: np.ascontiguousarray(np.asarray(a, dtype=np.float32))
    x = f(x)
    p = f(p)
    B = x.shape[0]
    NB = B // NCORES
    cvec = np.zeros((128, NCV), np.float32)
    cvec[:, 0:16] = f(b_gate)[0].reshape(16, 128).T
    cvec[:, 16:20] = f(conv_b)[0].reshape(4, 128).T
    cvec[:, 20:24] = f(conv_ln_g)[0].reshape(4, 128).T
    cvec[:, 24:28] = f(conv_ln_b)[0].reshape(4, 128).T
    cw = f(conv_w)[0]
    cvec[:, 28:152] = cw.T.reshape(4, 128, CW).transpose(1, 0, 2).reshape(128, 4 * CW)
    bt = f(bias_table)
    cvec[:, 152:160] = bt[31][None, :]
    cvec[:, 160] = EPS
    lnv = np.stack([np.broadcast_to(f(v)[0][None, :], (128, D)) for v in (ln_mix_g, ln_mix_b, ln_ffn_g, ln_ffn_b)], axis=1)
    lnv = np.ascontiguousarray(lnv)
    kk = np.arange(128)[:, None]
    cc = np.arange(256)[None, :]
    bidx = _t5_bucket_np(cc - kk)
    toep = np.ascontiguousarray(bt[bidx].transpose(0, 2, 1))
    onehot = (np.arange(S)[None, :] // 256 == np.arange(8)[:, None]).astype(np.float32)
    shared = {
        "w_in": f(w_in)[0], "w_att_out": f(w_att_out)[0], "w_conv_out": f(w_conv_out)[0], "w_mix_out": f(w_mix_out)[0],
        "w_ffn_gate": f(w_ffn_gate)[0], "w_ffn_up": f(w_ffn_up)[0], "w_ffn_down": f(w_ffn_down)[0],
        "w_ple": f(w_ple)[0], "w_ple_gate": f(w_ple_gate)[0], "cvec": cvec, "lnv": lnv,
        "bple": f(b_ple_gate).reshape(1, D), "toep": toep, "onehot": onehot,
    }
    if NB not in _NC_CACHE:
        _NC_CACHE[NB] = build_nc(NB)
    nc = _NC_CACHE[NB]
    in_maps = []
    for c in range(NCORES):
        m = dict(shared)
        m["x"] = x[c * NB:(c + 1) * NB]
        m["p"] = p[0, c * NB:(c + 1) * NB]
        in_maps.append(m)
    res = run_bass_kernel_spmd(nc, in_maps, core_ids=list(range(NCORES)))
    if DBG:
        _LAST.append(res.results[0])
    return np.concatenate([r["out"] for r in res.results], axis=0).astype(np.float32)
```

```python
import contextlib
import numpy as np
import concourse.bass as bass
import concourse.mybir as mybir
from concourse.bass_utils import run_bass_kernel_spmd

F32 = mybir.dt.float32
BF = mybir.dt.bfloat16
AF = mybir.ActivationFunctionType
ALU = mybir.AluOpType
AX = mybir.AxisListType

NCORES = 8
D = 1024
S = 2048
NH = 8
DH = 64
NIN = 4608
FFN = 2816
NJ = FFN // 128
PLE = 256
CW = 31
ALPHA = 2.0 ** 0.25
EPS = 1e-5
BIG = 30000.0
NCV = 161
DBG = False
PE_BIAS = False

ENGS = ("pe", "act", "dve", "pool", "sp")


class Tok:
    __slots__ = ("eng", "idx", "sem", "val", "is_dma", "stream", "key")

    def __init__(self, eng, idx):
        self.eng = eng
        self.idx = idx
        self.sem = None
        self.val = None
        self.is_dma = False
        self.stream = None
        self.key = eng


class Buf:
    def __init__(self, ap, name):
        self.ap = ap
        self.name = name
        self.w = {}
        self.r = {}


class Prog:
    SEM_ROT = 4000

    def __init__(self):
        self.ops = {e: [] for e in ENGS}
        self.streams = {}
        self.need = set()
        self.last = {}

    def op(self, eng, fn, deps=(), stream=None):
        t = Tok(eng, len(self.ops[eng]))
        deps = [d for d in deps if d is not None]
        if stream is not None:
            t.is_dma = True
            t.stream = stream
            t.key = "s:" + stream
            n = self.streams.get(stream, 0) + 1
            self.streams[stream] = n
            t.val = 16 * n
        self.ops[eng].append((t, fn, deps))
        for d in deps:
            if not d.is_dma:
                self.need.add((d.eng, d.idx))
        self.last[t.key] = t
        return t

    def E(self, eng, fn, reads=(), writes=(), stream=None, extra=()):
        deps = {}

        def add(t):
            k = t.key
            o = deps.get(k)
            if o is None or t.idx > o.idx or (t.is_dma and t.val > o.val):
                deps[k] = t

        for t in extra:
            if t is not None:
                add(t)
        for b in reads:
            for t in b.w.values():
                add(t)
        for b in writes:
            for t in b.w.values():
                add(t)
            for t in b.r.values():
                add(t)
        if eng == "pe":
            deps.pop("pe", None)
        tok = self.op(eng, fn, list(deps.values()), stream=stream)
        for b in reads:
            b.r[tok.key] = tok
        for b in writes:
            b.w[tok.key] = tok
        return tok

    def barrier(self, bufs):
        snap = dict(self.last)
        for b in bufs:
            for k, t in snap.items():
                b.r[k] = t
            b.w = {}

    def emit(self, nc, final_wait=()):
        with contextlib.ExitStack() as es:
            nsem = 0
            for e in ENGS:
                cnt = 0
                cur = None
                for t, fn, deps in self.ops[e]:
                    if t.is_dma:
                        continue
                    if (e, t.idx) in self.need:
                        if cur is None or cnt >= self.SEM_ROT:
                            cur = es.enter_context(nc.semaphore(f"c_{e}_{nsem}"))
                            nsem += 1
                            cnt = 0
                        cnt += 1
                        t.sem = cur
                        t.val = cnt
            dsem = {}
            for s in self.streams:
                dsem[s] = es.enter_context(nc.semaphore(f"d_{s}"))
            for e in ENGS:
                for t, fn, deps in self.ops[e]:
                    if t.is_dma:
                        t.sem = dsem[t.stream]
            block = es.enter_context(nc.Block())

            def make(e):
                def body(eng):
                    waited = {}
                    for t, fn, deps in self.ops[e]:
                        for d in deps:
                            k = id(d.sem)
                            if waited.get(k, 0) >= d.val:
                                continue
                            eng.wait_ge(d.sem, d.val)
                            waited[k] = d.val
                        ins = fn(eng)
                        if t.is_dma:
                            ins.then_inc(t.sem, 16)
                        elif t.sem is not None:
                            ins.then_inc(t.sem, 1)
                    if e == "sp":
                        for d in final_wait:
                            eng.wait_ge(d.sem, d.val)

                return body

            block.tensor(make("pe"))
            block.scalar(make("act"))
            block.vector(make("dve"))
            block.gpsimd(make("pool"))
            block.sync(make("sp"))


def build_nc(NB):
    nc = bass.Bass("TRN2", target_bir_lowering=False)
    dr = {}

    def din(name, shape):
        dr[name] = nc.dram_tensor(name, list(shape), F32, kind="ExternalInput").ap()
        return dr[name]

    x_d = din("x", [NB, S, D])
    p_d = din("p", [NB, S, PLE])
    w_in = din("w_in", [D, NIN]).rearrange("(kc p) n -> p kc n", p=128)
    w_ao = din("w_att_out", [512, D]).rearrange("(kc p) n -> p kc n", p=128)
    w_co = din("w_conv_out", [512, D]).rearrange("(kc p) n -> p kc n", p=128)
    w_mix = din("w_mix_out", [D, D]).rearrange("(kc p) n -> p kc n", p=128)
    w_fg = din("w_ffn_gate", [D, FFN]).rearrange("(kc p) n -> p kc n", p=128)
    w_fu = din("w_ffn_up", [D, FFN]).rearrange("(kc p) n -> p kc n", p=128)
    w_fd = din("w_ffn_down", [FFN, D]).rearrange("(kc p) n -> p kc n", p=128)
    w_pl = din("w_ple", [PLE, D]).rearrange("(kc p) n -> p kc n", p=128)
    w_pg = din("w_ple_gate", [D, D]).rearrange("(kc p) n -> p kc n", p=128)
    cvec_d = din("cvec", [128, NCV])
    lnv_d = din("lnv", [128, 4, D])
    bple_d = din("bple", [1, D])
    toep_d = din("toep", [128, NH, 256])
    oneh_d = din("onehot", [8, S])
    out_d = nc.dram_tensor("out", [NB, S, D], F32, kind="ExternalOutput").ap()

    P = Prog()
    E = P.E
    fin = []
    if DBG:
        dbg_att = nc.dram_tensor("dbg_att", [128, 4, S], F32, kind="ExternalOutput").ap()
        dbg_c = nc.dram_tensor("dbg_c", [128, 4, S], F32, kind="ExternalOutput").ap()

    with contextlib.ExitStack() as es:
        def sb(name, shape, dt):
            return es.enter_context(nc.sbuf_tensor("sb_" + name, list(shape), dt))

        def psum(name, shape, dt):
            return es.enter_context(nc.psum_tensor("ps_" + name, list(shape), dt))

        ident = Buf(sb("ident", [128, 128], BF)[:], "ident")
        ones_bf = Buf(sb("ones_bf", [128, 128], BF)[:], "ones")
        Ebf = Buf(sb("Ebf", [128, NH, 256], BF)[:], "Ebf")
        cvec = Buf(sb("cvec", [128, NCV], F32)[:], "cvec")
        negc = Buf(sb("negc", [128, NH], F32)[:], "negc")
        lnv = Buf(sb("lnv", [128, 4, D], F32)[:], "lnv")
        bple = Buf(sb("bple", [1, D], BF)[:], "bple")
        attT_t = sb("attT", [128, 4, S], BF)
        cT_t = sb("cT", [128, 4, S], BF)
        attT = [Buf(attT_t[:, :, g * 512:(g + 1) * 512], f"attT{g}") for g in range(4)]
        cT = [Buf(cT_t[:, c, :], f"cT{c}") for c in range(4)]
        NRING = 4
        ring = [Buf(sb(f"ring{i}", [128, 4096], BF)[:], f"ring{i}") for i in range(NRING)]
        ring_i = [0]

        def ring_next():
            b = ring[ring_i[0] % NRING]
            ring_i[0] += 1
            return b

        def slab(b, kc, n):
            return b.ap[:, 0:kc * n].rearrange("p (k n) -> p k n", k=kc)

        def wload(b, kc, n, src, k0=0, kn=None, n0=0):
            kn = kc - k0 if kn is None else kn
            nn = src.shape[2]
            dst = slab(b, kc, n)[:, k0:k0 + kn, n0:n0 + nn]
            return E("pool", lambda e: e.dma_start(out=dst, in_=src), writes=[b], stream=b.name)

        SD = []
        SD.append((4, 1024, w_ao, 4))
        SD.append((8, 512, w_in[:, :, 2560:3072], 8))
        SD.append((8, 512, w_in[:, :, 3072:3584], 8))
        SD.append((4, 1024, w_co, 4))
        SD.append((8, 512, w_in[:, :, 3584:4096], 8))
        SD.append((8, 512, w_in[:, :, 4096:4608], 8))
        SD.append((8, 512, w_mix[:, :, 0:512], 8))
        SD.append((8, 512, w_mix[:, :, 512:1024], 8))
        for jq in range(6):
            ncol = min(512, FFN - jq * 512)
            SD.append((8, 512, w_fg[:, :, jq * 512:jq * 512 + ncol], 8))
            SD.append((8, 512, w_fu[:, :, jq * 512:jq * 512 + ncol], 8))
        for hh in range(2):
            for sl in range(3):
                k0 = sl * 8
                kn = min(8, NJ - k0)
                SD.append((8, 512, w_fd[:, k0:k0 + kn, hh * 512:(hh + 1) * 512], kn))
        SD.append((2, 1024, w_pl, 2))
        SD.append((8, 512, w_pg[:, :, 0:512], 8))
        SD.append((8, 512, w_pg[:, :, 512:1024], 8))
        NSL = len(SD)
        wscr = nc.dram_tensor("wscr", [NSL, 128, 4096], BF, kind="Internal").ap()
        scrb = [Buf(None, f"scr{i}") for i in range(NSL)]

        def convert_slab(i):
            kc, n, src, kn = SD[i]
            rb = ring_next()
            wload(rb, kc, n, src, kn=kn)
            E("sp", lambda e: e.dma_start(out=wscr[i], in_=rb.ap), reads=[rb], writes=[scrb[i]], stream=f"scr{i}")

        def wl(i):
            rb = ring_next()
            E("sp", lambda e: e.dma_start(out=rb.ap, in_=wscr[i]), reads=[scrb[i]], writes=[rb], stream=rb.name)
            return rb

        NU = 57600
        U = sb("U", [128, NU], BF)
        off = [0]

        def carve(nel, dt):
            nb = nel * (4 if dt == F32 else 2)
            nb = (nb + 63) // 64 * 64
            a = off[0] // 2
            off[0] += nb
            assert off[0] <= NU * 2, ("union overflow", off[0])
            v = U[:, a:a + nb // 2]
            if dt == F32:
                v = v.bitcast(F32)[:, 0:nel]
            else:
                v = v[:, 0:nel]
            return v

        off[0] = 0
        xT_t = carve(8 * S, BF).rearrange("p (k t) -> p k t", k=8)
        xT = [Buf(xT_t[:, :, g * 512:(g + 1) * 512], f"xT{g}") for g in range(4)]
        uT_t = carve(4 * (S + 30), BF).rearrange("p (c t) -> p c t", c=4)
        uT = [Buf(uT_t[:, c, :], f"uT{c}") for c in range(4)]
        qT = Buf(carve(2 * S, BF).rearrange("p (h t) -> p h t", h=2), "qT")
        kT = Buf(carve(2 * S, BF).rearrange("p (h t) -> p h t", h=2), "kT")
        vA = Buf(carve(16 * 2 * 65, BF).rearrange("p (j h d) -> p j h d", j=16, h=2), "vA")
        xb = [Buf(carve(2 * D, BF).rearrange("p (j d) -> p j d", j=2), f"xb{i}") for i in range(2)]
        diag = Buf(carve(CW * 128, BF).rearrange("p (j m) -> p j m", j=CW), "diag")
        sgt = [Buf(carve(512, F32), f"sgt{i}") for i in range(2)]
        ysq = Buf(carve(4 * 512, BF).rearrange("p (c t) -> p c t", c=4), "ysq")
        mean_t = Buf(carve(512, F32), "mean")
        msq_t = Buf(carve(512, F32), "msq")
        rstd_t = Buf(carve(512, F32), "rstd")
        t1_t = [Buf(carve(512, F32), f"t1_{i}") for i in range(2)]
        PT = [Buf(carve(512, BF), f"PT{i}") for i in range(4)]
        att_tm = [Buf(carve(4 * 128, BF).rearrange("p (s d) -> p s d", s=4), f"atm{i}") for i in range(2)]
        ksum = Buf(carve(8, F32), "ksum")
        kmean = Buf(carve(8, BF), "kmean")
        gs = Buf(carve(128, F32).rearrange("p (a n) -> p a n", a=16), "gs")
        g2 = Buf(carve(128, F32).rearrange("p (a n) -> p a n", a=16), "g2")
        eq = Buf(carve(128, F32).rearrange("p (a n) -> p a n", a=16), "eq")
        mx = Buf(carve(16, F32), "mx")
        mpad = Buf(carve(8 * 2 * 72, BF).rearrange("p (q h n) -> p q h n", q=8, h=2), "mpad")
        rinv = Buf(carve(4, F32), "rinv")
        AB_END = off[0]
        AB_BUFS = (xT + uT + [qT, kT, vA] + xb + [diag] + sgt + [ysq, mean_t, msq_t, rstd_t] + t1_t + PT
                   + att_tm + [ksum, kmean, gs, g2, eq, mx, mpad, rinv])

        off[0] = 0
        xres = [Buf(carve(D, F32), f"xres{i}") for i in range(2)]
        z_t = carve(4 * D, F32).rearrange("p (t d) -> p t d", t=4)
        z = [Buf(z_t[:, t, :], f"z{t}") for t in range(4)]
        mT = Buf(carve(8 * 512, BF).rearrange("p (k t) -> p k t", k=8), "mT")
        csg = [Buf(carve(512, F32), f"csg{i}") for i in range(2)]
        cpc = [Buf(carve(512, F32), f"cpc{i}") for i in range(2)]
        x1b = [Buf(carve(D, BF), f"x1b{i}") for i in range(2)]
        x1T = Buf(carve(8 * 512, BF).rearrange("p (k t) -> p k t", k=8), "x1T")
        hT_raw = carve(NJ * 512, BF)
        hT = Buf(hT_raw.rearrange("p (j t) -> p j t", j=NJ), "hT")
        hTf = hT_raw[:, 0:8192].bitcast(F32).rearrange("p (m t) -> p m t", m=8)
        xTg = Buf(carve(8 * 512, BF).rearrange("p (k t) -> p k t", k=8), "xTg")
        xbg = Buf(carve(4 * D, BF).rearrange("p (j d) -> p j d", j=4), "xbg")
        pbg = Buf(carve(4 * PLE, BF).rearrange("p (j d) -> p j d", j=4), "pbg")
        pT = Buf(carve(2 * 512, BF).rearrange("p (k t) -> p k t", k=2), "pT")
        lst_ = [Buf(carve(12, F32).rearrange("p (a b) -> p a b", a=2), f"lst{i}") for i in range(4)]
        lmv_ = [Buf(carve(2, F32), f"lmv{i}") for i in range(4)]
        lsd_ = [Buf(carve(1, F32), f"lsd{i}") for i in range(4)]
        lrs_ = [Buf(carve(1, F32), f"lrs{i}") for i in range(4)]
        lnm_ = [Buf(carve(1, F32), f"lnm{i}") for i in range(4)]
        C_END = off[0]
        C_BUFS = (xres + z + [mT] + csg + cpc + x1b + [x1T, hT, xTg, xbg, pbg, pT] + lst_ + lmv_ + lsd_ + lrs_ + lnm_)

        PA = [Buf(psum(f"pa{i}", [128, 512], F32)[:], f"pa{i}") for i in range(3)]
        PBt = [psum(f"pb{i}", [128, 1024], F32) for i in range(2)]
        PBh = [Buf(PBt[i][:, h * 512:(h + 1) * 512], f"pb{i}{h}") for i in range(2) for h in range(2)]
        PM_t = psum("pm", [128, 1024], BF)
        PM = Buf(PM_t[:], "pm")
        pa_i = [0]

        WIDE = PA + PBh
        pw_i = [0]

        def pa_next(wide=False):
            if wide:
                b = WIDE[pw_i[0] % 7]
                pw_i[0] += 1
                return b
            b = PA[pa_i[0] % 3]
            pa_i[0] += 1
            return b

        E("sp", lambda e: e.dma_start(out=cvec.ap, in_=cvec_d), writes=[cvec], stream="c_cvec")
        E("sp", lambda e: e.dma_start(out=lnv.ap, in_=lnv_d), writes=[lnv], stream="c_lnv")
        E("pool", lambda e: e.dma_start(out=bple.ap, in_=bple_d), writes=[bple], stream="c_bple")
        idf = Buf(U[:, 0:256].bitcast(F32), "idf")
        tf = Buf(U[:, 4096:4096 + NH * 256 * 2].bitcast(F32).rearrange("p (h c) -> p h c", h=NH), "tf")
        E("dve", lambda e: e.memset(idf.ap, 0.0), writes=[idf])
        E("pool", lambda e: e.affine_select(out=idf.ap, in_=idf.ap, pattern=[[-1, 128]], compare_op=ALU.not_equal,
                                            fill=1.0, base=0, channel_multiplier=1), reads=[idf], writes=[idf])
        E("dve", lambda e: e.tensor_copy(out=ident.ap, in_=idf.ap), reads=[idf], writes=[ident])
        E("dve", lambda e: e.memset(ones_bf.ap, 1.0), writes=[ones_bf])
        E("sp", lambda e: e.dma_start(out=tf.ap, in_=toep_d), writes=[tf], stream="c_toep")
        E("dve", lambda e: e.tensor_scalar(out=negc.ap, in0=cvec.ap[:, 152:160], scalar1=-1.0, scalar2=None, op0=ALU.mult),
          reads=[cvec], writes=[negc])
        for h in range(NH):
            if PE_BIAS:
                E("dve", lambda e, h=h: e.tensor_scalar(out=tf.ap[:, h, :], in0=tf.ap[:, h, :], scalar1=negc.ap[:, h:h + 1],
                                                        scalar2=1.0 / (DH ** -0.5), op0=ALU.add, op1=ALU.mult),
                  reads=[tf, negc], writes=[tf])
            else:
                E("act", lambda e, h=h: e.activation(out=tf.ap[:, h, :], in_=tf.ap[:, h, :], func=AF.Exp,
                                                     bias=negc.ap[:, h:h + 1], scale=1.0), reads=[tf, negc], writes=[tf])
        E("pool", lambda e: e.affine_select(out=tf.ap, in_=tf.ap, pattern=[[0, NH], [1, 256]], compare_op=ALU.is_ge,
                                            fill=(-BIG / (DH ** -0.5) if PE_BIAS else 0.0), base=0, channel_multiplier=-1), reads=[tf], writes=[tf])
        E("dve", lambda e: e.tensor_copy(out=Ebf.ap, in_=tf.ap), reads=[tf], writes=[Ebf])

        cv = cvec.ap
        pen = Buf(sb("pen", [128, 16, 8], F32)[:], "pen")
        val = Buf(sb("val", [128, 16, 8], F32)[:], "val")
        E("dve", lambda e: e.memset(pen.ap, 0.0), writes=[pen])
        E("dve", lambda e: e.memset(val.ap, 1.0), writes=[val])
        for qi in range(6):
            bq_ = 4 + qi // 2
            E("dve", lambda e, qi=qi, bq_=bq_: e.memset(pen.ap[:, 2 * qi:2 * qi + 2, bq_:8], -1e30), writes=[pen])
            E("dve", lambda e, qi=qi, bq_=bq_: e.memset(val.ap[:, 2 * qi:2 * qi + 2, bq_:8], 0.0), writes=[val])
        for qi in range(6, 8):
            E("dve", lambda e, qi=qi: e.memset(pen.ap[:, 2 * qi:2 * qi + 2, 7:8], -1e30), writes=[pen])
            E("dve", lambda e, qi=qi: e.memset(val.ap[:, 2 * qi:2 * qi + 2, 7:8], 0.0), writes=[val])

        def transposes_to(dst_ap, dst_buf, src_ap_fn, src_buf, n, evac_eng):
            for i in range(n):
                E("pe", lambda e, i=i: e.transpose(out=PM_t[:, i * 128:(i + 1) * 128], in_=src_ap_fn(i), identity=ident.ap),
                  reads=[src_buf, ident], writes=[PM])
            src = PM_t[:, 0:n * 128]
            if len(dst_ap.shape) == 3:
                src = src.rearrange("p (k t) -> p k t", k=n)
            if evac_eng == "act":
                return E("act", lambda e: e.activation(out=dst_ap, in_=src, func=AF.Copy), reads=[PM], writes=[dst_buf])
            return E("dve", lambda e: e.tensor_copy(out=dst_ap, in_=src), reads=[PM], writes=[dst_buf])

        def layer_norm4(zs, gi):
            for ti, zb in enumerate(zs):
                for hh in range(2):
                    E("dve", lambda e, hh=hh, ti=ti, zb=zb: e.bn_stats(out=lst_[ti].ap[:, hh, :], in_=zb.ap[:, hh * 512:(hh + 1) * 512]),
                      reads=[zb], writes=[lst_[ti]])
                E("dve", lambda e, ti=ti: e.bn_aggr(out=lmv_[ti].ap, in_=lst_[ti].ap), reads=[lst_[ti]], writes=[lmv_[ti]])
            for ti, zb in enumerate(zs):
                E("act", lambda e, ti=ti: e.activation(out=lsd_[ti].ap, in_=lmv_[ti].ap[:, 1:2], func=AF.Sqrt, bias=cv[:, 160:161], scale=1.0),
                  reads=[lmv_[ti], cvec], writes=[lsd_[ti]])
            for ti, zb in enumerate(zs):
                E("dve", lambda e, ti=ti: e.reciprocal(out=lrs_[ti].ap, in_=lsd_[ti].ap), reads=[lsd_[ti]], writes=[lrs_[ti]])
                E("dve", lambda e, ti=ti, zb=zb: e.tensor_scalar(out=zb.ap, in0=zb.ap, scalar1=lmv_[ti].ap[:, 0:1], scalar2=lrs_[ti].ap,
                                                                op0=ALU.subtract, op1=ALU.mult), reads=[zb, lmv_[ti], lrs_[ti]], writes=[zb])
                E("dve", lambda e, zb=zb: e.tensor_tensor(out=zb.ap, in0=zb.ap, in1=lnv.ap[:, gi, :], op=ALU.mult),
                  reads=[zb, lnv], writes=[zb])
                E("dve", lambda e, zb=zb: e.tensor_tensor(out=zb.ap, in0=zb.ap, in1=lnv.ap[:, gi + 1, :], op=ALU.add),
                  reads=[zb, lnv], writes=[zb])

        def mm(out_ap, out_buf, lhsT, rhs, rd, start, stop):
            return E("pe", lambda e: e.matmul(out_ap, lhsT=lhsT, rhs=rhs, start=start, stop=stop),
                     reads=rd, writes=[out_buf])

        for b in range(NB):
            P.barrier(AB_BUFS)
            for c in range(4):
                E("dve", lambda e, c=c: e.memset(uT_t[:, c, 0:30], 0.0), writes=[uT[c]])
            E("dve", lambda e: e.memset(kT.ap[0:64, 1, :], 0.0), writes=[kT])
            E("dve", lambda e: e.memset(qT.ap[0:64, 1, :], 0.0), writes=[qT])
            E("dve", lambda e: e.memset(qT.ap[64:72, 0, :], 0.0), writes=[qT])
            E("dve", lambda e: e.memset(vA.ap[:, :, :, 64:65], 1.0), writes=[vA])
            E("dve", lambda e: e.memset(mpad.ap, 0.0), writes=[mpad])
            E("pool", lambda e: e.dma_start(out=kT.ap[64:72, 0, :], in_=oneh_d), writes=[kT], stream="oh0")
            E("pool", lambda e: e.dma_start(out=kT.ap[0:8, 1, :], in_=oneh_d), writes=[kT], stream="oh1")

            for i in range(8):
                xbb = xb[i % 2]
                src = x_d[b, i * 256:(i + 1) * 256, :].rearrange("(j p) d -> p j d", p=128)
                E("pool", lambda e, xbb=xbb, src=src: e.dma_start(out=xbb.ap, in_=src), writes=[xbb], stream=xbb.name)
                g = i // 2
                for j in range(2):
                    tt = (i % 2) * 2 + j
                    dst = xT[g].ap[:, :, tt * 128:(tt + 1) * 128]
                    transposes_to(dst, xT[g], lambda k, xbb=xbb, j=j: xbb.ap[:, j, k * 128:(k + 1) * 128], xbb, 8, "act")

            ra = ring_next()
            wload(ra, 8, 512, w_in[:, :, 1536:2048])
            rg = ring_next()
            wload(rg, 8, 512, w_in[:, :, 2048:2560])
            sa = slab(ra, 8, 512)
            sg_ = slab(rg, 8, 512)
            gi = 0
            for g in range(4):
                for c in range(4):
                    pa_ = pa_next(True)
                    pg_ = pa_next(True)
                    for kc in range(8):
                        mm(pa_.ap, pa_, sa[:, kc, c * 128:(c + 1) * 128], xT[g].ap[:, kc, :], [ra, xT[g]], kc == 0, kc == 7)
                    for kc in range(8):
                        mm(pg_.ap, pg_, sg_[:, kc, c * 128:(c + 1) * 128], xT[g].ap[:, kc, :], [rg, xT[g]], kc == 0, kc == 7)
                    st = sgt[gi % 2]
                    gi += 1
                    E("act", lambda e, st=st, pg_=pg_: e.activation(out=st.ap, in_=pg_.ap, func=AF.Sigmoid),
                      reads=[pg_], writes=[st])
                    E("dve", lambda e, st=st, pa_=pa_, c=c, g=g: e.tensor_tensor(
                        out=uT_t[:, c, 30 + g * 512:30 + (g + 1) * 512], in0=pa_.ap, in1=st.ap, op=ALU.mult),
                      reads=[pa_, st], writes=[uT[c]])

            def conv_diag(c):
                for j in range(CW):
                    E("dve", lambda e, j=j: e.tensor_scalar(out=diag.ap[:, j, :], in0=ident.ap, scalar1=cv[:, 28 + c * CW + j:29 + c * CW + j],
                                                            scalar2=None, op0=ALU.mult), reads=[ident, cvec], writes=[diag])

            def conv_chunk(c):
                for g in range(4):
                    pc_ = pa_next()
                    for j in range(CW):
                        mm(pc_.ap, pc_, diag.ap[:, j, :], uT_t[:, c, g * 512 + j:g * 512 + j + 512], [diag, uT[c]], j == 0, j == CW - 1)
                    E("act", lambda e, pc_=pc_, g=g: e.activation(out=cT_t[:, c, g * 512:(g + 1) * 512], in_=pc_.ap, func=AF.Identity,
                                                                 bias=cv[:, 16 + c:17 + c], scale=1.0), reads=[pc_, cvec], writes=[cT[c]])

            def conv_ln(g):
                gr = slice(g * 512, (g + 1) * 512)
                for c in range(4):
                    E("act", lambda e, c=c: e.activation(out=ysq.ap[:, c, :], in_=cT_t[:, c, gr], func=AF.Square),
                      reads=[cT[c]], writes=[ysq])
                s1 = pa_next()
                s2 = pa_next()
                for c in range(4):
                    mm(s1.ap, s1, ones_bf.ap, cT_t[:, c, gr], [ones_bf, cT[c]], c == 0, c == 3)
                for c in range(4):
                    mm(s2.ap, s2, ones_bf.ap, ysq.ap[:, c, :], [ones_bf, ysq], c == 0, c == 3)
                E("dve", lambda e: e.tensor_scalar(out=mean_t.ap, in0=s1.ap, scalar1=1.0 / 512, scalar2=None, op0=ALU.mult),
                  reads=[s1], writes=[mean_t])
                E("dve", lambda e: e.tensor_tensor(out=msq_t.ap, in0=mean_t.ap, in1=mean_t.ap, op=ALU.mult),
                  reads=[mean_t], writes=[msq_t])
                E("dve", lambda e: e.scalar_tensor_tensor(out=msq_t.ap, in0=s2.ap, scalar=1.0 / 512, in1=msq_t.ap,
                                                          op0=ALU.mult, op1=ALU.subtract), reads=[s2, msq_t], writes=[msq_t])
                E("act", lambda e: e.activation(out=rstd_t.ap, in_=msq_t.ap, func=AF.Sqrt, bias=cv[:, 160:161], scale=1.0),
                  reads=[msq_t, cvec], writes=[rstd_t])
                E("dve", lambda e: e.reciprocal(out=rstd_t.ap, in_=rstd_t.ap), reads=[rstd_t], writes=[rstd_t])
                for c in range(4):
                    t1 = t1_t[c % 2]
                    E("dve", lambda e, c=c, t1=t1: e.tensor_tensor(out=t1.ap, in0=cT_t[:, c, gr], in1=mean_t.ap, op=ALU.subtract),
                      reads=[cT[c], mean_t], writes=[t1])
                    E("dve", lambda e, t1=t1: e.tensor_tensor(out=t1.ap, in0=t1.ap, in1=rstd_t.ap, op=ALU.mult),
                      reads=[t1, rstd_t], writes=[t1])
                    E("act", lambda e, c=c, t1=t1: e.activation(out=cT_t[:, c, gr], in_=t1.ap, func=AF.Silu,
                                                               bias=cv[:, 24 + c:25 + c], scale=cv[:, 20 + c:21 + c]),
                      reads=[t1, cvec], writes=[cT[c]])

            for hp in range(4):
                rq = ring_next()
                wload(rq, 8, 384, w_in[:, :, hp * 128:(hp + 1) * 128], n0=0)
                wload(rq, 8, 384, w_in[:, :, 512 + hp * 128:512 + (hp + 1) * 128], n0=128)
                wload(rq, 8, 384, w_in[:, :, 1024 + hp * 128:1024 + (hp + 1) * 128], n0=256)
                sq = slab(rq, 8, 384)
                for g in range(4):
                    gr = slice(g * 512, (g + 1) * 512)
                    pq = pa_next()
                    for kc in range(8):
                        mm(pq.ap, pq, sq[:, kc, 0:128], xT[g].ap[:, kc, :], [rq, xT[g]], kc == 0, kc == 7)
                    E("act", lambda e, pq=pq, gr=gr: e.activation(out=qT.ap[0:64, 0, gr], in_=pq.ap[0:64, :], func=AF.Copy),
                      reads=[pq], writes=[qT])
                    E("dve", lambda e, pq=pq, gr=gr: e.tensor_copy(out=qT.ap[64:128, 1, gr], in_=pq.ap[64:128, :]),
                      reads=[pq], writes=[qT])
                    pk = pa_next()
                    for kc in range(8):
                        mm(pk.ap, pk, sq[:, kc, 128:256], xT[g].ap[:, kc, :], [rq, xT[g]], kc == 0, kc == 7)
                    E("act", lambda e, pk=pk, gr=gr: e.activation(out=kT.ap[0:64, 0, gr], in_=pk.ap[0:64, :], func=AF.Copy),
                      reads=[pk], writes=[kT])
                    E("dve", lambda e, pk=pk, gr=gr: e.tensor_copy(out=kT.ap[64:128, 1, gr], in_=pk.ap[64:128, :]),
                      reads=[pk], writes=[kT])
                    E("dve", lambda e, pk=pk, g=g: e.tensor_reduce(out=ksum.ap[:, 2 * g:2 * g + 2],
                                                                  in_=pk.ap.rearrange("p (a b) -> p a b", a=2),
                                                                  axis=AX.X, op=ALU.add), reads=[pk], writes=[ksum])
                    pv = pa_next()
                    for j in range(4):
                        for kc in range(8):
                            mm(pv.ap[:, j * 128:(j + 1) * 128], pv, xT[g].ap[:, kc, j * 128:(j + 1) * 128], sq[:, kc, 256:384],
                               [rq, xT[g]], kc == 0, kc == 7)
                    E("act", lambda e, pv=pv, g=g: e.activation(
                        out=vA.ap[:, 4 * g:4 * g + 4, :, 0:64],
                        in_=pv.ap.rearrange("p (j h d) -> p j h d", j=4, h=2), func=AF.Copy), reads=[pv], writes=[vA])
                E("dve", lambda e: e.tensor_scalar(out=kmean.ap, in0=ksum.ap, scalar1=1.0 / 256, scalar2=None, op0=ALU.mult),
                  reads=[ksum], writes=[kmean])
                if b == 0:
                    for i in range(NSL):
                        if i * 4 // NSL == hp:
                            convert_slab(i)
                conv_diag(hp)
                pg_ = pa_next()
                for qi in range(8):
                    qs = slice((8 + qi) * 128, (9 + qi) * 128)
                    mm(pg_.ap[:, qi * 16:qi * 16 + 8], pg_, qT.ap[0:64, 0, qs], kmean.ap[0:64, :], [qT, kmean], True, True)
                    mm(pg_.ap[:, qi * 16 + 8:qi * 16 + 16], pg_, qT.ap[64:128, 1, qs], kmean.ap[64:128, :], [qT, kmean], True, True)
                pgv = pg_.ap[:, 0:128].rearrange("p (a n) -> p a n", a=16)
                E("dve", lambda e, pgv=pgv: e.tensor_tensor(out=gs.ap, in0=pgv, in1=pen.ap, op=ALU.add), reads=[pg_, pen], writes=[gs])
                conv_chunk(hp)
                cur = gs
                for it in range(3):
                    E("dve", lambda e, cur=cur: e.tensor_reduce(out=mx.ap, in_=cur.ap, axis=AX.X, op=ALU.max), reads=[cur], writes=[mx])
                    if it == 2:
                        break
                    E("dve", lambda e, cur=cur: e.tensor_tensor(out=eq.ap, in0=cur.ap, in1=mx.ap.unsqueeze(2).to_broadcast([128, 16, 8]),
                                                               op=ALU.is_equal), reads=[cur, mx], writes=[eq])
                    E("dve", lambda e, cur=cur: e.scalar_tensor_tensor(out=g2.ap, in0=eq.ap, scalar=-1e30, in1=cur.ap,
                                                                      op0=ALU.mult, op1=ALU.add), reads=[eq, cur], writes=[g2])
                    cur = g2
                E("dve", lambda e: e.tensor_tensor(out=eq.ap, in0=gs.ap, in1=mx.ap.unsqueeze(2).to_broadcast([128, 16, 8]), op=ALU.is_ge),
                  reads=[gs, mx], writes=[eq])
                E("dve", lambda e: e.tensor_scalar(out=g2.ap, in0=eq.ap, scalar1=-1.0, scalar2=BIG, op0=ALU.add, op1=ALU.mult),
                  reads=[eq], writes=[g2])
                g2v = g2.ap.rearrange("p (q h) n -> p q h n", h=2)
                valv = val.ap.rearrange("p (q h) n -> p q h n", h=2)
                E("dve", lambda e: e.tensor_tensor(out=mpad.ap[:, :, 0, 64:72], in0=g2v[:, :, 0, :], in1=valv[:, :, 0, :], op=ALU.mult),
                  reads=[g2, val], writes=[mpad])
                E("dve", lambda e: e.tensor_tensor(out=mpad.ap[:, :, 1, 0:8], in0=g2v[:, :, 1, :], in1=valv[:, :, 1, :], op=ALU.mult),
                  reads=[g2, val], writes=[mpad])
                for qi in range(8):
                    mm(PBt[0][0:72, qi * 128:(qi + 1) * 128], PBh[qi // 4], mpad.ap[:, qi, 0, 0:72], ident.ap, [mpad, ident], True, True)
                for qi in range(8):
                    mm(PBt[1][0:8, qi * 128:(qi + 1) * 128], PBh[2 + qi // 4], mpad.ap[:, qi, 1, 0:8], ident.ap, [mpad, ident], True, True)
                E("act", lambda e: e.activation(out=qT.ap[64:72, 0, 1024:2048], in_=PBt[0][64:72, :], func=AF.Copy),
                  reads=[PBh[0], PBh[1]], writes=[qT])
                E("act", lambda e: e.activation(out=qT.ap[0:8, 1, 1024:2048], in_=PBt[1][0:8, :], func=AF.Copy),
                  reads=[PBh[2], PBh[3]], writes=[qT])
                cth = []
                it_n = 0
                its = [(G, h, j) for G in range(4) for h in range(2) for j in range(4 * G + 4)]
                LA = 2
                pts = {}

                def emit_scores(k):
                    G, h, j = its[k]
                    rows = slice(0, 72) if h == 0 else slice(0, 128)
                    d = j - 4 * G
                    c0 = max(0, d) * 128
                    N = 512 - c0
                    ps_ = pa_next()
                    hh = 2 * hp + h
                    mm(ps_.ap[:, 0:N], ps_, kT.ap[rows, h, j * 128:(j + 1) * 128],
                       qT.ap[rows, h, G * 512 + c0:(G + 1) * 512], [kT, qT], True, True)
                    pt = PT[k % 4]
                    pts[k] = pt
                    E("act", lambda e: e.activation(out=pt.ap[:, 0:N], in_=ps_.ap[:, 0:N], func=AF.Exp, scale=DH ** -0.5),
                      reads=[ps_], writes=[pt])
                    if d >= 0:
                        w_ = min(256, N)
                        E("dve", lambda e: e.tensor_tensor(out=pt.ap[:, 0:w_], in0=pt.ap[:, 0:w_], in1=Ebf.ap[:, hh, 0:w_], op=ALU.mult),
                          reads=[pt, Ebf], writes=[pt])
                    elif d == -1:
                        E("dve", lambda e: e.tensor_tensor(out=pt.ap[:, 0:128], in0=pt.ap[:, 0:128], in1=Ebf.ap[:, hh, 128:256], op=ALU.mult),
                          reads=[pt, Ebf], writes=[pt])

                def emit_pv(k):
                    G, h, j = its[k]
                    d = j - 4 * G
                    c0 = max(0, d) * 128
                    pt = pts.pop(k)
                    atm = att_tm[G % 2]
                    for s_ in range(max(0, d), 4):
                        lc = s_ * 128 - c0
                        mm(PBh[s_].ap[:, 0:65], PBh[s_], pt.ap[:, lc:lc + 128], vA.ap[:, j, h, :], [pt, vA], j == 0, j == 4 * G + s_)
                    if j == 4 * G + 3:
                        for s_ in range(4):
                            O = PBh[s_]
                            E("dve", lambda e, O=O, s_=s_: e.reciprocal(out=rinv.ap[:, s_:s_ + 1], in_=O.ap[:, 64:65]), reads=[O], writes=[rinv])
                            E("dve", lambda e, O=O, s_=s_: e.tensor_scalar(
                                out=atm.ap[:, s_, h * 64:(h + 1) * 64], in0=O.ap[:, 0:64], scalar1=rinv.ap[:, s_:s_ + 1], scalar2=None,
                                op0=ALU.mult), reads=[O, rinv], writes=[atm])
                        if h == 1:
                            transposes_to(attT_t[:, hp, G * 512:(G + 1) * 512], attT[G], lambda s_: atm.ap[:, s_, :], atm, 4, "act")

                for k in range(len(its) + LA):
                    if k < len(its):
                        emit_scores(k)
                    if k - LA >= 0:
                        emit_pv(k - LA)
            for g in range(4):
                conv_ln(g)

            if DBG and b == 0:
                fin.append(E("pool", lambda e: e.dma_start(out=dbg_att, in_=attT_t[:]), reads=attT, stream="dbg1"))
                fin.append(E("pool", lambda e: e.dma_start(out=dbg_c, in_=cT_t[:]), reads=cT, stream="dbg2"))
            P.barrier(C_BUFS)
            for g in range(4):
                gr = slice(g * 512, (g + 1) * 512)
                def load_xp(gg):
                    srcx = x_d[b, gg * 512:(gg + 1) * 512, :].rearrange("(j p) d -> p j d", p=128)
                    E("pool", lambda e: e.dma_start(out=xbg.ap, in_=srcx), writes=[xbg], stream="xbg")
                    srcp = p_d[b, gg * 512:(gg + 1) * 512, :].rearrange("(j p) d -> p j d", p=128)
                    E("pool", lambda e: e.dma_start(out=pbg.ap, in_=srcp), writes=[pbg], stream="pbg")
                if g == 0:
                    load_xp(0)
                for j in range(4):
                    transposes_to(xTg.ap[:, :, j * 128:(j + 1) * 128], xTg, lambda k, j=j: xbg.ap[:, j, k * 128:(k + 1) * 128],
                                  xbg, 8, "act" if j % 2 == 0 else "dve")
                for j in range(4):
                    for k in range(2):
                        E("pe", lambda e, j=j, k=k: e.transpose(out=PM_t[:, (k * 4 + j) * 128:(k * 4 + j + 1) * 128],
                                                                in_=pbg.ap[:, j, k * 128:(k + 1) * 128], identity=ident.ap),
                          reads=[pbg, ident], writes=[PM])
                E("dve", lambda e: e.tensor_copy(out=pT.ap, in_=PM_t[:, 0:1024].rearrange("p (k t) -> p k t", k=2)),
                  reads=[PM], writes=[pT])
                if g < 3:
                    load_xp(g + 1)
                rao = wl(0)
                sao = slab(rao, 4, 1024)
                ci = 0
                for part in range(2):
                    if part == 1:
                        rco = wl(3)
                        sco = slab(rco, 4, 1024)
                    for mh in range(2):
                        rgl = wl([1, 2, 4, 5][part * 2 + mh])
                        sgl = slab(rgl, 8, 512)
                        for mm_ in range(4):
                            m = mh * 4 + mm_
                            py = pa_next(True)
                            pl = pa_next(True)
                            for kc in range(4):
                                if part == 0:
                                    mm(py.ap, py, sao[:, kc, m * 128:(m + 1) * 128], attT[g].ap[:, kc, :], [rao, attT[g]], kc == 0, kc == 3)
                                else:
                                    mm(py.ap, py, sco[:, kc, m * 128:(m + 1) * 128], cT_t[:, kc, gr], [rco, cT[kc]], kc == 0, kc == 3)
                            for kc in range(8):
                                mm(pl.ap, pl, sgl[:, kc, mm_ * 128:(mm_ + 1) * 128], xTg.ap[:, kc, :], [rgl, xTg], kc == 0, kc == 7)
                            sgb = csg[ci % 2]
                            ci += 1
                            bcol = part * 8 + m
                            E("act", lambda e, sgb=sgb, pl=pl, bcol=bcol: e.activation(
                                out=sgb.ap, in_=pl.ap, func=AF.Sigmoid, bias=cv[:, bcol:bcol + 1], scale=1.0),
                              reads=[pl, cvec], writes=[sgb])
                            if part == 0:
                                E("dve", lambda e, py=py, sgb=sgb, m=m: e.tensor_tensor(
                                    out=hTf[:, m, :], in0=py.ap, in1=sgb.ap, op=ALU.mult), reads=[py, sgb], writes=[hT])
                            else:
                                pc = cpc[m % 2]
                                E("dve", lambda e, py=py, sgb=sgb, pc=pc: e.tensor_tensor(out=pc.ap, in0=py.ap, in1=sgb.ap, op=ALU.mult),
                                  reads=[py, sgb], writes=[pc])
                                E("dve", lambda e, pc=pc, m=m: e.tensor_tensor(out=mT.ap[:, m, :], in0=pc.ap, in1=hTf[:, m, :], op=ALU.add),
                                  reads=[pc, hT], writes=[mT])
                rm = [wl(6), wl(7)]
                for t in range(4):
                    xr = xres[t % 2]
                    srcr = x_d[b, g * 512 + t * 128:g * 512 + (t + 1) * 128, :]
                    E("sp", lambda e, xr=xr, srcr=srcr: e.dma_start(out=xr.ap, in_=srcr), writes=[xr], stream=xr.name)
                    pbi = t % 2
                    for hh in range(2):
                        ob = PBh[pbi * 2 + hh]
                        sm = slab(rm[hh], 8, 512)
                        for kc in range(8):
                            mm(ob.ap, ob, mT.ap[:, kc, t * 128:(t + 1) * 128], sm[:, kc, :], [mT, rm[hh]], kc == 0, kc == 7)
                        E("dve", lambda e, xr=xr, ob=ob, t=t, hh=hh: e.scalar_tensor_tensor(
                            out=z[t].ap[:, hh * 512:(hh + 1) * 512], in0=xr.ap[:, hh * 512:(hh + 1) * 512], scalar=ALPHA,
                            in1=ob.ap, op0=ALU.mult, op1=ALU.add), reads=[xr, ob], writes=[z[t]])
                layer_norm4(z, 0)
                for t in range(4):
                    xb1 = x1b[t % 2]
                    E("act", lambda e, xb1=xb1, t=t: e.activation(out=xb1.ap, in_=z[t].ap, func=AF.Copy), reads=[z[t]], writes=[xb1])
                    transposes_to(x1T.ap[:, :, t * 128:(t + 1) * 128], x1T, lambda k, xb1=xb1: xb1.ap[:, k * 128:(k + 1) * 128],
                                  xb1, 8, "dve")
                rfg = None
                rfu = None
                for j in range(NJ):
                    if j % 4 == 0:
                        rfg = wl(8 + 2 * (j // 4))
                        rfu = wl(9 + 2 * (j // 4))
                    sfg = slab(rfg, 8, 512)
                    sfu = slab(rfu, 8, 512)
                    jj = j % 4
                    pg_ = pa_next(True)
                    pu_ = pa_next(True)
                    for kc in range(8):
                        mm(pg_.ap, pg_, sfg[:, kc, jj * 128:(jj + 1) * 128], x1T.ap[:, kc, :], [rfg, x1T], kc == 0, kc == 7)
                    for kc in range(8):
                        mm(pu_.ap, pu_, sfu[:, kc, jj * 128:(jj + 1) * 128], x1T.ap[:, kc, :], [rfu, x1T], kc == 0, kc == 7)
                    sgb = csg[j % 2]
                    E("act", lambda e, sgb=sgb, pg_=pg_: e.activation(out=sgb.ap, in_=pg_.ap, func=AF.Silu), reads=[pg_], writes=[sgb])
                    E("dve", lambda e, sgb=sgb, pu_=pu_, j=j: e.tensor_tensor(out=hT.ap[:, j, :], in0=sgb.ap, in1=pu_.ap, op=ALU.mult),
                      reads=[sgb, pu_], writes=[hT])
                for hh in range(2):
                    for sl in range(3):
                        k0 = sl * 8
                        kn = min(8, NJ - k0)
                        rd_ = wl(20 + hh * 3 + sl)
                        sd = slab(rd_, 8, 512)
                        for t in range(4):
                            ob = PBh[t]
                            for kk in range(kn):
                                j = k0 + kk
                                mm(ob.ap, ob, hT.ap[:, j, t * 128:(t + 1) * 128], sd[:, kk, :], [hT, rd_], j == 0, j == NJ - 1)
                    for t in range(4):
                        ob = PBh[t]
                        E("dve", lambda e, ob=ob, t=t, hh=hh: e.scalar_tensor_tensor(
                            out=z[t].ap[:, hh * 512:(hh + 1) * 512], in0=z[t].ap[:, hh * 512:(hh + 1) * 512], scalar=ALPHA,
                            in1=ob.ap, op0=ALU.mult, op1=ALU.add), reads=[ob, z[t]], writes=[z[t]])
                rpl = wl(26)
                spl = slab(rpl, 2, 1024)
                for hh in range(2):
                    rpg = wl(27 + hh)
                    spg = slab(rpg, 8, 512)
                    for t in range(4):
                        ppl = pa_next()
                        ppg = pa_next()
                        for kc in range(2):
                            mm(ppl.ap, ppl, pT.ap[:, kc, t * 128:(t + 1) * 128], spl[:, kc, hh * 512:(hh + 1) * 512], [pT, rpl], kc == 0, kc == 1)
                        for kc in range(8):
                            mm(ppg.ap, ppg, x1T.ap[:, kc, t * 128:(t + 1) * 128], spg[:, kc, :], [x1T, rpg], kc == 0, False)
                        mm(ppg.ap, ppg, ones_bf.ap[0:1, :], bple.ap[0:1, hh * 512:(hh + 1) * 512], [ones_bf, bple], False, True)
                        sgb = csg[t % 2]
                        E("act", lambda e, sgb=sgb, ppg=ppg: e.activation(out=sgb.ap, in_=ppg.ap, func=AF.Sigmoid), reads=[ppg], writes=[sgb])
                        pc = cpc[t % 2]
                        E("dve", lambda e, pc=pc, sgb=sgb, ppl=ppl: e.tensor_tensor(out=pc.ap, in0=sgb.ap, in1=ppl.ap, op=ALU.mult),
                          reads=[sgb, ppl], writes=[pc])
                        E("dve", lambda e, pc=pc, t=t, hh=hh: e.tensor_tensor(
                            out=z[t].ap[:, hh * 512:(hh + 1) * 512], in0=z[t].ap[:, hh * 512:(hh + 1) * 512], in1=pc.ap, op=ALU.add),
                          reads=[pc, z[t]], writes=[z[t]])
                layer_norm4(z, 2)
                for t in range(4):
                    dsto = out_d[b, g * 512 + t * 128:g * 512 + (t + 1) * 128, :]
                    tok = E("pool", lambda e, t=t, dsto=dsto: e.dma_start(out=dsto, in_=z[t].ap), reads=[z[t]], stream=f"out{t}")
                    fin.append(tok)
        P.emit(nc, final_wait=fin[-8:] + fin[:2])
    return nc


def _t5_bucket_np(rel):
    n = np.maximum(rel, 0)
    max_exact = 16
    nf = np.maximum(n, 1).astype(np.float32)
    large = max_exact + (np.log(nf / np.float32(max_exact)) / np.float32(np.log(128 / max_exact))
                         * np.float32(32 - max_exact)).astype(np.int32)
    large = np.minimum(large, 31)
    return np.where(n < max_exact, n, large)


_NC_CACHE = {}
_LAST = []


def kernel(x, p, w_in, b_gate, bias_table, w_att_out, conv_w, conv_b, conv_ln_g, conv_ln_b,
           w_conv_out, w_mix_out, ln_mix_g, ln_mix_b, w_ffn_gate, w_ffn_up, w_ffn_down,
           w_ple, w_ple_gate, b_ple_gate, ln_ffn_g, ln_ffn_b):
    f = lambda a: np.ascontiguousarray(np.asarray(a, dtype=np.float32))
    x = f(x)
    p = f(p)
    B = x.shape[0]
    NB = B // NCORES
    cvec = np.zeros((128, NCV), np.float32)
    cvec[:, 0:16] = f(b_gate)[0].reshape(16, 128).T
    cvec[:, 16:20] = f(conv_b)[0].reshape(4, 128).T
    cvec[:, 20:24] = f(conv_ln_g)[0].reshape(4, 128).T
    cvec[:, 24:28] = f(conv_ln_b)[0].reshape(4, 128).T
    cw = f(conv_w)[0]
    cvec[:, 28:152] = cw.T.reshape(4, 128, CW).transpose(1, 0, 2).reshape(128, 4 * CW)
    bt = f(bias_table)
    cvec[:, 152:160] = bt[31][None, :]
    cvec[:, 160] = EPS
    lnv = np.stack([np.broadcast_to(f(v)[0][None, :], (128, D)) for v in (ln_mix_g, ln_mix_b, ln_ffn_g, ln_ffn_b)], axis=1)
    lnv = np.ascontiguousarray(lnv)
    kk = np.arange(128)[:, None]
    cc = np.arange(256)[None, :]
    bidx = _t5_bucket_np(cc - kk)
    toep = np.ascontiguousarray(bt[bidx].transpose(0, 2, 1))
    onehot = (np.arange(S)[None, :] // 256 == np.arange(8)[:, None]).astype(np.float32)
    shared = {
        "w_in": f(w_in)[0], "w_att_out": f(w_att_out)[0], "w_conv_out": f(w_conv_out)[0], "w_mix_out": f(w_mix_out)[0],
        "w_ffn_gate": f(w_ffn_gate)[0], "w_ffn_up": f(w_ffn_up)[0], "w_ffn_down": f(w_ffn_down)[0],
        "w_ple": f(w_ple)[0], "w_ple_gate": f(w_ple_gate)[0], "cvec": cvec, "lnv": lnv,
        "bple": f(b_ple_gate).reshape(1, D), "toep": toep, "onehot": onehot,
    }
    if NB not in _NC_CACHE:
        _NC_CACHE[NB] = build_nc(NB)
    nc = _NC_CACHE[NB]
    in_maps = []
    for c in range(NCORES):
        m = dict(shared)
        m["x"] = x[c * NB:(c + 1) * NB]
        m["p"] = p[0, c * NB:(c + 1) * NB]
        in_maps.append(m)
    res = run_bass_kernel_spmd(nc, in_maps, core_ids=list(range(NCORES)))
    if DBG:
        _LAST.append(res.results[0])
    return np.concatenate([r["out"] for r in res.results], axis=0).astype(np.float32)
```

```python
import contextlib
import numpy as np
import concourse.bass as bass
import concourse.mybir as mybir
from concourse.bass_utils import run_bass_kernel_spmd

F32 = mybir.dt.float32
BF = mybir.dt.bfloat16
AF = mybir.ActivationFunctionType
ALU = mybir.AluOpType
AX = mybir.AxisListType

NCORES = 8
D = 1024
S = 2048
NH = 8
DH = 64
NIN = 4608
FFN = 2816
NJ = FFN // 128
PLE = 256
CW = 31
ALPHA = 2.0 ** 0.25
EPS = 1e-5
BIG = 30000.0
NCV = 161
DBG = False
PE_BIAS = False

ENGS = ("pe", "act", "dve", "pool", "sp")


class Tok:
    __slots__ = ("eng", "idx", "sem", "val", "is_dma", "stream", "key")

    def __init__(self, eng, idx):
        self.eng = eng
        self.idx = idx
        self.sem = None
        self.val = None
        self.is_dma = False
        self.stream = None
        self.key = eng


class Buf:
    def __init__(self, ap, name):
        self.ap = ap
        self.name = name
        self.w = {}
        self.r = {}


class Prog:
    SEM_ROT = 4000

    def __init__(self):
        self.ops = {e: [] for e in ENGS}
        self.streams = {}
        self.need = set()
        self.last = {}

    def op(self, eng, fn, deps=(), stream=None):
        t = Tok(eng, len(self.ops[eng]))
        deps = [d for d in deps if d is not None]
        if stream is not None:
            t.is_dma = True
            t.stream = stream
            t.key = "s:" + stream
            n = self.streams.get(stream, 0) + 1
            self.streams[stream] = n
            t.val = 16 * n
        self.ops[eng].append((t, fn, deps))
        for d in deps:
            if not d.is_dma:
                self.need.add((d.eng, d.idx))
        self.last[t.key] = t
        return t

    def E(self, eng, fn, reads=(), writes=(), stream=None, extra=()):
        deps = {}

        def add(t):
            k = t.key
            o = deps.get(k)
            if o is None or t.idx > o.idx or (t.is_dma and t.val > o.val):
                deps[k] = t

        for t in extra:
            if t is not None:
                add(t)
        for b in reads:
            for t in b.w.values():
                add(t)
        for b in writes:
            for t in b.w.values():
                add(t)
            for t in b.r.values():
                add(t)
        if eng == "pe":
            deps.pop("pe", None)
        tok = self.op(eng, fn, list(deps.values()), stream=stream)
        for b in reads:
            b.r[tok.key] = tok
        for b in writes:
            b.w[tok.key] = tok
        return tok

    def barrier(self, bufs):
        snap = dict(self.last)
        for b in bufs:
            for k, t in snap.items():
                b.r[k] = t
            b.w = {}

    def emit(self, nc, final_wait=()):
        with contextlib.ExitStack() as es:
            nsem = 0
            for e in ENGS:
                cnt = 0
                cur = None
                for t, fn, deps in self.ops[e]:
                    if t.is_dma:
                        continue
                    if (e, t.idx) in self.need:
                        if cur is None or cnt >= self.SEM_ROT:
                            cur = es.enter_context(nc.semaphore(f"c_{e}_{nsem}"))
                            nsem += 1
                            cnt = 0
                        cnt += 1
                        t.sem = cur
                        t.val = cnt
            dsem = {}
            for s in self.streams:
                dsem[s] = es.enter_context(nc.semaphore(f"d_{s}"))
            for e in ENGS:
                for t, fn, deps in self.ops[e]:
                    if t.is_dma:
                        t.sem = dsem[t.stream]
            block = es.enter_context(nc.Block())

            def make(e):
                def body(eng):
                    waited = {}
                    for t, fn, deps in self.ops[e]:
                        for d in deps:
                            k = id(d.sem)
                            if waited.get(k, 0) >= d.val:
                                continue
                            eng.wait_ge(d.sem, d.val)
                            waited[k] = d.val
                        ins = fn(eng)
                        if t.is_dma:
                            ins.then_inc(t.sem, 16)
                        elif t.sem is not None:
                            ins.then_inc(t.sem, 1)
                    if e == "sp":
                        for d in final_wait:
                            eng.wait_ge(d.sem, d.val)

                return body

            block.tensor(make("pe"))
            block.scalar(make("act"))
            block.vector(make("dve"))
            block.gpsimd(make("pool"))
            block.sync(make("sp"))


def build_nc(NB):
    nc = bass.Bass("TRN2", target_bir_lowering=False)
    dr = {}

    def din(name, shape):
        dr[name] = nc.dram_tensor(name, list(shape), F32, kind="ExternalInput").ap()
        return dr[name]

    x_d = din("x", [NB, S, D])
    p_d = din("p", [NB, S, PLE])
    w_in = din("w_in", [D, NIN]).rearrange("(kc p) n -> p kc n", p=128)
    w_ao = din("w_att_out", [512, D]).rearrange("(kc p) n -> p kc n", p=128)
    w_co = din("w_conv_out", [512, D]).rearrange("(kc p) n -> p kc n", p=128)
    w_mix = din("w_mix_out", [D, D]).rearrange("(kc p) n -> p kc n", p=128)
    w_fg = din("w_ffn_gate", [D, FFN]).rearrange("(kc p) n -> p kc n", p=128)
    w_fu = din("w_ffn_up", [D, FFN]).rearrange("(kc p) n -> p kc n", p=128)
    w_fd = din("w_ffn_down", [FFN, D]).rearrange("(kc p) n -> p kc n", p=128)
    w_pl = din("w_ple", [PLE, D]).rearrange("(kc p) n -> p kc n", p=128)
    w_pg = din("w_ple_gate", [D, D]).rearrange("(kc p) n -> p kc n", p=128)
    cvec_d = din("cvec", [128, NCV])
    lnv_d = din("lnv", [128, 4, D])
    bple_d = din("bple", [1, D])
    toep_d = din("toep", [128, NH, 256])
    oneh_d = din("onehot", [8, S])
    out_d = nc.dram_tensor("out", [NB, S, D], F32, kind="ExternalOutput").ap()

    P = Prog()
    E = P.E
    fin = []
    if DBG:
        dbg_att = nc.dram_tensor("dbg_att", [128, 4, S], F32, kind="ExternalOutput").ap()
        dbg_c = nc.dram_tensor("dbg_c", [128, 4, S], F32, kind="ExternalOutput").ap()

    with contextlib.ExitStack() as es:
        def sb(name, shape, dt):
            return es.enter_context(nc.sbuf_tensor("sb_" + name, list(shape), dt))

        def psum(name, shape, dt):
            return es.enter_context(nc.psum_tensor("ps_" + name, list(shape), dt))

        ident = Buf(sb("ident", [128, 128], BF)[:], "ident")
        ones_bf = Buf(sb("ones_bf", [128, 128], BF)[:], "ones")
        Ebf = Buf(sb("Ebf", [128, NH, 256], BF)[:], "Ebf")
        cvec = Buf(sb("cvec", [128, NCV], F32)[:], "cvec")
        negc = Buf(sb("negc", [128, NH], F32)[:], "negc")
        lnv = Buf(sb("lnv", [128, 4, D], F32)[:], "lnv")
        bple = Buf(sb("bple", [1, D], BF)[:], "bple")
        attT_t = sb("attT", [128, 4, S], BF)
        cT_t = sb("cT", [128, 4, S], BF)
        attT = [Buf(attT_t[:, :, g * 512:(g + 1) * 512], f"attT{g}") for g in range(4)]
        cT = [Buf(cT_t[:, c, :], f"cT{c}") for c in range(4)]
        NRING = 6
        ring = [Buf(sb(f"ring{i}", [128, 4096], BF)[:], f"ring{i}") for i in range(NRING)]
        ring_i = [0]

        def ring_next():
            b = ring[ring_i[0] % NRING]
            ring_i[0] += 1
            return b

        def slab(b, kc, n):
            return b.ap[:, 0:kc * n].rearrange("p (k n) -> p k n", k=kc)

        def wload(b, kc, n, src, k0=0, kn=None, n0=0):
            kn = kc - k0 if kn is None else kn
            nn = src.shape[2]
            dst = slab(b, kc, n)[:, k0:k0 + kn, n0:n0 + nn]
            return E("pool", lambda e: e.dma_start(out=dst, in_=src), writes=[b], stream=b.name)

        SD = []
        SD.append((4, 1024, w_ao, 4))
        SD.append((8, 512, w_in[:, :, 2560:3072], 8))
        SD.append((8, 512, w_in[:, :, 3072:3584], 8))
        SD.append((4, 1024, w_co, 4))
        SD.append((8, 512, w_in[:, :, 3584:4096], 8))
        SD.append((8, 512, w_in[:, :, 4096:4608], 8))
        SD.append((8, 512, w_mix[:, :, 0:512], 8))
        SD.append((8, 512, w_mix[:, :, 512:1024], 8))
        for jq in range(6):
            ncol = min(512, FFN - jq * 512)
            SD.append((8, 512, w_fg[:, :, jq * 512:jq * 512 + ncol], 8))
            SD.append((8, 512, w_fu[:, :, jq * 512:jq * 512 + ncol], 8))
        for hh in range(2):
            for sl in range(3):
                k0 = sl * 8
                kn = min(8, NJ - k0)
                SD.append((8, 512, w_fd[:, k0:k0 + kn, hh * 512:(hh + 1) * 512], kn))
        SD.append((2, 1024, w_pl, 2))
        SD.append((8, 512, w_pg[:, :, 0:512], 8))
        SD.append((8, 512, w_pg[:, :, 512:1024], 8))
        NSL = len(SD)
        wscr = nc.dram_tensor("wscr", [NSL, 128, 4096], BF, kind="Internal").ap()
        scrb = [Buf(None, f"scr{i}") for i in range(NSL)]

        def convert_slab(i):
            kc, n, src, kn = SD[i]
            rb = ring_next()
            wload(rb, kc, n, src, kn=kn)
            E("sp", lambda e: e.dma_start(out=wscr[i], in_=rb.ap), reads=[rb], writes=[scrb[i]], stream=f"scr{i}")

        def wl(i):
            rb = ring_next()
            E("sp", lambda e: e.dma_start(out=rb.ap, in_=wscr[i]), reads=[scrb[i]], writes=[rb], stream=rb.name)
            return rb

        NU = 50240
        U = sb("U", [128, NU], BF)
        off = [0]

        def carve(nel, dt):
            nb = nel * (4 if dt == F32 else 2)
            nb = (nb + 63) // 64 * 64
            a = off[0] // 2
            off[0] += nb
            assert off[0] <= NU * 2, ("union overflow", off[0])
            v = U[:, a:a + nb // 2]
            if dt == F32:
                v = v.bitcast(F32)[:, 0:nel]
            else:
                v = v[:, 0:nel]
            return v

        off[0] = 0
        xT_t = carve(8 * S, BF).rearrange("p (k t) -> p k t", k=8)
        xT = [Buf(xT_t[:, :, g * 512:(g + 1) * 512], f"xT{g}") for g in range(4)]
        uT_t = carve(4 * (S + 30), BF).rearrange("p (c t) -> p c t", c=4)
        uT = [Buf(uT_t[:, c, :], f"uT{c}") for c in range(4)]
        qT = Buf(carve(2 * S, BF).rearrange("p (h t) -> p h t", h=2), "qT")
        kT = Buf(carve(2 * S, BF).rearrange("p (h t) -> p h t", h=2), "kT")
        vA = Buf(carve(16 * 2 * 65, BF).rearrange("p (j h d) -> p j h d", j=16, h=2), "vA")
        off_alias = off[0]
        xb = [Buf(carve(2 * D, BF).rearrange("p (j d) -> p j d", j=2), f"xb{i}") for i in range(2)]
        diag = Buf(carve(CW * 128, BF).rearrange("p (j m) -> p j m", j=CW), "diag")
        sgt = [Buf(carve(512, F32), f"sgt{i}") for i in range(2)]
        off_alias_end = off[0]
        PT = [Buf(carve(512, BF), f"PT{i}") for i in range(4)]
        att_tm = [Buf(carve(4 * 128, BF).rearrange("p (s d) -> p s d", s=4), f"atm{i}") for i in range(2)]
        ksum = Buf(carve(8, F32), "ksum")
        kmean = Buf(carve(8, BF), "kmean")
        gs = Buf(carve(128, F32).rearrange("p (a n) -> p a n", a=16), "gs")
        g2 = Buf(carve(128, F32).rearrange("p (a n) -> p a n", a=16), "g2")
        eq = Buf(carve(128, F32).rearrange("p (a n) -> p a n", a=16), "eq")
        mx = Buf(carve(16, F32), "mx")
        mpad = Buf(carve(8 * 2 * 72, BF).rearrange("p (q h n) -> p q h n", q=8, h=2), "mpad")
        rinv = Buf(carve(4, F32), "rinv")
        AB_END = off[0]
        off[0] = off_alias
        ysq = Buf(carve(4 * 512, BF).rearrange("p (c t) -> p c t", c=4), "ysq")
        mean_t = Buf(carve(512, F32), "mean")
        msq_t = Buf(carve(512, F32), "msq")
        rstd_t = Buf(carve(512, F32), "rstd")
        t1_t = [Buf(carve(512, F32), f"t1_{i}") for i in range(2)]
        assert off[0] <= off_alias_end, (off[0], off_alias_end)
        CLN_BUFS = [ysq, mean_t, msq_t, rstd_t] + t1_t
        AB_BUFS = (xT + uT + [qT, kT, vA] + xb + [diag] + sgt + [ysq, mean_t, msq_t, rstd_t] + t1_t + PT
                   + att_tm + [ksum, kmean, gs, g2, eq, mx, mpad, rinv])

        off[0] = 0
        xres = [Buf(carve(D, F32), f"xres{i}") for i in range(2)]
        z_t = carve(4 * D, F32).rearrange("p (t d) -> p t d", t=4)
        z = [Buf(z_t[:, t, :], f"z{t}") for t in range(4)]
        mT = Buf(carve(8 * 512, BF).rearrange("p (k t) -> p k t", k=8), "mT")
        csg = [Buf(carve(512, F32), f"csg{i}") for i in range(2)]
        cpc = [Buf(carve(512, F32), f"cpc{i}") for i in range(2)]
        x1b = [Buf(carve(D, BF), f"x1b{i}") for i in range(2)]
        x1T = Buf(carve(8 * 512, BF).rearrange("p (k t) -> p k t", k=8), "x1T")
        hT_raw = carve(NJ * 512, BF)
        hT = Buf(hT_raw.rearrange("p (j t) -> p j t", j=NJ), "hT")
        hTf = hT_raw[:, 0:8192].bitcast(F32).rearrange("p (m t) -> p m t", m=8)
        xTg = Buf(carve(8 * 512, BF).rearrange("p (k t) -> p k t", k=8), "xTg")
        xbg = Buf(carve(4 * D, BF).rearrange("p (j d) -> p j d", j=4), "xbg")
        pbg = Buf(carve(4 * PLE, BF).rearrange("p (j d) -> p j d", j=4), "pbg")
        pT = Buf(carve(2 * 512, BF).rearrange("p (k t) -> p k t", k=2), "pT")
        lst_ = [Buf(carve(12, F32).rearrange("p (a b) -> p a b", a=2), f"lst{i}") for i in range(4)]
        lmv_ = [Buf(carve(2, F32), f"lmv{i}") for i in range(4)]
        lsd_ = [Buf(carve(1, F32), f"lsd{i}") for i in range(4)]
        lrs_ = [Buf(carve(1, F32), f"lrs{i}") for i in range(4)]
        lnm_ = [Buf(carve(1, F32), f"lnm{i}") for i in range(4)]
        C_END = off[0]
        C_BUFS = (xres + z + [mT] + csg + cpc + x1b + [x1T, hT, xTg, xbg, pbg, pT] + lst_ + lmv_ + lsd_ + lrs_ + lnm_)

        PA = [Buf(psum(f"pa{i}", [128, 512], F32)[:], f"pa{i}") for i in range(3)]
        PBt = [psum(f"pb{i}", [128, 1024], F32) for i in range(2)]
        PBh = [Buf(PBt[i][:, h * 512:(h + 1) * 512], f"pb{i}{h}") for i in range(2) for h in range(2)]
        PM_t = psum("pm", [128, 1024], BF)
        PM = Buf(PM_t[:], "pm")
        pa_i = [0]

        WIDE = PA + PBh
        pw_i = [0]

        def pa_next(wide=False):
            if wide:
                b = WIDE[pw_i[0] % 7]
                pw_i[0] += 1
                return b
            b = PA[pa_i[0] % 3]
            pa_i[0] += 1
            return b

        E("sp", lambda e: e.dma_start(out=cvec.ap, in_=cvec_d), writes=[cvec], stream="c_cvec")
        E("sp", lambda e: e.dma_start(out=lnv.ap, in_=lnv_d), writes=[lnv], stream="c_lnv")
        E("pool", lambda e: e.dma_start(out=bple.ap, in_=bple_d), writes=[bple], stream="c_bple")
        idf = Buf(U[:, 0:256].bitcast(F32), "idf")
        tf = Buf(U[:, 4096:4096 + NH * 256 * 2].bitcast(F32).rearrange("p (h c) -> p h c", h=NH), "tf")
        E("dve", lambda e: e.memset(idf.ap, 0.0), writes=[idf])
        E("pool", lambda e: e.affine_select(out=idf.ap, in_=idf.ap, pattern=[[-1, 128]], compare_op=ALU.not_equal,
                                            fill=1.0, base=0, channel_multiplier=1), reads=[idf], writes=[idf])
        E("dve", lambda e: e.tensor_copy(out=ident.ap, in_=idf.ap), reads=[idf], writes=[ident])
        E("dve", lambda e: e.memset(ones_bf.ap, 1.0), writes=[ones_bf])
        E("sp", lambda e: e.dma_start(out=tf.ap, in_=toep_d), writes=[tf], stream="c_toep")
        E("dve", lambda e: e.tensor_scalar(out=negc.ap, in0=cvec.ap[:, 152:160], scalar1=-1.0, scalar2=None, op0=ALU.mult),
          reads=[cvec], writes=[negc])
        for h in range(NH):
            if PE_BIAS:
                E("dve", lambda e, h=h: e.tensor_scalar(out=tf.ap[:, h, :], in0=tf.ap[:, h, :], scalar1=negc.ap[:, h:h + 1],
                                                        scalar2=1.0 / (DH ** -0.5), op0=ALU.add, op1=ALU.mult),
                  reads=[tf, negc], writes=[tf])
            else:
                E("act", lambda e, h=h: e.activation(out=tf.ap[:, h, :], in_=tf.ap[:, h, :], func=AF.Exp,
                                                     bias=negc.ap[:, h:h + 1], scale=1.0), reads=[tf, negc], writes=[tf])
        E("pool", lambda e: e.affine_select(out=tf.ap, in_=tf.ap, pattern=[[0, NH], [1, 256]], compare_op=ALU.is_ge,
                                            fill=(-BIG / (DH ** -0.5) if PE_BIAS else 0.0), base=0, channel_multiplier=-1), reads=[tf], writes=[tf])
        E("dve", lambda e: e.tensor_copy(out=Ebf.ap, in_=tf.ap), reads=[tf], writes=[Ebf])

        cv = cvec.ap
        pen = Buf(sb("pen", [128, 16, 8], F32)[:], "pen")
        val = Buf(sb("val", [128, 16, 8], F32)[:], "val")
        E("dve", lambda e: e.memset(pen.ap, 0.0), writes=[pen])
        E("dve", lambda e: e.memset(val.ap, 1.0), writes=[val])
        for qi in range(6):
            bq_ = 4 + qi // 2
            E("dve", lambda e, qi=qi, bq_=bq_: e.memset(pen.ap[:, 2 * qi:2 * qi + 2, bq_:8], -1e30), writes=[pen])
            E("dve", lambda e, qi=qi, bq_=bq_: e.memset(val.ap[:, 2 * qi:2 * qi + 2, bq_:8], 0.0), writes=[val])
        for qi in range(6, 8):
            E("dve", lambda e, qi=qi: e.memset(pen.ap[:, 2 * qi:2 * qi + 2, 7:8], -1e30), writes=[pen])
            E("dve", lambda e, qi=qi: e.memset(val.ap[:, 2 * qi:2 * qi + 2, 7:8], 0.0), writes=[val])

        def transposes_to(dst_ap, dst_buf, src_ap_fn, src_buf, n, evac_eng):
            for i in range(n):
                E("pe", lambda e, i=i: e.transpose(out=PM_t[:, i * 128:(i + 1) * 128], in_=src_ap_fn(i), identity=ident.ap),
                  reads=[src_buf, ident], writes=[PM])
            src = PM_t[:, 0:n * 128]
            if len(dst_ap.shape) == 3:
                src = src.rearrange("p (k t) -> p k t", k=n)
            if evac_eng == "act":
                return E("act", lambda e: e.activation(out=dst_ap, in_=src, func=AF.Copy), reads=[PM], writes=[dst_buf])
            return E("dve", lambda e: e.tensor_copy(out=dst_ap, in_=src), reads=[PM], writes=[dst_buf])

        def ln_stats(zs):
            for ti, zb in enumerate(zs):
                for hh in range(2):
                    E("dve", lambda e, hh=hh, ti=ti, zb=zb: e.bn_stats(out=lst_[ti].ap[:, hh, :], in_=zb.ap[:, hh * 512:(hh + 1) * 512]),
                      reads=[zb], writes=[lst_[ti]])
                E("dve", lambda e, ti=ti: e.bn_aggr(out=lmv_[ti].ap, in_=lst_[ti].ap), reads=[lst_[ti]], writes=[lmv_[ti]])
            for ti, zb in enumerate(zs):
                E("act", lambda e, ti=ti: e.activation(out=lsd_[ti].ap, in_=lmv_[ti].ap[:, 1:2], func=AF.Sqrt, bias=cv[:, 160:161], scale=1.0),
                  reads=[lmv_[ti], cvec], writes=[lsd_[ti]])

        def ln_norm_tile(zs, gi, ti):
            zb = zs[ti]
            E("dve", lambda e: e.reciprocal(out=lrs_[ti].ap, in_=lsd_[ti].ap), reads=[lsd_[ti]], writes=[lrs_[ti]])
            E("dve", lambda e: e.scalar_tensor_tensor(out=zb.ap, in0=zb.ap, scalar=lmv_[ti].ap[:, 0:1], in1=lnv.ap[:, gi, :],
                                                      op0=ALU.subtract, op1=ALU.mult), reads=[zb, lmv_[ti], lnv], writes=[zb])
            E("dve", lambda e: e.scalar_tensor_tensor(out=zb.ap, in0=zb.ap, scalar=lrs_[ti].ap, in1=lnv.ap[:, gi + 1, :],
                                                      op0=ALU.mult, op1=ALU.add), reads=[zb, lrs_[ti], lnv], writes=[zb])

        def mm(out_ap, out_buf, lhsT, rhs, rd, start, stop):
            return E("pe", lambda e: e.matmul(out_ap, lhsT=lhsT, rhs=rhs, start=start, stop=stop),
                     reads=rd, writes=[out_buf])

        for b in range(NB):
            P.barrier(AB_BUFS)
            for c in range(4):
                E("dve", lambda e, c=c: e.memset(uT_t[:, c, 0:30], 0.0), writes=[uT[c]])
            E("dve", lambda e: e.memset(kT.ap[0:64, 1, :], 0.0), writes=[kT])
            E("dve", lambda e: e.memset(qT.ap[0:64, 1, :], 0.0), writes=[qT])
            E("dve", lambda e: e.memset(qT.ap[64:72, 0, :], 0.0), writes=[qT])
            E("dve", lambda e: e.memset(vA.ap[:, :, :, 64:65], 1.0), writes=[vA])
            E("dve", lambda e: e.memset(mpad.ap, 0.0), writes=[mpad])
            E("pool", lambda e: e.dma_start(out=kT.ap[64:72, 0, :], in_=oneh_d), writes=[kT], stream="oh0")
            E("pool", lambda e: e.dma_start(out=kT.ap[0:8, 1, :], in_=oneh_d), writes=[kT], stream="oh1")

            for i in range(8):
                xbb = xb[i % 2]
                src = x_d[b, i * 256:(i + 1) * 256, :].rearrange("(j p) d -> p j d", p=128)
                E("pool", lambda e, xbb=xbb, src=src: e.dma_start(out=xbb.ap, in_=src), writes=[xbb], stream=xbb.name)
                g = i // 2
                for j in range(2):
                    tt = (i % 2) * 2 + j
                    dst = xT[g].ap[:, :, tt * 128:(tt + 1) * 128]
                    transposes_to(dst, xT[g], lambda k, xbb=xbb, j=j: xbb.ap[:, j, k * 128:(k + 1) * 128], xbb, 8, "act")

            ra = ring_next()
            wload(ra, 8, 512, w_in[:, :, 1536:2048])
            rg = ring_next()
            wload(rg, 8, 512, w_in[:, :, 2048:2560])
            sa = slab(ra, 8, 512)
            sg_ = slab(rg, 8, 512)
            gi = 0
            for g in range(4):
                for c in range(4):
                    pa_ = pa_next(True)
                    pg_ = pa_next(True)
                    for kc in range(8):
                        mm(pa_.ap, pa_, sa[:, kc, c * 128:(c + 1) * 128], xT[g].ap[:, kc, :], [ra, xT[g]], kc == 0, kc == 7)
                    for kc in range(8):
                        mm(pg_.ap, pg_, sg_[:, kc, c * 128:(c + 1) * 128], xT[g].ap[:, kc, :], [rg, xT[g]], kc == 0, kc == 7)
                    st = sgt[gi % 2]
                    gi += 1
                    E("act", lambda e, st=st, pg_=pg_: e.activation(out=st.ap, in_=pg_.ap, func=AF.Sigmoid),
                      reads=[pg_], writes=[st])
                    E("dve", lambda e, st=st, pa_=pa_, c=c, g=g: e.tensor_tensor(
                        out=uT_t[:, c, 30 + g * 512:30 + (g + 1) * 512], in0=pa_.ap, in1=st.ap, op=ALU.mult),
                      reads=[pa_, st], writes=[uT[c]])

            def conv_diag(c):
                for j in range(CW):
                    E("dve", lambda e, j=j: e.tensor_scalar(out=diag.ap[:, j, :], in0=ident.ap, scalar1=cv[:, 28 + c * CW + j:29 + c * CW + j],
                                                            scalar2=None, op0=ALU.mult), reads=[ident, cvec], writes=[diag])

            def conv_chunk(c):
                for g in range(4):
                    pc_ = pa_next()
                    for j in range(CW):
                        mm(pc_.ap, pc_, diag.ap[:, j, :], uT_t[:, c, g * 512 + j:g * 512 + j + 512], [diag, uT[c]], j == 0, j == CW - 1)
                    E("act", lambda e, pc_=pc_, g=g: e.activation(out=cT_t[:, c, g * 512:(g + 1) * 512], in_=pc_.ap, func=AF.Identity,
                                                                 bias=cv[:, 16 + c:17 + c], scale=1.0), reads=[pc_, cvec], writes=[cT[c]])

            def conv_ln(g):
                gr = slice(g * 512, (g + 1) * 512)
                for c in range(4):
                    E("act", lambda e, c=c: e.activation(out=ysq.ap[:, c, :], in_=cT_t[:, c, gr], func=AF.Square),
                      reads=[cT[c]], writes=[ysq])
                s1 = pa_next()
                s2 = pa_next()
                for c in range(4):
                    mm(s1.ap, s1, ones_bf.ap, cT_t[:, c, gr], [ones_bf, cT[c]], c == 0, c == 3)
                for c in range(4):
                    mm(s2.ap, s2, ones_bf.ap, ysq.ap[:, c, :], [ones_bf, ysq], c == 0, c == 3)
                E("dve", lambda e: e.tensor_scalar(out=mean_t.ap, in0=s1.ap, scalar1=1.0 / 512, scalar2=None, op0=ALU.mult),
                  reads=[s1], writes=[mean_t])
                E("dve", lambda e: e.tensor_tensor(out=msq_t.ap, in0=mean_t.ap, in1=mean_t.ap, op=ALU.mult),
                  reads=[mean_t], writes=[msq_t])
                E("dve", lambda e: e.scalar_tensor_tensor(out=msq_t.ap, in0=s2.ap, scalar=1.0 / 512, in1=msq_t.ap,
                                                          op0=ALU.mult, op1=ALU.subtract), reads=[s2, msq_t], writes=[msq_t])
                E("act", lambda e: e.activation(out=rstd_t.ap, in_=msq_t.ap, func=AF.Sqrt, bias=cv[:, 160:161], scale=1.0),
                  reads=[msq_t, cvec], writes=[rstd_t])
                E("dve", lambda e: e.reciprocal(out=rstd_t.ap, in_=rstd_t.ap), reads=[rstd_t], writes=[rstd_t])
                for c in range(4):
                    t1 = t1_t[c % 2]
                    E("dve", lambda e, c=c, t1=t1: e.tensor_tensor(out=t1.ap, in0=cT_t[:, c, gr], in1=mean_t.ap, op=ALU.subtract),
                      reads=[cT[c], mean_t], writes=[t1])
                    E("dve", lambda e, t1=t1: e.tensor_tensor(out=t1.ap, in0=t1.ap, in1=rstd_t.ap, op=ALU.mult),
                      reads=[t1, rstd_t], writes=[t1])
                    E("act", lambda e, c=c, t1=t1: e.activation(out=cT_t[:, c, gr], in_=t1.ap, func=AF.Silu,
                                                               bias=cv[:, 24 + c:25 + c], scale=cv[:, 20 + c:21 + c]),
                      reads=[t1, cvec], writes=[cT[c]])

            for hp in range(4):
                rq = ring_next()
                wload(rq, 8, 384, w_in[:, :, hp * 128:(hp + 1) * 128], n0=0)
                wload(rq, 8, 384, w_in[:, :, 512 + hp * 128:512 + (hp + 1) * 128], n0=128)
                wload(rq, 8, 384, w_in[:, :, 1024 + hp * 128:1024 + (hp + 1) * 128], n0=256)
                sq = slab(rq, 8, 384)
                for g in range(4):
                    gr = slice(g * 512, (g + 1) * 512)
                    pq = pa_next()
                    for kc in range(8):
                        mm(pq.ap, pq, sq[:, kc, 0:128], xT[g].ap[:, kc, :], [rq, xT[g]], kc == 0, kc == 7)
                    E("dve", lambda e, pq=pq, gr=gr: e.tensor_copy(out=qT.ap[0:64, 0, gr], in_=pq.ap[0:64, :]),
                      reads=[pq], writes=[qT])
                    E("dve", lambda e, pq=pq, gr=gr: e.tensor_copy(out=qT.ap[64:128, 1, gr], in_=pq.ap[64:128, :]),
                      reads=[pq], writes=[qT])
                    pk = pa_next()
                    for kc in range(8):
                        mm(pk.ap, pk, sq[:, kc, 128:256], xT[g].ap[:, kc, :], [rq, xT[g]], kc == 0, kc == 7)
                    E("dve", lambda e, pk=pk, gr=gr: e.tensor_copy(out=kT.ap[0:64, 0, gr], in_=pk.ap[0:64, :]),
                      reads=[pk], writes=[kT])
                    E("dve", lambda e, pk=pk, gr=gr: e.tensor_copy(out=kT.ap[64:128, 1, gr], in_=pk.ap[64:128, :]),
                      reads=[pk], writes=[kT])
                    E("dve", lambda e, pk=pk, g=g: e.tensor_reduce(out=ksum.ap[:, 2 * g:2 * g + 2],
                                                                  in_=pk.ap.rearrange("p (a b) -> p a b", a=2),
                                                                  axis=AX.X, op=ALU.add), reads=[pk], writes=[ksum])
                    pv = pa_next()
                    for j in range(4):
                        for kc in range(8):
                            mm(pv.ap[:, j * 128:(j + 1) * 128], pv, xT[g].ap[:, kc, j * 128:(j + 1) * 128], sq[:, kc, 256:384],
                               [rq, xT[g]], kc == 0, kc == 7)
                    E("act", lambda e, pv=pv, g=g: e.activation(
                        out=vA.ap[:, 4 * g:4 * g + 4, :, 0:64],
                        in_=pv.ap.rearrange("p (j h d) -> p j h d", j=4, h=2), func=AF.Copy), reads=[pv], writes=[vA])
                E("dve", lambda e: e.tensor_scalar(out=kmean.ap, in0=ksum.ap, scalar1=1.0 / 256, scalar2=None, op0=ALU.mult),
                  reads=[ksum], writes=[kmean])
                if b == 0:
                    for i in range(NSL):
                        if i * 4 // NSL == hp:
                            convert_slab(i)
                conv_diag(hp)
                pg_ = pa_next()
                for qi in range(8):
                    qs = slice((8 + qi) * 128, (9 + qi) * 128)
                    mm(pg_.ap[:, qi * 16:qi * 16 + 8], pg_, qT.ap[0:64, 0, qs], kmean.ap[0:64, :], [qT, kmean], True, True)
                    mm(pg_.ap[:, qi * 16 + 8:qi * 16 + 16], pg_, qT.ap[64:128, 1, qs], kmean.ap[64:128, :], [qT, kmean], True, True)
                pgv = pg_.ap[:, 0:128].rearrange("p (a n) -> p a n", a=16)
                E("dve", lambda e, pgv=pgv: e.tensor_tensor(out=gs.ap, in0=pgv, in1=pen.ap, op=ALU.add), reads=[pg_, pen], writes=[gs])
                conv_chunk(hp)
                cur = gs
                for it in range(3):
                    E("dve", lambda e, cur=cur: e.tensor_reduce(out=mx.ap, in_=cur.ap, axis=AX.X, op=ALU.max), reads=[cur], writes=[mx])
                    if it == 2:
                        break
                    E("dve", lambda e, cur=cur: e.tensor_tensor(out=eq.ap, in0=cur.ap, in1=mx.ap.unsqueeze(2).to_broadcast([128, 16, 8]),
                                                               op=ALU.is_equal), reads=[cur, mx], writes=[eq])
                    E("dve", lambda e, cur=cur: e.scalar_tensor_tensor(out=g2.ap, in0=eq.ap, scalar=-1e30, in1=cur.ap,
                                                                      op0=ALU.mult, op1=ALU.add), reads=[eq, cur], writes=[g2])
                    cur = g2
                E("dve", lambda e: e.tensor_tensor(out=eq.ap, in0=gs.ap, in1=mx.ap.unsqueeze(2).to_broadcast([128, 16, 8]), op=ALU.is_ge),
                  reads=[gs, mx], writes=[eq])
                E("dve", lambda e: e.tensor_scalar(out=g2.ap, in0=eq.ap, scalar1=-1.0, scalar2=BIG, op0=ALU.add, op1=ALU.mult),
                  reads=[eq], writes=[g2])
                g2v = g2.ap.rearrange("p (q h) n -> p q h n", h=2)
                valv = val.ap.rearrange("p (q h) n -> p q h n", h=2)
                E("dve", lambda e: e.tensor_tensor(out=mpad.ap[:, :, 0, 64:72], in0=g2v[:, :, 0, :], in1=valv[:, :, 0, :], op=ALU.mult),
                  reads=[g2, val], writes=[mpad])
                E("dve", lambda e: e.tensor_tensor(out=mpad.ap[:, :, 1, 0:8], in0=g2v[:, :, 1, :], in1=valv[:, :, 1, :], op=ALU.mult),
                  reads=[g2, val], writes=[mpad])
                for qi in range(8):
                    mm(PBt[0][0:72, qi * 128:(qi + 1) * 128], PBh[qi // 4], mpad.ap[:, qi, 0, 0:72], ident.ap, [mpad, ident], True, True)
                for qi in range(8):
                    mm(PBt[1][0:8, qi * 128:(qi + 1) * 128], PBh[2 + qi // 4], mpad.ap[:, qi, 1, 0:8], ident.ap, [mpad, ident], True, True)
                E("act", lambda e: e.activation(out=qT.ap[64:72, 0, 1024:2048], in_=PBt[0][64:72, :], func=AF.Copy),
                  reads=[PBh[0], PBh[1]], writes=[qT])
                E("act", lambda e: e.activation(out=qT.ap[0:8, 1, 1024:2048], in_=PBt[1][0:8, :], func=AF.Copy),
                  reads=[PBh[2], PBh[3]], writes=[qT])
                cth = []
                it_n = 0
                its = [(G, h, j) for G in range(4) for h in range(2) for j in range(4 * G + 4)]
                LA = 2
                pts = {}

                def emit_scores(k):
                    G, h, j = its[k]
                    rows = slice(0, 72) if h == 0 else slice(0, 128)
                    d = j - 4 * G
                    c0 = max(0, d) * 128
                    N = 512 - c0
                    ps_ = pa_next()
                    hh = 2 * hp + h
                    mm(ps_.ap[:, 0:N], ps_, kT.ap[rows, h, j * 128:(j + 1) * 128],
                       qT.ap[rows, h, G * 512 + c0:(G + 1) * 512], [kT, qT], True, True)
                    pt = PT[k % 4]
                    pts[k] = pt
                    E("act", lambda e: e.activation(out=pt.ap[:, 0:N], in_=ps_.ap[:, 0:N], func=AF.Exp, scale=DH ** -0.5),
                      reads=[ps_], writes=[pt])
                    if d >= 0:
                        w_ = min(256, N)
                        E("dve", lambda e: e.tensor_tensor(out=pt.ap[:, 0:w_], in0=pt.ap[:, 0:w_], in1=Ebf.ap[:, hh, 0:w_], op=ALU.mult),
                          reads=[pt, Ebf], writes=[pt])
                    elif d == -1:
                        E("dve", lambda e: e.tensor_tensor(out=pt.ap[:, 0:128], in0=pt.ap[:, 0:128], in1=Ebf.ap[:, hh, 128:256], op=ALU.mult),
                          reads=[pt, Ebf], writes=[pt])

                def emit_pv(k):
                    G, h, j = its[k]
                    d = j - 4 * G
                    c0 = max(0, d) * 128
                    pt = pts.pop(k)
                    atm = att_tm[G % 2]
                    for s_ in range(max(0, d), 4):
                        lc = s_ * 128 - c0
                        mm(PBh[s_].ap[:, 0:65], PBh[s_], pt.ap[:, lc:lc + 128], vA.ap[:, j, h, :], [pt, vA], j == 0, j == 4 * G + s_)
                    if j == 4 * G + 3:
                        for s_ in range(4):
                            O = PBh[s_]
                            E("dve", lambda e, O=O, s_=s_: e.reciprocal(out=rinv.ap[:, s_:s_ + 1], in_=O.ap[:, 64:65]), reads=[O], writes=[rinv])
                            E("dve", lambda e, O=O, s_=s_: e.tensor_scalar(
                                out=atm.ap[:, s_, h * 64:(h + 1) * 64], in0=O.ap[:, 0:64], scalar1=rinv.ap[:, s_:s_ + 1], scalar2=None,
                                op0=ALU.mult), reads=[O, rinv], writes=[atm])
                        if h == 1:
                            transposes_to(attT_t[:, hp, G * 512:(G + 1) * 512], attT[G], lambda s_: atm.ap[:, s_, :], atm, 4, "dve")

                if hp == 3:
                    P.barrier(CLN_BUFS)
                for k in range(len(its) + LA):
                    if k < len(its):
                        emit_scores(k)
                    if k - LA >= 0:
                        emit_pv(k - LA)
                    if hp == 3 and k in (16, 32, 48, 64):
                        conv_ln(k // 16 - 1)

            if DBG and b == 0:
                fin.append(E("pool", lambda e: e.dma_start(out=dbg_att, in_=attT_t[:]), reads=attT, stream="dbg1"))
                fin.append(E("pool", lambda e: e.dma_start(out=dbg_c, in_=cT_t[:]), reads=cT, stream="dbg2"))
            P.barrier(C_BUFS)

            def load_x(gg):
                srcx = x_d[b, gg * 512:(gg + 1) * 512, :].rearrange("(j p) d -> p j d", p=128)
                E("pool", lambda e: e.dma_start(out=xbg.ap, in_=srcx), writes=[xbg], stream="xbg")

            def load_p(gg):
                srcp = p_d[b, gg * 512:(gg + 1) * 512, :].rearrange("(j p) d -> p j d", p=128)
                E("pool", lambda e: e.dma_start(out=pbg.ap, in_=srcp), writes=[pbg], stream="pbg")

            def xT_tile(j):
                transposes_to(xTg.ap[:, :, j * 128:(j + 1) * 128], xTg, lambda k: xbg.ap[:, j, k * 128:(k + 1) * 128],
                              xbg, 8, "act" if j % 2 == 0 else "dve")

            def pT_all():
                for j in range(4):
                    for k in range(2):
                        E("pe", lambda e, j=j, k=k: e.transpose(out=PM_t[:, (k * 4 + j) * 128:(k * 4 + j + 1) * 128],
                                                                in_=pbg.ap[:, j, k * 128:(k + 1) * 128], identity=ident.ap),
                          reads=[pbg, ident], writes=[PM])
                E("dve", lambda e: e.tensor_copy(out=pT.ap, in_=PM_t[:, 0:1024].rearrange("p (k t) -> p k t", k=2)),
                  reads=[PM], writes=[pT])

            def c1(g, hook=None):
                gr = slice(g * 512, (g + 1) * 512)
                rao = wl(0)
                sao = slab(rao, 4, 1024)
                it = 0
                for part in range(2):
                    if part == 1:
                        rco = wl(3)
                        sco = slab(rco, 4, 1024)
                    for mh in range(2):
                        rgl = wl([1, 2, 4, 5][part * 2 + mh])
                        sgl = slab(rgl, 8, 512)
                        for mm_ in range(4):
                            m = mh * 4 + mm_
                            py = pa_next(True)
                            pl = pa_next(True)
                            for kc in range(4):
                                if part == 0:
                                    mm(py.ap, py, sao[:, kc, m * 128:(m + 1) * 128], attT[g].ap[:, kc, :], [rao, attT[g]], kc == 0, kc == 3)
                                else:
                                    mm(py.ap, py, sco[:, kc, m * 128:(m + 1) * 128], cT_t[:, kc, gr], [rco, cT[kc]], kc == 0, kc == 3)
                            for kc in range(8):
                                mm(pl.ap, pl, sgl[:, kc, mm_ * 128:(mm_ + 1) * 128], xTg.ap[:, kc, :], [rgl, xTg], kc == 0, kc == 7)
                            sgb = csg[it % 2]
                            bcol = part * 8 + m
                            E("act", lambda e, sgb=sgb, pl=pl, bcol=bcol: e.activation(
                                out=sgb.ap, in_=pl.ap, func=AF.Sigmoid, bias=cv[:, bcol:bcol + 1], scale=1.0),
                              reads=[pl, cvec], writes=[sgb])
                            if part == 0:
                                E("dve", lambda e, py=py, sgb=sgb, m=m: e.tensor_tensor(
                                    out=hTf[:, m, :], in0=py.ap, in1=sgb.ap, op=ALU.mult), reads=[py, sgb], writes=[hT])
                            else:
                                pc = cpc[m % 2]
                                E("dve", lambda e, py=py, sgb=sgb, pc=pc: e.tensor_tensor(out=pc.ap, in0=py.ap, in1=sgb.ap, op=ALU.mult),
                                  reads=[py, sgb], writes=[pc])
                                E("dve", lambda e, pc=pc, m=m: e.tensor_tensor(out=mT.ap[:, m, :], in0=pc.ap, in1=hTf[:, m, :], op=ALU.add),
                                  reads=[pc, hT], writes=[mT])
                            if hook is not None:
                                hook(it)
                            it += 1

            def mixed(g, with_xT):
                rm = [wl(6), wl(7)]
                for t in range(4):
                    xr = xres[t % 2]
                    srcr = x_d[b, g * 512 + t * 128:g * 512 + (t + 1) * 128, :]
                    E("sp", lambda e, xr=xr, srcr=srcr: e.dma_start(out=xr.ap, in_=srcr), writes=[xr], stream=xr.name)
                    pbi = t % 2
                    for hh in range(2):
                        ob = PBh[pbi * 2 + hh]
                        sm = slab(rm[hh], 8, 512)
                        for kc in range(8):
                            mm(ob.ap, ob, mT.ap[:, kc, t * 128:(t + 1) * 128], sm[:, kc, :], [mT, rm[hh]], kc == 0, kc == 7)
                        E("dve", lambda e, xr=xr, ob=ob, t=t, hh=hh: e.scalar_tensor_tensor(
                            out=z[t].ap[:, hh * 512:(hh + 1) * 512], in0=xr.ap[:, hh * 512:(hh + 1) * 512], scalar=ALPHA,
                            in1=ob.ap, op0=ALU.mult, op1=ALU.add), reads=[xr, ob], writes=[z[t]])
                    if with_xT:
                        xT_tile(t)

            def x1T_tile(t):
                ln_norm_tile(z, 0, t)
                xb1 = x1b[t % 2]
                E("act", lambda e: e.activation(out=xb1.ap, in_=z[t].ap, func=AF.Copy), reads=[z[t]], writes=[xb1])
                transposes_to(x1T.ap[:, :, t * 128:(t + 1) * 128], x1T, lambda k: xb1.ap[:, k * 128:(k + 1) * 128], xb1, 8, "dve")

            def ffn(g):
                rfg = None
                rfu = None
                for j in range(NJ):
                    if j % 4 == 0:
                        rfg = wl(8 + 2 * (j // 4))
                        rfu = wl(9 + 2 * (j // 4))
                    sfg = slab(rfg, 8, 512)
                    sfu = slab(rfu, 8, 512)
                    jj = j % 4
                    pg_ = pa_next(True)
                    pu_ = pa_next(True)
                    for kc in range(8):
                        mm(pg_.ap, pg_, sfg[:, kc, jj * 128:(jj + 1) * 128], x1T.ap[:, kc, :], [rfg, x1T], kc == 0, kc == 7)
                    for kc in range(8):
                        mm(pu_.ap, pu_, sfu[:, kc, jj * 128:(jj + 1) * 128], x1T.ap[:, kc, :], [rfu, x1T], kc == 0, kc == 7)
                    sgb = csg[j % 2]
                    E("act", lambda e, sgb=sgb, pg_=pg_: e.activation(out=sgb.ap, in_=pg_.ap, func=AF.Silu), reads=[pg_], writes=[sgb])
                    E("dve", lambda e, sgb=sgb, pu_=pu_, j=j: e.tensor_tensor(out=hT.ap[:, j, :], in0=sgb.ap, in1=pu_.ap, op=ALU.mult),
                      reads=[sgb, pu_], writes=[hT])
                    if j == 8:
                        pT_all()
                        if g < 3:
                            load_p(g + 1)

            def down_ple_out(g):
                for hh in range(2):
                    for sl in range(3):
                        k0 = sl * 8
                        kn = min(8, NJ - k0)
                        rd_ = wl(20 + hh * 3 + sl)
                        sd = slab(rd_, 8, 512)
                        for t in range(4):
                            ob = PBh[t]
                            for kk in range(kn):
                                j = k0 + kk
                                mm(ob.ap, ob, hT.ap[:, j, t * 128:(t + 1) * 128], sd[:, kk, :], [hT, rd_], j == 0, j == NJ - 1)
                    for t in range(4):
                        ob = PBh[t]
                        E("dve", lambda e, ob=ob, t=t, hh=hh: e.scalar_tensor_tensor(
                            out=z[t].ap[:, hh * 512:(hh + 1) * 512], in0=z[t].ap[:, hh * 512:(hh + 1) * 512], scalar=ALPHA,
                            in1=ob.ap, op0=ALU.mult, op1=ALU.add), reads=[ob, z[t]], writes=[z[t]])
                rpl = wl(26)
                spl = slab(rpl, 2, 1024)
                for hh in range(2):
                    rpg = wl(27 + hh)
                    spg = slab(rpg, 8, 512)
                    for t in range(4):
                        ppl = pa_next(True)
                        ppg = pa_next(True)
                        for kc in range(2):
                            mm(ppl.ap, ppl, pT.ap[:, kc, t * 128:(t + 1) * 128], spl[:, kc, hh * 512:(hh + 1) * 512], [pT, rpl], kc == 0, kc == 1)
                        for kc in range(8):
                            mm(ppg.ap, ppg, x1T.ap[:, kc, t * 128:(t + 1) * 128], spg[:, kc, :], [x1T, rpg], kc == 0, False)
                        mm(ppg.ap, ppg, ones_bf.ap[0:1, :], bple.ap[0:1, hh * 512:(hh + 1) * 512], [ones_bf, bple], False, True)
                        sgb = csg[t % 2]
                        E("act", lambda e, sgb=sgb, ppg=ppg: e.activation(out=sgb.ap, in_=ppg.ap, func=AF.Sigmoid), reads=[ppg], writes=[sgb])
                        pc = cpc[t % 2]
                        E("dve", lambda e, pc=pc, sgb=sgb, ppl=ppl: e.tensor_tensor(out=pc.ap, in0=sgb.ap, in1=ppl.ap, op=ALU.mult),
                          reads=[sgb, ppl], writes=[pc])
                        E("dve", lambda e, pc=pc, t=t, hh=hh: e.tensor_tensor(
                            out=z[t].ap[:, hh * 512:(hh + 1) * 512], in0=z[t].ap[:, hh * 512:(hh + 1) * 512], in1=pc.ap, op=ALU.add),
                          reads=[pc, z[t]], writes=[z[t]])
                ln_stats(z)
                for t in range(4):
                    ln_norm_tile(z, 2, t)
                for t in range(4):
                    dsto = out_d[b, g * 512 + t * 128:g * 512 + (t + 1) * 128, :]
                    tok = E("pool", lambda e, t=t, dsto=dsto: e.dma_start(out=dsto, in_=z[t].ap), reads=[z[t]], stream=f"out{t}")
                    fin.append(tok)

            load_x(0)
            load_p(0)
            for j in range(4):
                xT_tile(j)
            load_x(1)
            c1(0)
            for g in range(4):
                mixed(g, g < 3)
                if g + 2 <= 3:
                    load_x(g + 2)
                ln_stats(z)
                if g < 3:
                    c1(g + 1, hook=lambda it: x1T_tile(it // 4) if it % 4 == 3 else None)
                else:
                    for t in range(4):
                        x1T_tile(t)
                ffn(g)
                down_ple_out(g)
        P.emit(nc, final_wait=fin[-8:] + fin[:2])
    return nc


def _t5_bucket_np(rel):
    n = np.maximum(rel, 0)
    max_exact = 16
    nf = np.maximum(n, 1).astype(np.float32)
    large = max_exact + (np.log(nf / np.float32(max_exact)) / np.float32(np.log(128 / max_exact))
                         * np.float32(32 - max_exact)).astype(np.int32)
    large = np.minimum(large, 31)
    return np.where(n < max_exact, n, large)


_NC_CACHE = {}
_LAST = []


def kernel(x, p, w_in, b_gate, bias_table, w_att_out, conv_w, conv_b, conv_ln_g, conv_ln_b,
           w_conv_out, w_mix_out, ln_mix_g, ln_mix_b, w_ffn_gate, w_ffn_up, w_ffn_down,
           w_ple, w_ple_gate, b_ple_gate, ln_ffn_g, ln_ffn_b):
    f = lambda a: np.ascontiguousarray(np.asarray(a, dtype=np.float32))
    x = f(x)
    p = f(p)
    B = x.shape[0]
    NB = B // NCORES
    cvec = np.zeros((128, NCV), np.float32)
    cvec[:, 0:16] = f(b_gate)[0].reshape(16, 128).T
    cvec[:, 16:20] = f(conv_b)[0].reshape(4, 128).T
    cvec[:, 20:24] = f(conv_ln_g)[0].reshape(4, 128).T
    cvec[:, 24:28] = f(conv_ln_b)[0].reshape(4, 128).T
    cw = f(conv_w)[0]
    cvec[:, 28:152] = cw.T.reshape(4, 128, CW).transpose(1, 0, 2).reshape(128, 4 * CW)
    bt = f(bias_table)
    cvec[:, 152:160] = bt[31][None, :]
    cvec[:, 160] = EPS
    lnv = np.stack([np.broadcast_to(f(v)[0][None, :], (128, D)) for v in (ln_mix_g, ln_mix_b, ln_ffn_g, ln_ffn_b)], axis=1)
    lnv = np.ascontiguousarray(lnv)
    kk = np.arange(128)[:, None]
    cc = np.arange(256)[None, :]
    bidx = _t5_bucket_np(cc - kk)
    toep = np.ascontiguousarray(bt[bidx].transpose(0, 2, 1))
    onehot = (np.arange(S)[None, :] // 256 == np.arange(8)[:, None]).astype(np.float32)
    shared = {
        "w_in": f(w_in)[0], "w_att_out": f(w_att_out)[0], "w_conv_out": f(w_conv_out)[0], "w_mix_out": f(w_mix_out)[0],
        "w_ffn_gate": f(w_ffn_gate)[0], "w_ffn_up": f(w_ffn_up)[0], "w_ffn_down": f(w_ffn_down)[0],
        "w_ple": f(w_ple)[0], "w_ple_gate": f(w_ple_gate)[0], "cvec": cvec, "lnv": lnv,
        "bple": f(b_ple_gate).reshape(1, D), "toep": toep, "onehot": onehot,
    }
    if NB not in _NC_CACHE:
        _NC_CACHE[NB] = build_nc(NB)
    nc = _NC_CACHE[NB]
    in_maps = []
    for c in range(NCORES):
        m = dict(shared)
        m["x"] = x[c * NB:(c + 1) * NB]
        m["p"] = p[0, c * NB:(c + 1) * NB]
        in_maps.append(m)
    res = run_bass_kernel_spmd(nc, in_maps, core_ids=list(range(NCORES)))
    if DBG:
        _LAST.append(res.results[0])
    return np.concatenate([r["out"] for r in res.results], axis=0).astype(np.float32)
```

```python
import contextlib
import numpy as np
import concourse.bass as bass
import concourse.mybir as mybir
from concourse.bass_utils import run_bass_kernel_spmd

F32 = mybir.dt.float32
BF = mybir.dt.bfloat16
AF = mybir.ActivationFunctionType
ALU = mybir.AluOpType
AX = mybir.AxisListType

NCORES = 8
D = 1024
S = 2048
NH = 8
DH = 64
NIN = 4608
FFN = 2816
NJ = FFN // 128
PLE = 256
CW = 31
ALPHA = 2.0 ** 0.25
EPS = 1e-5
BIG = 30000.0
NCV = 161
DBG = False
PE_BIAS = False

ENGS = ("pe", "act", "dve", "pool", "sp")


class Tok:
    __slots__ = ("eng", "idx", "sem", "val", "is_dma", "stream", "key")

    def __init__(self, eng, idx):
        self.eng = eng
        self.idx = idx
        self.sem = None
        self.val = None
        self.is_dma = False
        self.stream = None
        self.key = eng


class Buf:
    def __init__(self, ap, name):
        self.ap = ap
        self.name = name
        self.w = {}
        self.r = {}


class Prog:
    SEM_ROT = 4000

    def __init__(self):
        self.ops = {e: [] for e in ENGS}
        self.streams = {}
        self.need = set()
        self.last = {}

    def op(self, eng, fn, deps=(), stream=None):
        t = Tok(eng, len(self.ops[eng]))
        deps = [d for d in deps if d is not None]
        if stream is not None:
            t.is_dma = True
            t.stream = stream
            t.key = "s:" + stream
            n = self.streams.get(stream, 0) + 1
            self.streams[stream] = n
            t.val = 16 * n
        self.ops[eng].append((t, fn, deps))
        for d in deps:
            if not d.is_dma:
                self.need.add((d.eng, d.idx))
        self.last[t.key] = t
        return t

    def E(self, eng, fn, reads=(), writes=(), stream=None, extra=()):
        deps = {}

        def add(t):
            k = t.key
            o = deps.get(k)
            if o is None or t.idx > o.idx or (t.is_dma and t.val > o.val):
                deps[k] = t

        for t in extra:
            if t is not None:
                add(t)
        for b in reads:
            for t in b.w.values():
                add(t)
        for b in writes:
            for t in b.w.values():
                add(t)
            for t in b.r.values():
                add(t)
        if eng == "pe":
            deps.pop("pe", None)
        tok = self.op(eng, fn, list(deps.values()), stream=stream)
        for b in reads:
            b.r[tok.key] = tok
        for b in writes:
            b.w[tok.key] = tok
        return tok

    def barrier(self, bufs):
        snap = dict(self.last)
        for b in bufs:
            for k, t in snap.items():
                b.r[k] = t
            b.w = {}

    def emit(self, nc, final_wait=()):
        with contextlib.ExitStack() as es:
            nsem = 0
            for e in ENGS:
                cnt = 0
                cur = None
                for t, fn, deps in self.ops[e]:
                    if t.is_dma:
                        continue
                    if (e, t.idx) in self.need:
                        if cur is None or cnt >= self.SEM_ROT:
                            cur = es.enter_context(nc.semaphore(f"c_{e}_{nsem}"))
                            nsem += 1
                            cnt = 0
                        cnt += 1
                        t.sem = cur
                        t.val = cnt
            dsem = {}
            for s in self.streams:
                dsem[s] = es.enter_context(nc.semaphore(f"d_{s}"))
            for e in ENGS:
                for t, fn, deps in self.ops[e]:
                    if t.is_dma:
                        t.sem = dsem[t.stream]
            block = es.enter_context(nc.Block())

            def make(e):
                def body(eng):
                    waited = {}
                    for t, fn, deps in self.ops[e]:
                        for d in deps:
                            k = id(d.sem)
                            if waited.get(k, 0) >= d.val:
                                continue
                            eng.wait_ge(d.sem, d.val)
                            waited[k] = d.val
                        ins = fn(eng)
                        if t.is_dma:
                            ins.then_inc(t.sem, 16)
                        elif t.sem is not None:
                            ins.then_inc(t.sem, 1)
                    if e == "sp":
                        for d in final_wait:
                            eng.wait_ge(d.sem, d.val)

                return body

            block.tensor(make("pe"))
            block.scalar(make("act"))
            block.vector(make("dve"))
            block.gpsimd(make("pool"))
            block.sync(make("sp"))


def build_nc(NB):
    nc = bass.Bass("TRN2", target_bir_lowering=False)
    dr = {}

    def din(name, shape):
        dr[name] = nc.dram_tensor(name, list(shape), F32, kind="ExternalInput").ap()
        return dr[name]

    x_d = din("x", [NB, S, D])
    p_d = din("p", [NB, S, PLE])
    w_in = din("w_in", [D, NIN]).rearrange("(kc p) n -> p kc n", p=128)
    w_ao = din("w_att_out", [512, D]).rearrange("(kc p) n -> p kc n", p=128)
    w_co = din("w_conv_out", [512, D]).rearrange("(kc p) n -> p kc n", p=128)
    w_mix = din("w_mix_out", [D, D]).rearrange("(kc p) n -> p kc n", p=128)
    w_fg = din("w_ffn_gate", [D, FFN]).rearrange("(kc p) n -> p kc n", p=128)
    w_fu = din("w_ffn_up", [D, FFN]).rearrange("(kc p) n -> p kc n", p=128)
    w_fd = din("w_ffn_down", [FFN, D]).rearrange("(kc p) n -> p kc n", p=128)
    w_pl = din("w_ple", [PLE, D]).rearrange("(kc p) n -> p kc n", p=128)
    w_pg = din("w_ple_gate", [D, D]).rearrange("(kc p) n -> p kc n", p=128)
    cvec_d = din("cvec", [128, NCV])
    lnv_d = din("lnv", [128, 4, D])
    bple_d = din("bple", [1, D])
    toep_d = din("toep", [128, NH, 256])
    oneh_d = din("onehot", [8, S])
    out_d = nc.dram_tensor("out", [NB, S, D], F32, kind="ExternalOutput").ap()

    P = Prog()
    E = P.E
    fin = []
    if DBG:
        dbg_att = nc.dram_tensor("dbg_att", [128, 4, S], F32, kind="ExternalOutput").ap()
        dbg_c = nc.dram_tensor("dbg_c", [128, 4, S], F32, kind="ExternalOutput").ap()

    with contextlib.ExitStack() as es:
        def sb(name, shape, dt):
            return es.enter_context(nc.sbuf_tensor("sb_" + name, list(shape), dt))

        def psum(name, shape, dt):
            return es.enter_context(nc.psum_tensor("ps_" + name, list(shape), dt))

        ident = Buf(sb("ident", [128, 128], BF)[:], "ident")
        ones_bf = Buf(sb("ones_bf", [128, 128], BF)[:], "ones")
        Ebf = Buf(sb("Ebf", [128, NH, 256], BF)[:], "Ebf")
        cvec = Buf(sb("cvec", [128, NCV], F32)[:], "cvec")
        negc = Buf(sb("negc", [128, NH], F32)[:], "negc")
        lnv = Buf(sb("lnv", [128, 4, D], F32)[:], "lnv")
        bple = Buf(sb("bple", [1, D], BF)[:], "bple")
        attT_t = sb("attT", [128, 4, S], BF)
        cT_t = sb("cT", [128, 4, S], BF)
        attT = [Buf(attT_t[:, :, g * 512:(g + 1) * 512], f"attT{g}") for g in range(4)]
        cT = [Buf(cT_t[:, c, :], f"cT{c}") for c in range(4)]
        NRING = 6
        ring = [Buf(sb(f"ring{i}", [128, 4096], BF)[:], f"ring{i}") for i in range(NRING)]
        ring_i = [0]

        def ring_next():
            b = ring[ring_i[0] % NRING]
            ring_i[0] += 1
            return b

        def slab(b, kc, n):
            return b.ap[:, 0:kc * n].rearrange("p (k n) -> p k n", k=kc)

        def wload(b, kc, n, src, k0=0, kn=None, n0=0):
            kn = kc - k0 if kn is None else kn
            nn = src.shape[2]
            dst = slab(b, kc, n)[:, k0:k0 + kn, n0:n0 + nn]
            return E("pool", lambda e: e.dma_start(out=dst, in_=src), writes=[b], stream=b.name)

        SD = []
        SD.append((4, 1024, w_ao, 4))
        SD.append((8, 512, w_in[:, :, 2560:3072], 8))
        SD.append((8, 512, w_in[:, :, 3072:3584], 8))
        SD.append((4, 1024, w_co, 4))
        SD.append((8, 512, w_in[:, :, 3584:4096], 8))
        SD.append((8, 512, w_in[:, :, 4096:4608], 8))
        SD.append((8, 512, w_mix[:, :, 0:512], 8))
        SD.append((8, 512, w_mix[:, :, 512:1024], 8))
        for jq in range(6):
            ncol = min(512, FFN - jq * 512)
            SD.append((8, 512, w_fg[:, :, jq * 512:jq * 512 + ncol], 8))
            SD.append((8, 512, w_fu[:, :, jq * 512:jq * 512 + ncol], 8))
        for hh in range(2):
            for sl in range(3):
                k0 = sl * 8
                kn = min(8, NJ - k0)
                SD.append((8, 512, w_fd[:, k0:k0 + kn, hh * 512:(hh + 1) * 512], kn))
        SD.append((2, 1024, w_pl, 2))
        SD.append((8, 512, w_pg[:, :, 0:512], 8))
        SD.append((8, 512, w_pg[:, :, 512:1024], 8))
        NSL = len(SD)
        wscr = nc.dram_tensor("wscr", [NSL, 128, 4096], BF, kind="Internal").ap()
        scrb = [Buf(None, f"scr{i}") for i in range(NSL)]

        def convert_slab(i):
            kc, n, src, kn = SD[i]
            rb = ring_next()
            wload(rb, kc, n, src, kn=kn)
            E("sp", lambda e: e.dma_start(out=wscr[i], in_=rb.ap), reads=[rb], writes=[scrb[i]], stream=f"scr{i}")

        def wl(i):
            rb = ring_next()
            E("sp", lambda e: e.dma_start(out=rb.ap, in_=wscr[i]), reads=[scrb[i]], writes=[rb], stream=rb.name)
            return rb

        NU = 50240
        U = sb("U", [128, NU], BF)
        off = [0]

        def carve(nel, dt):
            nb = nel * (4 if dt == F32 else 2)
            nb = (nb + 63) // 64 * 64
            a = off[0] // 2
            off[0] += nb
            assert off[0] <= NU * 2, ("union overflow", off[0])
            v = U[:, a:a + nb // 2]
            if dt == F32:
                v = v.bitcast(F32)[:, 0:nel]
            else:
                v = v[:, 0:nel]
            return v

        off[0] = 0
        xT_t = carve(8 * S, BF).rearrange("p (k t) -> p k t", k=8)
        xT = [Buf(xT_t[:, :, g * 512:(g + 1) * 512], f"xT{g}") for g in range(4)]
        uT_t = carve(4 * (S + 30), BF).rearrange("p (c t) -> p c t", c=4)
        uT = [Buf(uT_t[:, c, :], f"uT{c}") for c in range(4)]
        qT = Buf(carve(2 * S, BF).rearrange("p (h t) -> p h t", h=2), "qT")
        kT = Buf(carve(2 * S, BF).rearrange("p (h t) -> p h t", h=2), "kT")
        vA = Buf(carve(16 * 2 * 65, BF).rearrange("p (j h d) -> p j h d", j=16, h=2), "vA")
        off_alias = off[0]
        xb = [Buf(carve(2 * D, BF).rearrange("p (j d) -> p j d", j=2), f"xb{i}") for i in range(2)]
        diag = Buf(carve(CW * 128, BF).rearrange("p (j m) -> p j m", j=CW), "diag")
        sgt = [Buf(carve(512, F32), f"sgt{i}") for i in range(2)]
        off_alias_end = off[0]
        PT = [Buf(carve(512, BF), f"PT{i}") for i in range(4)]
        att_tm = [Buf(carve(4 * 128, BF).rearrange("p (s d) -> p s d", s=4), f"atm{i}") for i in range(2)]
        ksum = Buf(carve(8, F32), "ksum")
        kmean = Buf(carve(8, BF), "kmean")
        gs = Buf(carve(128, F32).rearrange("p (a n) -> p a n", a=16), "gs")
        g2 = Buf(carve(128, F32).rearrange("p (a n) -> p a n", a=16), "g2")
        eq = Buf(carve(128, F32).rearrange("p (a n) -> p a n", a=16), "eq")
        mx = Buf(carve(16, F32), "mx")
        mpad = Buf(carve(8 * 2 * 72, BF).rearrange("p (q h n) -> p q h n", q=8, h=2), "mpad")
        rinv = Buf(carve(4, F32), "rinv")
        AB_END = off[0]
        off[0] = off_alias
        ysq = Buf(carve(4 * 512, BF).rearrange("p (c t) -> p c t", c=4), "ysq")
        mean_t = Buf(carve(512, F32), "mean")
        msq_t = Buf(carve(512, F32), "msq")
        rstd_t = Buf(carve(512, F32), "rstd")
        t1_t = [Buf(carve(512, F32), f"t1_{i}") for i in range(2)]
        assert off[0] <= off_alias_end, (off[0], off_alias_end)
        CLN_BUFS = [ysq, mean_t, msq_t, rstd_t] + t1_t
        AB_BUFS = (xT + uT + [qT, kT, vA] + xb + [diag] + sgt + [ysq, mean_t, msq_t, rstd_t] + t1_t + PT
                   + att_tm + [ksum, kmean, gs, g2, eq, mx, mpad, rinv])

        off[0] = 0
        xres = [Buf(carve(D, F32), f"xres{i}") for i in range(2)]
        z_t = carve(4 * D, F32).rearrange("p (t d) -> p t d", t=4)
        z = [Buf(z_t[:, t, :], f"z{t}") for t in range(4)]
        mT = Buf(carve(8 * 512, BF).rearrange("p (k t) -> p k t", k=8), "mT")
        csg = [Buf(carve(512, F32), f"csg{i}") for i in range(2)]
        cpc = [Buf(carve(512, F32), f"cpc{i}") for i in range(2)]
        x1b = [Buf(carve(D, BF), f"x1b{i}") for i in range(2)]
        x1T = Buf(carve(8 * 512, BF).rearrange("p (k t) -> p k t", k=8), "x1T")
        hT_raw = carve(NJ * 512, BF)
        hT = Buf(hT_raw.rearrange("p (j t) -> p j t", j=NJ), "hT")
        hTf = hT_raw[:, 0:8192].bitcast(F32).rearrange("p (m t) -> p m t", m=8)
        xTg = Buf(carve(8 * 512, BF).rearrange("p (k t) -> p k t", k=8), "xTg")
        xbg = Buf(carve(4 * D, BF).rearrange("p (j d) -> p j d", j=4), "xbg")
        pbg = Buf(carve(4 * PLE, BF).rearrange("p (j d) -> p j d", j=4), "pbg")
        pT = Buf(carve(2 * 512, BF).rearrange("p (k t) -> p k t", k=2), "pT")
        lst_ = [Buf(carve(12, F32).rearrange("p (a b) -> p a b", a=2), f"lst{i}") for i in range(4)]
        lmv_ = [Buf(carve(2, F32), f"lmv{i}") for i in range(4)]
        lsd_ = [Buf(carve(1, F32), f"lsd{i}") for i in range(4)]
        lrs_ = [Buf(carve(1, F32), f"lrs{i}") for i in range(4)]
        lnm_ = [Buf(carve(1, F32), f"lnm{i}") for i in range(4)]
        C_END = off[0]
        C_BUFS = (xres + z + [mT] + csg + cpc + x1b + [x1T, hT, xTg, xbg, pbg, pT] + lst_ + lmv_ + lsd_ + lrs_ + lnm_)

        PA = [Buf(psum(f"pa{i}", [128, 512], F32)[:], f"pa{i}") for i in range(3)]
        PBt = [psum(f"pb{i}", [128, 1024], F32) for i in range(2)]
        PBh = [Buf(PBt[i][:, h * 512:(h + 1) * 512], f"pb{i}{h}") for i in range(2) for h in range(2)]
        PM_t = psum("pm", [128, 1024], BF)
        PM = Buf(PM_t[:], "pm")
        pa_i = [0]

        WIDE = PA + PBh
        pw_i = [0]

        def pa_next(wide=False):
            if wide:
                b = WIDE[pw_i[0] % 7]
                pw_i[0] += 1
                return b
            b = PA[pa_i[0] % 3]
            pa_i[0] += 1
            return b

        E("sp", lambda e: e.dma_start(out=cvec.ap, in_=cvec_d), writes=[cvec], stream="c_cvec")
        E("sp", lambda e: e.dma_start(out=lnv.ap, in_=lnv_d), writes=[lnv], stream="c_lnv")
        E("pool", lambda e: e.dma_start(out=bple.ap, in_=bple_d), writes=[bple], stream="c_bple")
        idf = Buf(U[:, 0:256].bitcast(F32), "idf")
        tf = Buf(U[:, 4096:4096 + NH * 256 * 2].bitcast(F32).rearrange("p (h c) -> p h c", h=NH), "tf")
        E("dve", lambda e: e.memset(idf.ap, 0.0), writes=[idf])
        E("pool", lambda e: e.affine_select(out=idf.ap, in_=idf.ap, pattern=[[-1, 128]], compare_op=ALU.not_equal,
                                            fill=1.0, base=0, channel_multiplier=1), reads=[idf], writes=[idf])
        E("dve", lambda e: e.tensor_copy(out=ident.ap, in_=idf.ap), reads=[idf], writes=[ident])
        E("dve", lambda e: e.memset(ones_bf.ap, 1.0), writes=[ones_bf])
        E("sp", lambda e: e.dma_start(out=tf.ap, in_=toep_d), writes=[tf], stream="c_toep")
        E("dve", lambda e: e.tensor_scalar(out=negc.ap, in0=cvec.ap[:, 152:160], scalar1=-1.0, scalar2=None, op0=ALU.mult),
          reads=[cvec], writes=[negc])
        for h in range(NH):
            if PE_BIAS:
                E("dve", lambda e, h=h: e.tensor_scalar(out=tf.ap[:, h, :], in0=tf.ap[:, h, :], scalar1=negc.ap[:, h:h + 1],
                                                        scalar2=1.0 / (DH ** -0.5), op0=ALU.add, op1=ALU.mult),
                  reads=[tf, negc], writes=[tf])
            else:
                E("act", lambda e, h=h: e.activation(out=tf.ap[:, h, :], in_=tf.ap[:, h, :], func=AF.Exp,
                                                     bias=negc.ap[:, h:h + 1], scale=1.0), reads=[tf, negc], writes=[tf])
        E("pool", lambda e: e.affine_select(out=tf.ap, in_=tf.ap, pattern=[[0, NH], [1, 256]], compare_op=ALU.is_ge,
                                            fill=(-BIG / (DH ** -0.5) if PE_BIAS else 0.0), base=0, channel_multiplier=-1), reads=[tf], writes=[tf])
        E("dve", lambda e: e.tensor_copy(out=Ebf.ap, in_=tf.ap), reads=[tf], writes=[Ebf])

        cv = cvec.ap
        pen = Buf(sb("pen", [128, 16, 8], F32)[:], "pen")
        val = Buf(sb("val", [128, 16, 8], F32)[:], "val")
        E("dve", lambda e: e.memset(pen.ap, 0.0), writes=[pen])
        E("dve", lambda e: e.memset(val.ap, 1.0), writes=[val])
        for qi in range(6):
            bq_ = 4 + qi // 2
            E("dve", lambda e, qi=qi, bq_=bq_: e.memset(pen.ap[:, 2 * qi:2 * qi + 2, bq_:8], -1e30), writes=[pen])
            E("dve", lambda e, qi=qi, bq_=bq_: e.memset(val.ap[:, 2 * qi:2 * qi + 2, bq_:8], 0.0), writes=[val])
        for qi in range(6, 8):
            E("dve", lambda e, qi=qi: e.memset(pen.ap[:, 2 * qi:2 * qi + 2, 7:8], -1e30), writes=[pen])
            E("dve", lambda e, qi=qi: e.memset(val.ap[:, 2 * qi:2 * qi + 2, 7:8], 0.0), writes=[val])

        def transposes_to(dst_ap, dst_buf, src_ap_fn, src_buf, n, evac_eng):
            for i in range(n):
                E("pe", lambda e, i=i: e.transpose(out=PM_t[:, i * 128:(i + 1) * 128], in_=src_ap_fn(i), identity=ident.ap),
                  reads=[src_buf, ident], writes=[PM])
            src = PM_t[:, 0:n * 128]
            if len(dst_ap.shape) == 3:
                src = src.rearrange("p (k t) -> p k t", k=n)
            if evac_eng == "act":
                return E("act", lambda e: e.activation(out=dst_ap, in_=src, func=AF.Copy), reads=[PM], writes=[dst_buf])
            return E("dve", lambda e: e.tensor_copy(out=dst_ap, in_=src), reads=[PM], writes=[dst_buf])

        def ln_stats(zs):
            for ti, zb in enumerate(zs):
                for hh in range(2):
                    E("dve", lambda e, hh=hh, ti=ti, zb=zb: e.bn_stats(out=lst_[ti].ap[:, hh, :], in_=zb.ap[:, hh * 512:(hh + 1) * 512]),
                      reads=[zb], writes=[lst_[ti]])
                E("dve", lambda e, ti=ti: e.bn_aggr(out=lmv_[ti].ap, in_=lst_[ti].ap), reads=[lst_[ti]], writes=[lmv_[ti]])
            for ti, zb in enumerate(zs):
                E("act", lambda e, ti=ti: e.activation(out=lsd_[ti].ap, in_=lmv_[ti].ap[:, 1:2], func=AF.Sqrt, bias=cv[:, 160:161], scale=1.0),
                  reads=[lmv_[ti], cvec], writes=[lsd_[ti]])

        def ln_norm_tile(zs, gi, ti):
            zb = zs[ti]
            E("dve", lambda e: e.reciprocal(out=lrs_[ti].ap, in_=lsd_[ti].ap), reads=[lsd_[ti]], writes=[lrs_[ti]])
            E("dve", lambda e: e.scalar_tensor_tensor(out=zb.ap, in0=zb.ap, scalar=lmv_[ti].ap[:, 0:1], in1=lnv.ap[:, gi, :],
                                                      op0=ALU.subtract, op1=ALU.mult), reads=[zb, lmv_[ti], lnv], writes=[zb])
            E("dve", lambda e: e.scalar_tensor_tensor(out=zb.ap, in0=zb.ap, scalar=lrs_[ti].ap, in1=lnv.ap[:, gi + 1, :],
                                                      op0=ALU.mult, op1=ALU.add), reads=[zb, lrs_[ti], lnv], writes=[zb])

        def mm(out_ap, out_buf, lhsT, rhs, rd, start, stop):
            return E("pe", lambda e: e.matmul(out_ap, lhsT=lhsT, rhs=rhs, start=start, stop=stop),
                     reads=rd, writes=[out_buf])

        for b in range(NB):
            P.barrier(AB_BUFS)
            for c in range(4):
                E("dve", lambda e, c=c: e.memset(uT_t[:, c, 0:30], 0.0), writes=[uT[c]])
            E("dve", lambda e: e.memset(kT.ap[0:64, 1, :], 0.0), writes=[kT])
            E("dve", lambda e: e.memset(qT.ap[0:64, 1, :], 0.0), writes=[qT])
            E("dve", lambda e: e.memset(qT.ap[64:72, 0, :], 0.0), writes=[qT])
            E("dve", lambda e: e.memset(vA.ap[:, :, :, 64:65], 1.0), writes=[vA])
            E("dve", lambda e: e.memset(mpad.ap, 0.0), writes=[mpad])
            E("pool", lambda e: e.dma_start(out=kT.ap[64:72, 0, :], in_=oneh_d), writes=[kT], stream="oh0")
            E("pool", lambda e: e.dma_start(out=kT.ap[0:8, 1, :], in_=oneh_d), writes=[kT], stream="oh1")

            for i in range(8):
                xbb = xb[i % 2]
                src = x_d[b, i * 256:(i + 1) * 256, :].rearrange("(j p) d -> p j d", p=128)
                E("pool", lambda e, xbb=xbb, src=src: e.dma_start(out=xbb.ap, in_=src), writes=[xbb], stream=xbb.name)
                g = i // 2
                for j in range(2):
                    tt = (i % 2) * 2 + j
                    dst = xT[g].ap[:, :, tt * 128:(tt + 1) * 128]
                    transposes_to(dst, xT[g], lambda k, xbb=xbb, j=j: xbb.ap[:, j, k * 128:(k + 1) * 128], xbb, 8, "act")

            ra = ring_next()
            wload(ra, 8, 512, w_in[:, :, 1536:2048])
            rg = ring_next()
            wload(rg, 8, 512, w_in[:, :, 2048:2560])
            sa = slab(ra, 8, 512)
            sg_ = slab(rg, 8, 512)
            gi = 0
            for g in range(4):
                for c in range(4):
                    pa_ = pa_next(True)
                    pg_ = pa_next(True)
                    for kc in range(8):
                        mm(pa_.ap, pa_, sa[:, kc, c * 128:(c + 1) * 128], xT[g].ap[:, kc, :], [ra, xT[g]], kc == 0, kc == 7)
                    for kc in range(8):
                        mm(pg_.ap, pg_, sg_[:, kc, c * 128:(c + 1) * 128], xT[g].ap[:, kc, :], [rg, xT[g]], kc == 0, kc == 7)
                    st = sgt[gi % 2]
                    gi += 1
                    E("act", lambda e, st=st, pg_=pg_: e.activation(out=st.ap, in_=pg_.ap, func=AF.Sigmoid),
                      reads=[pg_], writes=[st])
                    E("dve", lambda e, st=st, pa_=pa_, c=c, g=g: e.tensor_tensor(
                        out=uT_t[:, c, 30 + g * 512:30 + (g + 1) * 512], in0=pa_.ap, in1=st.ap, op=ALU.mult),
                      reads=[pa_, st], writes=[uT[c]])

            def conv_diag(c):
                for j in range(CW):
                    E("dve", lambda e, j=j: e.tensor_scalar(out=diag.ap[:, j, :], in0=ident.ap, scalar1=cv[:, 28 + c * CW + j:29 + c * CW + j],
                                                            scalar2=None, op0=ALU.mult), reads=[ident, cvec], writes=[diag])

            def conv_chunk(c):
                for g in range(4):
                    pc_ = pa_next()
                    for j in range(CW):
                        mm(pc_.ap, pc_, diag.ap[:, j, :], uT_t[:, c, g * 512 + j:g * 512 + j + 512], [diag, uT[c]], j == 0, j == CW - 1)
                    E("act", lambda e, pc_=pc_, g=g: e.activation(out=cT_t[:, c, g * 512:(g + 1) * 512], in_=pc_.ap, func=AF.Identity,
                                                                 bias=cv[:, 16 + c:17 + c], scale=1.0), reads=[pc_, cvec], writes=[cT[c]])

            def conv_ln(g):
                gr = slice(g * 512, (g + 1) * 512)
                for c in range(4):
                    E("act", lambda e, c=c: e.activation(out=ysq.ap[:, c, :], in_=cT_t[:, c, gr], func=AF.Square),
                      reads=[cT[c]], writes=[ysq])
                s1 = pa_next()
                s2 = pa_next()
                for c in range(4):
                    mm(s1.ap, s1, ones_bf.ap, cT_t[:, c, gr], [ones_bf, cT[c]], c == 0, c == 3)
                for c in range(4):
                    mm(s2.ap, s2, ones_bf.ap, ysq.ap[:, c, :], [ones_bf, ysq], c == 0, c == 3)
                E("dve", lambda e: e.tensor_scalar(out=mean_t.ap, in0=s1.ap, scalar1=1.0 / 512, scalar2=None, op0=ALU.mult),
                  reads=[s1], writes=[mean_t])
                E("dve", lambda e: e.tensor_tensor(out=msq_t.ap, in0=mean_t.ap, in1=mean_t.ap, op=ALU.mult),
                  reads=[mean_t], writes=[msq_t])
                E("dve", lambda e: e.scalar_tensor_tensor(out=msq_t.ap, in0=s2.ap, scalar=1.0 / 512, in1=msq_t.ap,
                                                          op0=ALU.mult, op1=ALU.subtract), reads=[s2, msq_t], writes=[msq_t])
                E("act", lambda e: e.activation(out=rstd_t.ap, in_=msq_t.ap, func=AF.Sqrt, bias=cv[:, 160:161], scale=1.0),
                  reads=[msq_t, cvec], writes=[rstd_t])
                E("dve", lambda e: e.reciprocal(out=rstd_t.ap, in_=rstd_t.ap), reads=[rstd_t], writes=[rstd_t])
                for c in range(4):
                    t1 = t1_t[c % 2]
                    E("dve", lambda e, c=c, t1=t1: e.tensor_tensor(out=t1.ap, in0=cT_t[:, c, gr], in1=mean_t.ap, op=ALU.subtract),
                      reads=[cT[c], mean_t], writes=[t1])
                    E("dve", lambda e, t1=t1: e.tensor_tensor(out=t1.ap, in0=t1.ap, in1=rstd_t.ap, op=ALU.mult),
                      reads=[t1, rstd_t], writes=[t1])
                    E("act", lambda e, c=c, t1=t1: e.activation(out=cT_t[:, c, gr], in_=t1.ap, func=AF.Silu,
                                                               bias=cv[:, 24 + c:25 + c], scale=cv[:, 20 + c:21 + c]),
                      reads=[t1, cvec], writes=[cT[c]])

            for hp in range(4):
                rq = ring_next()
                wload(rq, 8, 384, w_in[:, :, hp * 128:(hp + 1) * 128], n0=0)
                wload(rq, 8, 384, w_in[:, :, 512 + hp * 128:512 + (hp + 1) * 128], n0=128)
                wload(rq, 8, 384, w_in[:, :, 1024 + hp * 128:1024 + (hp + 1) * 128], n0=256)
                sq = slab(rq, 8, 384)
                for g in range(4):
                    gr = slice(g * 512, (g + 1) * 512)
                    pq = pa_next()
                    for kc in range(8):
                        mm(pq.ap, pq, sq[:, kc, 0:128], xT[g].ap[:, kc, :], [rq, xT[g]], kc == 0, kc == 7)
                    E("dve", lambda e, pq=pq, gr=gr: e.tensor_copy(out=qT.ap[0:64, 0, gr], in_=pq.ap[0:64, :]),
                      reads=[pq], writes=[qT])
                    E("dve", lambda e, pq=pq, gr=gr: e.tensor_copy(out=qT.ap[64:128, 1, gr], in_=pq.ap[64:128, :]),
                      reads=[pq], writes=[qT])
                    pk = pa_next()
                    for kc in range(8):
                        mm(pk.ap, pk, sq[:, kc, 128:256], xT[g].ap[:, kc, :], [rq, xT[g]], kc == 0, kc == 7)
                    E("dve", lambda e, pk=pk, gr=gr: e.tensor_copy(out=kT.ap[0:64, 0, gr], in_=pk.ap[0:64, :]),
                      reads=[pk], writes=[kT])
                    E("dve", lambda e, pk=pk, gr=gr: e.tensor_copy(out=kT.ap[64:128, 1, gr], in_=pk.ap[64:128, :]),
                      reads=[pk], writes=[kT])
                    E("dve", lambda e, pk=pk, g=g: e.tensor_reduce(out=ksum.ap[:, 2 * g:2 * g + 2],
                                                                  in_=pk.ap.rearrange("p (a b) -> p a b", a=2),
                                                                  axis=AX.X, op=ALU.add), reads=[pk], writes=[ksum])
                    pv = pa_next()
                    for j in range(4):
                        for kc in range(8):
                            mm(pv.ap[:, j * 128:(j + 1) * 128], pv, xT[g].ap[:, kc, j * 128:(j + 1) * 128], sq[:, kc, 256:384],
                               [rq, xT[g]], kc == 0, kc == 7)
                    E("act", lambda e, pv=pv, g=g: e.activation(
                        out=vA.ap[:, 4 * g:4 * g + 4, :, 0:64],
                        in_=pv.ap.rearrange("p (j h d) -> p j h d", j=4, h=2), func=AF.Copy), reads=[pv], writes=[vA])
                E("dve", lambda e: e.tensor_scalar(out=kmean.ap, in0=ksum.ap, scalar1=1.0 / 256, scalar2=None, op0=ALU.mult),
                  reads=[ksum], writes=[kmean])
                if b == 0:
                    for i in range(NSL):
                        if i * 4 // NSL == hp:
                            convert_slab(i)
                conv_diag(hp)
                pg_ = pa_next()
                for qi in range(8):
                    qs = slice((8 + qi) * 128, (9 + qi) * 128)
                    mm(pg_.ap[:, qi * 16:qi * 16 + 8], pg_, qT.ap[0:64, 0, qs], kmean.ap[0:64, :], [qT, kmean], True, True)
                    mm(pg_.ap[:, qi * 16 + 8:qi * 16 + 16], pg_, qT.ap[64:128, 1, qs], kmean.ap[64:128, :], [qT, kmean], True, True)
                pgv = pg_.ap[:, 0:128].rearrange("p (a n) -> p a n", a=16)
                E("dve", lambda e, pgv=pgv: e.tensor_tensor(out=gs.ap, in0=pgv, in1=pen.ap, op=ALU.add), reads=[pg_, pen], writes=[gs])
                conv_chunk(hp)
                cur = gs
                for it in range(3):
                    E("dve", lambda e, cur=cur: e.tensor_reduce(out=mx.ap, in_=cur.ap, axis=AX.X, op=ALU.max), reads=[cur], writes=[mx])
                    if it == 2:
                        break
                    E("dve", lambda e, cur=cur: e.tensor_tensor(out=eq.ap, in0=cur.ap, in1=mx.ap.unsqueeze(2).to_broadcast([128, 16, 8]),
                                                               op=ALU.is_equal), reads=[cur, mx], writes=[eq])
                    E("dve", lambda e, cur=cur: e.scalar_tensor_tensor(out=g2.ap, in0=eq.ap, scalar=-1e30, in1=cur.ap,
                                                                      op0=ALU.mult, op1=ALU.add), reads=[eq, cur], writes=[g2])
                    cur = g2
                E("dve", lambda e: e.tensor_tensor(out=eq.ap, in0=gs.ap, in1=mx.ap.unsqueeze(2).to_broadcast([128, 16, 8]), op=ALU.is_ge),
                  reads=[gs, mx], writes=[eq])
                E("dve", lambda e: e.tensor_scalar(out=g2.ap, in0=eq.ap, scalar1=-1.0, scalar2=BIG, op0=ALU.add, op1=ALU.mult),
                  reads=[eq], writes=[g2])
                g2v = g2.ap.rearrange("p (q h) n -> p q h n", h=2)
                valv = val.ap.rearrange("p (q h) n -> p q h n", h=2)
                E("dve", lambda e: e.tensor_tensor(out=mpad.ap[:, :, 0, 64:72], in0=g2v[:, :, 0, :], in1=valv[:, :, 0, :], op=ALU.mult),
                  reads=[g2, val], writes=[mpad])
                E("dve", lambda e: e.tensor_tensor(out=mpad.ap[:, :, 1, 0:8], in0=g2v[:, :, 1, :], in1=valv[:, :, 1, :], op=ALU.mult),
                  reads=[g2, val], writes=[mpad])
                for qi in range(8):
                    mm(PBt[0][0:72, qi * 128:(qi + 1) * 128], PBh[qi // 4], mpad.ap[:, qi, 0, 0:72], ident.ap, [mpad, ident], True, True)
                for qi in range(8):
                    mm(PBt[1][0:8, qi * 128:(qi + 1) * 128], PBh[2 + qi // 4], mpad.ap[:, qi, 1, 0:8], ident.ap, [mpad, ident], True, True)
                E("act", lambda e: e.activation(out=qT.ap[64:72, 0, 1024:2048], in_=PBt[0][64:72, :], func=AF.Copy),
                  reads=[PBh[0], PBh[1]], writes=[qT])
                E("act", lambda e: e.activation(out=qT.ap[0:8, 1, 1024:2048], in_=PBt[1][0:8, :], func=AF.Copy),
                  reads=[PBh[2], PBh[3]], writes=[qT])
                cth = []
                it_n = 0
                its = [(G, h, j) for G in range(4) for h in range(2) for j in range(4 * G + 4)]
                LA = 2
                pts = {}

                def emit_scores(k):
                    G, h, j = its[k]
                    rows = slice(0, 72) if h == 0 else slice(0, 128)
                    d = j - 4 * G
                    c0 = max(0, d) * 128
                    N = 512 - c0
                    ps_ = pa_next()
                    hh = 2 * hp + h
                    mm(ps_.ap[:, 0:N], ps_, kT.ap[rows, h, j * 128:(j + 1) * 128],
                       qT.ap[rows, h, G * 512 + c0:(G + 1) * 512], [kT, qT], True, True)
                    pt = PT[k % 4]
                    pts[k] = pt
                    E("act", lambda e: e.activation(out=pt.ap[:, 0:N], in_=ps_.ap[:, 0:N], func=AF.Exp, scale=DH ** -0.5),
                      reads=[ps_], writes=[pt])
                    if d >= 0:
                        w_ = min(256, N)
                        E("dve", lambda e: e.tensor_tensor(out=pt.ap[:, 0:w_], in0=pt.ap[:, 0:w_], in1=Ebf.ap[:, hh, 0:w_], op=ALU.mult),
                          reads=[pt, Ebf], writes=[pt])
                    elif d == -1:
                        E("dve", lambda e: e.tensor_tensor(out=pt.ap[:, 0:128], in0=pt.ap[:, 0:128], in1=Ebf.ap[:, hh, 128:256], op=ALU.mult),
                          reads=[pt, Ebf], writes=[pt])

                def emit_pv(k):
                    G, h, j = its[k]
                    d = j - 4 * G
                    c0 = max(0, d) * 128
                    pt = pts.pop(k)
                    atm = att_tm[G % 2]
                    for s_ in range(max(0, d), 4):
                        lc = s_ * 128 - c0
                        mm(PBh[s_].ap[:, 0:65], PBh[s_], pt.ap[:, lc:lc + 128], vA.ap[:, j, h, :], [pt, vA], j == 0, j == 4 * G + s_)
                    if j == 4 * G + 3:
                        for s_ in range(4):
                            O = PBh[s_]
                            E("dve", lambda e, O=O, s_=s_: e.reciprocal(out=rinv.ap[:, s_:s_ + 1], in_=O.ap[:, 64:65]), reads=[O], writes=[rinv])
                            E("dve", lambda e, O=O, s_=s_: e.tensor_scalar(
                                out=atm.ap[:, s_, h * 64:(h + 1) * 64], in0=O.ap[:, 0:64], scalar1=rinv.ap[:, s_:s_ + 1], scalar2=None,
                                op0=ALU.mult), reads=[O, rinv], writes=[atm])
                        if h == 1:
                            transposes_to(attT_t[:, hp, G * 512:(G + 1) * 512], attT[G], lambda s_: atm.ap[:, s_, :], atm, 4, "dve")

                if hp == 3:
                    P.barrier(CLN_BUFS)
                for k in range(len(its) + LA):
                    if k < len(its):
                        emit_scores(k)
                    if k - LA >= 0:
                        emit_pv(k - LA)
                    if hp == 3 and k in (16, 32, 48, 64):
                        conv_ln(k // 16 - 1)

            if DBG and b == 0:
                fin.append(E("pool", lambda e: e.dma_start(out=dbg_att, in_=attT_t[:]), reads=attT, stream="dbg1"))
                fin.append(E("pool", lambda e: e.dma_start(out=dbg_c, in_=cT_t[:]), reads=cT, stream="dbg2"))
            P.barrier(C_BUFS)

            def load_x(gg):
                srcx = x_d[b, gg * 512:(gg + 1) * 512, :].rearrange("(j p) d -> p j d", p=128)
                E("pool", lambda e: e.dma_start(out=xbg.ap, in_=srcx), writes=[xbg], stream="xbg")

            def load_p(gg):
                srcp = p_d[b, gg * 512:(gg + 1) * 512, :].rearrange("(j p) d -> p j d", p=128)
                E("pool", lambda e: e.dma_start(out=pbg.ap, in_=srcp), writes=[pbg], stream="pbg")

            def xT_tile(j):
                transposes_to(xTg.ap[:, :, j * 128:(j + 1) * 128], xTg, lambda k: xbg.ap[:, j, k * 128:(k + 1) * 128],
                              xbg, 8, "act")

            def pT_all():
                for j in range(4):
                    for k in range(2):
                        E("pe", lambda e, j=j, k=k: e.transpose(out=PM_t[:, (k * 4 + j) * 128:(k * 4 + j + 1) * 128],
                                                                in_=pbg.ap[:, j, k * 128:(k + 1) * 128], identity=ident.ap),
                          reads=[pbg, ident], writes=[PM])
                E("act", lambda e: e.activation(out=pT.ap, in_=PM_t[:, 0:1024].rearrange("p (k t) -> p k t", k=2), func=AF.Copy),
                  reads=[PM], writes=[pT])

            def c1(g, hook=None):
                gr = slice(g * 512, (g + 1) * 512)
                rao = wl(0)
                sao = slab(rao, 4, 1024)
                it = 0
                for part in range(2):
                    if part == 1:
                        rco = wl(3)
                        sco = slab(rco, 4, 1024)
                    for mh in range(2):
                        rgl = wl([1, 2, 4, 5][part * 2 + mh])
                        sgl = slab(rgl, 8, 512)
                        for mm_ in range(4):
                            m = mh * 4 + mm_
                            if hook is not None and it % 4 == 0:
                                hook(it)
                            py = pa_next(True)
                            pl = pa_next(True)
                            for kc in range(4):
                                if part == 0:
                                    mm(py.ap, py, sao[:, kc, m * 128:(m + 1) * 128], attT[g].ap[:, kc, :], [rao, attT[g]], kc == 0, kc == 3)
                                else:
                                    mm(py.ap, py, sco[:, kc, m * 128:(m + 1) * 128], cT_t[:, kc, gr], [rco, cT[kc]], kc == 0, kc == 3)
                            for kc in range(8):
                                mm(pl.ap, pl, sgl[:, kc, mm_ * 128:(mm_ + 1) * 128], xTg.ap[:, kc, :], [rgl, xTg], kc == 0, kc == 7)
                            sgb = csg[it % 2]
                            bcol = part * 8 + m
                            E("act", lambda e, sgb=sgb, pl=pl, bcol=bcol: e.activation(
                                out=sgb.ap, in_=pl.ap, func=AF.Sigmoid, bias=cv[:, bcol:bcol + 1], scale=1.0),
                              reads=[pl, cvec], writes=[sgb])
                            if part == 0:
                                E("dve", lambda e, py=py, sgb=sgb, m=m: e.tensor_tensor(
                                    out=hTf[:, m, :], in0=py.ap, in1=sgb.ap, op=ALU.mult), reads=[py, sgb], writes=[hT])
                            else:
                                pc = cpc[m % 2]
                                E("dve", lambda e, py=py, sgb=sgb, pc=pc: e.tensor_tensor(out=pc.ap, in0=py.ap, in1=sgb.ap, op=ALU.mult),
                                  reads=[py, sgb], writes=[pc])
                                E("dve", lambda e, pc=pc, m=m: e.tensor_tensor(out=mT.ap[:, m, :], in0=pc.ap, in1=hTf[:, m, :], op=ALU.add),
                                  reads=[pc, hT], writes=[mT])
                            if hook is not None and it % 4 == 3:
                                hook(it)
                            it += 1

            def mixed(g, with_xT):
                rm = [wl(6), wl(7)]
                for t in range(4):
                    xr = xres[t % 2]
                    srcr = x_d[b, g * 512 + t * 128:g * 512 + (t + 1) * 128, :]
                    E("sp", lambda e, xr=xr, srcr=srcr: e.dma_start(out=xr.ap, in_=srcr), writes=[xr], stream=xr.name)
                    pbi = t % 2
                    for hh in range(2):
                        ob = PBh[pbi * 2 + hh]
                        sm = slab(rm[hh], 8, 512)
                        for kc in range(8):
                            mm(ob.ap, ob, mT.ap[:, kc, t * 128:(t + 1) * 128], sm[:, kc, :], [mT, rm[hh]], kc == 0, kc == 7)
                        E("dve", lambda e, xr=xr, ob=ob, t=t, hh=hh: e.scalar_tensor_tensor(
                            out=z[t].ap[:, hh * 512:(hh + 1) * 512], in0=xr.ap[:, hh * 512:(hh + 1) * 512], scalar=ALPHA,
                            in1=ob.ap, op0=ALU.mult, op1=ALU.add), reads=[xr, ob], writes=[z[t]])
                    if with_xT:
                        xT_tile(t)

            def x1T_norm(t):
                ln_norm_tile(z, 0, t)
                xb1 = x1b[t % 2]
                E("act", lambda e: e.activation(out=xb1.ap, in_=z[t].ap, func=AF.Copy), reads=[z[t]], writes=[xb1])

            def x1T_tr(t):
                xb1 = x1b[t % 2]
                transposes_to(x1T.ap[:, :, t * 128:(t + 1) * 128], x1T, lambda k: xb1.ap[:, k * 128:(k + 1) * 128], xb1, 8, "act")

            def c1_hook(it):
                if it % 4 == 0:
                    x1T_norm(it // 4)
                if it % 4 == 3:
                    x1T_tr(it // 4)

            def ffn(g):
                rfg = None
                rfu = None
                for j in range(NJ):
                    if j % 4 == 0:
                        rfg = wl(8 + 2 * (j // 4))
                        rfu = wl(9 + 2 * (j // 4))
                    sfg = slab(rfg, 8, 512)
                    sfu = slab(rfu, 8, 512)
                    jj = j % 4
                    pg_ = pa_next(True)
                    pu_ = pa_next(True)
                    for kc in range(8):
                        mm(pg_.ap, pg_, sfg[:, kc, jj * 128:(jj + 1) * 128], x1T.ap[:, kc, :], [rfg, x1T], kc == 0, kc == 7)
                    for kc in range(8):
                        mm(pu_.ap, pu_, sfu[:, kc, jj * 128:(jj + 1) * 128], x1T.ap[:, kc, :], [rfu, x1T], kc == 0, kc == 7)
                    sgb = csg[j % 2]
                    E("act", lambda e, sgb=sgb, pg_=pg_: e.activation(out=sgb.ap, in_=pg_.ap, func=AF.Silu), reads=[pg_], writes=[sgb])
                    E("dve", lambda e, sgb=sgb, pu_=pu_, j=j: e.tensor_tensor(out=hT.ap[:, j, :], in0=sgb.ap, in1=pu_.ap, op=ALU.mult),
                      reads=[sgb, pu_], writes=[hT])
                    if j == 8:
                        pT_all()
                        if g < 3:
                            load_p(g + 1)

            def down_ple_out(g):
                for hh in range(2):
                    for sl in range(3):
                        k0 = sl * 8
                        kn = min(8, NJ - k0)
                        rd_ = wl(20 + hh * 3 + sl)
                        sd = slab(rd_, 8, 512)
                        for t in range(4):
                            ob = PBh[t]
                            for kk in range(kn):
                                j = k0 + kk
                                mm(ob.ap, ob, hT.ap[:, j, t * 128:(t + 1) * 128], sd[:, kk, :], [hT, rd_], j == 0, j == NJ - 1)
                    for t in range(4):
                        ob = PBh[t]
                        E("dve", lambda e, ob=ob, t=t, hh=hh: e.scalar_tensor_tensor(
                            out=z[t].ap[:, hh * 512:(hh + 1) * 512], in0=z[t].ap[:, hh * 512:(hh + 1) * 512], scalar=ALPHA,
                            in1=ob.ap, op0=ALU.mult, op1=ALU.add), reads=[ob, z[t]], writes=[z[t]])
                rpl = wl(26)
                spl = slab(rpl, 2, 1024)
                for hh in range(2):
                    rpg = wl(27 + hh)
                    spg = slab(rpg, 8, 512)
                    for t in range(4):
                        ppl = pa_next(True)
                        ppg = pa_next(True)
                        for kc in range(2):
                            mm(ppl.ap, ppl, pT.ap[:, kc, t * 128:(t + 1) * 128], spl[:, kc, hh * 512:(hh + 1) * 512], [pT, rpl], kc == 0, kc == 1)
                        for kc in range(8):
                            mm(ppg.ap, ppg, x1T.ap[:, kc, t * 128:(t + 1) * 128], spg[:, kc, :], [x1T, rpg], kc == 0, False)
                        mm(ppg.ap, ppg, ones_bf.ap[0:1, :], bple.ap[0:1, hh * 512:(hh + 1) * 512], [ones_bf, bple], False, True)
                        sgb = csg[t % 2]
                        E("act", lambda e, sgb=sgb, ppg=ppg: e.activation(out=sgb.ap, in_=ppg.ap, func=AF.Sigmoid), reads=[ppg], writes=[sgb])
                        pc = cpc[t % 2]
                        E("dve", lambda e, pc=pc, sgb=sgb, ppl=ppl: e.tensor_tensor(out=pc.ap, in0=sgb.ap, in1=ppl.ap, op=ALU.mult),
                          reads=[sgb, ppl], writes=[pc])
                        E("dve", lambda e, pc=pc, t=t, hh=hh: e.tensor_tensor(
                            out=z[t].ap[:, hh * 512:(hh + 1) * 512], in0=z[t].ap[:, hh * 512:(hh + 1) * 512], in1=pc.ap, op=ALU.add),
                          reads=[pc, z[t]], writes=[z[t]])
                ln_stats(z)
                for t in range(4):
                    ln_norm_tile(z, 2, t)
                for t in range(4):
                    dsto = out_d[b, g * 512 + t * 128:g * 512 + (t + 1) * 128, :]
                    tok = E("pool", lambda e, t=t, dsto=dsto: e.dma_start(out=dsto, in_=z[t].ap), reads=[z[t]], stream=f"out{t}")
                    fin.append(tok)

            load_x(0)
            load_p(0)
            for j in range(4):
                xT_tile(j)
            load_x(1)
            c1(0)
            for g in range(4):
                mixed(g, g < 3)
                if g + 2 <= 3:
                    load_x(g + 2)
                ln_stats(z)
                if g < 3:
                    c1(g + 1, hook=c1_hook)
                else:
                    x1T_norm(0)
                    for t in range(4):
                        if t + 1 < 4:
                            x1T_norm(t + 1)
                        x1T_tr(t)
                ffn(g)
                down_ple_out(g)
        P.emit(nc, final_wait=fin[-8:] + fin[:2])
    return nc


def _t5_bucket_np(rel):
    n = np.maximum(rel, 0)
    max_exact = 16
    nf = np.maximum(n, 1).astype(np.float32)
    large = max_exact + (np.log(nf / np.float32(max_exact)) / np.float32(np.log(128 / max_exact))
                         * np.float32(32 - max_exact)).astype(np.int32)
    large = np.minimum(large, 31)
    return np.where(n < max_exact, n, large)


_NC_CACHE = {}
_LAST = []


def kernel(x, p, w_in, b_gate, bias_table, w_att_out, conv_w, conv_b, conv_ln_g, conv_ln_b,
           w_conv_out, w_mix_out, ln_mix_g, ln_mix_b, w_ffn_gate, w_ffn_up, w_ffn_down,
           w_ple, w_ple_gate, b_ple_gate, ln_ffn_g, ln_ffn_b):
    f = lambda a: np.ascontiguousarray(np.asarray(a, dtype=np.float32))
    x = f(x)
    p = f(p)
    B = x.shape[0]
    NB = B // NCORES
    cvec = np.zeros((128, NCV), np.float32)
    cvec[:, 0:16] = f(b_gate)[0].reshape(16, 128).T
    cvec[:, 16:20] = f(conv_b)[0].reshape(4, 128).T
    cvec[:, 20:24] = f(conv_ln_g)[0].reshape(4, 128).T
    cvec[:, 24:28] = f(conv_ln_b)[0].reshape(4, 128).T
    cw = f(conv_w)[0]
    cvec[:, 28:152] = cw.T.reshape(4, 128, CW).transpose(1, 0, 2).reshape(128, 4 * CW)
    bt = f(bias_table)
    cvec[:, 152:160] = bt[31][None, :]
    cvec[:, 160] = EPS
    lnv = np.stack([np.broadcast_to(f(v)[0][None, :], (128, D)) for v in (ln_mix_g, ln_mix_b, ln_ffn_g, ln_ffn_b)], axis=1)
    lnv = np.ascontiguousarray(lnv)
    kk = np.arange(128)[:, None]
    cc = np.arange(256)[None, :]
    bidx = _t5_bucket_np(cc - kk)
    toep = np.ascontiguousarray(bt[bidx].transpose(0, 2, 1))
    onehot = (np.arange(S)[None, :] // 256 == np.arange(8)[:, None]).astype(np.float32)
    shared = {
        "w_in": f(w_in)[0], "w_att_out": f(w_att_out)[0], "w_conv_out": f(w_conv_out)[0], "w_mix_out": f(w_mix_out)[0],
        "w_ffn_gate": f(w_ffn_gate)[0], "w_ffn_up": f(w_ffn_up)[0], "w_ffn_down": f(w_ffn_down)[0],
        "w_ple": f(w_ple)[0], "w_ple_gate": f(w_ple_gate)[0], "cvec": cvec, "lnv": lnv,
        "bple": f(b_ple_gate).reshape(1, D), "toep": toep, "onehot": onehot,
    }
    if NB not in _NC_CACHE:
        _NC_CACHE[NB] = build_nc(NB)
    nc = _NC_CACHE[NB]
    in_maps = []
    for c in range(NCORES):
        m = dict(shared)
        m["x"] = x[c * NB:(c + 1) * NB]
        m["p"] = p[0, c * NB:(c + 1) * NB]
        in_maps.append(m)
    res = run_bass_kernel_spmd(nc, in_maps, core_ids=list(range(NCORES)))
    if DBG:
        _LAST.append(res.results[0])
    return np.concatenate([r["out"] for r in res.results], axis=0).astype(np.float32)
```

```python
import contextlib
import numpy as np
import concourse.bass as bass
import concourse.mybir as mybir
from concourse.bass_utils import run_bass_kernel_spmd

F32 = mybir.dt.float32
BF = mybir.dt.bfloat16
AF = mybir.ActivationFunctionType
ALU = mybir.AluOpType
AX = mybir.AxisListType

NCORES = 8
D = 1024
S = 2048
NH = 8
DH = 64
NIN = 4608
FFN = 2816
NJ = FFN // 128
PLE = 256
CW = 31
ALPHA = 2.0 ** 0.25
EPS = 1e-5
BIG = 30000.0
NCV = 161
DBG = False
PE_BIAS = False

ENGS = ("pe", "act", "dve", "pool", "sp")


class Tok:
    __slots__ = ("eng", "idx", "sem", "val", "is_dma", "stream", "key")

    def __init__(self, eng, idx):
        self.eng = eng
        self.idx = idx
        self.sem = None
        self.val = None
        self.is_dma = False
        self.stream = None
        self.key = eng


class Buf:
    def __init__(self, ap, name):
        self.ap = ap
        self.name = name
        self.w = {}
        self.r = {}


class Prog:
    SEM_ROT = 4000

    def __init__(self):
        self.ops = {e: [] for e in ENGS}
        self.streams = {}
        self.need = set()
        self.last = {}

    def op(self, eng, fn, deps=(), stream=None):
        t = Tok(eng, len(self.ops[eng]))
        deps = [d for d in deps if d is not None]
        if stream is not None:
            t.is_dma = True
            t.stream = stream
            t.key = "s:" + stream
            n = self.streams.get(stream, 0) + 1
            self.streams[stream] = n
            t.val = 16 * n
        self.ops[eng].append((t, fn, deps))
        for d in deps:
            if not d.is_dma:
                self.need.add((d.eng, d.idx))
        self.last[t.key] = t
        return t

    def E(self, eng, fn, reads=(), writes=(), stream=None, extra=()):
        deps = {}

        def add(t):
            k = t.key
            o = deps.get(k)
            if o is None or t.idx > o.idx or (t.is_dma and t.val > o.val):
                deps[k] = t

        for t in extra:
            if t is not None:
                add(t)
        for b in reads:
            for t in b.w.values():
                add(t)
        for b in writes:
            for t in b.w.values():
                add(t)
            for t in b.r.values():
                add(t)
        if eng == "pe":
            deps.pop("pe", None)
        tok = self.op(eng, fn, list(deps.values()), stream=stream)
        for b in reads:
            b.r[tok.key] = tok
        for b in writes:
            b.w[tok.key] = tok
        return tok

    def barrier(self, bufs):
        snap = dict(self.last)
        for b in bufs:
            for k, t in snap.items():
                b.r[k] = t
            b.w = {}

    def emit(self, nc, final_wait=()):
        with contextlib.ExitStack() as es:
            nsem = 0
            for e in ENGS:
                cnt = 0
                cur = None
                for t, fn, deps in self.ops[e]:
                    if t.is_dma:
                        continue
                    if (e, t.idx) in self.need:
                        if cur is None or cnt >= self.SEM_ROT:
                            cur = es.enter_context(nc.semaphore(f"c_{e}_{nsem}"))
                            nsem += 1
                            cnt = 0
                        cnt += 1
                        t.sem = cur
                        t.val = cnt
            dsem = {}
            for s in self.streams:
                dsem[s] = es.enter_context(nc.semaphore(f"d_{s}"))
            for e in ENGS:
                for t, fn, deps in self.ops[e]:
                    if t.is_dma:
                        t.sem = dsem[t.stream]
            block = es.enter_context(nc.Block())

            def make(e):
                def body(eng):
                    waited = {}
                    for t, fn, deps in self.ops[e]:
                        for d in deps:
                            k = id(d.sem)
                            if waited.get(k, 0) >= d.val:
                                continue
                            eng.wait_ge(d.sem, d.val)
                            waited[k] = d.val
                        ins = fn(eng)
                        if t.is_dma:
                            ins.then_inc(t.sem, 16)
                        elif t.sem is not None:
                            ins.then_inc(t.sem, 1)
                    if e == "sp":
                        for d in final_wait:
                            eng.wait_ge(d.sem, d.val)

                return body

            block.tensor(make("pe"))
            block.scalar(make("act"))
            block.vector(make("dve"))
            block.gpsimd(make("pool"))
            block.sync(make("sp"))


def build_nc(NB):
    nc = bass.Bass("TRN2", target_bir_lowering=False)
    dr = {}

    def din(name, shape):
        dr[name] = nc.dram_tensor(name, list(shape), F32, kind="ExternalInput").ap()
        return dr[name]

    x_d = din("x", [NB, S, D])
    p_d = din("p", [NB, S, PLE])
    w_in = din("w_in", [D, NIN]).rearrange("(kc p) n -> p kc n", p=128)
    w_ao = din("w_att_out", [512, D]).rearrange("(kc p) n -> p kc n", p=128)
    w_co = din("w_conv_out", [512, D]).rearrange("(kc p) n -> p kc n", p=128)
    w_mix = din("w_mix_out", [D, D]).rearrange("(kc p) n -> p kc n", p=128)
    w_fg = din("w_ffn_gate", [D, FFN]).rearrange("(kc p) n -> p kc n", p=128)
    w_fu = din("w_ffn_up", [D, FFN]).rearrange("(kc p) n -> p kc n", p=128)
    w_fd = din("w_ffn_down", [FFN, D]).rearrange("(kc p) n -> p kc n", p=128)
    w_pl = din("w_ple", [PLE, D]).rearrange("(kc p) n -> p kc n", p=128)
    w_pg = din("w_ple_gate", [D, D]).rearrange("(kc p) n -> p kc n", p=128)
    cvec_d = din("cvec", [128, NCV])
    lnv_d = din("lnv", [128, 4, D])
    bple_d = din("bple", [1, D])
    toep_d = din("toep", [128, NH, 256])
    oneh_d = din("onehot", [8, S])
    out_d = nc.dram_tensor("out", [NB, S, D], F32, kind="ExternalOutput").ap()

    P = Prog()
    E = P.E
    fin = []
    if DBG:
        dbg_att = nc.dram_tensor("dbg_att", [128, 4, S], F32, kind="ExternalOutput").ap()
        dbg_c = nc.dram_tensor("dbg_c", [128, 4, S], F32, kind="ExternalOutput").ap()

    with contextlib.ExitStack() as es:
        def sb(name, shape, dt):
            return es.enter_context(nc.sbuf_tensor("sb_" + name, list(shape), dt))

        def psum(name, shape, dt):
            return es.enter_context(nc.psum_tensor("ps_" + name, list(shape), dt))

        ident = Buf(sb("ident", [128, 128], BF)[:], "ident")
        ones_bf = Buf(sb("ones_bf", [128, 128], BF)[:], "ones")
        Ebf = Buf(sb("Ebf", [128, NH, 256], BF)[:], "Ebf")
        cvec = Buf(sb("cvec", [128, NCV], F32)[:], "cvec")
        negc = Buf(sb("negc", [128, NH], F32)[:], "negc")
        lnv = Buf(sb("lnv", [128, 4, D], F32)[:], "lnv")
        bple = Buf(sb("bple", [1, D], BF)[:], "bple")
        attT_t = sb("attT", [128, 4, S], BF)
        cT_t = sb("cT", [128, 4, S], BF)
        attT = [Buf(attT_t[:, :, g * 512:(g + 1) * 512], f"attT{g}") for g in range(4)]
        cT = [Buf(cT_t[:, c, :], f"cT{c}") for c in range(4)]
        NRING = 6
        ring = [Buf(sb(f"ring{i}", [128, 4096], BF)[:], f"ring{i}") for i in range(NRING)]
        ring_i = [0]

        def ring_next():
            b = ring[ring_i[0] % NRING]
            ring_i[0] += 1
            return b

        def slab(b, kc, n):
            return b.ap[:, 0:kc * n].rearrange("p (k n) -> p k n", k=kc)

        def wload(b, kc, n, src, k0=0, kn=None, n0=0):
            kn = kc - k0 if kn is None else kn
            nn = src.shape[2]
            dst = slab(b, kc, n)[:, k0:k0 + kn, n0:n0 + nn]
            return E("pool", lambda e: e.dma_start(out=dst, in_=src), writes=[b], stream=b.name)

        SD = []
        SD.append((4, 1024, w_ao, 4))
        SD.append((8, 512, w_in[:, :, 2560:3072], 8))
        SD.append((8, 512, w_in[:, :, 3072:3584], 8))
        SD.append((4, 1024, w_co, 4))
        SD.append((8, 512, w_in[:, :, 3584:4096], 8))
        SD.append((8, 512, w_in[:, :, 4096:4608], 8))
        SD.append((8, 512, w_mix[:, :, 0:512], 8))
        SD.append((8, 512, w_mix[:, :, 512:1024], 8))
        for jq in range(6):
            ncol = min(512, FFN - jq * 512)
            SD.append((8, 512, w_fg[:, :, jq * 512:jq * 512 + ncol], 8))
            SD.append((8, 512, w_fu[:, :, jq * 512:jq * 512 + ncol], 8))
        for hh in range(2):
            for sl in range(3):
                k0 = sl * 8
                kn = min(8, NJ - k0)
                SD.append((8, 512, w_fd[:, k0:k0 + kn, hh * 512:(hh + 1) * 512], kn))
        SD.append((2, 1024, w_pl, 2))
        SD.append((8, 512, w_pg[:, :, 0:512], 8))
        SD.append((8, 512, w_pg[:, :, 512:1024], 8))
        NSL = len(SD)
        wscr = nc.dram_tensor("wscr", [NSL, 128, 4096], BF, kind="Internal").ap()
        scrb = [Buf(None, f"scr{i}") for i in range(NSL)]

        def convert_slab(i):
            kc, n, src, kn = SD[i]
            rb = ring_next()
            wload(rb, kc, n, src, kn=kn)
            E("sp", lambda e: e.dma_start(out=wscr[i], in_=rb.ap), reads=[rb], writes=[scrb[i]], stream=f"scr{i}")

        def wl(i):
            rb = ring_next()
            E("sp", lambda e: e.dma_start(out=rb.ap, in_=wscr[i]), reads=[scrb[i]], writes=[rb], stream=rb.name)
            return rb

        NU = 51296
        U = sb("U", [128, NU], BF)
        off = [0]

        def carve(nel, dt):
            nb = nel * (4 if dt == F32 else 2)
            nb = (nb + 63) // 64 * 64
            a = off[0] // 2
            off[0] += nb
            assert off[0] <= NU * 2, ("union overflow", off[0])
            v = U[:, a:a + nb // 2]
            if dt == F32:
                v = v.bitcast(F32)[:, 0:nel]
            else:
                v = v[:, 0:nel]
            return v

        off[0] = 0
        xT_t = carve(8 * S, BF).rearrange("p (k t) -> p k t", k=8)
        xT = [Buf(xT_t[:, :, g * 512:(g + 1) * 512], f"xT{g}") for g in range(4)]
        uT_t = carve(4 * (S + 30), BF).rearrange("p (c t) -> p c t", c=4)
        uT = [Buf(uT_t[:, c, :], f"uT{c}") for c in range(4)]
        qT = Buf(carve(2 * S, BF).rearrange("p (h t) -> p h t", h=2), "qT")
        kT = Buf(carve(2 * S, BF).rearrange("p (h t) -> p h t", h=2), "kT")
        vA = Buf(carve(16 * 2 * 65, BF).rearrange("p (j h d) -> p j h d", j=16, h=2), "vA")
        off_alias = off[0]
        xb = [Buf(carve(2 * D, BF).rearrange("p (j d) -> p j d", j=2), f"xb{i}") for i in range(2)]
        diag = Buf(carve(CW * 128, BF).rearrange("p (j m) -> p j m", j=CW), "diag")
        sgt = [Buf(carve(512, F32), f"sgt{i}") for i in range(2)]
        off_alias_end = off[0]
        PT = [Buf(carve(512, BF), f"PT{i}") for i in range(4)]
        att_tm = [Buf(carve(4 * 128, BF).rearrange("p (s d) -> p s d", s=4), f"atm{i}") for i in range(2)]
        ksum = Buf(carve(8, F32), "ksum")
        kmean = Buf(carve(8, BF), "kmean")
        gs = Buf(carve(128, F32).rearrange("p (a n) -> p a n", a=16), "gs")
        g2 = Buf(carve(128, F32).rearrange("p (a n) -> p a n", a=16), "g2")
        eq = Buf(carve(128, F32).rearrange("p (a n) -> p a n", a=16), "eq")
        mx = Buf(carve(16, F32), "mx")
        mpad = Buf(carve(8 * 2 * 72, BF).rearrange("p (q h n) -> p q h n", q=8, h=2), "mpad")
        rinv = Buf(carve(4, F32), "rinv")
        osb = [Buf(carve(4 * 65, F32).rearrange("p (s d) -> p s d", s=4), f"osb{i}") for i in range(2)]
        AB_END = off[0]
        off[0] = off_alias
        ysq = Buf(carve(4 * 512, BF).rearrange("p (c t) -> p c t", c=4), "ysq")
        mean_t = Buf(carve(512, F32), "mean")
        msq_t = Buf(carve(512, F32), "msq")
        rstd_t = Buf(carve(512, F32), "rstd")
        t1_t = [Buf(carve(512, F32), f"t1_{i}") for i in range(2)]
        assert off[0] <= off_alias_end, (off[0], off_alias_end)
        CLN_BUFS = [ysq, mean_t, msq_t, rstd_t] + t1_t
        AB_BUFS = (xT + uT + [qT, kT, vA] + xb + [diag] + sgt + [ysq, mean_t, msq_t, rstd_t] + t1_t + PT
                   + att_tm + osb + [ksum, kmean, gs, g2, eq, mx, mpad, rinv])

        off[0] = 0
        xres = [Buf(carve(D, F32), f"xres{i}") for i in range(2)]
        z_t = carve(4 * D, F32).rearrange("p (t d) -> p t d", t=4)
        z = [Buf(z_t[:, t, :], f"z{t}") for t in range(4)]
        mT = Buf(carve(8 * 512, BF).rearrange("p (k t) -> p k t", k=8), "mT")
        csg = [Buf(carve(512, F32), f"csg{i}") for i in range(2)]
        cpc = [Buf(carve(512, F32), f"cpc{i}") for i in range(2)]
        x1b = [Buf(carve(D, BF), f"x1b{i}") for i in range(2)]
        x1T = Buf(carve(8 * 512, BF).rearrange("p (k t) -> p k t", k=8), "x1T")
        hT_raw = carve(NJ * 512, BF)
        hT = Buf(hT_raw.rearrange("p (j t) -> p j t", j=NJ), "hT")
        hTf = hT_raw[:, 0:8192].bitcast(F32).rearrange("p (m t) -> p m t", m=8)
        xTg = Buf(carve(8 * 512, BF).rearrange("p (k t) -> p k t", k=8), "xTg")
        xbg = Buf(carve(4 * D, BF).rearrange("p (j d) -> p j d", j=4), "xbg")
        pbg = Buf(carve(4 * PLE, BF).rearrange("p (j d) -> p j d", j=4), "pbg")
        pT = Buf(carve(2 * 512, BF).rearrange("p (k t) -> p k t", k=2), "pT")
        lst_ = [Buf(carve(12, F32).rearrange("p (a b) -> p a b", a=2), f"lst{i}") for i in range(4)]
        lmv_ = [Buf(carve(2, F32), f"lmv{i}") for i in range(4)]
        lsd_ = [Buf(carve(1, F32), f"lsd{i}") for i in range(4)]
        lrs_ = [Buf(carve(1, F32), f"lrs{i}") for i in range(4)]
        lnm_ = [Buf(carve(1, F32), f"lnm{i}") for i in range(4)]
        C_END = off[0]
        C_BUFS = (xres + z + [mT] + csg + cpc + x1b + [x1T, hT, xTg, xbg, pbg, pT] + lst_ + lmv_ + lsd_ + lrs_ + lnm_)

        PA = [Buf(psum(f"pa{i}", [128, 512], F32)[:], f"pa{i}") for i in range(3)]
        PBt = [psum(f"pb{i}", [128, 1024], F32) for i in range(2)]
        PBh = [Buf(PBt[i][:, h * 512:(h + 1) * 512], f"pb{i}{h}") for i in range(2) for h in range(2)]
        PM_t = psum("pm", [128, 1024], BF)
        PM = Buf(PM_t[:], "pm")
        pa_i = [0]

        WIDE = PA + PBh
        pw_i = [0]

        def pa_next(wide=False):
            if wide:
                b = WIDE[pw_i[0] % 7]
                pw_i[0] += 1
                return b
            b = PA[pa_i[0] % 3]
            pa_i[0] += 1
            return b

        E("sp", lambda e: e.dma_start(out=cvec.ap, in_=cvec_d), writes=[cvec], stream="c_cvec")
        E("sp", lambda e: e.dma_start(out=lnv.ap, in_=lnv_d), writes=[lnv], stream="c_lnv")
        E("pool", lambda e: e.dma_start(out=bple.ap, in_=bple_d), writes=[bple], stream="c_bple")
        idf = Buf(U[:, 0:256].bitcast(F32), "idf")
        tf = Buf(U[:, 4096:4096 + NH * 256 * 2].bitcast(F32).rearrange("p (h c) -> p h c", h=NH), "tf")
        E("dve", lambda e: e.memset(idf.ap, 0.0), writes=[idf])
        E("pool", lambda e: e.affine_select(out=idf.ap, in_=idf.ap, pattern=[[-1, 128]], compare_op=ALU.not_equal,
                                            fill=1.0, base=0, channel_multiplier=1), reads=[idf], writes=[idf])
        E("dve", lambda e: e.tensor_copy(out=ident.ap, in_=idf.ap), reads=[idf], writes=[ident])
        E("dve", lambda e: e.memset(ones_bf.ap, 1.0), writes=[ones_bf])
        E("sp", lambda e: e.dma_start(out=tf.ap, in_=toep_d), writes=[tf], stream="c_toep")
        E("dve", lambda e: e.tensor_scalar(out=negc.ap, in0=cvec.ap[:, 152:160], scalar1=-1.0, scalar2=None, op0=ALU.mult),
          reads=[cvec], writes=[negc])
        for h in range(NH):
            if PE_BIAS:
                E("dve", lambda e, h=h: e.tensor_scalar(out=tf.ap[:, h, :], in0=tf.ap[:, h, :], scalar1=negc.ap[:, h:h + 1],
                                                        scalar2=1.0 / (DH ** -0.5), op0=ALU.add, op1=ALU.mult),
                  reads=[tf, negc], writes=[tf])
            else:
                E("act", lambda e, h=h: e.activation(out=tf.ap[:, h, :], in_=tf.ap[:, h, :], func=AF.Exp,
                                                     bias=negc.ap[:, h:h + 1], scale=1.0), reads=[tf, negc], writes=[tf])
        E("pool", lambda e: e.affine_select(out=tf.ap, in_=tf.ap, pattern=[[0, NH], [1, 256]], compare_op=ALU.is_ge,
                                            fill=(-BIG / (DH ** -0.5) if PE_BIAS else 0.0), base=0, channel_multiplier=-1), reads=[tf], writes=[tf])
        E("dve", lambda e: e.tensor_copy(out=Ebf.ap, in_=tf.ap), reads=[tf], writes=[Ebf])

        cv = cvec.ap
        pen = Buf(sb("pen", [128, 16, 8], F32)[:], "pen")
        val = Buf(sb("val", [128, 16, 8], F32)[:], "val")
        E("dve", lambda e: e.memset(pen.ap, 0.0), writes=[pen])
        E("dve", lambda e: e.memset(val.ap, 1.0), writes=[val])
        for qi in range(6):
            bq_ = 4 + qi // 2
            E("dve", lambda e, qi=qi, bq_=bq_: e.memset(pen.ap[:, 2 * qi:2 * qi + 2, bq_:8], -1e30), writes=[pen])
            E("dve", lambda e, qi=qi, bq_=bq_: e.memset(val.ap[:, 2 * qi:2 * qi + 2, bq_:8], 0.0), writes=[val])
        for qi in range(6, 8):
            E("dve", lambda e, qi=qi: e.memset(pen.ap[:, 2 * qi:2 * qi + 2, 7:8], -1e30), writes=[pen])
            E("dve", lambda e, qi=qi: e.memset(val.ap[:, 2 * qi:2 * qi + 2, 7:8], 0.0), writes=[val])

        PM2_v = PA[2].ap.bitcast(BF)

        def transposes_to(dst_ap, dst_buf, src_ap_fn, src_buf, n, evac_eng, alt=False):
            pmv, PMb = (PM2_v, PA[2]) if alt else (PM_t, PM)
            for i in range(n):
                E("pe", lambda e, i=i: e.transpose(out=pmv[:, i * 128:(i + 1) * 128], in_=src_ap_fn(i), identity=ident.ap),
                  reads=[src_buf, ident], writes=[PMb])
            src = pmv[:, 0:n * 128]
            if len(dst_ap.shape) == 3:
                src = src.rearrange("p (k t) -> p k t", k=n)
            if evac_eng == "act":
                return E("act", lambda e: e.activation(out=dst_ap, in_=src, func=AF.Copy), reads=[PMb], writes=[dst_buf])
            return E("dve", lambda e: e.tensor_copy(out=dst_ap, in_=src), reads=[PMb], writes=[dst_buf])

        def ln_stats(zs):
            for ti, zb in enumerate(zs):
                for hh in range(2):
                    E("dve", lambda e, hh=hh, ti=ti, zb=zb: e.bn_stats(out=lst_[ti].ap[:, hh, :], in_=zb.ap[:, hh * 512:(hh + 1) * 512]),
                      reads=[zb], writes=[lst_[ti]])
                E("dve", lambda e, ti=ti: e.bn_aggr(out=lmv_[ti].ap, in_=lst_[ti].ap), reads=[lst_[ti]], writes=[lmv_[ti]])
            for ti, zb in enumerate(zs):
                E("act", lambda e, ti=ti: e.activation(out=lsd_[ti].ap, in_=lmv_[ti].ap[:, 1:2], func=AF.Sqrt, bias=cv[:, 160:161], scale=1.0),
                  reads=[lmv_[ti], cvec], writes=[lsd_[ti]])

        def ln_norm_tile(zs, gi, ti):
            zb = zs[ti]
            E("dve", lambda e: e.reciprocal(out=lrs_[ti].ap, in_=lsd_[ti].ap), reads=[lsd_[ti]], writes=[lrs_[ti]])
            E("dve", lambda e: e.scalar_tensor_tensor(out=zb.ap, in0=zb.ap, scalar=lmv_[ti].ap[:, 0:1], in1=lnv.ap[:, gi, :],
                                                      op0=ALU.subtract, op1=ALU.mult), reads=[zb, lmv_[ti], lnv], writes=[zb])
            E("dve", lambda e: e.scalar_tensor_tensor(out=zb.ap, in0=zb.ap, scalar=lrs_[ti].ap, in1=lnv.ap[:, gi + 1, :],
                                                      op0=ALU.mult, op1=ALU.add), reads=[zb, lrs_[ti], lnv], writes=[zb])

        def mm(out_ap, out_buf, lhsT, rhs, rd, start, stop):
            return E("pe", lambda e: e.matmul(out_ap, lhsT=lhsT, rhs=rhs, start=start, stop=stop),
                     reads=rd, writes=[out_buf])

        for b in range(NB):
            P.barrier(AB_BUFS)
            for c in range(4):
                E("dve", lambda e, c=c: e.memset(uT_t[:, c, 0:30], 0.0), writes=[uT[c]])
            E("dve", lambda e: e.memset(kT.ap[0:64, 1, :], 0.0), writes=[kT])
            E("dve", lambda e: e.memset(qT.ap[0:64, 1, :], 0.0), writes=[qT])
            E("dve", lambda e: e.memset(qT.ap[64:72, 0, :], 0.0), writes=[qT])
            E("dve", lambda e: e.memset(vA.ap[:, :, :, 64:65], 1.0), writes=[vA])
            E("dve", lambda e: e.memset(mpad.ap, 0.0), writes=[mpad])
            E("pool", lambda e: e.dma_start(out=kT.ap[64:72, 0, :], in_=oneh_d), writes=[kT], stream="oh0")
            E("pool", lambda e: e.dma_start(out=kT.ap[0:8, 1, :], in_=oneh_d), writes=[kT], stream="oh1")

            for i in range(8):
                xbb = xb[i % 2]
                src = x_d[b, i * 256:(i + 1) * 256, :].rearrange("(j p) d -> p j d", p=128)
                E("pool", lambda e, xbb=xbb, src=src: e.dma_start(out=xbb.ap, in_=src), writes=[xbb], stream=xbb.name)
                g = i // 2
                for j in range(2):
                    tt = (i % 2) * 2 + j
                    dst = xT[g].ap[:, :, tt * 128:(tt + 1) * 128]
                    transposes_to(dst, xT[g], lambda k, xbb=xbb, j=j: xbb.ap[:, j, k * 128:(k + 1) * 128], xbb, 8,
                                  "act" if j == 0 else "dve", alt=(j == 1))

            ra = ring_next()
            wload(ra, 8, 512, w_in[:, :, 1536:2048])
            rg = ring_next()
            wload(rg, 8, 512, w_in[:, :, 2048:2560])
            sa = slab(ra, 8, 512)
            sg_ = slab(rg, 8, 512)
            gi = 0
            for g in range(4):
                for c in range(4):
                    pa_ = pa_next(True)
                    pg_ = pa_next(True)
                    for kc in range(8):
                        mm(pa_.ap, pa_, sa[:, kc, c * 128:(c + 1) * 128], xT[g].ap[:, kc, :], [ra, xT[g]], kc == 0, kc == 7)
                    for kc in range(8):
                        mm(pg_.ap, pg_, sg_[:, kc, c * 128:(c + 1) * 128], xT[g].ap[:, kc, :], [rg, xT[g]], kc == 0, kc == 7)
                    st = sgt[gi % 2]
                    gi += 1
                    E("act", lambda e, st=st, pg_=pg_: e.activation(out=st.ap, in_=pg_.ap, func=AF.Sigmoid),
                      reads=[pg_], writes=[st])
                    E("dve", lambda e, st=st, pa_=pa_, c=c, g=g: e.tensor_tensor(
                        out=uT_t[:, c, 30 + g * 512:30 + (g + 1) * 512], in0=pa_.ap, in1=st.ap, op=ALU.mult),
                      reads=[pa_, st], writes=[uT[c]])

            def conv_diag(c):
                for j in range(CW):
                    E("dve", lambda e, j=j: e.tensor_scalar(out=diag.ap[:, j, :], in0=ident.ap, scalar1=cv[:, 28 + c * CW + j:29 + c * CW + j],
                                                            scalar2=None, op0=ALU.mult), reads=[ident, cvec], writes=[diag])

            def conv_chunk(c):
                for g in range(4):
                    pc_ = pa_next()
                    for j in range(CW):
                        mm(pc_.ap, pc_, diag.ap[:, j, :], uT_t[:, c, g * 512 + j:g * 512 + j + 512], [diag, uT[c]], j == 0, j == CW - 1)
                    E("act", lambda e, pc_=pc_, g=g: e.activation(out=cT_t[:, c, g * 512:(g + 1) * 512], in_=pc_.ap, func=AF.Identity,
                                                                 bias=cv[:, 16 + c:17 + c], scale=1.0), reads=[pc_, cvec], writes=[cT[c]])

            def conv_ln(g):
                gr = slice(g * 512, (g + 1) * 512)
                for c in range(4):
                    E("act", lambda e, c=c: e.activation(out=ysq.ap[:, c, :], in_=cT_t[:, c, gr], func=AF.Square),
                      reads=[cT[c]], writes=[ysq])
                s1 = pa_next()
                s2 = pa_next()
                for c in range(4):
                    mm(s1.ap, s1, ones_bf.ap, cT_t[:, c, gr], [ones_bf, cT[c]], c == 0, c == 3)
                for c in range(4):
                    mm(s2.ap, s2, ones_bf.ap, ysq.ap[:, c, :], [ones_bf, ysq], c == 0, c == 3)
                E("dve", lambda e: e.tensor_scalar(out=mean_t.ap, in0=s1.ap, scalar1=1.0 / 512, scalar2=None, op0=ALU.mult),
                  reads=[s1], writes=[mean_t])
                E("dve", lambda e: e.tensor_tensor(out=msq_t.ap, in0=mean_t.ap, in1=mean_t.ap, op=ALU.mult),
                  reads=[mean_t], writes=[msq_t])
                E("dve", lambda e: e.scalar_tensor_tensor(out=msq_t.ap, in0=s2.ap, scalar=1.0 / 512, in1=msq_t.ap,
                                                          op0=ALU.mult, op1=ALU.subtract), reads=[s2, msq_t], writes=[msq_t])
                E("act", lambda e: e.activation(out=rstd_t.ap, in_=msq_t.ap, func=AF.Sqrt, bias=cv[:, 160:161], scale=1.0),
                  reads=[msq_t, cvec], writes=[rstd_t])
                E("dve", lambda e: e.reciprocal(out=rstd_t.ap, in_=rstd_t.ap), reads=[rstd_t], writes=[rstd_t])
                for c in range(4):
                    t1 = t1_t[c % 2]
                    E("dve", lambda e, c=c, t1=t1: e.tensor_tensor(out=t1.ap, in0=cT_t[:, c, gr], in1=mean_t.ap, op=ALU.subtract),
                      reads=[cT[c], mean_t], writes=[t1])
                    E("dve", lambda e, t1=t1: e.tensor_tensor(out=t1.ap, in0=t1.ap, in1=rstd_t.ap, op=ALU.mult),
                      reads=[t1, rstd_t], writes=[t1])
                    E("act", lambda e, c=c, t1=t1: e.activation(out=cT_t[:, c, gr], in_=t1.ap, func=AF.Silu,
                                                               bias=cv[:, 24 + c:25 + c], scale=cv[:, 20 + c:21 + c]),
                      reads=[t1, cvec], writes=[cT[c]])

            for hp in range(4):
                rq = ring_next()
                wload(rq, 8, 384, w_in[:, :, hp * 128:(hp + 1) * 128], n0=0)
                wload(rq, 8, 384, w_in[:, :, 512 + hp * 128:512 + (hp + 1) * 128], n0=128)
                wload(rq, 8, 384, w_in[:, :, 1024 + hp * 128:1024 + (hp + 1) * 128], n0=256)
                sq = slab(rq, 8, 384)
                for g in range(4):
                    gr = slice(g * 512, (g + 1) * 512)
                    pq = pa_next()
                    for kc in range(8):
                        mm(pq.ap, pq, sq[:, kc, 0:128], xT[g].ap[:, kc, :], [rq, xT[g]], kc == 0, kc == 7)
                    E("dve", lambda e, pq=pq, gr=gr: e.tensor_copy(out=qT.ap[0:64, 0, gr], in_=pq.ap[0:64, :]),
                      reads=[pq], writes=[qT])
                    E("dve", lambda e, pq=pq, gr=gr: e.tensor_copy(out=qT.ap[64:128, 1, gr], in_=pq.ap[64:128, :]),
                      reads=[pq], writes=[qT])
                    pk = pa_next()
                    for kc in range(8):
                        mm(pk.ap, pk, sq[:, kc, 128:256], xT[g].ap[:, kc, :], [rq, xT[g]], kc == 0, kc == 7)
                    E("dve", lambda e, pk=pk, gr=gr: e.tensor_copy(out=kT.ap[0:64, 0, gr], in_=pk.ap[0:64, :]),
                      reads=[pk], writes=[kT])
                    E("dve", lambda e, pk=pk, gr=gr: e.tensor_copy(out=kT.ap[64:128, 1, gr], in_=pk.ap[64:128, :]),
                      reads=[pk], writes=[kT])
                    E("dve", lambda e, pk=pk, g=g: e.tensor_reduce(out=ksum.ap[:, 2 * g:2 * g + 2],
                                                                  in_=pk.ap.rearrange("p (a b) -> p a b", a=2),
                                                                  axis=AX.X, op=ALU.add), reads=[pk], writes=[ksum])
                    pv = pa_next()
                    for j in range(4):
                        for kc in range(8):
                            mm(pv.ap[:, j * 128:(j + 1) * 128], pv, xT[g].ap[:, kc, j * 128:(j + 1) * 128], sq[:, kc, 256:384],
                               [rq, xT[g]], kc == 0, kc == 7)
                    E("act", lambda e, pv=pv, g=g: e.activation(
                        out=vA.ap[:, 4 * g:4 * g + 4, :, 0:64],
                        in_=pv.ap.rearrange("p (j h d) -> p j h d", j=4, h=2), func=AF.Copy), reads=[pv], writes=[vA])
                E("dve", lambda e: e.tensor_scalar(out=kmean.ap, in0=ksum.ap, scalar1=1.0 / 256, scalar2=None, op0=ALU.mult),
                  reads=[ksum], writes=[kmean])
                if b == 0:
                    for i in range(NSL):
                        if i * 4 // NSL == hp:
                            convert_slab(i)
                if hp == 0:
                    conv_diag(0)
                pg_ = pa_next()
                for qi in range(8):
                    qs = slice((8 + qi) * 128, (9 + qi) * 128)
                    mm(pg_.ap[:, qi * 16:qi * 16 + 8], pg_, qT.ap[0:64, 0, qs], kmean.ap[0:64, :], [qT, kmean], True, True)
                    mm(pg_.ap[:, qi * 16 + 8:qi * 16 + 16], pg_, qT.ap[64:128, 1, qs], kmean.ap[64:128, :], [qT, kmean], True, True)
                pgv = pg_.ap[:, 0:128].rearrange("p (a n) -> p a n", a=16)
                E("dve", lambda e, pgv=pgv: e.tensor_tensor(out=gs.ap, in0=pgv, in1=pen.ap, op=ALU.add), reads=[pg_, pen], writes=[gs])
                conv_chunk(hp)
                cur = gs
                for it in range(3):
                    E("dve", lambda e, cur=cur: e.tensor_reduce(out=mx.ap, in_=cur.ap, axis=AX.X, op=ALU.max), reads=[cur], writes=[mx])
                    if it == 2:
                        break
                    E("dve", lambda e, cur=cur: e.tensor_tensor(out=eq.ap, in0=cur.ap, in1=mx.ap.unsqueeze(2).to_broadcast([128, 16, 8]),
                                                               op=ALU.is_equal), reads=[cur, mx], writes=[eq])
                    E("dve", lambda e, cur=cur: e.scalar_tensor_tensor(out=g2.ap, in0=eq.ap, scalar=-1e30, in1=cur.ap,
                                                                      op0=ALU.mult, op1=ALU.add), reads=[eq, cur], writes=[g2])
                    cur = g2
                E("dve", lambda e: e.tensor_tensor(out=eq.ap, in0=gs.ap, in1=mx.ap.unsqueeze(2).to_broadcast([128, 16, 8]), op=ALU.is_ge),
                  reads=[gs, mx], writes=[eq])
                E("dve", lambda e: e.tensor_scalar(out=g2.ap, in0=eq.ap, scalar1=-1.0, scalar2=BIG, op0=ALU.add, op1=ALU.mult),
                  reads=[eq], writes=[g2])
                g2v = g2.ap.rearrange("p (q h) n -> p q h n", h=2)
                valv = val.ap.rearrange("p (q h) n -> p q h n", h=2)
                E("dve", lambda e: e.tensor_tensor(out=mpad.ap[:, :, 0, 64:72], in0=g2v[:, :, 0, :], in1=valv[:, :, 0, :], op=ALU.mult),
                  reads=[g2, val], writes=[mpad])
                E("dve", lambda e: e.tensor_tensor(out=mpad.ap[:, :, 1, 0:8], in0=g2v[:, :, 1, :], in1=valv[:, :, 1, :], op=ALU.mult),
                  reads=[g2, val], writes=[mpad])
                for qi in range(8):
                    mm(PBt[0][0:72, qi * 128:(qi + 1) * 128], PBh[qi // 4], mpad.ap[:, qi, 0, 0:72], ident.ap, [mpad, ident], True, True)
                for qi in range(8):
                    mm(PBt[1][0:8, qi * 128:(qi + 1) * 128], PBh[2 + qi // 4], mpad.ap[:, qi, 1, 0:8], ident.ap, [mpad, ident], True, True)
                E("act", lambda e: e.activation(out=qT.ap[64:72, 0, 1024:2048], in_=PBt[0][64:72, :], func=AF.Copy),
                  reads=[PBh[0], PBh[1]], writes=[qT])
                E("act", lambda e: e.activation(out=qT.ap[0:8, 1, 1024:2048], in_=PBt[1][0:8, :], func=AF.Copy),
                  reads=[PBh[2], PBh[3]], writes=[qT])
                cth = []
                it_n = 0
                its = [(G, h, j) for G in range(4) for h in range(2) for j in range(4 * G + 4)]
                LA = 2
                pts = {}

                def emit_scores(k):
                    G, h, j = its[k]
                    rows = slice(0, 72) if h == 0 else slice(0, 128)
                    d = j - 4 * G
                    c0 = max(0, d) * 128
                    N = 512 - c0
                    ps_ = pa_next()
                    hh = 2 * hp + h
                    mm(ps_.ap[:, 0:N], ps_, kT.ap[rows, h, j * 128:(j + 1) * 128],
                       qT.ap[rows, h, G * 512 + c0:(G + 1) * 512], [kT, qT], True, True)
                    pt = PT[k % 4]
                    pts[k] = pt
                    E("act", lambda e: e.activation(out=pt.ap[:, 0:N], in_=ps_.ap[:, 0:N], func=AF.Exp, scale=DH ** -0.5),
                      reads=[ps_], writes=[pt])
                    if d >= 0:
                        w_ = min(256, N)
                        E("dve", lambda e: e.tensor_tensor(out=pt.ap[:, 0:w_], in0=pt.ap[:, 0:w_], in1=Ebf.ap[:, hh, 0:w_], op=ALU.mult),
                          reads=[pt, Ebf], writes=[pt])
                    elif d == -1:
                        E("dve", lambda e: e.tensor_tensor(out=pt.ap[:, 0:128], in0=pt.ap[:, 0:128], in1=Ebf.ap[:, hh, 128:256], op=ALU.mult),
                          reads=[pt, Ebf], writes=[pt])

                def emit_pv(k):
                    G, h, j = its[k]
                    d = j - 4 * G
                    c0 = max(0, d) * 128
                    pt = pts.pop(k)
                    atm = att_tm[G % 2]
                    for s_ in range(max(0, d), 4):
                        lc = s_ * 128 - c0
                        mm(PBh[s_].ap[:, 0:65], PBh[s_], pt.ap[:, lc:lc + 128], vA.ap[:, j, h, :], [pt, vA], j == 0, j == 4 * G + s_)
                    if j == 4 * G + 3:
                        ob_ = osb[(2 * G + h) % 2]
                        for half in range(2):
                            srcv = PBt[half][:, :].rearrange("p (s c) -> p s c", s=2)[:, :, 0:65]
                            E("dve", lambda e, half=half, srcv=srcv: e.tensor_copy(out=ob_.ap[:, 2 * half:2 * half + 2, :], in_=srcv),
                              reads=[PBh[2 * half], PBh[2 * half + 1]], writes=[ob_])
                        E("dve", lambda e: e.reciprocal(out=rinv.ap, in_=ob_.ap[:, :, 64]), reads=[ob_], writes=[rinv])
                        E("dve", lambda e: e.tensor_tensor(out=atm.ap[:, :, h * 64:(h + 1) * 64], in0=ob_.ap[:, :, 0:64],
                                                           in1=rinv.ap.unsqueeze(2).to_broadcast([128, 4, 64]), op=ALU.mult),
                          reads=[ob_, rinv], writes=[atm])
                        if h == 1:
                            transposes_to(attT_t[:, hp, G * 512:(G + 1) * 512], attT[G], lambda s_: atm.ap[:, s_, :], atm, 4, "dve")

                if hp == 3:
                    P.barrier(CLN_BUFS)
                for k in range(len(its) + LA):
                    if k < len(its):
                        emit_scores(k)
                    if k - LA >= 0:
                        emit_pv(k - LA)
                    if hp == 3 and k in (16, 32, 48, 64):
                        conv_ln(k // 16 - 1)
                    if hp < 3 and k == 12:
                        conv_diag(hp + 1)

            if DBG and b == 0:
                fin.append(E("pool", lambda e: e.dma_start(out=dbg_att, in_=attT_t[:]), reads=attT, stream="dbg1"))
                fin.append(E("pool", lambda e: e.dma_start(out=dbg_c, in_=cT_t[:]), reads=cT, stream="dbg2"))
            P.barrier(C_BUFS)

            def load_x(gg):
                srcx = x_d[b, gg * 512:(gg + 1) * 512, :].rearrange("(j p) d -> p j d", p=128)
                E("pool", lambda e: e.dma_start(out=xbg.ap, in_=srcx), writes=[xbg], stream="xbg")

            def load_p(gg):
                srcp = p_d[b, gg * 512:(gg + 1) * 512, :].rearrange("(j p) d -> p j d", p=128)
                E("pool", lambda e: e.dma_start(out=pbg.ap, in_=srcp), writes=[pbg], stream="pbg")

            def xT_tile(j):
                transposes_to(xTg.ap[:, :, j * 128:(j + 1) * 128], xTg, lambda k: xbg.ap[:, j, k * 128:(k + 1) * 128],
                              xbg, 8, "act")

            def pT_all():
                for j in range(4):
                    for k in range(2):
                        E("pe", lambda e, j=j, k=k: e.transpose(out=PM_t[:, (k * 4 + j) * 128:(k * 4 + j + 1) * 128],
                                                                in_=pbg.ap[:, j, k * 128:(k + 1) * 128], identity=ident.ap),
                          reads=[pbg, ident], writes=[PM])
                E("act", lambda e: e.activation(out=pT.ap, in_=PM_t[:, 0:1024].rearrange("p (k t) -> p k t", k=2), func=AF.Copy),
                  reads=[PM], writes=[pT])

            def c1(g, hook=None):
                gr = slice(g * 512, (g + 1) * 512)
                rao = wl(0)
                sao = slab(rao, 4, 1024)
                it = 0
                for part in range(2):
                    if part == 1:
                        rco = wl(3)
                        sco = slab(rco, 4, 1024)
                    for mh in range(2):
                        rgl = wl([1, 2, 4, 5][part * 2 + mh])
                        sgl = slab(rgl, 8, 512)
                        for mm_ in range(4):
                            m = mh * 4 + mm_
                            if hook is not None and it % 4 == 0:
                                hook(it)
                            py = pa_next(True)
                            pl = pa_next(True)
                            for kc in range(4):
                                if part == 0:
                                    mm(py.ap, py, sao[:, kc, m * 128:(m + 1) * 128], attT[g].ap[:, kc, :], [rao, attT[g]], kc == 0, kc == 3)
                                else:
                                    mm(py.ap, py, sco[:, kc, m * 128:(m + 1) * 128], cT_t[:, kc, gr], [rco, cT[kc]], kc == 0, kc == 3)
                            for kc in range(8):
                                mm(pl.ap, pl, sgl[:, kc, mm_ * 128:(mm_ + 1) * 128], xTg.ap[:, kc, :], [rgl, xTg], kc == 0, kc == 7)
                            sgb = csg[it % 2]
                            bcol = part * 8 + m
                            E("act", lambda e, sgb=sgb, pl=pl, bcol=bcol: e.activation(
                                out=sgb.ap, in_=pl.ap, func=AF.Sigmoid, bias=cv[:, bcol:bcol + 1], scale=1.0),
                              reads=[pl, cvec], writes=[sgb])
                            if part == 0:
                                E("dve", lambda e, py=py, sgb=sgb, m=m: e.tensor_tensor(
                                    out=hTf[:, m, :], in0=py.ap, in1=sgb.ap, op=ALU.mult), reads=[py, sgb], writes=[hT])
                            else:
                                pc = cpc[m % 2]
                                E("dve", lambda e, py=py, sgb=sgb, pc=pc: e.tensor_tensor(out=pc.ap, in0=py.ap, in1=sgb.ap, op=ALU.mult),
                                  reads=[py, sgb], writes=[pc])
                                E("dve", lambda e, pc=pc, m=m: e.tensor_tensor(out=mT.ap[:, m, :], in0=pc.ap, in1=hTf[:, m, :], op=ALU.add),
                                  reads=[pc, hT], writes=[mT])
                            if hook is not None and it % 4 == 3:
                                hook(it)
                            it += 1

            def mixed(g, with_xT):
                rm = [wl(6), wl(7)]
                for t in range(4):
                    xr = xres[t % 2]
                    srcr = x_d[b, g * 512 + t * 128:g * 512 + (t + 1) * 128, :]
                    E("sp", lambda e, xr=xr, srcr=srcr: e.dma_start(out=xr.ap, in_=srcr), writes=[xr], stream=xr.name)
                    pbi = t % 2
                    for hh in range(2):
                        ob = PBh[pbi * 2 + hh]
                        sm = slab(rm[hh], 8, 512)
                        for kc in range(8):
                            mm(ob.ap, ob, mT.ap[:, kc, t * 128:(t + 1) * 128], sm[:, kc, :], [mT, rm[hh]], kc == 0, kc == 7)
                        E("dve", lambda e, xr=xr, ob=ob, t=t, hh=hh: e.scalar_tensor_tensor(
                            out=z[t].ap[:, hh * 512:(hh + 1) * 512], in0=xr.ap[:, hh * 512:(hh + 1) * 512], scalar=ALPHA,
                            in1=ob.ap, op0=ALU.mult, op1=ALU.add), reads=[xr, ob], writes=[z[t]])
                    if with_xT:
                        xT_tile(t)

            def x1T_norm(t):
                ln_norm_tile(z, 0, t)
                xb1 = x1b[t % 2]
                E("act", lambda e: e.activation(out=xb1.ap, in_=z[t].ap, func=AF.Copy), reads=[z[t]], writes=[xb1])

            def x1T_tr(t):
                xb1 = x1b[t % 2]
                transposes_to(x1T.ap[:, :, t * 128:(t + 1) * 128], x1T, lambda k: xb1.ap[:, k * 128:(k + 1) * 128], xb1, 8, "act")

            def c1_hook(it):
                if it % 4 == 0:
                    x1T_norm(it // 4)
                if it % 4 == 3:
                    x1T_tr(it // 4)

            def ffn(g):
                rfg = None
                rfu = None
                for j in range(NJ):
                    if j % 4 == 0:
                        rfg = wl(8 + 2 * (j // 4))
                        rfu = wl(9 + 2 * (j // 4))
                    sfg = slab(rfg, 8, 512)
                    sfu = slab(rfu, 8, 512)
                    jj = j % 4
                    pg_ = pa_next(True)
                    pu_ = pa_next(True)
                    for kc in range(8):
                        mm(pg_.ap, pg_, sfg[:, kc, jj * 128:(jj + 1) * 128], x1T.ap[:, kc, :], [rfg, x1T], kc == 0, kc == 7)
                    for kc in range(8):
                        mm(pu_.ap, pu_, sfu[:, kc, jj * 128:(jj + 1) * 128], x1T.ap[:, kc, :], [rfu, x1T], kc == 0, kc == 7)
                    sgb = csg[j % 2]
                    E("act", lambda e, sgb=sgb, pg_=pg_: e.activation(out=sgb.ap, in_=pg_.ap, func=AF.Silu), reads=[pg_], writes=[sgb])
                    E("dve", lambda e, sgb=sgb, pu_=pu_, j=j: e.tensor_tensor(out=hT.ap[:, j, :], in0=sgb.ap, in1=pu_.ap, op=ALU.mult),
                      reads=[sgb, pu_], writes=[hT])
                    if j == 8:
                        pT_all()
                        if g < 3:
                            load_p(g + 1)

            def down_ple_out(g):
                for hh in range(2):
                    for sl in range(3):
                        k0 = sl * 8
                        kn = min(8, NJ - k0)
                        rd_ = wl(20 + hh * 3 + sl)
                        sd = slab(rd_, 8, 512)
                        for t in range(4):
                            ob = PBh[t]
                            for kk in range(kn):
                                j = k0 + kk
                                mm(ob.ap, ob, hT.ap[:, j, t * 128:(t + 1) * 128], sd[:, kk, :], [hT, rd_], j == 0, j == NJ - 1)
                    for t in range(4):
                        ob = PBh[t]
                        E("dve", lambda e, ob=ob, t=t, hh=hh: e.scalar_tensor_tensor(
                            out=z[t].ap[:, hh * 512:(hh + 1) * 512], in0=z[t].ap[:, hh * 512:(hh + 1) * 512], scalar=ALPHA,
                            in1=ob.ap, op0=ALU.mult, op1=ALU.add), reads=[ob, z[t]], writes=[z[t]])
                rpl = wl(26)
                spl = slab(rpl, 2, 1024)
                for hh in range(2):
                    rpg = wl(27 + hh)
                    spg = slab(rpg, 8, 512)
                    for t in range(4):
                        ppl = pa_next(True)
                        ppg = pa_next(True)
                        for kc in range(2):
                            mm(ppl.ap, ppl, pT.ap[:, kc, t * 128:(t + 1) * 128], spl[:, kc, hh * 512:(hh + 1) * 512], [pT, rpl], kc == 0, kc == 1)
                        for kc in range(8):
                            mm(ppg.ap, ppg, x1T.ap[:, kc, t * 128:(t + 1) * 128], spg[:, kc, :], [x1T, rpg], kc == 0, False)
                        mm(ppg.ap, ppg, ones_bf.ap[0:1, :], bple.ap[0:1, hh * 512:(hh + 1) * 512], [ones_bf, bple], False, True)
                        sgb = csg[t % 2]
                        E("act", lambda e, sgb=sgb, ppg=ppg: e.activation(out=sgb.ap, in_=ppg.ap, func=AF.Sigmoid), reads=[ppg], writes=[sgb])
                        pc = cpc[t % 2]
                        E("dve", lambda e, pc=pc, sgb=sgb, ppl=ppl: e.tensor_tensor(out=pc.ap, in0=sgb.ap, in1=ppl.ap, op=ALU.mult),
                          reads=[sgb, ppl], writes=[pc])
                        E("dve", lambda e, pc=pc, t=t, hh=hh: e.tensor_tensor(
                            out=z[t].ap[:, hh * 512:(hh + 1) * 512], in0=z[t].ap[:, hh * 512:(hh + 1) * 512], in1=pc.ap, op=ALU.add),
                          reads=[pc, z[t]], writes=[z[t]])
                ln_stats(z)
                for t in range(4):
                    ln_norm_tile(z, 2, t)
                for t in range(4):
                    dsto = out_d[b, g * 512 + t * 128:g * 512 + (t + 1) * 128, :]
                    tok = E("pool", lambda e, t=t, dsto=dsto: e.dma_start(out=dsto, in_=z[t].ap), reads=[z[t]], stream=f"out{t}")
                    fin.append(tok)

            load_x(0)
            load_p(0)
            for j in range(4):
                xT_tile(j)
            load_x(1)
            c1(0)
            for g in range(4):
                mixed(g, g < 3)
                if g + 2 <= 3:
                    load_x(g + 2)
                ln_stats(z)
                if g < 3:
                    c1(g + 1, hook=c1_hook)
                else:
                    x1T_norm(0)
                    for t in range(4):
                        if t + 1 < 4:
                            x1T_norm(t + 1)
                        x1T_tr(t)
                ffn(g)
                down_ple_out(g)
        P.emit(nc, final_wait=fin[-8:] + fin[:2])
    return nc


def _t5_bucket_np(rel):
    n = np.maximum(rel, 0)
    max_exact = 16
    nf = np.maximum(n, 1).astype(np.float32)
    large = max_exact + (np.log(nf / np.float32(max_exact)) / np.float32(np.log(128 / max_exact))
                         * np.float32(32 - max_exact)).astype(np.int32)
    large = np.minimum(large, 31)
    return np.where(n < max_exact, n, large)


_NC_CACHE = {}
_LAST = []


def kernel(x, p, w_in, b_gate, bias_table, w_att_out, conv_w, conv_b, conv_ln_g, conv_ln_b,
           w_conv_out, w_mix_out, ln_mix_g, ln_mix_b, w_ffn_gate, w_ffn_up, w_ffn_down,
           w_ple, w_ple_gate, b_ple_gate, ln_ffn_g, ln_ffn_b):
    f = lambda a: np.ascontiguousarray(np.asarray(a, dtype=np.float32))
    x = f(x)
    p = f(p)
    B = x.shape[0]
    NB = B // NCORES
    cvec = np.zeros((128, NCV), np.float32)
    cvec[:, 0:16] = f(b_gate)[0].reshape(16, 128).T
    cvec[:, 16:20] = f(conv_b)[0].reshape(4, 128).T
    cvec[:, 20:24] = f(conv_ln_g)[0].reshape(4, 128).T
    cvec[:, 24:28] = f(conv_ln_b)[0].reshape(4, 128).T
    cw = f(conv_w)[0]
    cvec[:, 28:152] = cw.T.reshape(4, 128, CW).transpose(1, 0, 2).reshape(128, 4 * CW)
    bt = f(bias_table)
    cvec[:, 152:160] = bt[31][None, :]
    cvec[:, 160] = EPS
    lnv = np.stack([np.broadcast_to(f(v)[0][None, :], (128, D)) for v in (ln_mix_g, ln_mix_b, ln_ffn_g, ln_ffn_b)], axis=1)
    lnv = np.ascontiguousarray(lnv)
    kk = np.arange(128)[:, None]
    cc = np.arange(256)[None, :]
    bidx = _t5_bucket_np(cc - kk)
    toep = np.ascontiguousarray(bt[bidx].transpose(0, 2, 1))
    onehot = (np.arange(S)[None, :] // 256 == np.arange(8)[:, None]).astype(np.float32)
    shared = {
        "w_in": f(w_in)[0], "w_att_out": f(w_att_out)[0], "w_conv_out": f(w_conv_out)[0], "w_mix_out": f(w_mix_out)[0],
        "w_ffn_gate": f(w_ffn_gate)[0], "w_ffn_up": f(w_ffn_up)[0], "w_ffn_down": f(w_ffn_down)[0],
        "w_ple": f(w_ple)[0], "w_ple_gate": f(w_ple_gate)[0], "cvec": cvec, "lnv": lnv,
        "bple": f(b_ple_gate).reshape(1, D), "toep": toep, "onehot": onehot,
    }
    if NB not in _NC_CACHE:
        _NC_CACHE[NB] = build_nc(NB)
    nc = _NC_CACHE[NB]
    in_maps = []
    for c in range(NCORES):
        m = dict(shared)
        m["x"] = x[c * NB:(c + 1) * NB]
        m["p"] = p[0, c * NB:(c + 1) * NB]
        in_maps.append(m)
    res = run_bass_kernel_spmd(nc, in_maps, core_ids=list(range(NCORES)))
    if DBG:
        _LAST.append(res.results[0])
    return np.concatenate([r["out"] for r in res.results], axis=0).astype(np.float32)
```

```python
import contextlib
import numpy as np
import concourse.bass as bass
import concourse.mybir as mybir
from concourse.bass_utils import run_bass_kernel_spmd

F32 = mybir.dt.float32
BF = mybir.dt.bfloat16
AF = mybir.ActivationFunctionType
ALU = mybir.AluOpType
AX = mybir.AxisListType

NCORES = 8
D = 1024
S = 2048
NH = 8
DH = 64
NIN = 4608
FFN = 2816
NJ = FFN // 128
PLE = 256
CW = 31
ALPHA = 2.0 ** 0.25
EPS = 1e-5
BIG = 30000.0
NCV = 161
DBG = False
PE_BIAS = False

ENGS = ("pe", "act", "dve", "pool", "sp")


class Tok:
    __slots__ = ("eng", "idx", "sem", "val", "is_dma", "stream", "key")

    def __init__(self, eng, idx):
        self.eng = eng
        self.idx = idx
        self.sem = None
        self.val = None
        self.is_dma = False
        self.stream = None
        self.key = eng


class Buf:
    def __init__(self, ap, name):
        self.ap = ap
        self.name = name
        self.w = {}
        self.r = {}


class Prog:
    SEM_ROT = 4000

    def __init__(self):
        self.ops = {e: [] for e in ENGS}
        self.streams = {}
        self.need = set()
        self.last = {}

    def op(self, eng, fn, deps=(), stream=None):
        t = Tok(eng, len(self.ops[eng]))
        deps = [d for d in deps if d is not None]
        if stream is not None:
            t.is_dma = True
            t.stream = stream
            t.key = "s:" + stream
            n = self.streams.get(stream, 0) + 1
            self.streams[stream] = n
            t.val = 16 * n
        self.ops[eng].append((t, fn, deps))
        for d in deps:
            if not d.is_dma:
                self.need.add((d.eng, d.idx))
        self.last[t.key] = t
        return t

    def E(self, eng, fn, reads=(), writes=(), stream=None, extra=()):
        deps = {}

        def add(t):
            k = t.key
            o = deps.get(k)
            if o is None or t.idx > o.idx or (t.is_dma and t.val > o.val):
                deps[k] = t

        for t in extra:
            if t is not None:
                add(t)
        for b in reads:
            for t in b.w.values():
                add(t)
        for b in writes:
            for t in b.w.values():
                add(t)
            for t in b.r.values():
                add(t)
        if eng == "pe":
            deps.pop("pe", None)
        tok = self.op(eng, fn, list(deps.values()), stream=stream)
        for b in reads:
            b.r[tok.key] = tok
        for b in writes:
            b.w[tok.key] = tok
        return tok

    def barrier(self, bufs):
        snap = dict(self.last)
        for b in bufs:
            for k, t in snap.items():
                b.r[k] = t
            b.w = {}

    def emit(self, nc, final_wait=()):
        with contextlib.ExitStack() as es:
            nsem = 0
            for e in ENGS:
                cnt = 0
                cur = None
                for t, fn, deps in self.ops[e]:
                    if t.is_dma:
                        continue
                    if (e, t.idx) in self.need:
                        if cur is None or cnt >= self.SEM_ROT:
                            cur = es.enter_context(nc.semaphore(f"c_{e}_{nsem}"))
                            nsem += 1
                            cnt = 0
                        cnt += 1
                        t.sem = cur
                        t.val = cnt
            dsem = {}
            for s in self.streams:
                dsem[s] = es.enter_context(nc.semaphore(f"d_{s}"))
            for e in ENGS:
                for t, fn, deps in self.ops[e]:
                    if t.is_dma:
                        t.sem = dsem[t.stream]
            block = es.enter_context(nc.Block())

            def make(e):
                def body(eng):
                    waited = {}
                    for t, fn, deps in self.ops[e]:
                        for d in deps:
                            k = id(d.sem)
                            if waited.get(k, 0) >= d.val:
                                continue
                            eng.wait_ge(d.sem, d.val)
                            waited[k] = d.val
                        ins = fn(eng)
                        if t.is_dma:
                            ins.then_inc(t.sem, 16)
                        elif t.sem is not None:
                            ins.then_inc(t.sem, 1)
                    if e == "sp":
                        for d in final_wait:
                            eng.wait_ge(d.sem, d.val)

                return body

            block.tensor(make("pe"))
            block.scalar(make("act"))
            block.vector(make("dve"))
            block.gpsimd(make("pool"))
            block.sync(make("sp"))


def build_nc(NB):
    nc = bass.Bass("TRN2", target_bir_lowering=False)
    dr = {}

    def din(name, shape):
        dr[name] = nc.dram_tensor(name, list(shape), F32, kind="ExternalInput").ap()
        return dr[name]

    x_d = din("x", [NB, S, D])
    p_d = din("p", [NB, S, PLE])
    w_in = din("w_in", [D, NIN]).rearrange("(kc p) n -> p kc n", p=128)
    w_ao = din("w_att_out", [512, D]).rearrange("(kc p) n -> p kc n", p=128)
    w_co = din("w_conv_out", [512, D]).rearrange("(kc p) n -> p kc n", p=128)
    w_mix = din("w_mix_out", [D, D]).rearrange("(kc p) n -> p kc n", p=128)
    w_fg = din("w_ffn_gate", [D, FFN]).rearrange("(kc p) n -> p kc n", p=128)
    w_fu = din("w_ffn_up", [D, FFN]).rearrange("(kc p) n -> p kc n", p=128)
    w_fd = din("w_ffn_down", [FFN, D]).rearrange("(kc p) n -> p kc n", p=128)
    w_pl = din("w_ple", [PLE, D]).rearrange("(kc p) n -> p kc n", p=128)
    w_pg = din("w_ple_gate", [D, D]).rearrange("(kc p) n -> p kc n", p=128)
    cvec_d = din("cvec", [128, NCV])
    lnv_d = din("lnv", [128, 4, D])
    bple_d = din("bple", [1, D])
    toep_d = din("toep", [128, NH, 256])
    oneh_d = din("onehot", [8, S])
    out_d = nc.dram_tensor("out", [NB, S, D], F32, kind="ExternalOutput").ap()

    P = Prog()
    E = P.E
    fin = []
    if DBG:
        dbg_att = nc.dram_tensor("dbg_att", [128, 4, S], F32, kind="ExternalOutput").ap()
        dbg_c = nc.dram_tensor("dbg_c", [128, 4, S], F32, kind="ExternalOutput").ap()

    with contextlib.ExitStack() as es:
        def sb(name, shape, dt):
            return es.enter_context(nc.sbuf_tensor("sb_" + name, list(shape), dt))

        def psum(name, shape, dt):
            return es.enter_context(nc.psum_tensor("ps_" + name, list(shape), dt))

        ident = Buf(sb("ident", [128, 128], BF)[:], "ident")
        ones_bf = Buf(sb("ones_bf", [128, 128], BF)[:], "ones")
        Ebf = Buf(sb("Ebf", [128, NH, 256], BF)[:], "Ebf")
        cvec = Buf(sb("cvec", [128, NCV], F32)[:], "cvec")
        negc = Buf(sb("negc", [128, NH], F32)[:], "negc")
        lnv = Buf(sb("lnv", [128, 4, D], F32)[:], "lnv")
        bple = Buf(sb("bple", [1, D], BF)[:], "bple")
        attT_t = sb("attT", [128, 4, S], BF)
        cT_t = sb("cT", [128, 4, S], BF)
        attT = [Buf(attT_t[:, :, g * 512:(g + 1) * 512], f"attT{g}") for g in range(4)]
        cT = [Buf(cT_t[:, c, :], f"cT{c}") for c in range(4)]
        NRING = 6
        ring = [Buf(sb(f"ring{i}", [128, 4096], BF)[:], f"ring{i}") for i in range(NRING)]
        ring_i = [0]

        def ring_next():
            b = ring[ring_i[0] % NRING]
            ring_i[0] += 1
            return b

        def slab(b, kc, n):
            return b.ap[:, 0:kc * n].rearrange("p (k n) -> p k n", k=kc)

        def wload(b, kc, n, src, k0=0, kn=None, n0=0):
            kn = kc - k0 if kn is None else kn
            nn = src.shape[2]
            dst = slab(b, kc, n)[:, k0:k0 + kn, n0:n0 + nn]
            return E("pool", lambda e: e.dma_start(out=dst, in_=src), writes=[b], stream=b.name)

        SD = []
        SD.append((4, 1024, w_ao, 4))
        SD.append((8, 512, w_in[:, :, 2560:3072], 8))
        SD.append((8, 512, w_in[:, :, 3072:3584], 8))
        SD.append((4, 1024, w_co, 4))
        SD.append((8, 512, w_in[:, :, 3584:4096], 8))
        SD.append((8, 512, w_in[:, :, 4096:4608], 8))
        SD.append((8, 512, w_mix[:, :, 0:512], 8))
        SD.append((8, 512, w_mix[:, :, 512:1024], 8))
        for jq in range(6):
            ncol = min(512, FFN - jq * 512)
            SD.append((8, 512, w_fg[:, :, jq * 512:jq * 512 + ncol], 8))
            SD.append((8, 512, w_fu[:, :, jq * 512:jq * 512 + ncol], 8))
        for hh in range(2):
            for sl in range(3):
                k0 = sl * 8
                kn = min(8, NJ - k0)
                SD.append((8, 512, w_fd[:, k0:k0 + kn, hh * 512:(hh + 1) * 512], kn))
        SD.append((2, 1024, w_pl, 2))
        SD.append((8, 512, w_pg[:, :, 0:512], 8))
        SD.append((8, 512, w_pg[:, :, 512:1024], 8))
        NSL = len(SD)
        wscr = nc.dram_tensor("wscr", [NSL, 128, 4096], BF, kind="Internal").ap()
        scrb = [Buf(None, f"scr{i}") for i in range(NSL)]

        def convert_slab(i):
            kc, n, src, kn = SD[i]
            rb = ring_next()
            wload(rb, kc, n, src, kn=kn)
            E("sp", lambda e: e.dma_start(out=wscr[i], in_=rb.ap), reads=[rb], writes=[scrb[i]], stream=f"scr{i}")

        def wl(i):
            rb = ring_next()
            E("sp", lambda e: e.dma_start(out=rb.ap, in_=wscr[i]), reads=[scrb[i]], writes=[rb], stream=rb.name)
            return rb

        NU = 51296
        U = sb("U", [128, NU], BF)
        off = [0]

        def carve(nel, dt):
            nb = nel * (4 if dt == F32 else 2)
            nb = (nb + 63) // 64 * 64
            a = off[0] // 2
            off[0] += nb
            assert off[0] <= NU * 2, ("union overflow", off[0])
            v = U[:, a:a + nb // 2]
            if dt == F32:
                v = v.bitcast(F32)[:, 0:nel]
            else:
                v = v[:, 0:nel]
            return v

        off[0] = 0
        xT_t = carve(8 * S, BF).rearrange("p (k t) -> p k t", k=8)
        xT = [Buf(xT_t[:, :, g * 512:(g + 1) * 512], f"xT{g}") for g in range(4)]
        uT_t = carve(4 * (S + 30), BF).rearrange("p (c t) -> p c t", c=4)
        uT = [Buf(uT_t[:, c, :], f"uT{c}") for c in range(4)]
        qT = Buf(carve(2 * S, BF).rearrange("p (h t) -> p h t", h=2), "qT")
        kT = Buf(carve(2 * S, BF).rearrange("p (h t) -> p h t", h=2), "kT")
        vA = Buf(carve(16 * 2 * 65, BF).rearrange("p (j h d) -> p j h d", j=16, h=2), "vA")
        off_alias = off[0]
        xb = [Buf(carve(2 * D, BF).rearrange("p (j d) -> p j d", j=2), f"xb{i}") for i in range(2)]
        diag = Buf(carve(CW * 128, BF).rearrange("p (j m) -> p j m", j=CW), "diag")
        sgt = [Buf(carve(512, F32), f"sgt{i}") for i in range(2)]
        off_alias_end = off[0]
        PT = [Buf(carve(512, BF), f"PT{i}") for i in range(4)]
        att_tm = [Buf(carve(4 * 128, BF).rearrange("p (s d) -> p s d", s=4), f"atm{i}") for i in range(2)]
        ksum = Buf(carve(8, F32), "ksum")
        kmean = Buf(carve(8, BF), "kmean")
        gs = Buf(carve(128, F32).rearrange("p (a n) -> p a n", a=16), "gs")
        g2 = Buf(carve(128, F32).rearrange("p (a n) -> p a n", a=16), "g2")
        eq = Buf(carve(128, F32).rearrange("p (a n) -> p a n", a=16), "eq")
        mx = Buf(carve(16, F32), "mx")
        mpad = Buf(carve(8 * 2 * 72, BF).rearrange("p (q h n) -> p q h n", q=8, h=2), "mpad")
        rinv = Buf(carve(4, F32), "rinv")
        osb = [Buf(carve(4 * 65, F32).rearrange("p (s d) -> p s d", s=4), f"osb{i}") for i in range(2)]
        AB_END = off[0]
        off[0] = off_alias
        ysq = Buf(carve(4 * 512, BF).rearrange("p (c t) -> p c t", c=4), "ysq")
        mean_t = Buf(carve(512, F32), "mean")
        msq_t = Buf(carve(512, F32), "msq")
        rstd_t = Buf(carve(512, F32), "rstd")
        t1_t = [Buf(carve(512, F32), f"t1_{i}") for i in range(2)]
        assert off[0] <= off_alias_end, (off[0], off_alias_end)
        CLN_BUFS = [ysq, mean_t, msq_t, rstd_t] + t1_t
        AB_BUFS = (xT + uT + [qT, kT, vA] + xb + [diag] + sgt + [ysq, mean_t, msq_t, rstd_t] + t1_t + PT
                   + att_tm + osb + [ksum, kmean, gs, g2, eq, mx, mpad, rinv])

        off[0] = 0
        xres = [Buf(carve(D, F32), f"xres{i}") for i in range(2)]
        z_t = carve(4 * D, F32).rearrange("p (t d) -> p t d", t=4)
        z = [Buf(z_t[:, t, :], f"z{t}") for t in range(4)]
        mT = Buf(carve(8 * 512, BF).rearrange("p (k t) -> p k t", k=8), "mT")
        csg = [Buf(carve(512, F32), f"csg{i}") for i in range(2)]
        cpc = [Buf(carve(512, F32), f"cpc{i}") for i in range(2)]
        x1b = [Buf(carve(D, BF), f"x1b{i}") for i in range(2)]
        x1T = Buf(carve(8 * 512, BF).rearrange("p (k t) -> p k t", k=8), "x1T")
        hT_raw = carve(NJ * 512, BF)
        hT = Buf(hT_raw.rearrange("p (j t) -> p j t", j=NJ), "hT")
        hTf = hT_raw[:, 0:8192].bitcast(F32).rearrange("p (m t) -> p m t", m=8)
        xTg = Buf(carve(8 * 512, BF).rearrange("p (k t) -> p k t", k=8), "xTg")
        xbg = Buf(carve(4 * D, BF).rearrange("p (j d) -> p j d", j=4), "xbg")
        pbg = Buf(carve(4 * PLE, BF).rearrange("p (j d) -> p j d", j=4), "pbg")
        pT = Buf(carve(2 * 512, BF).rearrange("p (k t) -> p k t", k=2), "pT")
        lst_ = [Buf(carve(12, F32).rearrange("p (a b) -> p a b", a=2), f"lst{i}") for i in range(4)]
        lmv_ = [Buf(carve(2, F32), f"lmv{i}") for i in range(4)]
        lsd_ = [Buf(carve(1, F32), f"lsd{i}") for i in range(4)]
        lrs_ = [Buf(carve(1, F32), f"lrs{i}") for i in range(4)]
        lnm_ = [Buf(carve(1, F32), f"lnm{i}") for i in range(4)]
        C_END = off[0]
        C_BUFS = (xres + z + [mT] + csg + cpc + x1b + [x1T, hT, xTg, xbg, pbg, pT] + lst_ + lmv_ + lsd_ + lrs_ + lnm_)

        PA = [Buf(psum(f"pa{i}", [128, 512], F32)[:], f"pa{i}") for i in range(3)]
        PBt = [psum(f"pb{i}", [128, 1024], F32) for i in range(2)]
        PBh = [Buf(PBt[i][:, h * 512:(h + 1) * 512], f"pb{i}{h}") for i in range(2) for h in range(2)]
        PM_t = psum("pm", [128, 1024], BF)
        PM = Buf(PM_t[:], "pm")
        pa_i = [0]

        WIDE = PA + PBh
        pw_i = [0]

        def pa_next(wide=False):
            if wide:
                b = WIDE[pw_i[0] % 7]
                pw_i[0] += 1
                return b
            b = PA[pa_i[0] % 3]
            pa_i[0] += 1
            return b

        E("sp", lambda e: e.dma_start(out=cvec.ap, in_=cvec_d), writes=[cvec], stream="c_cvec")
        E("sp", lambda e: e.dma_start(out=lnv.ap, in_=lnv_d), writes=[lnv], stream="c_lnv")
        E("pool", lambda e: e.dma_start(out=bple.ap, in_=bple_d), writes=[bple], stream="c_bple")
        idf = Buf(U[:, 0:256].bitcast(F32), "idf")
        tf = Buf(U[:, 4096:4096 + NH * 256 * 2].bitcast(F32).rearrange("p (h c) -> p h c", h=NH), "tf")
        E("dve", lambda e: e.memset(idf.ap, 0.0), writes=[idf])
        E("pool", lambda e: e.affine_select(out=idf.ap, in_=idf.ap, pattern=[[-1, 128]], compare_op=ALU.not_equal,
                                            fill=1.0, base=0, channel_multiplier=1), reads=[idf], writes=[idf])
        E("dve", lambda e: e.tensor_copy(out=ident.ap, in_=idf.ap), reads=[idf], writes=[ident])
        E("dve", lambda e: e.memset(ones_bf.ap, 1.0), writes=[ones_bf])
        E("sp", lambda e: e.dma_start(out=tf.ap, in_=toep_d), writes=[tf], stream="c_toep")
        E("dve", lambda e: e.tensor_scalar(out=negc.ap, in0=cvec.ap[:, 152:160], scalar1=-1.0, scalar2=None, op0=ALU.mult),
          reads=[cvec], writes=[negc])
        for h in range(NH):
            if PE_BIAS:
                E("dve", lambda e, h=h: e.tensor_scalar(out=tf.ap[:, h, :], in0=tf.ap[:, h, :], scalar1=negc.ap[:, h:h + 1],
                                                        scalar2=1.0 / (DH ** -0.5), op0=ALU.add, op1=ALU.mult),
                  reads=[tf, negc], writes=[tf])
            else:
                E("act", lambda e, h=h: e.activation(out=tf.ap[:, h, :], in_=tf.ap[:, h, :], func=AF.Exp,
                                                     bias=negc.ap[:, h:h + 1], scale=1.0), reads=[tf, negc], writes=[tf])
        E("pool", lambda e: e.affine_select(out=tf.ap, in_=tf.ap, pattern=[[0, NH], [1, 256]], compare_op=ALU.is_ge,
                                            fill=(-BIG / (DH ** -0.5) if PE_BIAS else 0.0), base=0, channel_multiplier=-1), reads=[tf], writes=[tf])
        E("dve", lambda e: e.tensor_copy(out=Ebf.ap, in_=tf.ap), reads=[tf], writes=[Ebf])

        cv = cvec.ap
        pen = Buf(sb("pen", [128, 16, 8], F32)[:], "pen")
        val = Buf(sb("val", [128, 16, 8], F32)[:], "val")
        E("dve", lambda e: e.memset(pen.ap, 0.0), writes=[pen])
        E("dve", lambda e: e.memset(val.ap, 1.0), writes=[val])
        for qi in range(6):
            bq_ = 4 + qi // 2
            E("dve", lambda e, qi=qi, bq_=bq_: e.memset(pen.ap[:, 2 * qi:2 * qi + 2, bq_:8], -1e30), writes=[pen])
            E("dve", lambda e, qi=qi, bq_=bq_: e.memset(val.ap[:, 2 * qi:2 * qi + 2, bq_:8], 0.0), writes=[val])
        for qi in range(6, 8):
            E("dve", lambda e, qi=qi: e.memset(pen.ap[:, 2 * qi:2 * qi + 2, 7:8], -1e30), writes=[pen])
            E("dve", lambda e, qi=qi: e.memset(val.ap[:, 2 * qi:2 * qi + 2, 7:8], 0.0), writes=[val])

        PM2_v = PA[2].ap.bitcast(BF)

        def transposes_to(dst_ap, dst_buf, src_ap_fn, src_buf, n, evac_eng, alt=False):
            pmv, PMb = (PM2_v, PA[2]) if alt else (PM_t, PM)
            for i in range(n):
                E("pe", lambda e, i=i: e.transpose(out=pmv[:, i * 128:(i + 1) * 128], in_=src_ap_fn(i), identity=ident.ap),
                  reads=[src_buf, ident], writes=[PMb])
            src = pmv[:, 0:n * 128]
            if len(dst_ap.shape) == 3:
                src = src.rearrange("p (k t) -> p k t", k=n)
            if evac_eng == "act":
                return E("act", lambda e: e.activation(out=dst_ap, in_=src, func=AF.Copy), reads=[PMb], writes=[dst_buf])
            return E("dve", lambda e: e.tensor_copy(out=dst_ap, in_=src), reads=[PMb], writes=[dst_buf])

        def ln_stats(zs):
            for ti, zb in enumerate(zs):
                for hh in range(2):
                    E("dve", lambda e, hh=hh, ti=ti, zb=zb: e.bn_stats(out=lst_[ti].ap[:, hh, :], in_=zb.ap[:, hh * 512:(hh + 1) * 512]),
                      reads=[zb], writes=[lst_[ti]])
                E("dve", lambda e, ti=ti: e.bn_aggr(out=lmv_[ti].ap, in_=lst_[ti].ap), reads=[lst_[ti]], writes=[lmv_[ti]])
            for ti, zb in enumerate(zs):
                E("act", lambda e, ti=ti: e.activation(out=lsd_[ti].ap, in_=lmv_[ti].ap[:, 1:2], func=AF.Sqrt, bias=cv[:, 160:161], scale=1.0),
                  reads=[lmv_[ti], cvec], writes=[lsd_[ti]])

        def ln_norm_tile(zs, gi, ti):
            zb = zs[ti]
            E("dve", lambda e: e.reciprocal(out=lrs_[ti].ap, in_=lsd_[ti].ap), reads=[lsd_[ti]], writes=[lrs_[ti]])
            E("dve", lambda e: e.scalar_tensor_tensor(out=zb.ap, in0=zb.ap, scalar=lmv_[ti].ap[:, 0:1], in1=lnv.ap[:, gi, :],
                                                      op0=ALU.subtract, op1=ALU.mult), reads=[zb, lmv_[ti], lnv], writes=[zb])
            E("dve", lambda e: e.scalar_tensor_tensor(out=zb.ap, in0=zb.ap, scalar=lrs_[ti].ap, in1=lnv.ap[:, gi + 1, :],
                                                      op0=ALU.mult, op1=ALU.add), reads=[zb, lrs_[ti], lnv], writes=[zb])

        def mm(out_ap, out_buf, lhsT, rhs, rd, start, stop):
            return E("pe", lambda e: e.matmul(out_ap, lhsT=lhsT, rhs=rhs, start=start, stop=stop),
                     reads=rd, writes=[out_buf])

        for b in range(NB):
            P.barrier(AB_BUFS)
            for c in range(4):
                E("dve", lambda e, c=c: e.memset(uT_t[:, c, 0:30], 0.0), writes=[uT[c]])
            E("dve", lambda e: e.memset(kT.ap[0:64, 1, :], 0.0), writes=[kT])
            E("dve", lambda e: e.memset(qT.ap[0:64, 1, :], 0.0), writes=[qT])
            E("dve", lambda e: e.memset(qT.ap[64:72, 0, :], 0.0), writes=[qT])
            E("dve", lambda e: e.memset(vA.ap[:, :, :, 64:65], 1.0), writes=[vA])
            E("dve", lambda e: e.memset(mpad.ap, 0.0), writes=[mpad])
            E("pool", lambda e: e.dma_start(out=kT.ap[64:72, 0, :], in_=oneh_d), writes=[kT], stream="oh0")
            E("pool", lambda e: e.dma_start(out=kT.ap[0:8, 1, :], in_=oneh_d), writes=[kT], stream="oh1")

            for i in range(8):
                xbb = xb[i % 2]
                src = x_d[b, i * 256:(i + 1) * 256, :].rearrange("(j p) d -> p j d", p=128)
                E("pool", lambda e, xbb=xbb, src=src: e.dma_start(out=xbb.ap, in_=src), writes=[xbb], stream=xbb.name)
                g = i // 2
                for j in range(2):
                    tt = (i % 2) * 2 + j
                    dst = xT[g].ap[:, :, tt * 128:(tt + 1) * 128]
                    transposes_to(dst, xT[g], lambda k, xbb=xbb, j=j: xbb.ap[:, j, k * 128:(k + 1) * 128], xbb, 8,
                                  "act" if j == 0 else "dve", alt=(j == 1))

            ra = ring_next()
            wload(ra, 8, 512, w_in[:, :, 1536:2048])
            rg = ring_next()
            wload(rg, 8, 512, w_in[:, :, 2048:2560])
            sa = slab(ra, 8, 512)
            sg_ = slab(rg, 8, 512)
            gi = 0
            for g in range(4):
                for c in range(4):
                    pa_ = pa_next(True)
                    pg_ = pa_next(True)
                    for kc in range(8):
                        mm(pa_.ap, pa_, sa[:, kc, c * 128:(c + 1) * 128], xT[g].ap[:, kc, :], [ra, xT[g]], kc == 0, kc == 7)
                    for kc in range(8):
                        mm(pg_.ap, pg_, sg_[:, kc, c * 128:(c + 1) * 128], xT[g].ap[:, kc, :], [rg, xT[g]], kc == 0, kc == 7)
                    st = sgt[gi % 2]
                    gi += 1
                    E("act", lambda e, st=st, pg_=pg_: e.activation(out=st.ap, in_=pg_.ap, func=AF.Sigmoid),
                      reads=[pg_], writes=[st])
                    E("dve", lambda e, st=st, pa_=pa_, c=c, g=g: e.tensor_tensor(
                        out=uT_t[:, c, 30 + g * 512:30 + (g + 1) * 512], in0=pa_.ap, in1=st.ap, op=ALU.mult),
                      reads=[pa_, st], writes=[uT[c]])

            def conv_diag(c, part=None):
                for q_ in (range(4) if part is None else [part]):
                    j0 = q_ * 8
                    nj = min(8, CW - j0)
                    E("dve", lambda e, j0=j0, nj=nj: e.tensor_tensor(
                        out=diag.ap[:, j0:j0 + nj, :], in0=ident.ap.unsqueeze(1).to_broadcast([128, nj, 128]),
                        in1=cv[:, 28 + c * CW + j0:28 + c * CW + j0 + nj].unsqueeze(2).to_broadcast([128, nj, 128]), op=ALU.mult),
                      reads=[ident, cvec], writes=[diag])

            def conv_chunk(c):
                for g in range(4):
                    pc_ = pa_next()
                    for j in range(CW):
                        mm(pc_.ap, pc_, diag.ap[:, j, :], uT_t[:, c, g * 512 + j:g * 512 + j + 512], [diag, uT[c]], j == 0, j == CW - 1)
                    E("act", lambda e, pc_=pc_, g=g: e.activation(out=cT_t[:, c, g * 512:(g + 1) * 512], in_=pc_.ap, func=AF.Identity,
                                                                 bias=cv[:, 16 + c:17 + c], scale=1.0), reads=[pc_, cvec], writes=[cT[c]])

            def conv_ln(g):
                gr = slice(g * 512, (g + 1) * 512)
                for c in range(4):
                    E("dve", lambda e, c=c: e.tensor_tensor(out=ysq.ap[:, c, :], in0=cT_t[:, c, gr], in1=cT_t[:, c, gr], op=ALU.mult),
                      reads=[cT[c]], writes=[ysq])
                s1 = pa_next()
                s2 = pa_next()
                for c in range(4):
                    mm(s1.ap, s1, ones_bf.ap, cT_t[:, c, gr], [ones_bf, cT[c]], c == 0, c == 3)
                for c in range(4):
                    mm(s2.ap, s2, ones_bf.ap, ysq.ap[:, c, :], [ones_bf, ysq], c == 0, c == 3)
                E("dve", lambda e: e.tensor_scalar(out=mean_t.ap, in0=s1.ap, scalar1=1.0 / 512, scalar2=None, op0=ALU.mult),
                  reads=[s1], writes=[mean_t])
                E("dve", lambda e: e.tensor_tensor(out=msq_t.ap, in0=mean_t.ap, in1=mean_t.ap, op=ALU.mult),
                  reads=[mean_t], writes=[msq_t])
                E("dve", lambda e: e.scalar_tensor_tensor(out=msq_t.ap, in0=s2.ap, scalar=1.0 / 512, in1=msq_t.ap,
                                                          op0=ALU.mult, op1=ALU.subtract), reads=[s2, msq_t], writes=[msq_t])
                E("act", lambda e: e.activation(out=rstd_t.ap, in_=msq_t.ap, func=AF.Sqrt, bias=cv[:, 160:161], scale=1.0),
                  reads=[msq_t, cvec], writes=[rstd_t])
                E("dve", lambda e: e.reciprocal(out=rstd_t.ap, in_=rstd_t.ap), reads=[rstd_t], writes=[rstd_t])
                for c in range(4):
                    t1 = t1_t[c % 2]
                    E("dve", lambda e, c=c, t1=t1: e.tensor_tensor(out=t1.ap, in0=cT_t[:, c, gr], in1=mean_t.ap, op=ALU.subtract),
                      reads=[cT[c], mean_t], writes=[t1])
                    E("dve", lambda e, t1=t1: e.tensor_tensor(out=t1.ap, in0=t1.ap, in1=rstd_t.ap, op=ALU.mult),
                      reads=[t1, rstd_t], writes=[t1])
                    E("act", lambda e, c=c, t1=t1: e.activation(out=cT_t[:, c, gr], in_=t1.ap, func=AF.Silu,
                                                               bias=cv[:, 24 + c:25 + c], scale=cv[:, 20 + c:21 + c]),
                      reads=[t1, cvec], writes=[cT[c]])

            conv_diag(0)
            for hp in range(4):
                rq = ring_next()
                wload(rq, 8, 384, w_in[:, :, hp * 128:(hp + 1) * 128], n0=0)
                wload(rq, 8, 384, w_in[:, :, 512 + hp * 128:512 + (hp + 1) * 128], n0=128)
                wload(rq, 8, 384, w_in[:, :, 1024 + hp * 128:1024 + (hp + 1) * 128], n0=256)
                sq = slab(rq, 8, 384)
                for g in range(4):
                    gr = slice(g * 512, (g + 1) * 512)
                    pq = pa_next()
                    for kc in range(8):
                        mm(pq.ap, pq, sq[:, kc, 0:128], xT[g].ap[:, kc, :], [rq, xT[g]], kc == 0, kc == 7)
                    E("dve", lambda e, pq=pq, gr=gr: e.tensor_copy(out=qT.ap[0:64, 0, gr], in_=pq.ap[0:64, :]),
                      reads=[pq], writes=[qT])
                    E("dve", lambda e, pq=pq, gr=gr: e.tensor_copy(out=qT.ap[64:128, 1, gr], in_=pq.ap[64:128, :]),
                      reads=[pq], writes=[qT])
                    pk = pa_next()
                    for kc in range(8):
                        mm(pk.ap, pk, sq[:, kc, 128:256], xT[g].ap[:, kc, :], [rq, xT[g]], kc == 0, kc == 7)
                    E("dve", lambda e, pk=pk, gr=gr: e.tensor_copy(out=kT.ap[0:64, 0, gr], in_=pk.ap[0:64, :]),
                      reads=[pk], writes=[kT])
                    E("dve", lambda e, pk=pk, gr=gr: e.tensor_copy(out=kT.ap[64:128, 1, gr], in_=pk.ap[64:128, :]),
                      reads=[pk], writes=[kT])
                    E("dve", lambda e, pk=pk, g=g: e.tensor_reduce(out=ksum.ap[:, 2 * g:2 * g + 2],
                                                                  in_=pk.ap.rearrange("p (a b) -> p a b", a=2),
                                                                  axis=AX.X, op=ALU.add), reads=[pk], writes=[ksum])
                    pv = pa_next()
                    for j in range(4):
                        for kc in range(8):
                            mm(pv.ap[:, j * 128:(j + 1) * 128], pv, xT[g].ap[:, kc, j * 128:(j + 1) * 128], sq[:, kc, 256:384],
                               [rq, xT[g]], kc == 0, kc == 7)
                    E("act", lambda e, pv=pv, g=g: e.activation(
                        out=vA.ap[:, 4 * g:4 * g + 4, :, 0:64],
                        in_=pv.ap.rearrange("p (j h d) -> p j h d", j=4, h=2), func=AF.Copy), reads=[pv], writes=[vA])
                E("dve", lambda e: e.tensor_scalar(out=kmean.ap, in0=ksum.ap, scalar1=1.0 / 256, scalar2=None, op0=ALU.mult),
                  reads=[ksum], writes=[kmean])
                if b == 0:
                    for i in range(NSL):
                        if i * 4 // NSL == hp:
                            convert_slab(i)
                pg_ = pa_next()
                for qi in range(8):
                    qs = slice((8 + qi) * 128, (9 + qi) * 128)
                    mm(pg_.ap[:, qi * 16:qi * 16 + 8], pg_, qT.ap[0:64, 0, qs], kmean.ap[0:64, :], [qT, kmean], True, True)
                    mm(pg_.ap[:, qi * 16 + 8:qi * 16 + 16], pg_, qT.ap[64:128, 1, qs], kmean.ap[64:128, :], [qT, kmean], True, True)
                pgv = pg_.ap[:, 0:128].rearrange("p (a n) -> p a n", a=16)
                E("dve", lambda e, pgv=pgv: e.tensor_tensor(out=gs.ap, in0=pgv, in1=pen.ap, op=ALU.add), reads=[pg_, pen], writes=[gs])
                conv_chunk(hp)
                cur = gs
                for it in range(3):
                    E("dve", lambda e, cur=cur: e.tensor_reduce(out=mx.ap, in_=cur.ap, axis=AX.X, op=ALU.max), reads=[cur], writes=[mx])
                    if it == 2:
                        break
                    E("dve", lambda e, cur=cur: e.tensor_tensor(out=eq.ap, in0=cur.ap, in1=mx.ap.unsqueeze(2).to_broadcast([128, 16, 8]),
                                                               op=ALU.is_equal), reads=[cur, mx], writes=[eq])
                    E("dve", lambda e, cur=cur: e.scalar_tensor_tensor(out=g2.ap, in0=eq.ap, scalar=-1e30, in1=cur.ap,
                                                                      op0=ALU.mult, op1=ALU.add), reads=[eq, cur], writes=[g2])
                    cur = g2
                E("dve", lambda e: e.tensor_tensor(out=eq.ap, in0=gs.ap, in1=mx.ap.unsqueeze(2).to_broadcast([128, 16, 8]), op=ALU.is_ge),
                  reads=[gs, mx], writes=[eq])
                E("dve", lambda e: e.tensor_scalar(out=g2.ap, in0=eq.ap, scalar1=-1.0, scalar2=BIG, op0=ALU.add, op1=ALU.mult),
                  reads=[eq], writes=[g2])
                g2v = g2.ap.rearrange("p (q h) n -> p q h n", h=2)
                valv = val.ap.rearrange("p (q h) n -> p q h n", h=2)
                E("dve", lambda e: e.tensor_tensor(out=mpad.ap[:, :, 0, 64:72], in0=g2v[:, :, 0, :], in1=valv[:, :, 0, :], op=ALU.mult),
                  reads=[g2, val], writes=[mpad])
                E("dve", lambda e: e.tensor_tensor(out=mpad.ap[:, :, 1, 0:8], in0=g2v[:, :, 1, :], in1=valv[:, :, 1, :], op=ALU.mult),
                  reads=[g2, val], writes=[mpad])
                for qi in range(8):
                    mm(PBt[0][0:72, qi * 128:(qi + 1) * 128], PBh[qi // 4], mpad.ap[:, qi, 0, 0:72], ident.ap, [mpad, ident], True, True)
                for qi in range(8):
                    mm(PBt[1][0:8, qi * 128:(qi + 1) * 128], PBh[2 + qi // 4], mpad.ap[:, qi, 1, 0:8], ident.ap, [mpad, ident], True, True)
                E("act", lambda e: e.activation(out=qT.ap[64:72, 0, 1024:2048], in_=PBt[0][64:72, :], func=AF.Copy),
                  reads=[PBh[0], PBh[1]], writes=[qT])
                E("act", lambda e: e.activation(out=qT.ap[0:8, 1, 1024:2048], in_=PBt[1][0:8, :], func=AF.Copy),
                  reads=[PBh[2], PBh[3]], writes=[qT])
                cth = []
                it_n = 0
                its = [(G, h, j) for G in range(4) for h in range(2) for j in range(4 * G + 4)]
                LA = 2
                pts = {}

                def emit_scores(k):
                    G, h, j = its[k]
                    rows = slice(0, 72) if h == 0 else slice(0, 128)
                    d = j - 4 * G
                    c0 = max(0, d) * 128
                    N = 512 - c0
                    ps_ = pa_next()
                    hh = 2 * hp + h
                    mm(ps_.ap[:, 0:N], ps_, kT.ap[rows, h, j * 128:(j + 1) * 128],
                       qT.ap[rows, h, G * 512 + c0:(G + 1) * 512], [kT, qT], True, True)
                    pt = PT[k % 4]
                    pts[k] = pt
                    E("act", lambda e: e.activation(out=pt.ap[:, 0:N], in_=ps_.ap[:, 0:N], func=AF.Exp, scale=DH ** -0.5),
                      reads=[ps_], writes=[pt])
                    if d >= 0:
                        w_ = min(256, N)
                        E("dve", lambda e: e.tensor_tensor(out=pt.ap[:, 0:w_], in0=pt.ap[:, 0:w_], in1=Ebf.ap[:, hh, 0:w_], op=ALU.mult),
                          reads=[pt, Ebf], writes=[pt])
                    elif d == -1:
                        E("dve", lambda e: e.tensor_tensor(out=pt.ap[:, 0:128], in0=pt.ap[:, 0:128], in1=Ebf.ap[:, hh, 128:256], op=ALU.mult),
                          reads=[pt, Ebf], writes=[pt])

                def emit_pv(k):
                    G, h, j = its[k]
                    d = j - 4 * G
                    c0 = max(0, d) * 128
                    pt = pts.pop(k)
                    atm = att_tm[G % 2]
                    for s_ in range(max(0, d), 4):
                        lc = s_ * 128 - c0
                        mm(PBh[s_].ap[:, 0:65], PBh[s_], pt.ap[:, lc:lc + 128], vA.ap[:, j, h, :], [pt, vA], j == 0, j == 4 * G + s_)
                    if j == 4 * G + 3:
                        ob_ = osb[(2 * G + h) % 2]
                        for half in range(2):
                            srcv = PBt[half][:, :].rearrange("p (s c) -> p s c", s=2)[:, :, 0:65]
                            E("dve", lambda e, half=half, srcv=srcv: e.tensor_copy(out=ob_.ap[:, 2 * half:2 * half + 2, :], in_=srcv),
                              reads=[PBh[2 * half], PBh[2 * half + 1]], writes=[ob_])
                        E("dve", lambda e: e.reciprocal(out=rinv.ap, in_=ob_.ap[:, :, 64]), reads=[ob_], writes=[rinv])
                        E("dve", lambda e: e.tensor_tensor(out=atm.ap[:, :, h * 64:(h + 1) * 64], in0=ob_.ap[:, :, 0:64],
                                                           in1=rinv.ap.unsqueeze(2).to_broadcast([128, 4, 64]), op=ALU.mult),
                          reads=[ob_, rinv], writes=[atm])
                        if h == 1:
                            transposes_to(attT_t[:, hp, G * 512:(G + 1) * 512], attT[G], lambda s_: atm.ap[:, s_, :], atm, 4, "dve")

                if hp == 3:
                    P.barrier(CLN_BUFS)
                for k in range(len(its) + LA):
                    if k < len(its):
                        emit_scores(k)
                    if k - LA >= 0:
                        emit_pv(k - LA)
                    if hp == 3 and k in (16, 32, 48, 64):
                        conv_ln(k // 16 - 1)
                    if hp < 3 and k in (12, 20, 28, 36):
                        conv_diag(hp + 1, (k - 12) // 8)

            if DBG and b == 0:
                fin.append(E("pool", lambda e: e.dma_start(out=dbg_att, in_=attT_t[:]), reads=attT, stream="dbg1"))
                fin.append(E("pool", lambda e: e.dma_start(out=dbg_c, in_=cT_t[:]), reads=cT, stream="dbg2"))
            P.barrier(C_BUFS)

            def load_x(gg):
                srcx = x_d[b, gg * 512:(gg + 1) * 512, :].rearrange("(j p) d -> p j d", p=128)
                E("pool", lambda e: e.dma_start(out=xbg.ap, in_=srcx), writes=[xbg], stream="xbg")

            def load_p(gg):
                srcp = p_d[b, gg * 512:(gg + 1) * 512, :].rearrange("(j p) d -> p j d", p=128)
                E("pool", lambda e: e.dma_start(out=pbg.ap, in_=srcp), writes=[pbg], stream="pbg")

            def xT_tile(j):
                transposes_to(xTg.ap[:, :, j * 128:(j + 1) * 128], xTg, lambda k: xbg.ap[:, j, k * 128:(k + 1) * 128],
                              xbg, 8, "act")

            def pT_all():
                for j in range(4):
                    for k in range(2):
                        E("pe", lambda e, j=j, k=k: e.transpose(out=PM_t[:, (k * 4 + j) * 128:(k * 4 + j + 1) * 128],
                                                                in_=pbg.ap[:, j, k * 128:(k + 1) * 128], identity=ident.ap),
                          reads=[pbg, ident], writes=[PM])
                E("act", lambda e: e.activation(out=pT.ap, in_=PM_t[:, 0:1024].rearrange("p (k t) -> p k t", k=2), func=AF.Copy),
                  reads=[PM], writes=[pT])

            def c1(g, hook=None):
                gr = slice(g * 512, (g + 1) * 512)
                rao = wl(0)
                sao = slab(rao, 4, 1024)
                it = 0
                for part in range(2):
                    if part == 1:
                        rco = wl(3)
                        sco = slab(rco, 4, 1024)
                    for mh in range(2):
                        rgl = wl([1, 2, 4, 5][part * 2 + mh])
                        sgl = slab(rgl, 8, 512)
                        for mm_ in range(4):
                            m = mh * 4 + mm_
                            if hook is not None and it % 4 == 0:
                                hook(it)
                            py = pa_next(True)
                            pl = pa_next(True)
                            for kc in range(4):
                                if part == 0:
                                    mm(py.ap, py, sao[:, kc, m * 128:(m + 1) * 128], attT[g].ap[:, kc, :], [rao, attT[g]], kc == 0, kc == 3)
                                else:
                                    mm(py.ap, py, sco[:, kc, m * 128:(m + 1) * 128], cT_t[:, kc, gr], [rco, cT[kc]], kc == 0, kc == 3)
                            for kc in range(8):
                                mm(pl.ap, pl, sgl[:, kc, mm_ * 128:(mm_ + 1) * 128], xTg.ap[:, kc, :], [rgl, xTg], kc == 0, kc == 7)
                            sgb = csg[it % 2]
                            bcol = part * 8 + m
                            E("act", lambda e, sgb=sgb, pl=pl, bcol=bcol: e.activation(
                                out=sgb.ap, in_=pl.ap, func=AF.Sigmoid, bias=cv[:, bcol:bcol + 1], scale=1.0),
                              reads=[pl, cvec], writes=[sgb])
                            if part == 0:
                                E("dve", lambda e, py=py, sgb=sgb, m=m: e.tensor_tensor(
                                    out=hTf[:, m, :], in0=py.ap, in1=sgb.ap, op=ALU.mult), reads=[py, sgb], writes=[hT])
                            else:
                                pc = cpc[m % 2]
                                E("dve", lambda e, py=py, sgb=sgb, pc=pc: e.tensor_tensor(out=pc.ap, in0=py.ap, in1=sgb.ap, op=ALU.mult),
                                  reads=[py, sgb], writes=[pc])
                                E("dve", lambda e, pc=pc, m=m: e.tensor_tensor(out=mT.ap[:, m, :], in0=pc.ap, in1=hTf[:, m, :], op=ALU.add),
                                  reads=[pc, hT], writes=[mT])
                            if hook is not None and it % 4 == 3:
                                hook(it)
                            it += 1

            def mixed(g, with_xT):
                rm = [wl(6), wl(7)]
                for t in range(4):
                    xr = xres[t % 2]
                    srcr = x_d[b, g * 512 + t * 128:g * 512 + (t + 1) * 128, :]
                    E("sp", lambda e, xr=xr, srcr=srcr: e.dma_start(out=xr.ap, in_=srcr), writes=[xr], stream=xr.name)
                    pbi = t % 2
                    for hh in range(2):
                        ob = PBh[pbi * 2 + hh]
                        sm = slab(rm[hh], 8, 512)
                        for kc in range(8):
                            mm(ob.ap, ob, mT.ap[:, kc, t * 128:(t + 1) * 128], sm[:, kc, :], [mT, rm[hh]], kc == 0, kc == 7)
                        E("dve", lambda e, xr=xr, ob=ob, t=t, hh=hh: e.scalar_tensor_tensor(
                            out=z[t].ap[:, hh * 512:(hh + 1) * 512], in0=xr.ap[:, hh * 512:(hh + 1) * 512], scalar=ALPHA,
                            in1=ob.ap, op0=ALU.mult, op1=ALU.add), reads=[xr, ob], writes=[z[t]])
                    if with_xT:
                        xT_tile(t)

            def x1T_norm(t):
                ln_norm_tile(z, 0, t)
                xb1 = x1b[t % 2]
                E("act", lambda e: e.activation(out=xb1.ap, in_=z[t].ap, func=AF.Copy), reads=[z[t]], writes=[xb1])

            def x1T_tr(t):
                xb1 = x1b[t % 2]
                transposes_to(x1T.ap[:, :, t * 128:(t + 1) * 128], x1T, lambda k: xb1.ap[:, k * 128:(k + 1) * 128], xb1, 8, "act")

            def c1_hook(it):
                if it % 4 == 0:
                    x1T_norm(it // 4)
                if it % 4 == 3:
                    x1T_tr(it // 4)

            def ffn(g):
                rfg = None
                rfu = None
                for j in range(NJ):
                    if j % 4 == 0:
                        rfg = wl(8 + 2 * (j // 4))
                        rfu = wl(9 + 2 * (j // 4))
                    sfg = slab(rfg, 8, 512)
                    sfu = slab(rfu, 8, 512)
                    jj = j % 4
                    pg_ = pa_next(True)
                    pu_ = pa_next(True)
                    for kc in range(8):
                        mm(pg_.ap, pg_, sfg[:, kc, jj * 128:(jj + 1) * 128], x1T.ap[:, kc, :], [rfg, x1T], kc == 0, kc == 7)
                    for kc in range(8):
                        mm(pu_.ap, pu_, sfu[:, kc, jj * 128:(jj + 1) * 128], x1T.ap[:, kc, :], [rfu, x1T], kc == 0, kc == 7)
                    sgb = csg[j % 2]
                    E("act", lambda e, sgb=sgb, pg_=pg_: e.activation(out=sgb.ap, in_=pg_.ap, func=AF.Silu), reads=[pg_], writes=[sgb])
                    E("dve", lambda e, sgb=sgb, pu_=pu_, j=j: e.tensor_tensor(out=hT.ap[:, j, :], in0=sgb.ap, in1=pu_.ap, op=ALU.mult),
                      reads=[sgb, pu_], writes=[hT])
                    if j == 8:
                        pT_all()
                        if g < 3:
                            load_p(g + 1)

            def down_ple_out(g):
                for hh in range(2):
                    for sl in range(3):
                        k0 = sl * 8
                        kn = min(8, NJ - k0)
                        rd_ = wl(20 + hh * 3 + sl)
                        sd = slab(rd_, 8, 512)
                        for t in range(4):
                            ob = PBh[t]
                            for kk in range(kn):
                                j = k0 + kk
                                mm(ob.ap, ob, hT.ap[:, j, t * 128:(t + 1) * 128], sd[:, kk, :], [hT, rd_], j == 0, j == NJ - 1)
                    for t in range(4):
                        ob = PBh[t]
                        E("dve", lambda e, ob=ob, t=t, hh=hh: e.scalar_tensor_tensor(
                            out=z[t].ap[:, hh * 512:(hh + 1) * 512], in0=z[t].ap[:, hh * 512:(hh + 1) * 512], scalar=ALPHA,
                            in1=ob.ap, op0=ALU.mult, op1=ALU.add), reads=[ob, z[t]], writes=[z[t]])
                rpl = wl(26)
                spl = slab(rpl, 2, 1024)
                for hh in range(2):
                    rpg = wl(27 + hh)
                    spg = slab(rpg, 8, 512)
                    for t in range(4):
                        ppl = pa_next(True)
                        ppg = pa_next(True)
                        for kc in range(2):
                            mm(ppl.ap, ppl, pT.ap[:, kc, t * 128:(t + 1) * 128], spl[:, kc, hh * 512:(hh + 1) * 512], [pT, rpl], kc == 0, kc == 1)
                        for kc in range(8):
                            mm(ppg.ap, ppg, x1T.ap[:, kc, t * 128:(t + 1) * 128], spg[:, kc, :], [x1T, rpg], kc == 0, False)
                        mm(ppg.ap, ppg, ones_bf.ap[0:1, :], bple.ap[0:1, hh * 512:(hh + 1) * 512], [ones_bf, bple], False, True)
                        sgb = csg[t % 2]
                        E("act", lambda e, sgb=sgb, ppg=ppg: e.activation(out=sgb.ap, in_=ppg.ap, func=AF.Sigmoid), reads=[ppg], writes=[sgb])
                        pc = cpc[t % 2]
                        E("dve", lambda e, pc=pc, sgb=sgb, ppl=ppl: e.tensor_tensor(out=pc.ap, in0=sgb.ap, in1=ppl.ap, op=ALU.mult),
                          reads=[sgb, ppl], writes=[pc])
                        E("dve", lambda e, pc=pc, t=t, hh=hh: e.tensor_tensor(
                            out=z[t].ap[:, hh * 512:(hh + 1) * 512], in0=z[t].ap[:, hh * 512:(hh + 1) * 512], in1=pc.ap, op=ALU.add),
                          reads=[pc, z[t]], writes=[z[t]])
                ln_stats(z)
                for t in range(4):
                    ln_norm_tile(z, 2, t)
                for t in range(4):
                    dsto = out_d[b, g * 512 + t * 128:g * 512 + (t + 1) * 128, :]
                    tok = E("pool", lambda e, t=t, dsto=dsto: e.dma_start(out=dsto, in_=z[t].ap), reads=[z[t]], stream=f"out{t}")
                    fin.append(tok)

            load_x(0)
            load_p(0)
            for j in range(4):
                xT_tile(j)
            load_x(1)
            c1(0)
            for g in range(4):
                mixed(g, g < 3)
                if g + 2 <= 3:
                    load_x(g + 2)
                ln_stats(z)
                if g < 3:
                    c1(g + 1, hook=c1_hook)
                else:
                    x1T_norm(0)
                    for t in range(4):
                        if t + 1 < 4:
                            x1T_norm(t + 1)
                        x1T_tr(t)
                ffn(g)
                down_ple_out(g)
        P.emit(nc, final_wait=fin[-8:] + fin[:2])
    return nc


def _t5_bucket_np(rel):
    n = np.maximum(rel, 0)
    max_exact = 16
    nf = np.maximum(n, 1).astype(np.float32)
    large = max_exact + (np.log(nf / np.float32(max_exact)) / np.float32(np.log(128 / max_exact))
                         * np.float32(32 - max_exact)).astype(np.int32)
    large = np.minimum(large, 31)
    return np.where(n < max_exact, n, large)


_NC_CACHE = {}
_LAST = []


def kernel(x, p, w_in, b_gate, bias_table, w_att_out, conv_w, conv_b, conv_ln_g, conv_ln_b,
           w_conv_out, w_mix_out, ln_mix_g, ln_mix_b, w_ffn_gate, w_ffn_up, w_ffn_down,
           w_ple, w_ple_gate, b_ple_gate, ln_ffn_g, ln_ffn_b):
    f = lambda a: np.ascontiguousarray(np.asarray(a, dtype=np.float32))
    x = f(x)
    p = f(p)
    B = x.shape[0]
    NB = B // NCORES
    cvec = np.zeros((128, NCV), np.float32)
    cvec[:, 0:16] = f(b_gate)[0].reshape(16, 128).T
    cvec[:, 16:20] = f(conv_b)[0].reshape(4, 128).T
    cvec[:, 20:24] = f(conv_ln_g)[0].reshape(4, 128).T
    cvec[:, 24:28] = f(conv_ln_b)[0].reshape(4, 128).T
    cw = f(conv_w)[0]
    cvec[:, 28:152] = cw.T.reshape(4, 128, CW).transpose(1, 0, 2).reshape(128, 4 * CW)
    bt = f(bias_table)
    cvec[:, 152:160] = bt[31][None, :]
    cvec[:, 160] = EPS
    lnv = np.stack([np.broadcast_to(f(v)[0][None, :], (128, D)) for v in (ln_mix_g, ln_mix_b, ln_ffn_g, ln_ffn_b)], axis=1)
    lnv = np.ascontiguousarray(lnv)
    kk = np.arange(128)[:, None]
    cc = np.arange(256)[None, :]
    bidx = _t5_bucket_np(cc - kk)
    toep = np.ascontiguousarray(bt[bidx].transpose(0, 2, 1))
    onehot = (np.arange(S)[None, :] // 256 == np.arange(8)[:, None]).astype(np.float32)
    shared = {
        "w_in": f(w_in)[0], "w_att_out": f(w_att_out)[0], "w_conv_out": f(w_conv_out)[0], "w_mix_out": f(w_mix_out)[0],
        "w_ffn_gate": f(w_ffn_gate)[0], "w_ffn_up": f(w_ffn_up)[0], "w_ffn_down": f(w_ffn_down)[0],
        "w_ple": f(w_ple)[0], "w_ple_gate": f(w_ple_gate)[0], "cvec": cvec, "lnv": lnv,
        "bple": f(b_ple_gate).reshape(1, D), "toep": toep, "onehot": onehot,
    }
    if NB not in _NC_CACHE:
        _NC_CACHE[NB] = build_nc(NB)
    nc = _NC_CACHE[NB]
    in_maps = []
    for c in range(NCORES):
        m = dict(shared)
        m["x"] = x[c * NB:(c + 1) * NB]
        m["p"] = p[0, c * NB:(c + 1) * NB]
        in_maps.append(m)
    res = run_bass_kernel_spmd(nc, in_maps, core_ids=list(range(NCORES)))
    if DBG:
        _LAST.append(res.results[0])
    return np.concatenate([r["out"] for r in res.results], axis=0).astype(np.float32)
```

```python
import contextlib
import numpy as np
import concourse.bass as bass
import concourse.mybir as mybir
from concourse.bass_utils import run_bass_kernel_spmd

F32 = mybir.dt.float32
BF = mybir.dt.bfloat16
AF = mybir.ActivationFunctionType
ALU = mybir.AluOpType
AX = mybir.AxisListType

NCORES = 8
D = 1024
S = 2048
NH = 8
DH = 64
NIN = 4608
FFN = 2816
NJ = FFN // 128
PLE = 256
CW = 31
ALPHA = 2.0 ** 0.25
EPS = 1e-5
BIG = 30000.0
NCV = 161
DBG = False
PE_BIAS = False

ENGS = ("pe", "act", "dve", "pool", "sp")


class Tok:
    __slots__ = ("eng", "idx", "sem", "val", "is_dma", "stream", "key")

    def __init__(self, eng, idx):
        self.eng = eng
        self.idx = idx
        self.sem = None
        self.val = None
        self.is_dma = False
        self.stream = None
        self.key = eng


class Buf:
    def __init__(self, ap, name):
        self.ap = ap
        self.name = name
        self.w = {}
        self.r = {}


class Prog:
    SEM_ROT = 4000

    def __init__(self):
        self.ops = {e: [] for e in ENGS}
        self.streams = {}
        self.need = set()
        self.last = {}

    def op(self, eng, fn, deps=(), stream=None):
        t = Tok(eng, len(self.ops[eng]))
        deps = [d for d in deps if d is not None]
        if stream is not None:
            t.is_dma = True
            t.stream = stream
            t.key = "s:" + stream
            n = self.streams.get(stream, 0) + 1
            self.streams[stream] = n
            t.val = 16 * n
        self.ops[eng].append((t, fn, deps))
        for d in deps:
            if not d.is_dma:
                self.need.add((d.eng, d.idx))
        self.last[t.key] = t
        return t

    def E(self, eng, fn, reads=(), writes=(), stream=None, extra=()):
        deps = {}

        def add(t):
            k = t.key
            o = deps.get(k)
            if o is None or t.idx > o.idx or (t.is_dma and t.val > o.val):
                deps[k] = t

        for t in extra:
            if t is not None:
                add(t)
        for b in reads:
            for t in b.w.values():
                add(t)
        for b in writes:
            for t in b.w.values():
                add(t)
            for t in b.r.values():
                add(t)
        if eng == "pe":
            deps.pop("pe", None)
        tok = self.op(eng, fn, list(deps.values()), stream=stream)
        for b in reads:
            b.r[tok.key] = tok
        for b in writes:
            b.w[tok.key] = tok
        return tok

    def barrier(self, bufs):
        snap = dict(self.last)
        for b in bufs:
            for k, t in snap.items():
                b.r[k] = t
            b.w = {}

    def emit(self, nc, final_wait=()):
        with contextlib.ExitStack() as es:
            nsem = 0
            for e in ENGS:
                cnt = 0
                cur = None
                for t, fn, deps in self.ops[e]:
                    if t.is_dma:
                        continue
                    if (e, t.idx) in self.need:
                        if cur is None or cnt >= self.SEM_ROT:
                            cur = es.enter_context(nc.semaphore(f"c_{e}_{nsem}"))
                            nsem += 1
                            cnt = 0
                        cnt += 1
                        t.sem = cur
                        t.val = cnt
            dsem = {}
            for s in self.streams:
                dsem[s] = es.enter_context(nc.semaphore(f"d_{s}"))
            for e in ENGS:
                for t, fn, deps in self.ops[e]:
                    if t.is_dma:
                        t.sem = dsem[t.stream]
            block = es.enter_context(nc.Block())

            def make(e):
                def body(eng):
                    waited = {}
                    for t, fn, deps in self.ops[e]:
                        for d in deps:
                            k = id(d.sem)
                            if waited.get(k, 0) >= d.val:
                                continue
                            eng.wait_ge(d.sem, d.val)
                            waited[k] = d.val
                        ins = fn(eng)
                        if t.is_dma:
                            ins.then_inc(t.sem, 16)
                        elif t.sem is not None:
                            ins.then_inc(t.sem, 1)
                    if e == "sp":
                        for d in final_wait:
                            eng.wait_ge(d.sem, d.val)

                return body

            block.tensor(make("pe"))
            block.scalar(make("act"))
            block.vector(make("dve"))
            block.gpsimd(make("pool"))
            block.sync(make("sp"))


def build_nc(NB):
    nc = bass.Bass("TRN2", target_bir_lowering=False)
    dr = {}

    def din(name, shape):
        dr[name] = nc.dram_tensor(name, list(shape), F32, kind="ExternalInput").ap()
        return dr[name]

    x_d = din("x", [NB, S, D])
    p_d = din("p", [NB, S, PLE])
    w_in = din("w_in", [D, NIN]).rearrange("(kc p) n -> p kc n", p=128)
    w_ao = din("w_att_out", [512, D]).rearrange("(kc p) n -> p kc n", p=128)
    w_co = din("w_conv_out", [512, D]).rearrange("(kc p) n -> p kc n", p=128)
    w_mix = din("w_mix_out", [D, D]).rearrange("(kc p) n -> p kc n", p=128)
    w_fg = din("w_ffn_gate", [D, FFN]).rearrange("(kc p) n -> p kc n", p=128)
    w_fu = din("w_ffn_up", [D, FFN]).rearrange("(kc p) n -> p kc n", p=128)
    w_fd = din("w_ffn_down", [FFN, D]).rearrange("(kc p) n -> p kc n", p=128)
    w_pl = din("w_ple", [PLE, D]).rearrange("(kc p) n -> p kc n", p=128)
    w_pg = din("w_ple_gate", [D, D]).rearrange("(kc p) n -> p kc n", p=128)
    cvec_d = din("cvec", [128, NCV])
    lnv_d = din("lnv", [128, 4, D])
    bple_d = din("bple", [1, D])
    toep_d = din("toep", [128, NH, 256])
    oneh_d = din("onehot", [8, S])
    out_d = nc.dram_tensor("out", [NB, S, D], F32, kind="ExternalOutput").ap()

    P = Prog()
    E = P.E
    fin = []
    if DBG:
        dbg_att = nc.dram_tensor("dbg_att", [128, 4, S], F32, kind="ExternalOutput").ap()
        dbg_c = nc.dram_tensor("dbg_c", [128, 4, S], F32, kind="ExternalOutput").ap()

    with contextlib.ExitStack() as es:
        def sb(name, shape, dt):
            return es.enter_context(nc.sbuf_tensor("sb_" + name, list(shape), dt))

        def psum(name, shape, dt):
            return es.enter_context(nc.psum_tensor("ps_" + name, list(shape), dt))

        ident = Buf(sb("ident", [128, 128], BF)[:], "ident")
        ones_bf = Buf(sb("ones_bf", [128, 128], BF)[:], "ones")
        Ebf = Buf(sb("Ebf", [128, NH, 256], BF)[:], "Ebf")
        cvec = Buf(sb("cvec", [128, NCV], F32)[:], "cvec")
        negc = Buf(sb("negc", [128, NH], F32)[:], "negc")
        lnv = Buf(sb("lnv", [128, 4, D], F32)[:], "lnv")
        bple = Buf(sb("bple", [1, D], BF)[:], "bple")
        attT_t = sb("attT", [128, 4, S], BF)
        cT_t = sb("cT", [128, 4, S], BF)
        attT = [Buf(attT_t[:, :, g * 512:(g + 1) * 512], f"attT{g}") for g in range(4)]
        cT = [Buf(cT_t[:, c, :], f"cT{c}") for c in range(4)]
        NRING = 6
        ring = [Buf(sb(f"ring{i}", [128, 4096], BF)[:], f"ring{i}") for i in range(NRING)]
        ring_i = [0]

        def ring_next():
            b = ring[ring_i[0] % NRING]
            ring_i[0] += 1
            return b

        def slab(b, kc, n):
            return b.ap[:, 0:kc * n].rearrange("p (k n) -> p k n", k=kc)

        def wload(b, kc, n, src, k0=0, kn=None, n0=0):
            kn = kc - k0 if kn is None else kn
            nn = src.shape[2]
            dst = slab(b, kc, n)[:, k0:k0 + kn, n0:n0 + nn]
            return E("pool", lambda e: e.dma_start(out=dst, in_=src), writes=[b], stream=b.name)

        SD = []
        SD.append((4, 1024, w_ao, 4))
        SD.append((8, 512, w_in[:, :, 2560:3072], 8))
        SD.append((8, 512, w_in[:, :, 3072:3584], 8))
        SD.append((4, 1024, w_co, 4))
        SD.append((8, 512, w_in[:, :, 3584:4096], 8))
        SD.append((8, 512, w_in[:, :, 4096:4608], 8))
        SD.append((8, 512, w_mix[:, :, 0:512], 8))
        SD.append((8, 512, w_mix[:, :, 512:1024], 8))
        for jq in range(6):
            ncol = min(512, FFN - jq * 512)
            SD.append((8, 512, w_fg[:, :, jq * 512:jq * 512 + ncol], 8))
            SD.append((8, 512, w_fu[:, :, jq * 512:jq * 512 + ncol], 8))
        for hh in range(2):
            for sl in range(3):
                k0 = sl * 8
                kn = min(8, NJ - k0)
                SD.append((8, 512, w_fd[:, k0:k0 + kn, hh * 512:(hh + 1) * 512], kn))
        SD.append((2, 1024, w_pl, 2))
        SD.append((8, 512, w_pg[:, :, 0:512], 8))
        SD.append((8, 512, w_pg[:, :, 512:1024], 8))
        NSL = len(SD)
        wscr = nc.dram_tensor("wscr", [NSL, 128, 4096], BF, kind="Internal").ap()
        scrb = [Buf(None, f"scr{i}") for i in range(NSL)]

        def convert_slab(i):
            kc, n, src, kn = SD[i]
            rb = ring_next()
            wload(rb, kc, n, src, kn=kn)
            E("sp", lambda e: e.dma_start(out=wscr[i], in_=rb.ap), reads=[rb], writes=[scrb[i]], stream=f"scr{i}")

        def wl(i):
            rb = ring_next()
            E("sp", lambda e: e.dma_start(out=rb.ap, in_=wscr[i]), reads=[scrb[i]], writes=[rb], stream=rb.name)
            return rb

        NU = 51296
        U = sb("U", [128, NU], BF)
        off = [0]

        def carve(nel, dt):
            nb = nel * (4 if dt == F32 else 2)
            nb = (nb + 63) // 64 * 64
            a = off[0] // 2
            off[0] += nb
            assert off[0] <= NU * 2, ("union overflow", off[0])
            v = U[:, a:a + nb // 2]
            if dt == F32:
                v = v.bitcast(F32)[:, 0:nel]
            else:
                v = v[:, 0:nel]
            return v

        off[0] = 0
        xT_t = carve(8 * S, BF).rearrange("p (k t) -> p k t", k=8)
        xT = [Buf(xT_t[:, :, g * 512:(g + 1) * 512], f"xT{g}") for g in range(4)]
        uT_t = carve(4 * (S + 30), BF).rearrange("p (c t) -> p c t", c=4)
        uT = [Buf(uT_t[:, c, :], f"uT{c}") for c in range(4)]
        qT = Buf(carve(2 * S, BF).rearrange("p (h t) -> p h t", h=2), "qT")
        kT = Buf(carve(2 * S, BF).rearrange("p (h t) -> p h t", h=2), "kT")
        vA = Buf(carve(16 * 2 * 65, BF).rearrange("p (j h d) -> p j h d", j=16, h=2), "vA")
        off_alias = off[0]
        xb = [Buf(carve(2 * D, BF).rearrange("p (j d) -> p j d", j=2), f"xb{i}") for i in range(2)]
        diag = Buf(carve(CW * 128, BF).rearrange("p (j m) -> p j m", j=CW), "diag")
        sgt = [Buf(carve(512, F32), f"sgt{i}") for i in range(2)]
        off_alias_end = off[0]
        PT = [Buf(carve(512, BF), f"PT{i}") for i in range(4)]
        att_tm = [Buf(carve(4 * 128, BF).rearrange("p (s d) -> p s d", s=4), f"atm{i}") for i in range(2)]
        ksum = Buf(carve(8, F32), "ksum")
        kmean = Buf(carve(8, BF), "kmean")
        gs = Buf(carve(128, F32).rearrange("p (a n) -> p a n", a=16), "gs")
        g2 = Buf(carve(128, F32).rearrange("p (a n) -> p a n", a=16), "g2")
        eq = Buf(carve(128, F32).rearrange("p (a n) -> p a n", a=16), "eq")
        mx = Buf(carve(16, F32), "mx")
        mpad = Buf(carve(8 * 2 * 72, BF).rearrange("p (q h n) -> p q h n", q=8, h=2), "mpad")
        rinv = Buf(carve(4, F32), "rinv")
        osb = [Buf(carve(4 * 65, F32).rearrange("p (s d) -> p s d", s=4), f"osb{i}") for i in range(2)]
        AB_END = off[0]
        off[0] = off_alias
        ysq = Buf(carve(4 * 512, BF).rearrange("p (c t) -> p c t", c=4), "ysq")
        mean_t = Buf(carve(512, F32), "mean")
        msq_t = Buf(carve(512, F32), "msq")
        rstd_t = Buf(carve(512, F32), "rstd")
        t1_t = [Buf(carve(512, F32), f"t1_{i}") for i in range(2)]
        assert off[0] <= off_alias_end, (off[0], off_alias_end)
        CLN_BUFS = [ysq, mean_t, msq_t, rstd_t] + t1_t
        AB_BUFS = (xT + uT + [qT, kT, vA] + xb + [diag] + sgt + [ysq, mean_t, msq_t, rstd_t] + t1_t + PT
                   + att_tm + osb + [ksum, kmean, gs, g2, eq, mx, mpad, rinv])

        off[0] = 0
        xres = [Buf(carve(D, F32), f"xres{i}") for i in range(2)]
        z_t = carve(4 * D, F32).rearrange("p (t d) -> p t d", t=4)
        z = [Buf(z_t[:, t, :], f"z{t}") for t in range(4)]
        mT = Buf(carve(8 * 512, BF).rearrange("p (k t) -> p k t", k=8), "mT")
        csg = [Buf(carve(512, F32), f"csg{i}") for i in range(2)]
        cpc = [Buf(carve(512, F32), f"cpc{i}") for i in range(2)]
        x1b = [Buf(carve(D, BF), f"x1b{i}") for i in range(2)]
        x1T = Buf(carve(8 * 512, BF).rearrange("p (k t) -> p k t", k=8), "x1T")
        hT_raw = carve(NJ * 512, BF)
        hT = Buf(hT_raw.rearrange("p (j t) -> p j t", j=NJ), "hT")
        hTf = hT_raw[:, 0:8192].bitcast(F32).rearrange("p (m t) -> p m t", m=8)
        xTg = Buf(carve(8 * 512, BF).rearrange("p (k t) -> p k t", k=8), "xTg")
        xbg = Buf(carve(4 * D, BF).rearrange("p (j d) -> p j d", j=4), "xbg")
        pbg = Buf(carve(4 * PLE, BF).rearrange("p (j d) -> p j d", j=4), "pbg")
        pT = Buf(carve(2 * 512, BF).rearrange("p (k t) -> p k t", k=2), "pT")
        lst_ = [Buf(carve(12, F32).rearrange("p (a b) -> p a b", a=2), f"lst{i}") for i in range(4)]
        lmv_ = [Buf(carve(2, F32), f"lmv{i}") for i in range(4)]
        lsd_ = [Buf(carve(1, F32), f"lsd{i}") for i in range(4)]
        lrs_ = [Buf(carve(1, F32), f"lrs{i}") for i in range(4)]
        lnm_ = [Buf(carve(1, F32), f"lnm{i}") for i in range(4)]
        C_END = off[0]
        C_BUFS = (xres + z + [mT] + csg + cpc + x1b + [x1T, hT, xTg, xbg, pbg, pT] + lst_ + lmv_ + lsd_ + lrs_ + lnm_)

        PA = [Buf(psum(f"pa{i}", [128, 512], F32)[:], f"pa{i}") for i in range(3)]
        PBt = [psum(f"pb{i}", [128, 1024], F32) for i in range(2)]
        PBh = [Buf(PBt[i][:, h * 512:(h + 1) * 512], f"pb{i}{h}") for i in range(2) for h in range(2)]
        PM_t = psum("pm", [128, 1024], BF)
        PM = Buf(PM_t[:], "pm")
        pa_i = [0]

        WIDE = PA + PBh
        pw_i = [0]

        def pa_next(wide=False):
            if wide:
                b = WIDE[pw_i[0] % 7]
                pw_i[0] += 1
                return b
            b = PA[pa_i[0] % 3]
            pa_i[0] += 1
            return b

        E("sp", lambda e: e.dma_start(out=cvec.ap, in_=cvec_d), writes=[cvec], stream="c_cvec")
        E("sp", lambda e: e.dma_start(out=lnv.ap, in_=lnv_d), writes=[lnv], stream="c_lnv")
        E("pool", lambda e: e.dma_start(out=bple.ap, in_=bple_d), writes=[bple], stream="c_bple")
        idf = Buf(U[:, 0:256].bitcast(F32), "idf")
        tf = Buf(U[:, 4096:4096 + NH * 256 * 2].bitcast(F32).rearrange("p (h c) -> p h c", h=NH), "tf")
        E("dve", lambda e: e.memset(idf.ap, 0.0), writes=[idf])
        E("pool", lambda e: e.affine_select(out=idf.ap, in_=idf.ap, pattern=[[-1, 128]], compare_op=ALU.not_equal,
                                            fill=1.0, base=0, channel_multiplier=1), reads=[idf], writes=[idf])
        E("dve", lambda e: e.tensor_copy(out=ident.ap, in_=idf.ap), reads=[idf], writes=[ident])
        E("dve", lambda e: e.memset(ones_bf.ap, 1.0), writes=[ones_bf])
        E("sp", lambda e: e.dma_start(out=tf.ap, in_=toep_d), writes=[tf], stream="c_toep")
        E("dve", lambda e: e.tensor_scalar(out=negc.ap, in0=cvec.ap[:, 152:160], scalar1=-1.0, scalar2=None, op0=ALU.mult),
          reads=[cvec], writes=[negc])
        for h in range(NH):
            if PE_BIAS:
                E("dve", lambda e, h=h: e.tensor_scalar(out=tf.ap[:, h, :], in0=tf.ap[:, h, :], scalar1=negc.ap[:, h:h + 1],
                                                        scalar2=1.0 / (DH ** -0.5), op0=ALU.add, op1=ALU.mult),
                  reads=[tf, negc], writes=[tf])
            else:
                E("act", lambda e, h=h: e.activation(out=tf.ap[:, h, :], in_=tf.ap[:, h, :], func=AF.Exp,
                                                     bias=negc.ap[:, h:h + 1], scale=1.0), reads=[tf, negc], writes=[tf])
        E("pool", lambda e: e.affine_select(out=tf.ap, in_=tf.ap, pattern=[[0, NH], [1, 256]], compare_op=ALU.is_ge,
                                            fill=(-BIG / (DH ** -0.5) if PE_BIAS else 0.0), base=0, channel_multiplier=-1), reads=[tf], writes=[tf])
        E("dve", lambda e: e.tensor_copy(out=Ebf.ap, in_=tf.ap), reads=[tf], writes=[Ebf])

        cv = cvec.ap
        pen = Buf(sb("pen", [128, 16, 8], F32)[:], "pen")
        val = Buf(sb("val", [128, 16, 8], F32)[:], "val")
        E("dve", lambda e: e.memset(pen.ap, 0.0), writes=[pen])
        E("dve", lambda e: e.memset(val.ap, 1.0), writes=[val])
        for qi in range(6):
            bq_ = 4 + qi // 2
            E("dve", lambda e, qi=qi, bq_=bq_: e.memset(pen.ap[:, 2 * qi:2 * qi + 2, bq_:8], -1e30), writes=[pen])
            E("dve", lambda e, qi=qi, bq_=bq_: e.memset(val.ap[:, 2 * qi:2 * qi + 2, bq_:8], 0.0), writes=[val])
        for qi in range(6, 8):
            E("dve", lambda e, qi=qi: e.memset(pen.ap[:, 2 * qi:2 * qi + 2, 7:8], -1e30), writes=[pen])
            E("dve", lambda e, qi=qi: e.memset(val.ap[:, 2 * qi:2 * qi + 2, 7:8], 0.0), writes=[val])

        PM2_v = PA[2].ap.bitcast(BF)

        def transposes_to(dst_ap, dst_buf, src_ap_fn, src_buf, n, evac_eng, alt=False):
            pmv, PMb = (PM2_v, PA[2]) if alt else (PM_t, PM)
            for i in range(n):
                E("pe", lambda e, i=i: e.transpose(out=pmv[:, i * 128:(i + 1) * 128], in_=src_ap_fn(i), identity=ident.ap),
                  reads=[src_buf, ident], writes=[PMb])
            src = pmv[:, 0:n * 128]
            if len(dst_ap.shape) == 3:
                src = src.rearrange("p (k t) -> p k t", k=n)
            if evac_eng == "act":
                return E("act", lambda e: e.activation(out=dst_ap, in_=src, func=AF.Copy), reads=[PMb], writes=[dst_buf])
            return E("dve", lambda e: e.tensor_copy(out=dst_ap, in_=src), reads=[PMb], writes=[dst_buf])

        def ln_stats(zs):
            for ti, zb in enumerate(zs):
                for hh in range(2):
                    E("dve", lambda e, hh=hh, ti=ti, zb=zb: e.bn_stats(out=lst_[ti].ap[:, hh, :], in_=zb.ap[:, hh * 512:(hh + 1) * 512]),
                      reads=[zb], writes=[lst_[ti]])
                E("dve", lambda e, ti=ti: e.bn_aggr(out=lmv_[ti].ap, in_=lst_[ti].ap), reads=[lst_[ti]], writes=[lmv_[ti]])
            for ti, zb in enumerate(zs):
                E("act", lambda e, ti=ti: e.activation(out=lsd_[ti].ap, in_=lmv_[ti].ap[:, 1:2], func=AF.Sqrt, bias=cv[:, 160:161], scale=1.0),
                  reads=[lmv_[ti], cvec], writes=[lsd_[ti]])

        def ln_norm_tile(zs, gi, ti):
            zb = zs[ti]
            E("dve", lambda e: e.reciprocal(out=lrs_[ti].ap, in_=lsd_[ti].ap), reads=[lsd_[ti]], writes=[lrs_[ti]])
            E("dve", lambda e: e.scalar_tensor_tensor(out=zb.ap, in0=zb.ap, scalar=lmv_[ti].ap[:, 0:1], in1=lnv.ap[:, gi, :],
                                                      op0=ALU.subtract, op1=ALU.mult), reads=[zb, lmv_[ti], lnv], writes=[zb])
            E("dve", lambda e: e.scalar_tensor_tensor(out=zb.ap, in0=zb.ap, scalar=lrs_[ti].ap, in1=lnv.ap[:, gi + 1, :],
                                                      op0=ALU.mult, op1=ALU.add), reads=[zb, lrs_[ti], lnv], writes=[zb])

        def mm(out_ap, out_buf, lhsT, rhs, rd, start, stop):
            return E("pe", lambda e: e.matmul(out_ap, lhsT=lhsT, rhs=rhs, start=start, stop=stop),
                     reads=rd, writes=[out_buf])

        for b in range(NB):
            P.barrier(AB_BUFS)
            for c in range(4):
                E("dve", lambda e, c=c: e.memset(uT_t[:, c, 0:30], 0.0), writes=[uT[c]])
            E("dve", lambda e: e.memset(kT.ap[0:64, 1, :], 0.0), writes=[kT])
            E("dve", lambda e: e.memset(qT.ap[0:64, 1, :], 0.0), writes=[qT])
            E("dve", lambda e: e.memset(qT.ap[64:72, 0, :], 0.0), writes=[qT])
            E("dve", lambda e: e.memset(vA.ap[:, :, :, 64:65], 1.0), writes=[vA])
            E("dve", lambda e: e.memset(mpad.ap, 0.0), writes=[mpad])
            E("pool", lambda e: e.dma_start(out=kT.ap[64:72, 0, :], in_=oneh_d), writes=[kT], stream="oh0")
            E("pool", lambda e: e.dma_start(out=kT.ap[0:8, 1, :], in_=oneh_d), writes=[kT], stream="oh1")

            for i in range(8):
                xbb = xb[i % 2]
                src = x_d[b, i * 256:(i + 1) * 256, :].rearrange("(j p) d -> p j d", p=128)
                E("pool", lambda e, xbb=xbb, src=src: e.dma_start(out=xbb.ap, in_=src), writes=[xbb], stream=xbb.name)
                g = i // 2
                for j in range(2):
                    tt = (i % 2) * 2 + j
                    dst = xT[g].ap[:, :, tt * 128:(tt + 1) * 128]
                    transposes_to(dst, xT[g], lambda k, xbb=xbb, j=j: xbb.ap[:, j, k * 128:(k + 1) * 128], xbb, 8,
                                  "act" if j == 0 else "dve", alt=(j == 1))

            ra = ring_next()
            wload(ra, 8, 512, w_in[:, :, 1536:2048])
            rg = ring_next()
            wload(rg, 8, 512, w_in[:, :, 2048:2560])
            sa = slab(ra, 8, 512)
            sg_ = slab(rg, 8, 512)
            gi = 0
            for g in range(4):
                for c in range(4):
                    pa_ = pa_next(True)
                    pg_ = pa_next(True)
                    for kc in range(8):
                        mm(pa_.ap, pa_, sa[:, kc, c * 128:(c + 1) * 128], xT[g].ap[:, kc, :], [ra, xT[g]], kc == 0, kc == 7)
                    for kc in range(8):
                        mm(pg_.ap, pg_, sg_[:, kc, c * 128:(c + 1) * 128], xT[g].ap[:, kc, :], [rg, xT[g]], kc == 0, kc == 7)
                    st = sgt[gi % 2]
                    gi += 1
                    E("act", lambda e, st=st, pg_=pg_: e.activation(out=st.ap, in_=pg_.ap, func=AF.Sigmoid),
                      reads=[pg_], writes=[st])
                    E("dve", lambda e, st=st, pa_=pa_, c=c, g=g: e.tensor_tensor(
                        out=uT_t[:, c, 30 + g * 512:30 + (g + 1) * 512], in0=pa_.ap, in1=st.ap, op=ALU.mult),
                      reads=[pa_, st], writes=[uT[c]])

            def conv_diag(c, part=None):
                for q_ in (range(4) if part is None else [part]):
                    j0 = q_ * 8
                    nj = min(8, CW - j0)
                    E("dve", lambda e, j0=j0, nj=nj: e.tensor_tensor(
                        out=diag.ap[:, j0:j0 + nj, :], in0=ident.ap.unsqueeze(1).to_broadcast([128, nj, 128]),
                        in1=cv[:, 28 + c * CW + j0:28 + c * CW + j0 + nj].unsqueeze(2).to_broadcast([128, nj, 128]), op=ALU.mult),
                      reads=[ident, cvec], writes=[diag])

            def conv_chunk(c):
                for g in range(4):
                    pc_ = pa_next()
                    for j in range(CW):
                        mm(pc_.ap, pc_, diag.ap[:, j, :], uT_t[:, c, g * 512 + j:g * 512 + j + 512], [diag, uT[c]], j == 0, j == CW - 1)
                    E("act", lambda e, pc_=pc_, g=g: e.activation(out=cT_t[:, c, g * 512:(g + 1) * 512], in_=pc_.ap, func=AF.Identity,
                                                                 bias=cv[:, 16 + c:17 + c], scale=1.0), reads=[pc_, cvec], writes=[cT[c]])

            PMf = PM_t[:, :].bitcast(F32)

            def conv_thunks(c):
                th = []
                for g in range(4):
                    for j in range(CW):
                        th.append(("mm", lambda g=g, j=j: mm(PMf, PM, diag.ap[:, j, :], uT_t[:, c, g * 512 + j:g * 512 + j + 512],
                                                           [diag, uT[c]], j == 0, j == CW - 1)))
                    th.append(("end", lambda g=g: E("act", lambda e: e.activation(
                        out=cT_t[:, c, g * 512:(g + 1) * 512], in_=PMf, func=AF.Identity, bias=cv[:, 16 + c:17 + c], scale=1.0),
                        reads=[PM, cvec], writes=[cT[c]])))
                return th

            def conv_ln(g):
                gr = slice(g * 512, (g + 1) * 512)
                for c in range(4):
                    E("dve", lambda e, c=c: e.tensor_tensor(out=ysq.ap[:, c, :], in0=cT_t[:, c, gr], in1=cT_t[:, c, gr], op=ALU.mult),
                      reads=[cT[c]], writes=[ysq])
                s1 = pa_next()
                s2 = pa_next()
                for c in range(4):
                    mm(s1.ap, s1, ones_bf.ap, cT_t[:, c, gr], [ones_bf, cT[c]], c == 0, c == 3)
                for c in range(4):
                    mm(s2.ap, s2, ones_bf.ap, ysq.ap[:, c, :], [ones_bf, ysq], c == 0, c == 3)
                E("dve", lambda e: e.tensor_scalar(out=mean_t.ap, in0=s1.ap, scalar1=1.0 / 512, scalar2=None, op0=ALU.mult),
                  reads=[s1], writes=[mean_t])
                E("dve", lambda e: e.tensor_tensor(out=msq_t.ap, in0=mean_t.ap, in1=mean_t.ap, op=ALU.mult),
                  reads=[mean_t], writes=[msq_t])
                E("dve", lambda e: e.scalar_tensor_tensor(out=msq_t.ap, in0=s2.ap, scalar=1.0 / 512, in1=msq_t.ap,
                                                          op0=ALU.mult, op1=ALU.subtract), reads=[s2, msq_t], writes=[msq_t])
                E("act", lambda e: e.activation(out=rstd_t.ap, in_=msq_t.ap, func=AF.Sqrt, bias=cv[:, 160:161], scale=1.0),
                  reads=[msq_t, cvec], writes=[rstd_t])
                E("dve", lambda e: e.reciprocal(out=rstd_t.ap, in_=rstd_t.ap), reads=[rstd_t], writes=[rstd_t])
                for c in range(4):
                    t1 = t1_t[c % 2]
                    E("dve", lambda e, c=c, t1=t1: e.tensor_tensor(out=t1.ap, in0=cT_t[:, c, gr], in1=mean_t.ap, op=ALU.subtract),
                      reads=[cT[c], mean_t], writes=[t1])
                    E("dve", lambda e, t1=t1: e.tensor_tensor(out=t1.ap, in0=t1.ap, in1=rstd_t.ap, op=ALU.mult),
                      reads=[t1, rstd_t], writes=[t1])
                    E("act", lambda e, c=c, t1=t1: e.activation(out=cT_t[:, c, gr], in_=t1.ap, func=AF.Silu,
                                                               bias=cv[:, 24 + c:25 + c], scale=cv[:, 20 + c:21 + c]),
                      reads=[t1, cvec], writes=[cT[c]])

            conv_diag(0)
            for hp in range(4):
                rq = ring_next()
                wload(rq, 8, 384, w_in[:, :, hp * 128:(hp + 1) * 128], n0=0)
                wload(rq, 8, 384, w_in[:, :, 512 + hp * 128:512 + (hp + 1) * 128], n0=128)
                wload(rq, 8, 384, w_in[:, :, 1024 + hp * 128:1024 + (hp + 1) * 128], n0=256)
                sq = slab(rq, 8, 384)
                for g in range(4):
                    gr = slice(g * 512, (g + 1) * 512)
                    pq = pa_next()
                    for kc in range(8):
                        mm(pq.ap, pq, sq[:, kc, 0:128], xT[g].ap[:, kc, :], [rq, xT[g]], kc == 0, kc == 7)
                    E("dve", lambda e, pq=pq, gr=gr: e.tensor_copy(out=qT.ap[0:64, 0, gr], in_=pq.ap[0:64, :]),
                      reads=[pq], writes=[qT])
                    E("dve", lambda e, pq=pq, gr=gr: e.tensor_copy(out=qT.ap[64:128, 1, gr], in_=pq.ap[64:128, :]),
                      reads=[pq], writes=[qT])
                    pk = pa_next()
                    for kc in range(8):
                        mm(pk.ap, pk, sq[:, kc, 128:256], xT[g].ap[:, kc, :], [rq, xT[g]], kc == 0, kc == 7)
                    E("dve", lambda e, pk=pk, gr=gr: e.tensor_copy(out=kT.ap[0:64, 0, gr], in_=pk.ap[0:64, :]),
                      reads=[pk], writes=[kT])
                    E("dve", lambda e, pk=pk, gr=gr: e.tensor_copy(out=kT.ap[64:128, 1, gr], in_=pk.ap[64:128, :]),
                      reads=[pk], writes=[kT])
                    E("dve", lambda e, pk=pk, g=g: e.tensor_reduce(out=ksum.ap[:, 2 * g:2 * g + 2],
                                                                  in_=pk.ap.rearrange("p (a b) -> p a b", a=2),
                                                                  axis=AX.X, op=ALU.add), reads=[pk], writes=[ksum])
                    pv = pa_next()
                    for j in range(4):
                        for kc in range(8):
                            mm(pv.ap[:, j * 128:(j + 1) * 128], pv, xT[g].ap[:, kc, j * 128:(j + 1) * 128], sq[:, kc, 256:384],
                               [rq, xT[g]], kc == 0, kc == 7)
                    E("act", lambda e, pv=pv, g=g: e.activation(
                        out=vA.ap[:, 4 * g:4 * g + 4, :, 0:64],
                        in_=pv.ap.rearrange("p (j h d) -> p j h d", j=4, h=2), func=AF.Copy), reads=[pv], writes=[vA])
                E("dve", lambda e: e.tensor_scalar(out=kmean.ap, in0=ksum.ap, scalar1=1.0 / 256, scalar2=None, op0=ALU.mult),
                  reads=[ksum], writes=[kmean])
                if b == 0:
                    for i in range(NSL):
                        if i * 4 // NSL == hp:
                            convert_slab(i)
                pg_ = pa_next()
                for qi in range(8):
                    qs = slice((8 + qi) * 128, (9 + qi) * 128)
                    mm(pg_.ap[:, qi * 16:qi * 16 + 8], pg_, qT.ap[0:64, 0, qs], kmean.ap[0:64, :], [qT, kmean], True, True)
                    mm(pg_.ap[:, qi * 16 + 8:qi * 16 + 16], pg_, qT.ap[64:128, 1, qs], kmean.ap[64:128, :], [qT, kmean], True, True)
                pgv = pg_.ap[:, 0:128].rearrange("p (a n) -> p a n", a=16)
                E("dve", lambda e, pgv=pgv: e.tensor_tensor(out=gs.ap, in0=pgv, in1=pen.ap, op=ALU.add), reads=[pg_, pen], writes=[gs])
                if hp == 0:
                    conv_chunk(0)
                cur = gs
                for it in range(3):
                    E("dve", lambda e, cur=cur: e.tensor_reduce(out=mx.ap, in_=cur.ap, axis=AX.X, op=ALU.max), reads=[cur], writes=[mx])
                    if it == 2:
                        break
                    E("dve", lambda e, cur=cur: e.tensor_tensor(out=eq.ap, in0=cur.ap, in1=mx.ap.unsqueeze(2).to_broadcast([128, 16, 8]),
                                                               op=ALU.is_equal), reads=[cur, mx], writes=[eq])
                    E("dve", lambda e, cur=cur: e.scalar_tensor_tensor(out=g2.ap, in0=eq.ap, scalar=-1e30, in1=cur.ap,
                                                                      op0=ALU.mult, op1=ALU.add), reads=[eq, cur], writes=[g2])
                    cur = g2
                E("dve", lambda e: e.tensor_tensor(out=eq.ap, in0=gs.ap, in1=mx.ap.unsqueeze(2).to_broadcast([128, 16, 8]), op=ALU.is_ge),
                  reads=[gs, mx], writes=[eq])
                E("dve", lambda e: e.tensor_scalar(out=g2.ap, in0=eq.ap, scalar1=-1.0, scalar2=BIG, op0=ALU.add, op1=ALU.mult),
                  reads=[eq], writes=[g2])
                g2v = g2.ap.rearrange("p (q h) n -> p q h n", h=2)
                valv = val.ap.rearrange("p (q h) n -> p q h n", h=2)
                E("dve", lambda e: e.tensor_tensor(out=mpad.ap[:, :, 0, 64:72], in0=g2v[:, :, 0, :], in1=valv[:, :, 0, :], op=ALU.mult),
                  reads=[g2, val], writes=[mpad])
                E("dve", lambda e: e.tensor_tensor(out=mpad.ap[:, :, 1, 0:8], in0=g2v[:, :, 1, :], in1=valv[:, :, 1, :], op=ALU.mult),
                  reads=[g2, val], writes=[mpad])
                for qi in range(8):
                    mm(PBt[0][0:72, qi * 128:(qi + 1) * 128], PBh[qi // 4], mpad.ap[:, qi, 0, 0:72], ident.ap, [mpad, ident], True, True)
                for qi in range(8):
                    mm(PBt[1][0:8, qi * 128:(qi + 1) * 128], PBh[2 + qi // 4], mpad.ap[:, qi, 1, 0:8], ident.ap, [mpad, ident], True, True)
                E("act", lambda e: e.activation(out=qT.ap[64:72, 0, 1024:2048], in_=PBt[0][64:72, :], func=AF.Copy),
                  reads=[PBh[0], PBh[1]], writes=[qT])
                E("act", lambda e: e.activation(out=qT.ap[0:8, 1, 1024:2048], in_=PBt[1][0:8, :], func=AF.Copy),
                  reads=[PBh[2], PBh[3]], writes=[qT])
                cth = []
                it_n = 0
                its = [(G, h, j) for G in range(4) for h in range(2) for j in range(4 * G + 4)]
                LA = 2
                pts = {}

                def emit_scores(k):
                    G, h, j = its[k]
                    rows = slice(0, 72) if h == 0 else slice(0, 128)
                    d = j - 4 * G
                    c0 = max(0, d) * 128
                    N = 512 - c0
                    ps_ = pa_next()
                    hh = 2 * hp + h
                    mm(ps_.ap[:, 0:N], ps_, kT.ap[rows, h, j * 128:(j + 1) * 128],
                       qT.ap[rows, h, G * 512 + c0:(G + 1) * 512], [kT, qT], True, True)
                    pt = PT[k % 4]
                    pts[k] = pt
                    E("act", lambda e: e.activation(out=pt.ap[:, 0:N], in_=ps_.ap[:, 0:N], func=AF.Exp, scale=DH ** -0.5),
                      reads=[ps_], writes=[pt])
                    if d >= 0:
                        w_ = min(256, N)
                        E("dve", lambda e: e.tensor_tensor(out=pt.ap[:, 0:w_], in0=pt.ap[:, 0:w_], in1=Ebf.ap[:, hh, 0:w_], op=ALU.mult),
                          reads=[pt, Ebf], writes=[pt])
                    elif d == -1:
                        E("dve", lambda e: e.tensor_tensor(out=pt.ap[:, 0:128], in0=pt.ap[:, 0:128], in1=Ebf.ap[:, hh, 128:256], op=ALU.mult),
                          reads=[pt, Ebf], writes=[pt])

                def emit_pv(k):
                    G, h, j = its[k]
                    d = j - 4 * G
                    c0 = max(0, d) * 128
                    pt = pts.pop(k)
                    atm = att_tm[G % 2]
                    for s_ in range(max(0, d), 4):
                        lc = s_ * 128 - c0
                        mm(PBh[s_].ap[:, 0:65], PBh[s_], pt.ap[:, lc:lc + 128], vA.ap[:, j, h, :], [pt, vA], j == 0, j == 4 * G + s_)
                    if j == 4 * G + 3:
                        ob_ = osb[(2 * G + h) % 2]
                        for half in range(2):
                            srcv = PBt[half][:, :].rearrange("p (s c) -> p s c", s=2)[:, :, 0:65]
                            E("dve", lambda e, half=half, srcv=srcv: e.tensor_copy(out=ob_.ap[:, 2 * half:2 * half + 2, :], in_=srcv),
                              reads=[PBh[2 * half], PBh[2 * half + 1]], writes=[ob_])
                        E("dve", lambda e: e.reciprocal(out=rinv.ap, in_=ob_.ap[:, :, 64]), reads=[ob_], writes=[rinv])
                        E("dve", lambda e: e.tensor_tensor(out=atm.ap[:, :, h * 64:(h + 1) * 64], in0=ob_.ap[:, :, 0:64],
                                                           in1=rinv.ap.unsqueeze(2).to_broadcast([128, 4, 64]), op=ALU.mult),
                          reads=[ob_, rinv], writes=[atm])
                        if h == 1:
                            conv_flush()
                            transposes_to(attT_t[:, hp, G * 512:(G + 1) * 512], attT[G], lambda s_: atm.ap[:, s_, :], atm, 4, "dve")

                cq = conv_thunks(hp + 1) if hp < 3 else []
                cstate = {"open": False}

                def conv_pop(n):
                    for _ in range(n):
                        if not cq:
                            return
                        kind, fn_ = cq.pop(0)
                        fn_()
                        cstate["open"] = (kind == "mm")

                def conv_flush():
                    while cstate["open"]:
                        conv_pop(1)

                if hp == 3:
                    P.barrier(CLN_BUFS)
                for k in range(len(its) + LA):
                    if k < len(its):
                        emit_scores(k)
                    if k - LA >= 0:
                        emit_pv(k - LA)
                    if hp == 3 and k in (16, 32, 48, 64):
                        conv_ln(k // 16 - 1)
                    if hp < 3 and k < 4:
                        conv_diag(hp + 1, k)
                    if hp < 3 and k >= 10:
                        conv_pop(2)
                while cq:
                    conv_pop(1)

            if DBG and b == 0:
                fin.append(E("pool", lambda e: e.dma_start(out=dbg_att, in_=attT_t[:]), reads=attT, stream="dbg1"))
                fin.append(E("pool", lambda e: e.dma_start(out=dbg_c, in_=cT_t[:]), reads=cT, stream="dbg2"))
            P.barrier(C_BUFS)

            def load_x(gg):
                srcx = x_d[b, gg * 512:(gg + 1) * 512, :].rearrange("(j p) d -> p j d", p=128)
                E("pool", lambda e: e.dma_start(out=xbg.ap, in_=srcx), writes=[xbg], stream="xbg")

            def load_p(gg):
                srcp = p_d[b, gg * 512:(gg + 1) * 512, :].rearrange("(j p) d -> p j d", p=128)
                E("pool", lambda e: e.dma_start(out=pbg.ap, in_=srcp), writes=[pbg], stream="pbg")

            def xT_tile(j):
                transposes_to(xTg.ap[:, :, j * 128:(j + 1) * 128], xTg, lambda k: xbg.ap[:, j, k * 128:(k + 1) * 128],
                              xbg, 8, "act")

            def pT_all():
                for j in range(4):
                    for k in range(2):
                        E("pe", lambda e, j=j, k=k: e.transpose(out=PM_t[:, (k * 4 + j) * 128:(k * 4 + j + 1) * 128],
                                                                in_=pbg.ap[:, j, k * 128:(k + 1) * 128], identity=ident.ap),
                          reads=[pbg, ident], writes=[PM])
                E("act", lambda e: e.activation(out=pT.ap, in_=PM_t[:, 0:1024].rearrange("p (k t) -> p k t", k=2), func=AF.Copy),
                  reads=[PM], writes=[pT])

            def c1(g, hook=None):
                gr = slice(g * 512, (g + 1) * 512)
                rao = wl(0)
                sao = slab(rao, 4, 1024)
                it = 0
                for part in range(2):
                    if part == 1:
                        rco = wl(3)
                        sco = slab(rco, 4, 1024)
                    for mh in range(2):
                        rgl = wl([1, 2, 4, 5][part * 2 + mh])
                        sgl = slab(rgl, 8, 512)
                        for mm_ in range(4):
                            m = mh * 4 + mm_
                            if hook is not None and it % 4 == 0:
                                hook(it)
                            py = pa_next(True)
                            pl = pa_next(True)
                            for kc in range(4):
                                if part == 0:
                                    mm(py.ap, py, sao[:, kc, m * 128:(m + 1) * 128], attT[g].ap[:, kc, :], [rao, attT[g]], kc == 0, kc == 3)
                                else:
                                    mm(py.ap, py, sco[:, kc, m * 128:(m + 1) * 128], cT_t[:, kc, gr], [rco, cT[kc]], kc == 0, kc == 3)
                            for kc in range(8):
                                mm(pl.ap, pl, sgl[:, kc, mm_ * 128:(mm_ + 1) * 128], xTg.ap[:, kc, :], [rgl, xTg], kc == 0, kc == 7)
                            sgb = csg[it % 2]
                            bcol = part * 8 + m
                            E("act", lambda e, sgb=sgb, pl=pl, bcol=bcol: e.activation(
                                out=sgb.ap, in_=pl.ap, func=AF.Sigmoid, bias=cv[:, bcol:bcol + 1], scale=1.0),
                              reads=[pl, cvec], writes=[sgb])
                            if part == 0:
                                E("dve", lambda e, py=py, sgb=sgb, m=m: e.tensor_tensor(
                                    out=hTf[:, m, :], in0=py.ap, in1=sgb.ap, op=ALU.mult), reads=[py, sgb], writes=[hT])
                            else:
                                pc = cpc[m % 2]
                                E("dve", lambda e, py=py, sgb=sgb, pc=pc: e.tensor_tensor(out=pc.ap, in0=py.ap, in1=sgb.ap, op=ALU.mult),
                                  reads=[py, sgb], writes=[pc])
                                E("dve", lambda e, pc=pc, m=m: e.tensor_tensor(out=mT.ap[:, m, :], in0=pc.ap, in1=hTf[:, m, :], op=ALU.add),
                                  reads=[pc, hT], writes=[mT])
                            if hook is not None and it % 4 == 3:
                                hook(it)
                            it += 1

            def mixed(g, with_xT):
                rm = [wl(6), wl(7)]
                for t in range(4):
                    xr = xres[t % 2]
                    srcr = x_d[b, g * 512 + t * 128:g * 512 + (t + 1) * 128, :]
                    E("sp", lambda e, xr=xr, srcr=srcr: e.dma_start(out=xr.ap, in_=srcr), writes=[xr], stream=xr.name)
                    pbi = t % 2
                    for hh in range(2):
                        ob = PBh[pbi * 2 + hh]
                        sm = slab(rm[hh], 8, 512)
                        for kc in range(8):
                            mm(ob.ap, ob, mT.ap[:, kc, t * 128:(t + 1) * 128], sm[:, kc, :], [mT, rm[hh]], kc == 0, kc == 7)
                        E("dve", lambda e, xr=xr, ob=ob, t=t, hh=hh: e.scalar_tensor_tensor(
                            out=z[t].ap[:, hh * 512:(hh + 1) * 512], in0=xr.ap[:, hh * 512:(hh + 1) * 512], scalar=ALPHA,
                            in1=ob.ap, op0=ALU.mult, op1=ALU.add), reads=[xr, ob], writes=[z[t]])
                    if with_xT:
                        xT_tile(t)

            def x1T_norm(t):
                ln_norm_tile(z, 0, t)
                xb1 = x1b[t % 2]
                E("act", lambda e: e.activation(out=xb1.ap, in_=z[t].ap, func=AF.Copy), reads=[z[t]], writes=[xb1])

            def x1T_tr(t):
                xb1 = x1b[t % 2]
                transposes_to(x1T.ap[:, :, t * 128:(t + 1) * 128], x1T, lambda k: xb1.ap[:, k * 128:(k + 1) * 128], xb1, 8, "act")

            def c1_hook(it):
                if it % 4 == 0:
                    x1T_norm(it // 4)
                if it % 4 == 3:
                    x1T_tr(it // 4)

            def ffn(g):
                rfg = None
                rfu = None
                for j in range(NJ):
                    if j % 4 == 0:
                        rfg = wl(8 + 2 * (j // 4))
                        rfu = wl(9 + 2 * (j // 4))
                    sfg = slab(rfg, 8, 512)
                    sfu = slab(rfu, 8, 512)
                    jj = j % 4
                    pg_ = pa_next(True)
                    pu_ = pa_next(True)
                    for kc in range(8):
                        mm(pg_.ap, pg_, sfg[:, kc, jj * 128:(jj + 1) * 128], x1T.ap[:, kc, :], [rfg, x1T], kc == 0, kc == 7)
                    for kc in range(8):
                        mm(pu_.ap, pu_, sfu[:, kc, jj * 128:(jj + 1) * 128], x1T.ap[:, kc, :], [rfu, x1T], kc == 0, kc == 7)
                    sgb = csg[j % 2]
                    E("act", lambda e, sgb=sgb, pg_=pg_: e.activation(out=sgb.ap, in_=pg_.ap, func=AF.Silu), reads=[pg_], writes=[sgb])
                    E("dve", lambda e, sgb=sgb, pu_=pu_, j=j: e.tensor_tensor(out=hT.ap[:, j, :], in0=sgb.ap, in1=pu_.ap, op=ALU.mult),
                      reads=[sgb, pu_], writes=[hT])
                    if j == 8:
                        pT_all()
                        if g < 3:
                            load_p(g + 1)

            def down_ple_out(g):
                for hh in range(2):
                    for sl in range(3):
                        k0 = sl * 8
                        kn = min(8, NJ - k0)
                        rd_ = wl(20 + hh * 3 + sl)
                        sd = slab(rd_, 8, 512)
                        for t in range(4):
                            ob = PBh[t]
                            for kk in range(kn):
                                j = k0 + kk
                                mm(ob.ap, ob, hT.ap[:, j, t * 128:(t + 1) * 128], sd[:, kk, :], [hT, rd_], j == 0, j == NJ - 1)
                    for t in range(4):
                        ob = PBh[t]
                        E("dve", lambda e, ob=ob, t=t, hh=hh: e.scalar_tensor_tensor(
                            out=z[t].ap[:, hh * 512:(hh + 1) * 512], in0=z[t].ap[:, hh * 512:(hh + 1) * 512], scalar=ALPHA,
                            in1=ob.ap, op0=ALU.mult, op1=ALU.add), reads=[ob, z[t]], writes=[z[t]])
                rpl = wl(26)
                spl = slab(rpl, 2, 1024)
                for hh in range(2):
                    rpg = wl(27 + hh)
                    spg = slab(rpg, 8, 512)
                    for t in range(4):
                        ppl = pa_next(True)
                        ppg = pa_next(True)
                        for kc in range(2):
                            mm(ppl.ap, ppl, pT.ap[:, kc, t * 128:(t + 1) * 128], spl[:, kc, hh * 512:(hh + 1) * 512], [pT, rpl], kc == 0, kc == 1)
                        for kc in range(8):
                            mm(ppg.ap, ppg, x1T.ap[:, kc, t * 128:(t + 1) * 128], spg[:, kc, :], [x1T, rpg], kc == 0, False)
                        mm(ppg.ap, ppg, ones_bf.ap[0:1, :], bple.ap[0:1, hh * 512:(hh + 1) * 512], [ones_bf, bple], False, True)
                        sgb = csg[t % 2]
                        E("act", lambda e, sgb=sgb, ppg=ppg: e.activation(out=sgb.ap, in_=ppg.ap, func=AF.Sigmoid), reads=[ppg], writes=[sgb])
                        pc = cpc[t % 2]
                        E("dve", lambda e, pc=pc, sgb=sgb, ppl=ppl: e.tensor_tensor(out=pc.ap, in0=sgb.ap, in1=ppl.ap, op=ALU.mult),
                          reads=[sgb, ppl], writes=[pc])
                        E("dve", lambda e, pc=pc, t=t, hh=hh: e.tensor_tensor(
                            out=z[t].ap[:, hh * 512:(hh + 1) * 512], in0=z[t].ap[:, hh * 512:(hh + 1) * 512], in1=pc.ap, op=ALU.add),
                          reads=[pc, z[t]], writes=[z[t]])
                ln_stats(z)
                for t in range(4):
                    ln_norm_tile(z, 2, t)
                for t in range(4):
                    dsto = out_d[b, g * 512 + t * 128:g * 512 + (t + 1) * 128, :]
                    tok = E("pool", lambda e, t=t, dsto=dsto: e.dma_start(out=dsto, in_=z[t].ap), reads=[z[t]], stream=f"out{t}")
                    fin.append(tok)

            load_x(0)
            load_p(0)
            for j in range(4):
                xT_tile(j)
            load_x(1)
            c1(0)
            for g in range(4):
                mixed(g, g < 3)
                if g + 2 <= 3:
                    load_x(g + 2)
                ln_stats(z)
                if g < 3:
                    c1(g + 1, hook=c1_hook)
                else:
                    x1T_norm(0)
                    for t in range(4):
                        if t + 1 < 4:
                            x1T_norm(t + 1)
                        x1T_tr(t)
                ffn(g)
                down_ple_out(g)
        P.emit(nc, final_wait=fin[-8:] + fin[:2])
    return nc


def _t5_bucket_np(rel):
    n = np.maximum(rel, 0)
    max_exact = 16
    nf = np.maximum(n, 1).astype(np.float32)
    large = max_exact + (np.log(nf / np.float32(max_exact)) / np.float32(np.log(128 / max_exact))
                         * np.float32(32 - max_exact)).astype(np.int32)
    large = np.minimum(large, 31)
    return np.where(n < max_exact, n, large)


_NC_CACHE = {}
_LAST = []


def kernel(x, p, w_in, b_gate, bias_table, w_att_out, conv_w, conv_b, conv_ln_g, conv_ln_b,
           w_conv_out, w_mix_out, ln_mix_g, ln_mix_b, w_ffn_gate, w_ffn_up, w_ffn_down,
           w_ple, w_ple_gate, b_ple_gate, ln_ffn_g, ln_ffn_b):
    f = lambda a: np.ascontiguousarray(np.asarray(a, dtype=np.float32))
    x = f(x)
    p = f(p)
    B = x.shape[0]
    NB = B // NCORES
    cvec = np.zeros((128, NCV), np.float32)
    cvec[:, 0:16] = f(b_gate)[0].reshape(16, 128).T
    cvec[:, 16:20] = f(conv_b)[0].reshape(4, 128).T
    cvec[:, 20:24] = f(conv_ln_g)[0].reshape(4, 128).T
    cvec[:, 24:28] = f(conv_ln_b)[0].reshape(4, 128).T
    cw = f(conv_w)[0]
    cvec[:, 28:152] = cw.T.reshape(4, 128, CW).transpose(1, 0, 2).reshape(128, 4 * CW)
    bt = f(bias_table)
    cvec[:, 152:160] = bt[31][None, :]
    cvec[:, 160] = EPS
    lnv = np.stack([np.broadcast_to(f(v)[0][None, :], (128, D)) for v in (ln_mix_g, ln_mix_b, ln_ffn_g, ln_ffn_b)], axis=1)
    lnv = np.ascontiguousarray(lnv)
    kk = np.arange(128)[:, None]
    cc = np.arange(256)[None, :]
    bidx = _t5_bucket_np(cc - kk)
    toep = np.ascontiguousarray(bt[bidx].transpose(0, 2, 1))
    onehot = (np.arange(S)[None, :] // 256 == np.arange(8)[:, None]).astype(np.float32)
    shared = {
        "w_in": f(w_in)[0], "w_att_out": f(w_att_out)[0], "w_conv_out": f(w_conv_out)[0], "w_mix_out": f(w_mix_out)[0],
        "w_ffn_gate": f(w_ffn_gate)[0], "w_ffn_up": f(w_ffn_up)[0], "w_ffn_down": f(w_ffn_down)[0],
        "w_ple": f(w_ple)[0], "w_ple_gate": f(w_ple_gate)[0], "cvec": cvec, "lnv": lnv,
        "bple": f(b_ple_gate).reshape(1, D), "toep": toep, "onehot": onehot,
    }
    if NB not in _NC_CACHE:
        _NC_CACHE[NB] = build_nc(NB)
    nc = _NC_CACHE[NB]
    in_maps = []
    for c in range(NCORES):
        m = dict(shared)
        m["x"] = x[c * NB:(c + 1) * NB]
        m["p"] = p[0, c * NB:(c + 1) * NB]
        in_maps.append(m)
    res = run_bass_kernel_spmd(nc, in_maps, core_ids=list(range(NCORES)))
    if DBG:
        _LAST.append(res.results[0])
    return np.concatenate([r["out"] for r in res.results], axis=0).astype(np.float32)
```
